# Optimizing a Trainium2 kernel written in Bass

```python
import math
import jax
import jax.numpy as jnp
from jax import lax
import numpy as np

D_MODEL = 1024
BATCH = 32
SEQ = 2048
DEPTH = 2
DEC_BATCH = 16
DEC_SEQ = 32
PAST_LEN = 1024

CHUNK = 64
N_LEFT_CHUNKS = 8
N_BAND = N_LEFT_CHUNKS + 1
BAND_ROWS = N_LEFT_CHUNKS * CHUNK
HEAD_DIM = 64
N_HEADS_A = 8
W_A = N_HEADS_A * HEAD_DIM
REL_CLIP = 128
N_HEADS_B = 8
W_B = N_HEADS_B * HEAD_DIM
W_C = 512
N_BLOCKS_C = 8
BLOCK_C = W_C // N_BLOCKS_C
CONV_W = 4
LRU_C = 8.0
PLE_DIM = 256
N_BRANCH = 3
ROPE_BASE = 10000.0
ALPHA = (2 * DEPTH) ** 0.25
BETA = (8 * DEPTH) ** -0.25
LN_EPS = 1e-5
NEG_INF = -1e30
SPLITS = [W_A] * 4 + [W_B] * 4 + [W_C] * 2 + [D_MODEL] * 3
N_IN = sum(SPLITS)

kernel_name = 'hybrid_chunk_stream_encoder_step'


def split_cols(u):
    out = []
    off = 0
    for w in SPLITS:
        out.append(u[..., off:off + w])
        off += w
    return out


def layer_norm(x, g, b):
    xf = x.astype(jnp.float32)
    mu = jnp.mean(xf, -1, keepdims=True)
    var = jnp.mean(jnp.square(xf - mu), -1, keepdims=True)
    y = (xf - mu) * lax.rsqrt(var + LN_EPS) * g.astype(jnp.float32) + b.astype(jnp.float32)
    return y.astype(x.dtype)


def group_norm_heads(y, g):
    b, t, h, d = y.shape
    yf = y.astype(jnp.float32)
    mu = jnp.mean(yf, -1, keepdims=True)
    var = jnp.mean(jnp.square(yf - mu), -1, keepdims=True)
    out = ((yf - mu) * lax.rsqrt(var + LN_EPS)).reshape(b, t, h * d) * g.astype(jnp.float32)
    return out.astype(y.dtype)


def rope(x, pos):
    half = HEAD_DIM // 2
    inv = ROPE_BASE ** (-jnp.arange(half, dtype=jnp.float32) / half)
    ang = pos.astype(jnp.float32)[:, None] * inv[None, :]
    c = jnp.cos(ang)[:, None, :]
    s = jnp.sin(ang)[:, None, :]
    xf = x.astype(jnp.float32)
    x1, x2 = xf[..., :half], xf[..., half:]
    return jnp.concatenate([x1 * c - x2 * s, x1 * s + x2 * c], -1).astype(x.dtype)


def rel_bias(table, qpos, kpos):
    idx = jnp.clip(qpos[:, None] - kpos[None, :], -REL_CLIP, REL_CLIP) + REL_CLIP
    return table.astype(jnp.float32)[:, idx]


def band_attention_prompt(q, k, v, table):
    b, s, h, d = q.shape
    nc = s // CHUNK
    qc = q.reshape(b, nc, CHUNK, h, d) * (d ** -0.5)
    pad = ((0, 0), (N_LEFT_CHUNKS, 0), (0, 0), (0, 0), (0, 0))
    kp = jnp.pad(k.reshape(b, nc, CHUNK, h, d), pad)
    vp = jnp.pad(v.reshape(b, nc, CHUNK, h, d), pad)
    scores = jnp.concatenate(
        [jnp.einsum('bcqhd,bckhd->bhcqk', qc, kp[:, o:o + nc]) for o in range(N_BAND)],
        axis=-1).astype(jnp.float32)
    qpos = N_LEFT_CHUNKS * CHUNK + jnp.arange(CHUNK)
    kpos = jnp.arange(N_BAND * CHUNK)
    scores = scores + rel_bias(table, qpos, kpos)[None, :, None]
    valid = (jnp.arange(nc)[:, None] + jnp.arange(N_BAND)[None, :]) >= N_LEFT_CHUNKS
    valid = jnp.repeat(valid, CHUNK, axis=1)
    scores = jnp.where(valid[None, None, :, None, :], scores, NEG_INF)
    probs = jax.nn.softmax(scores, axis=-1).astype(v.dtype)
    out = sum(jnp.einsum('bhcqk,bckhd->bcqhd', probs[..., o * CHUNK:(o + 1) * CHUNK], vp[:, o:o + nc])
              for o in range(N_BAND))
    return out.reshape(b, s, h, d)


def band_attention_sample(q, k, v, k_cache, v_cache, table):
    t = q.shape[1]
    c = k_cache.shape[1]
    d = q.shape[-1]
    keys = jnp.concatenate([k_cache.astype(k.dtype), k], axis=1)
    vals = jnp.concatenate([v_cache.astype(v.dtype), v], axis=1)
    qpos = PAST_LEN + jnp.arange(t)
    kpos = PAST_LEN - c + jnp.arange(c + t)
    scores = jnp.einsum('bqhd,bkhd->bhqk', q * (d ** -0.5), keys).astype(jnp.float32)
    scores = scores + rel_bias(table, qpos, kpos)[None]
    probs = jax.nn.softmax(scores, axis=-1).astype(v.dtype)
    return jnp.einsum('bhqk,bkhd->bqhd', probs, vals)


def retention(q, k, v, s0, blk):
    dt = q.dtype
    b, t, h, d = q.shape
    n = t // blk
    q = q.astype(jnp.float32).reshape(b, n, blk, h, d)
    k = k.astype(jnp.float32).reshape(b, n, blk, h, d)
    v = v.astype(jnp.float32).reshape(b, n, blk, h, d)
    log_g = jnp.log1p(-jnp.exp2(-5.0 - jnp.arange(h, dtype=jnp.float32)))
    i = jnp.arange(blk, dtype=jnp.float32)
    diff = i[:, None] - i[None, :]
    decay = jnp.where(diff >= 0, jnp.exp(log_g[:, None, None] * jnp.maximum(diff, 0.0)), 0.0)
    inner = jnp.einsum('bnihd,bnjhd->bnhij', q, k) * decay
    y_in = jnp.einsum('bnhij,bnjhe->bnihe', inner, v)
    zeta = jnp.exp(log_g[:, None] * (blk - 1.0 - i)[None, :])
    kv = jnp.einsum('bnjhd,bnjhe,hj->nbhde', k, v, zeta)
    g_blk = jnp.exp(log_g * blk)[None, :, None, None]

    def step(state, kv_n):
        return g_blk * state + kv_n, state

    s_final, s_prev = lax.scan(step, s0.astype(jnp.float32), kv)
    xi = jnp.exp(log_g[:, None] * (i + 1.0)[None, :])
    y_x = jnp.einsum('bnihd,nbhde,hi->bnihe', q, s_prev, xi)
    return (y_in + y_x).reshape(b, t, h, d).astype(dt), s_final.astype(s0.dtype)


def _lin_combine(e1, e2):
    a1, b1 = e1
    a2, b2 = e2
    return a1 * a2, a2 * b1 + b2


def rg_lru(xr, s_conv, s_lru, conv_w, conv_b, w_gate_a, b_gate_a, w_gate_x, b_gate_x, lru_lambda):
    b, t, w = xr.shape
    xpad = jnp.concatenate([s_conv.astype(xr.dtype), xr], axis=1)
    xc = conv_b + sum(xpad[:, j:j + t] * conv_w[j] for j in range(CONV_W))
    new_conv = xpad[:, t:].astype(s_conv.dtype)
    xb = xc.reshape(b, t, N_BLOCKS_C, BLOCK_C)
    r = jax.nn.sigmoid((jnp.einsum('btni,nij->btnj', xb, w_gate_a).reshape(b, t, w) + b_gate_a).astype(jnp.float32))
    ig = jax.nn.sigmoid((jnp.einsum('btni,nij->btnj', xb, w_gate_x).reshape(b, t, w) + b_gate_x).astype(jnp.float32))
    log_a = -LRU_C * r * jax.nn.softplus(-lru_lambda.astype(jnp.float32))
    a = jnp.exp(log_a)
    bx = jnp.sqrt(-jnp.expm1(2.0 * log_a)) * ig * xc.astype(jnp.float32)
    acc_a, acc_b = lax.associative_scan(_lin_combine, (a, bx), axis=1)
    h = acc_a * s_lru.astype(jnp.float32)[:, None, :] + acc_b
    return h.astype(xr.dtype), h[:, -1].astype(s_lru.dtype), new_conv


def layer(x, pe, pos, w_in, rel_table, gn_gain, conv_w, conv_b, w_gate_a, b_gate_a, w_gate_x, b_gate_x,
          lru_lambda, w_branch, w_out, ln_gain, ln_bias, w_ple, w_ple_gate,
          k_cache, v_cache, s_ret, s_conv, s_lru):
    b, t, _ = x.shape
    u = x @ w_in
    qa, ka, va, za, qb, kb, vb, zb, xr, zc, ga, gb, gc = split_cols(u)

    def heads(z):
        return z.reshape(b, t, -1, HEAD_DIM)

    qa, ka, va = heads(qa), heads(ka), heads(va)
    if k_cache is None:
        ya = band_attention_prompt(qa, ka, va, rel_table)
        rows = min(BAND_ROWS, t)
        new_k, new_v = ka[:, t - rows:], va[:, t - rows:]
        ret_blk = CHUNK
    else:
        ya = band_attention_sample(qa, ka, va, k_cache, v_cache, rel_table)
        new_k, new_v = ka, va
        ret_blk = t
    ya = ya.reshape(b, t, W_A) * jax.nn.silu(za)

    qb = rope(heads(qb), pos)
    kb = rope(heads(kb), pos) * (HEAD_DIM ** -0.5)
    yb, new_ret = retention(qb, kb, heads(vb), s_ret, ret_blk)
    yb = group_norm_heads(yb, gn_gain) * jax.nn.silu(zb)

    yc, new_lru, new_conv = rg_lru(xr, s_conv, s_lru, conv_w, conv_b, w_gate_a, b_gate_a,
                                   w_gate_x, b_gate_x, lru_lambda)
    yc = yc * jax.nn.silu(zc)

    merged = (jax.nn.sigmoid(ga) * (ya @ w_branch[0])
              + jax.nn.sigmoid(gb) * (yb @ w_branch[1])
              + jax.nn.sigmoid(gc) * (yc @ w_branch[2]))
    r = ALPHA * x + merged @ w_out
    r = r + jax.nn.sigmoid(r @ w_ple_gate) * (pe @ w_ple)
    return layer_norm(r, ln_gain, ln_bias), (new_k, new_v, new_ret, new_conv, new_lru)


def setup_inputs(seed: int = 0) -> dict:
    key = jax.random.key(seed)
    ks = iter(jax.random.split(key, 32))

    def nrm(shape, scale):
        return jax.random.normal(next(ks), shape, jnp.float32) * scale

    rows = min(BAND_ROWS, PAST_LEN)
    a0 = jax.random.uniform(next(ks), (DEPTH, W_C), jnp.float32, 0.9, 0.999)
    sig = a0 ** (1.0 / LRU_C)
    return {
        'x_prompt': nrm((BATCH, SEQ, D_MODEL), 1.0),
        'x_sample': nrm((DEC_BATCH, DEC_SEQ, D_MODEL), 1.0),
        'p_prompt': nrm((DEPTH, BATCH, SEQ, PLE_DIM), 1.0),
        'p_sample': nrm((DEPTH, DEC_BATCH, DEC_SEQ, PLE_DIM), 1.0),
        'cache_k_a': nrm((DEPTH, DEC_BATCH, rows, N_HEADS_A, HEAD_DIM), 1.0),
        'cache_v_a': nrm((DEPTH, DEC_BATCH, rows, N_HEADS_A, HEAD_DIM), 1.0),
        'state_ret': nrm((DEPTH, DEC_BATCH, N_HEADS_B, HEAD_DIM, HEAD_DIM), 0.5),
        'state_conv': nrm((DEPTH, DEC_BATCH, CONV_W - 1, W_C), 1.0),
        'state_lru': nrm((DEPTH, DEC_BATCH, W_C), 0.5),
        'w_in': nrm((DEPTH, D_MODEL, N_IN), D_MODEL ** -0.5),
        'rel_table': nrm((DEPTH, N_HEADS_A, 2 * REL_CLIP + 1), 0.5),
        'gn_gain': 1.0 + nrm((DEPTH, W_B), 0.1),
        'conv_w': nrm((DEPTH, CONV_W, W_C), CONV_W ** -0.5),
        'conv_b': nrm((DEPTH, W_C), 0.02),
        'w_gate_a': nrm((DEPTH, N_BLOCKS_C, BLOCK_C, BLOCK_C), BLOCK_C ** -0.5),
        'b_gate_a': nrm((DEPTH, W_C), 0.02),
        'w_gate_x': nrm((DEPTH, N_BLOCKS_C, BLOCK_C, BLOCK_C), BLOCK_C ** -0.5),
        'b_gate_x': nrm((DEPTH, W_C), 0.02),
        'lru_lambda': jnp.log(sig) - jnp.log1p(-sig),
        'w_branch': nrm((DEPTH, N_BRANCH, W_A, D_MODEL), (W_A ** -0.5) * BETA),
        'w_out': nrm((DEPTH, D_MODEL, D_MODEL), (D_MODEL ** -0.5) * BETA),
        'ln_gain': 1.0 + nrm((DEPTH, D_MODEL), 0.05),
        'ln_bias': nrm((DEPTH, D_MODEL), 0.02),
        'w_ple': nrm((DEPTH, PLE_DIM, D_MODEL), PLE_DIM ** -0.5),
        'w_ple_gate': nrm((DEPTH, D_MODEL, D_MODEL), D_MODEL ** -0.5),
    }


def reference(x_prompt, x_sample, p_prompt, p_sample, cache_k_a, cache_v_a, state_ret, state_conv,
              state_lru, w_in, rel_table, gn_gain, conv_w, conv_b, w_gate_a, b_gate_a, w_gate_x,
              b_gate_x, lru_lambda, w_branch, w_out, ln_gain, ln_bias, w_ple, w_ple_gate):
    bp, tp = x_prompt.shape[0], x_prompt.shape[1]
    ts = x_sample.shape[1]
    dt = x_prompt.dtype
    pos_p = jnp.arange(tp)
    pos_s = PAST_LEN + jnp.arange(ts)
    hp, hs = x_prompt, x_sample
    kp_l, vp_l, rp_l, cp_l, lp_l = [], [], [], [], []
    ks_l, vs_l, rs_l, cs_l, ls_l = [], [], [], [], []
    for l in range(DEPTH):
        wl = (w_in[l], rel_table[l], gn_gain[l], conv_w[l], conv_b[l], w_gate_a[l], b_gate_a[l],
              w_gate_x[l], b_gate_x[l], lru_lambda[l], w_branch[l], w_out[l], ln_gain[l], ln_bias[l],
              w_ple[l], w_ple_gate[l])
        hp, (k_p, v_p, r_p, c_p, s_p) = layer(
            hp, p_prompt[l], pos_p, *wl, None, None,
            jnp.zeros((bp, N_HEADS_B, HEAD_DIM, HEAD_DIM), dt),
            jnp.zeros((bp, CONV_W - 1, W_C), dt),
            jnp.zeros((bp, W_C), dt))
        hs, (k_s, v_s, r_s, c_s, s_s) = layer(
            hs, p_sample[l], pos_s, *wl, cache_k_a[l], cache_v_a[l],
            state_ret[l], state_conv[l], state_lru[l])
        kp_l.append(k_p); vp_l.append(v_p); rp_l.append(r_p); cp_l.append(c_p); lp_l.append(s_p)
        ks_l.append(k_s); vs_l.append(v_s); rs_l.append(r_s); cs_l.append(c_s); ls_l.append(s_s)
    return (hp, hs,
            jnp.stack(kp_l), jnp.stack(vp_l), jnp.stack(ks_l), jnp.stack(vs_l),
            jnp.stack(rp_l), jnp.stack(rs_l), jnp.stack(cp_l), jnp.stack(cs_l),
            jnp.stack(lp_l), jnp.stack(ls_l))
```

```python
import numpy as np
from contextlib import ExitStack
import concourse.bass as bass
import concourse.mybir as mybir
from concourse.bass_utils import run_bass_kernel_spmd

F32 = mybir.dt.float32
BF16 = mybir.dt.bfloat16
AF = mybir.ActivationFunctionType
ALU = mybir.AluOpType
AX = mybir.AxisListType

ENGS = ("pe", "act", "dve", "pool", "sp")
N_CORES = 8
D = 1024
SEQ = 2048
NT = 1024
NB = NT // 128
NTT = NT // 512
NHF = SEQ // NT
ALPHA = (2 * 2) ** 0.25
LN_EPS = 1e-5


class Prog:
    def __init__(self, nc):
        self.nc = nc
        self.ops = {e: [] for e in ENGS}
        self.cnt = {e: 0 for e in ENGS}
        self.known = {e: {} for e in ENGS}
        self.tw = {}
        self.tr = {}
        n_lanes = {"sp": 6, "act": 2, "pool": 4}
        self.lanes = {q: [[f"dma_{q}_{i}", 0] for i in range(n)] for q, n in n_lanes.items()}
        self.lane_rr = {q: 0 for q in n_lanes}
        self.semkeys = [f"c_{e}" for e in ENGS if e != "sp"] + [l[0] for q in self.lanes for l in self.lanes[q]]
        import os
        self.maxop = int(os.environ.get("KMAXOP", "1000000000"))
        self.nrec = 0
        self.log = []

    def _deps(self, eng, reads, writes):
        deps = []
        for t in list(reads) + list(writes):
            ev = self.tw.get(t)
            if ev is not None:
                deps.append(ev)
        for t in writes:
            deps.extend(self.tr.get(t, ()))
        kn = self.known[eng]
        best = {}
        for (sk, v) in deps:
            if eng == "pe" and sk == "c_pe":
                continue
            if kn.get(sk, 0) >= v:
                continue
            if best.get(sk, 0) < v:
                best[sk] = v
        waits = []
        for sk, v in best.items():
            kn[sk] = v
            waits.append((sk, v))
        return waits

    def _commit(self, ev, reads, writes):
        for t in reads:
            self.tr.setdefault(t, []).append(ev)
        for t in writes:
            self.tw[t] = ev
            self.tr[t] = []

    def op(self, eng, fn, reads=(), writes=()):
        self.nrec += 1
        if self.nrec > self.maxop:
            return None
        self.log.append((self.nrec, eng, fn.__code__.co_firstlineno, tuple(writes)))
        PS = ("acc", "big", "sm0", "sm1")
        writes = list(writes) + [t for t in reads if t.startswith(PS)]
        reads = [t for t in reads if not t.startswith(PS)]
        waits = self._deps(eng, reads, writes)
        self.cnt[eng] += 1
        ev = (f"c_{eng}", self.cnt[eng])
        self.ops[eng].append((waits, fn, ev[0], 1))
        self._commit(ev, reads, writes)
        return ev

    def dma(self, q, fn, reads=(), writes=()):
        self.nrec += 1
        if self.nrec > self.maxop:
            return None
        self.log.append((self.nrec, "dma_" + q, fn.__code__.co_firstlineno, tuple(writes)))
        lanes = self.lanes[q]
        i = self.lane_rr[q]
        self.lane_rr[q] = (i + 1) % len(lanes)
        lane = lanes[i]
        waits = self._deps(q, reads, writes)
        if lane[1] > 0 and self.known[q].get(lane[0], 0) < lane[1]:
            self.known[q][lane[0]] = lane[1]
            waits.append((lane[0], lane[1]))
        lane[1] += 16
        ev = (lane[0], lane[1])
        self.ops[q].append((waits, fn, lane[0], 16))
        self._commit(ev, reads, writes)
        return ev

    def barrier(self):
        evs = [(f"c_{e}", self.cnt[e]) for e in ENGS if e != "sp" and self.cnt[e] > 0]
        evs += [(lane[0], lane[1]) for lane in self.lanes["sp"] if lane[1] > 0]
        for eng in ENGS:
            waits = []
            kn = self.known[eng]
            for (sk, v) in evs:
                if sk == f"c_{eng}" and eng == "pe":
                    continue
                if kn.get(sk, 0) >= v:
                    continue
                kn[sk] = v
                waits.append((sk, v))
            if waits:
                self.ops[eng].append((waits, None, None, 0))

    def wait_all(self, eng):
        waits = []
        for e in ENGS:
            if e == "sp" or self.cnt[e] == 0:
                continue
            waits.append((f"c_{e}", self.cnt[e]))
        for q in self.lanes:
            for lane in self.lanes[q]:
                if lane[1] > 0:
                    waits.append((lane[0], lane[1]))
        self.ops[eng].append((waits, None, None, 0))

    def emit(self, sems):
        nc = self.nc

        def run(engine, lst):
            for waits, fn, sk, inc in lst:
                for (wk, wv) in waits:
                    engine.wait_ge(sems[wk], wv)
                if fn is not None:
                    ins = fn(engine)
                    ins.then_inc(sems[sk], inc)

        with nc.Block() as block:
            @block.tensor
            def _(e):
                run(e, self.ops["pe"])

            @block.scalar
            def _(e):
                run(e, self.ops["act"])

            @block.vector
            def _(e):
                run(e, self.ops["dve"])

            @block.gpsimd
            def _(e):
                run(e, self.ops["pool"])

            @block.sync
            def _(e):
                run(e, self.ops["sp"])


def host_consts():
    half = 32
    inv = (10000.0 ** (-np.arange(half, dtype=np.float32) / half)).astype(np.float32)
    pos = np.arange(SEQ, dtype=np.float32)
    ang = pos[:, None] * inv[None, :]
    c = np.cos(ang).astype(np.float32)
    s = np.sin(ang).astype(np.float32)
    cc = np.concatenate([c, c], -1).reshape(SEQ // 128, 128, 64).transpose(1, 0, 2)
    ss = np.concatenate([-s, s], -1).reshape(SEQ // 128, 128, 64).transpose(1, 0, 2)
    h = np.arange(8, dtype=np.float32)
    log_g = np.log1p(-np.exp2(-5.0 - h)).astype(np.float64)
    p = np.arange(128, dtype=np.float64)
    xi = np.exp(log_g[None, :] * (p[:, None] + 1.0))
    zi = np.exp(-log_g[None, :] * (p[:, None] + 1.0)) * (64 ** -0.5)
    gt = np.zeros((128, 4, 64), np.float64)
    gt32 = np.zeros((128, 4, 64), np.float64)
    for pp in range(128):
        for cch in range(4):
            hh = 2 * cch + pp // 64
            gt[pp, cch, :] = np.exp(log_g[hh] * 128.0)
            gt32[pp, cch, :] = np.exp(log_g[hh] * 32.0)
    ident = np.eye(128, dtype=np.float32)
    jj = np.arange(128)
    mask = (jj[None, :] >= jj[:, None]).astype(np.float32)
    return {
        "c_cc": np.ascontiguousarray(cc, dtype=np.float32),
        "c_ss": np.ascontiguousarray(ss, dtype=np.float32),
        "c_xi": xi.astype(np.float32),
        "c_zi": zi.astype(np.float32),
        "c_gt": gt.astype(np.float32),
        "c_gt32": gt32.astype(np.float32),
        "c_ident": ident,
        "c_mask": mask,
        "c_anti": np.ascontiguousarray(ident[::-1]),
    }


W_NAMES = {
    "w_in": [2, 1024, 8192], "rel_table": [2, 8, 257], "gn_gain": [2, 512], "conv_w": [2, 4, 512],
    "conv_b": [2, 512], "w_gate_a": [2, 8, 64, 64], "b_gate_a": [2, 512], "w_gate_x": [2, 8, 64, 64],
    "b_gate_x": [2, 512], "lru_lambda": [2, 512], "w_branch": [2, 3, 512, 1024], "w_out": [2, 1024, 1024],
    "ln_gain": [2, 1024], "ln_bias": [2, 1024], "w_ple": [2, 256, 1024], "w_ple_gate": [2, 1024, 1024],
}
C_SHAPES = {"c_cc": [128, 16, 64], "c_ss": [128, 16, 64], "c_xi": [128, 8], "c_zi": [128, 8],
            "c_gt": [128, 4, 64], "c_gt32": [128, 4, 64], "c_ident": [128, 128], "c_mask": [128, 128], "c_anti": [128, 128]}


def build(NSEQ=4, L=2, NSMP=2):
    nc = bass.Bass("TRN2", target_bir_lowering=False)
    di = {}

    def din(name, shape):
        di[name] = nc.dram_tensor(name, shape, F32, kind="ExternalInput").ap()
        return di[name]

    def dout(name, shape):
        di[name] = nc.dram_tensor(name, shape, F32, kind="ExternalOutput").ap()
        return di[name]

    xp = din("x_prompt", [NSEQ, SEQ, D])
    pp_ = din("p_prompt", [L, NSEQ, SEQ, 256])
    for k, shp in W_NAMES.items():
        din(k, shp)
    for k, shp in C_SHAPES.items():
        din(k, shp)
    yo = dout("y_prompt", [NSEQ, SEQ, D])
    ko = dout("k_a_prompt", [L, NSEQ, 512, 8, 64])
    vo = dout("v_a_prompt", [L, NSEQ, 512, 8, 64])
    ro = dout("ret_prompt", [L, NSEQ, 8, 64, 64])
    co = dout("conv_prompt", [L, NSEQ, 3, 512])
    lo = dout("lru_prompt", [L, NSEQ, 512])
    xs_d = din("x_sample", [NSMP, 32, D])
    ps_d = din("p_sample", [L, NSMP, 32, 256])
    ck_d = din("cache_k_a", [L, NSMP, 512, 8, 64])
    cv_d = din("cache_v_a", [L, NSMP, 512, 8, 64])
    sr_d = din("state_ret", [L, NSMP, 8, 64, 64])
    sc_d = din("state_conv", [L, NSMP, 3, 512])
    sl_d = din("state_lru", [L, NSMP, 512])
    yso = dout("y_sample", [NSMP, 32, D])
    kso = dout("k_a_sample", [L, NSMP, 32, 8, 64])
    vso = dout("v_a_sample", [L, NSMP, 32, 8, 64])
    rso = dout("ret_sample", [L, NSMP, 8, 64, 64])
    cso = dout("conv_sample", [L, NSMP, 3, 512])
    lso = dout("lru_sample", [L, NSMP, 512])
    gt32 = None
    ext = nc.dram_tensor("ext_scratch", [L, 8, 768], F32, kind="Internal").ap()

    es = ExitStack()

    def sb(name, shape, dt=F32):
        return es.enter_context(nc.sbuf_tensor(name, shape, dt))

    def ps(name, shape, dt=F32):
        return es.enter_context(nc.psum_tensor(name, shape, dt))

    xT = sb("xT", [128, 8, NT])
    xB = sb("xB", [128, 8, NT], BF16)
    yg = [sb(f"yg{i}", [128, 4, NT], BF16) for i in range(3)]
    merged = sb("merged", [128, 8, NT], BF16)
    pT = sb("pT", [128, 2, NT], BF16)
    WBN = 6144
    wb = [sb(f"wb{i}", [128, WBN], BF16) for i in range(3)]
    EB = sb("EB", [128, 8, 5, 128], BF16)
    biasst = [sb("biasst0", [128, 5, 128])]
    cc = sb("cc", [128, 16, 64]); ss = sb("ss", [128, 16, 64])
    xi = sb("xi", [128, 8]); zi = sb("zi", [128, 8])
    gt = sb("gt", [128, 4, 64])
    gt32 = sb("gt32", [128, 4, 64])
    identf = sb("identf", [128, 128]); identb = sb("identb", [128, 128], BF16)
    maskb = sb("maskb", [128, 128], BF16)
    onesf = sb("onesf", [128, 128])
    antif = sb("antif", [128, 128])
    mhalf = sb("mhalf", [128, 16])
    NP = 64
    prm = sb("prm", [128, L, NP])
    WgA = sb("WgA", [128, L, 4, 128], BF16); WgX = sb("WgX", [128, L, 4, 128], BF16)
    kcar = sb("kcar", [128, L, 4, 512], BF16)
    vcar = sb("vcar", [128, L, 4, 4 * 130], BF16)
    Scar = sb("Scar", [128, L, 4, 64])
    convcar = sb("convcar", [128, L, 4, 3])
    hcar = sb("hcar", [128, L, 4])
    sz = sb("sz", [128, NT], BF16)
    SCRN = 7168
    scr = sb("scr", [128, SCRN])
    scrb = scr.bitcast(BF16)

    def carve(items, start=0):
        out = {}
        off = start
        for name, n, dt in items:
            nbytes = n * (4 if dt == F32 else 2)
            if dt == F32:
                out[name] = scr[:, off // 4: off // 4 + n]
            else:
                out[name] = scrb[:, off // 2: off // 2 + n]
            off += (nbytes + 63) // 64 * 64
        assert off <= SCRN * 4, (off, SCRN * 4)
        return out

    cv = carve([("xin0", 1024, F32), ("xin1", 1024, F32), ("pin0", 256, F32), ("pin1", 256, F32),
                ("rts", 257, F32), ("exts", 768, F32)])
    xin = [cv["xin0"], cv["xin1"]]; pin = [cv["pin0"], cv["pin1"]]
    rts = cv["rts"][0:8, :]; exts = cv["exts"][0:8, :]
    cv = carve([("QT", NT, BF16), ("KT", 512 + NT, BF16), ("Vv", (4 + NB) * 130, BF16), ("Eb0", 640, BF16), ("Eb1", 640, BF16),
                ("PT0", 640, BF16), ("PT1", 640, BF16), ("PT2", 640, BF16), ("PT3", 640, BF16), ("rcp", 2, F32), ("ya", 128, BF16), ("kvst0", 128, F32), ("kvst1", 128, F32), ("kctm", 512, BF16)])
    QT = cv["QT"]; KT = cv["KT"]
    Vv = cv["Vv"].rearrange("p (a h d) -> p a h d", a=4 + NB, h=2)
    Eb = [cv["Eb0"].rearrange("p (a q) -> p a q", a=5), cv["Eb1"].rearrange("p (a q) -> p a q", a=5)]
    PTb = [cv["PT0"].rearrange("p (a q) -> p a q", a=5), cv["PT1"].rearrange("p (a q) -> p a q", a=5)]
    PTd = [[cv[f"PT{2 * par + hh}"].rearrange("p (a q) -> p a q", a=5) for hh in range(2)] for par in range(2)]
    rcp = cv["rcp"]; ya = cv["ya"]; kvst = [cv["kvst0"], cv["kvst1"]]
    kctm = cv["kctm"].rearrange("p (a c) -> p a c", a=4)
    cv = carve([("QTr", NT, BF16), ("KTr", NT, BF16), ("Vb", NB * 128, BF16), ("rt1", 128, F32), ("rt2", 128, F32),
                ("Qt", 128, BF16), ("Kt", 128, BF16), ("kvbuf", 64 * NB, F32), ("Sall", 64 * NB, F32), ("Gpat", 64 * NB, F32),
                ("Sop", NB * 64, BF16), ("Am", 256, BF16), ("ysb", 128, F32), ("ysq", 128, F32), ("gst", 16, F32), ("ynb", 128, BF16), ("S0f", 64, F32), ("Snew", 64, F32),
                ("QtA", NB * 128, BF16), ("KtA", NB * 128, BF16), ("AmA", NB * 256, BF16), ("gstA", 96, F32), ("ynbA", NB * 128, BF16)])
    QTr = cv["QTr"]; KTr = cv["KTr"]; Vb = cv["Vb"].rearrange("p (n c) -> p n c", n=NB)
    rt1 = cv["rt1"]; rt2 = cv["rt2"]; Qt = cv["Qt"]; Kt = cv["Kt"]
    kvbuf = cv["kvbuf"].rearrange("p (e n) -> p e n", n=NB); Sall = cv["Sall"].rearrange("p (e n) -> p e n", n=NB)
    Gpat = cv["Gpat"].rearrange("p (e n) -> p e n", n=NB); Sop = cv["Sop"].rearrange("p (n e) -> p n e", n=NB)
    S0f = cv["S0f"]; Snew = cv["Snew"]
    QtA = cv["QtA"].rearrange("p (t c) -> p t c", t=NB); KtA = cv["KtA"].rearrange("p (t c) -> p t c", t=NB)
    AmA = cv["AmA"].rearrange("p (t h i) -> p t h i", t=NB, h=2); gstA = cv["gstA"].rearrange("p (k n) -> p k n", k=6)
    ynbA = cv["ynbA"].rearrange("p (t c) -> p t c", t=NB)
    mgf = merged.bitcast(F32)[:].rearrange("p a b -> p (a b)")
    qkraw = mgf[:, 0:2048].rearrange("p (t c) -> p t c", t=NB)
    rt1A = mgf[:, 2048:3072].rearrange("p (t c) -> p t c", t=NB)
    rt2A = mgf[:, 3072:4096].rearrange("p (t c) -> p t c", t=NB)
    ysbA = mgf[:, 0:1024].rearrange("p (t c) -> p t c", t=NB)
    ysqA = mgf[:, 1024:2048].rearrange("p (t c) -> p t c", t=NB)
    Am = cv["Am"].rearrange("p (h i) -> p h i", h=2); ysb = cv["ysb"]; ysq = cv["ysq"]; gst = cv["gst"]; ynb = cv["ynb"]
    cv = carve([("bB", NT, F32), ("bC", NT, F32), ("szc", NT, BF16)], start=18432)
    bB = cv["bB"]; bC = cv["bC"]; szc_buf = cv["szc"]
    _mg = merged.bitcast(F32)[:].rearrange("p a b -> p (a b)")
    xrbuf = _mg[:, 0:3 + NT]; xc = _mg[:, 1028:1028 + NT]; bA = _mg[:, 2052:2052 + NT]
    xcb = merged[:].rearrange("p a b -> p (a b)")[:, 2 * 3076:2 * 3076 + NT]
    cv = carve([("gbuf0", 512, BF16), ("gbuf1", 512, BF16), ("gbuf2", 512, BF16), ("tb3_0", 512, F32), ("tb3_1", 512, F32), ("tb3_2", 512, F32),
                ("lnst", 8 * NB, F32), ("lntmp", 128, F32), ("dA", 128, F32), ("dB", 128, F32), ("lnt", 1024, F32),
                ("yst0", 1024, F32), ("yst1", 1024, F32)])
    gbuf = [cv["gbuf0"], cv["gbuf1"], cv["gbuf2"]]; tb3 = [cv["tb3_0"], cv["tb3_1"], cv["tb3_2"]]
    lnst = cv["lnst"]; lntmp = cv["lntmp"]; dA = cv["dA"]; dB = cv["dB"]
    lnt = cv["lnt"].rearrange("p (c t) -> p c t", c=8); yst = [cv["yst0"], cv["yst1"]]

    acc = [ps(f"acc{i}", [128, 512]) for i in range(2)]
    big = [ps(f"big{i}", [128, 1024]) for i in range(2)]
    sm0 = ps("sm0", [128, 512])
    sm1 = ps("sm1", [128, 1024], BF16)

    P = Prog(nc)
    sems = {k: es.enter_context(nc.semaphore(k)) for k in P.semkeys}
    rr = {"acc": 0, "ev": 0, "xin": 0, "pin": 0, "wb": 0, "kvst": 0, "bst": 0, "yst": 0}

    def nxt(key, n):
        v = rr[key]
        rr[key] = (v + 1) % n
        return v

    def bc(ap, shape):
        return ap.to_broadcast(shape)

    def mm_group(out_ap, out_tok, pairs, rtoks):
        def fn(e):
            ins = None
            n = len(pairs)
            for i, (l_, r_) in enumerate(pairs):
                ins = e.matmul(out_ap, l_, r_, start=(i == 0), stop=(i == n - 1))
            return ins
        P.op("pe", fn, reads=rtoks, writes=[out_tok])

    def xb_toks(tt):
        return [f"xB{kc}_{tt}" for kc in range(8)]

    def xf_toks(tt):
        return [f"xT{kc}_{tt}" for kc in range(8)]

    def evac_engine():
        return "act" if nxt("ev", 2) == 0 else "dve"

    def copy_op(eng, out_ap, in_ap, reads, writes, scale=None):
        if eng == "act":
            if scale is None:
                P.op("act", lambda e: e.copy(out_ap, in_ap), reads=reads, writes=writes)
            else:
                P.op("act", lambda e: e.mul(out_ap, in_ap, scale), reads=reads, writes=writes)
        else:
            if scale is None:
                P.op(eng, lambda e: e.tensor_scalar(out_ap, in_ap, 1.0, None, ALU.mult), reads=reads, writes=writes)
            else:
                P.op(eng, lambda e: e.tensor_scalar(out_ap, in_ap, scale, None, ALU.mult), reads=reads, writes=writes)

    def act_fn(out_ap, in_ap, func, reads, writes, bias=None, scale=None):
        kw = {}
        if bias is not None:
            kw["bias"] = bias
        if scale is not None:
            kw["scale"] = scale
        P.op("act", lambda e: e.activation(out_ap, in_ap, func, **kw), reads=reads, writes=writes)

    import os as _os
    def load_const(dst, name, toks):
        P.dma("sp", lambda e: e.dma_start(out=dst, in_=di[name]), writes=toks)

    load_const(cc[:], "c_cc", ["cc"]); load_const(ss[:], "c_ss", ["ss"])
    load_const(xi[:], "c_xi", ["xi"]); load_const(zi[:], "c_zi", ["zi"])
    load_const(gt[:], "c_gt", ["gt"]); load_const(identf[:], "c_ident", ["identf"]); load_const(antif[:], "c_anti", ["antif"]); load_const(gt32[:], "c_gt32", ["gt32"])
    P.dma("pool", lambda e: e.dma_start(out=identb[:], in_=di["c_ident"]), writes=["identb"])
    P.dma("pool", lambda e: e.dma_start(out=maskb[:], in_=di["c_mask"]), writes=["maskb"])
    P.op("dve", lambda e: e.memset(onesf[:], 1.0), writes=["onesf"])
    P.op("dve", lambda e: e.memset(mhalf[:], -0.5), writes=["mhalf"])
    P.op("dve", lambda e: e.memset(WgA[:], 0.0), writes=["WgA"])
    P.op("dve", lambda e: e.memset(WgX[:], 0.0), writes=["WgX"])
    P.op("dve", lambda e: e.memset(prm[:], 0.0), writes=["prm"])

    def pcol(l, a, b):
        return prm[:, l, a:b]

    for l in range(L):
        def vec_load(name, col, n, l=l):
            src = di[name][l].rearrange("(c p) -> p c", p=128)
            P.dma("sp", lambda e: e.dma_start(out=prm[:, l, col:col + n], in_=src), reads=["prm"], writes=[f"prm{l}_{col}"])
        vec_load("gn_gain", 0, 4)
        vec_load("conv_b", 4, 4)
        for tap in range(4):
            src = di["conv_w"][l, tap].rearrange("(c p) -> p c", p=128)
            P.dma("sp", lambda e, src=src, tap=tap, l=l: e.dma_start(out=prm[:, l, 8 + 4 * tap:12 + 4 * tap], in_=src),
                  reads=["prm"], writes=[f"prm{l}_cw{tap}"])
        vec_load("b_gate_a", 24, 4)
        vec_load("b_gate_x", 28, 4)
        vec_load("lru_lambda", 32, 4)
        vec_load("ln_gain", 40, 8)
        vec_load("ln_bias", 48, 8)
        act_fn(prm[:, l, 36:40], prm[:, l, 32:36], AF.Exp, [f"prm{l}_32"], [f"prm{l}_sp"], scale=-1.0)
        act_fn(prm[:, l, 36:40], prm[:, l, 36:40], AF.Ln, [f"prm{l}_sp"], [f"prm{l}_sp"], bias=onesf[:, 0:1])
        P.op("dve", lambda e, l=l: e.tensor_scalar(prm[:, l, 56:60], prm[:, l, 36:40], -16.0, None, ALU.mult),
             reads=[f"prm{l}_sp"], writes=[f"prm{l}_sp2"])
        P.op("dve", lambda e, l=l: e.tensor_scalar(prm[:, l, 36:40], prm[:, l, 36:40], -8.0, None, ALU.mult),
             reads=[f"prm{l}_sp", f"prm{l}_sp2"], writes=[f"prm{l}_sp"])
        for nm, Wt in (("w_gate_a", WgA), ("w_gate_x", WgX)):
            srcv = di[nm][l].rearrange("(c hh) i j -> hh i c j", hh=2)
            for hh in range(2):
                P.dma("pool", lambda e, Wt=Wt, srcv=srcv, hh=hh, l=l: e.dma_start(
                    out=Wt[hh * 64:(hh + 1) * 64, l, :, hh * 64:(hh + 1) * 64], in_=srcv[hh]),
                    reads=["WgA" if nm == "w_gate_a" else "WgX"], writes=[f"{nm}{l}_{hh}"])
        P.dma("sp", lambda e, l=l: e.dma_start(out=rts[:], in_=di["rel_table"][l]), writes=["rts"])
        P.op("dve", lambda e: e.tensor_copy(exts[:, 0:256], rts[:, 1:257]), reads=["rts"], writes=["exts"])
        P.op("dve", lambda e: e.tensor_copy(exts[:, 256:768], bc(rts[:, 256:257], [8, 512])),
             reads=["rts", "exts"], writes=["exts"])
        P.dma("sp", lambda e, l=l: e.dma_start(out=ext[l], in_=exts[:]), reads=["exts"], writes=[f"ext{l}"])
    WgTok = [[f"w_gate_a{l}_0", f"w_gate_a{l}_1", f"w_gate_x{l}_0", f"w_gate_x{l}_1", "WgA", "WgX"] for l in range(L)]
    prm_all = lambda l: ([f"prm{l}_{c}" for c in (0, 4, 24, 28, 40, 48)] + [f"prm{l}_cw{t}" for t in range(4)]
                         + [f"prm{l}_sp", f"prm{l}_sp2", "prm"])

    w_in = di["w_in"]
    P.barrier()

    def load_w(i, pieces):
        flat = []
        for dst, src in pieces:
            if len(dst.shape) == 4:
                for gi in range(dst.shape[2]):
                    flat.append((dst[:, :, gi, :], src[:, :, gi, :]))
            else:
                flat.append((dst, src))
        assert len(flat) <= 6
        for k, (dst, src) in enumerate(flat):
            P.dma("pool", lambda e, dst=dst, src=src: e.dma_start(out=dst, in_=src), writes=[f"wb{i}"] if k == 0 else [f"wb{i}_p{k}"])

    def wtoks(i, npieces):
        return [f"wb{i}"] + [f"wb{i}_p{k}" for k in range(1, 6)]

    def build_EB(l):
        for h in range(8):
            bst = biasst[0]
            base = ext[l, h, 0:1]
            src = bass.AP(base.tensor, base.offset, [[1, 128], [128, 5], [1, 128]])
            P.dma("sp", lambda e, bst=bst, src=src: e.dma_start(out=bst[:], in_=src), reads=[f"ext{l}"], writes=["bst0"])
            bv = big[0][:, 0:640]

            def fn(e, bst=bst, bv=bv):
                bf = bst[:].rearrange("p a q -> p (a q)")
                e.matmul(bv[:, 0:512], antif[:], bf[:, 0:512], start=True, stop=True)
                return e.matmul(bv[:, 512:640], antif[:], bf[:, 512:640], start=True, stop=True)
            P.op("pe", fn, reads=["bst0", "antif"], writes=["big0"])
            bv5 = bv.rearrange("p (a q) -> p a q", a=5)
            act_fn(EB[:, h, 0:4, :], bv5[:, 0:4, :], AF.Exp, ["big0"], ["EB"])
            act_fn(EB[:, h, 4:5, :], bv5[:, 4:5, :], AF.Exp, ["big0", "EB"], ["EB"])
        P.op("dve", lambda e: e.memset(EB[0:64, :, 4, 64:128], 0.0), reads=["EB"], writes=["EB"])
        P.op("dve", lambda e: e.memset(EB[64:128, :, 0, 0:64], 0.0), reads=["EB"], writes=["EB"])

    def load_x(b, hf):
        for tb in range(NB):
            xi_ = nxt("xin", 2)
            r0 = hf * NT + tb * 128
            P.dma("sp", lambda e, xi_=xi_, r0=r0: e.dma_start(out=xin[xi_][:], in_=xp[b, r0:r0 + 128, :]), writes=[f"xin{xi_}"])
            tt = tb // 4
            for hb in range(4):
                sv = sm0[:, 0:256].rearrange("p (c t) -> p c t", c=2)

                def fn(e, xi_=xi_, sv=sv, hb=hb):
                    ins = None
                    for c in range(2):
                        cg = hb * 2 + c
                        ins = e.transpose(sv[:, c, :], xin[xi_][:, cg * 128:(cg + 1) * 128], identf[:])
                    return ins
                P.op("pe", fn, reads=[f"xin{xi_}", "identf"], writes=["sm0"])
                cs = slice(hb * 2, hb * 2 + 2)
                if int(_os.environ.get("KX", "3")) & 1:
                    P.op("act", lambda e, tb=tb, sv=sv, cs=cs: e.copy(xT[:, cs, tb * 128:(tb + 1) * 128], sv),
                         reads=["sm0"], writes=[f"xT{c}_{tt}" for c in range(8)])
                if int(_os.environ.get("KX", "3")) & 2:
                    P.op("act", lambda e, tb=tb, sv=sv, cs=cs: e.copy(xB[:, cs, tb * 128:(tb + 1) * 128], sv),
                         reads=["sm0"], writes=[f"xB{c}_{tt}" for c in range(8)])

    def load_p(l, b, hf):
        for tb in range(NB):
            pi_ = nxt("pin", 2)
            r0 = hf * NT + tb * 128
            P.dma("sp", lambda e, pi_=pi_, r0=r0: e.dma_start(out=pin[pi_][:], in_=pp_[l, b, r0:r0 + 128, :]), writes=[f"pin{pi_}"])
            sv = sm0[:, 0:256].rearrange("p (c t) -> p c t", c=2)

            def fn(e, pi_=pi_, sv=sv):
                ins = None
                for c in range(2):
                    ins = e.transpose(sv[:, c, :], pin[pi_][:, c * 128:(c + 1) * 128], identf[:])
                return ins
            P.op("pe", fn, reads=[f"pin{pi_}", "identf"], writes=["sm0"])
            copy_op(evac_engine(), pT[:, :, tb * 128:(tb + 1) * 128], sv, ["sm0"], [f"pT_{tb // 4}"])

    CFG = {"w": 512, "ntt": NTT}

    def csl(tt):
        return slice(tt * CFG["w"], (tt + 1) * CFG["w"])

    def cw(ap):
        return ap[:, 0:CFG["w"]]

    def fm_mm(out_acc, ai, lhs_fn, tt, wt, nkc=8, rhs_src=None, rhs_toks=None):
        src = xB if rhs_src is None else rhs_src
        pairs = [(lhs_fn(kc), src[:, kc, csl(tt)]) for kc in range(nkc)]
        mm_group(cw(out_acc), f"acc{ai}", pairs, (xb_toks(tt) if rhs_toks is None else rhs_toks) + wt)

    def att_job(l, b, hf, j, wi, last):
        wt = wtoks(wi, 1)
        wv = wb[wi][:, 0:4096].rearrange("p (kc g c) -> p kc g c", kc=8, g=4)
        P.op("dve", lambda e: e.memset(Vv[:, :, :, 64:65], 1.0), writes=["Vv"])
        if hf > 0:
            copy_op("dve", KT[:, 0:512], kcar[:, l, j, :], [f"kcar{l}_{j}"], ["KT"])
            copy_op("act", Vv[:, 0:4, :, :], vcar[:, l, j, :].rearrange("p (a h d) -> p a h d", a=4, h=2), [f"vcar{l}_{j}"], ["Vv"])
        for tt in range(NTT):
            ai = nxt("acc", 2)
            fm_mm(acc[ai], ai, lambda kc: wv[:, kc, 0, :], tt, wt)
            copy_op("act", QT[:, tt * 512:(tt + 1) * 512], acc[ai][:], [f"acc{ai}"], ["QT"], scale=0.125)
            ai = nxt("acc", 2)
            fm_mm(acc[ai], ai, lambda kc: wv[:, kc, 1, :], tt, wt)
            copy_op("dve", KT[:, 512 + tt * 512:512 + (tt + 1) * 512], acc[ai][:], [f"acc{ai}"], ["KT"])
            ai = nxt("acc", 2)
            fm_mm(acc[ai], ai, lambda kc: wv[:, kc, 3, :], tt, wt)
            act_fn(sz[:, tt * 512:(tt + 1) * 512], acc[ai][:], AF.Silu, [f"acc{ai}"], ["sz"])
            yield
        for tb in range(NB):
            yield
            tt = tb // 4
            ai = nxt("acc", 2)
            pairs = [(xB[:, kc, tb * 128:(tb + 1) * 128], wv[:, kc, 2, :]) for kc in range(8)]
            mm_group(acc[ai][:, 0:128], f"acc{ai}", pairs, xb_toks(tt) + wt)
            av = acc[ai][:, 0:128].rearrange("p (h d) -> p h d", h=2)
            copy_op("dve", Vv[:, 4 + tb, :, 0:64], av, [f"acc{ai}"], ["Vv"])
            if last and tb >= NB - 4:
                si = nxt("kvst", 2)
                copy_op("act", kvst[si][:], acc[ai][:, 0:128], [f"acc{ai}"], [f"kvst{si}"])
                r0 = (tb - (NB - 4)) * 128
                P.dma("sp", lambda e, si=si, r0=r0: e.dma_start(
                    out=vo[l, b, r0:r0 + 128, 2 * j:2 * j + 2, :], in_=kvst[si][:].rearrange("p (h d) -> p h d", h=2)),
                    reads=[f"kvst{si}"])
                ai2 = nxt("acc", 2)
                pairs = [(xB[:, kc, tb * 128:(tb + 1) * 128], wv[:, kc, 1, :]) for kc in range(8)]
                mm_group(acc[ai2][:, 0:128], f"acc{ai2}", pairs, xb_toks(tt) + wt)
                si = nxt("kvst", 2)
                copy_op("act", kvst[si][:], acc[ai2][:, 0:128], [f"acc{ai2}"], [f"kvst{si}"])
                P.dma("sp", lambda e, si=si, r0=r0: e.dma_start(
                    out=ko[l, b, r0:r0 + 128, 2 * j:2 * j + 2, :], in_=kvst[si][:].rearrange("p (h d) -> p h d", h=2)),
                    reads=[f"kvst{si}"])
        Ov = sm0[:, 0:130].rearrange("p (h d) -> p h d", h=2)

        def stageA(tb):
            gblk = hf * NB + tb
            njp = 5 - max(0, 4 - gblk)
            par = tb % 2
            for hh in range(2):
                h = 2 * j + hh
                pb = 64 * hh
                STv = big[hh][:, 0:640].rearrange("p (a q) -> p a q", a=5)

                def fn(e, STv=STv, pb=pb, tb=tb, njp=njp):
                    ins = None
                    for jp in range(njp):
                        kb = tb + 4 - jp
                        ins = e.matmul(STv[:, jp, :], KT[pb:pb + 64, kb * 128:(kb + 1) * 128],
                                       QT[pb:pb + 64, tb * 128:(tb + 1) * 128], start=True, stop=True)
                    return ins
                P.op("pe", fn, reads=["KT", "QT"], writes=[f"big{hh}"])
                act_fn(Eb[hh][:, 0:min(njp, 4), :], STv[:, 0:min(njp, 4), :], AF.Exp, [f"big{hh}"], [f"Eb{hh}"])
                if njp == 5:
                    act_fn(Eb[hh][:, 4:5, :], STv[:, 4:5, :], AF.Exp, [f"big{hh}", f"Eb{hh}"], [f"Eb{hh}"])
                P.op("dve", lambda e, hh=hh, h=h, njp=njp, par=par: e.tensor_tensor(PTd[par][hh][:, 0:njp, :], Eb[hh][:, 0:njp, :], EB[:, h, 0:njp, :], ALU.mult),
                     reads=[f"Eb{hh}", "EB"], writes=[f"PT{par}{hh}"])

        def stageB(tb):
            gblk = hf * NB + tb
            njp = 5 - max(0, 4 - gblk)
            par = tb % 2
            for hh in range(2):
                def fn2(e, hh=hh, tb=tb, njp=njp, par=par):
                    ins = None
                    for jp in range(njp):
                        ins = e.matmul(Ov[:, hh, :], PTd[par][hh][:, jp, :], Vv[:, tb + 4 - jp, hh, :], start=(jp == 0), stop=(jp == njp - 1))
                    return ins
                P.op("pe", fn2, reads=[f"PT{par}{hh}", "Vv"], writes=["sm0"])
            P.op("dve", lambda e: e.reciprocal(rcp[:].rearrange("p (h o) -> p h o", o=1), Ov[:, :, 64:65]), reads=["sm0"], writes=["rcp"])
            P.op("dve", lambda e: e.tensor_tensor(ya[:].rearrange("p (h d) -> p h d", h=2), Ov[:, :, 0:64],
                                                  bc(rcp[:].rearrange("p (h o) -> p h o", o=1), [128, 2, 64]), ALU.mult),
                 reads=["sm0", "rcp"], writes=["ya"])
            P.op("pe", lambda e: e.transpose(sm1[:, 0:128], ya[:], identb[:]), reads=["ya", "identb"], writes=["sm1"])
            P.op("dve", lambda e, tb=tb: e.tensor_tensor(yg[0][:, j, tb * 128:(tb + 1) * 128], sm1[:, 0:128], sz[:, tb * 128:(tb + 1) * 128], ALU.mult),
                 reads=["sm1", "sz"], writes=[f"yg0_{tb // 4}"])

        stageA(0)
        for tb in range(1, NB):
            yield
            stageA(tb)
            yield
            stageB(tb - 1)
        yield
        stageB(NB - 1)
        copy_op("dve", kcar[:, l, j, :], KT[:, NT:NT + 512], ["KT"], [f"kcar{l}_{j}"])
        copy_op("act", vcar[:, l, j, :].rearrange("p (a h d) -> p a h d", a=4, h=2), Vv[:, NB:NB + 4, :, :], ["Vv"], [f"vcar{l}_{j}"])

    def ret_job(l, b, hf, j, wi, last):
        wt = wtoks(wi, 1)
        wv = wb[wi][:, 0:4096].rearrange("p (kc g c) -> p kc g c", kc=8, g=4)
        szb = sz
        g0 = hf * NB
        for tt in range(NTT):
            ai = nxt("acc", 2)
            fm_mm(acc[ai], ai, lambda kc: wv[:, kc, 3, :], tt, wt)
            act_fn(szb[:, tt * 512:(tt + 1) * 512], acc[ai][:], AF.Silu, [f"acc{ai}"], ["sz"])
        P.op("dve", lambda e: e.tensor_copy(Gpat[:], bc(gt[:, j, :].rearrange("p (e o) -> p e o", o=1), [128, 64, NB])),
             reads=["gt"], writes=["Gpat"])
        P.op("dve", lambda e: e.memset(Gpat[:, :, 0:1], 0.0), reads=["Gpat"], writes=["Gpat"])
        for tb in range(NB):
            tt = tb // 4
            ai = nxt("acc", 2)
            pairs = [(xB[:, kc, tb * 128:(tb + 1) * 128], wv[:, kc, 0:3, :]) for kc in range(8)]
            mm_group(acc[ai][:, 0:384], f"acc{ai}", pairs, xb_toks(tt) + wt)
            copy_op("act", qkraw[:, tb, :], acc[ai][:, 0:256], [f"acc{ai}"], ["MB0"])
            copy_op("act", Vb[:, tb, :], acc[ai][:, 256:384], [f"acc{ai}"], ["Vb"])
        for qi, (dstA, sc_, dtok) in enumerate(((QtA, xi, "QtA"), (KtA, zi, "KtA"))):
            raw = qkraw[:, :, qi * 128:(qi + 1) * 128]
            raw4 = raw.rearrange("p t (h d) -> p t h d", h=2)
            raw5 = raw.rearrange("p t (h two d) -> p t h two d", h=2, two=2)
            t14 = rt1A.rearrange("p t (h d) -> p t h d", h=2)
            t25 = rt2A.rearrange("p t (h two d) -> p t h two d", h=2, two=2)
            ccv = cc[:, g0:g0 + NB, :].unsqueeze(2).to_broadcast([128, NB, 2, 64])
            P.op("dve", lambda e, t14=t14, raw4=raw4, ccv=ccv: e.tensor_tensor(t14, raw4, ccv, ALU.mult), reads=["MB0", "cc"], writes=["rt1A"])
            for hv in range(2):
                ssv = ss[:, g0:g0 + NB, hv * 32:(hv + 1) * 32].unsqueeze(2).to_broadcast([128, NB, 2, 32])
                P.op("dve", lambda e, t25=t25, raw5=raw5, ssv=ssv, hv=hv: e.tensor_tensor(t25[:, :, :, hv, :], raw5[:, :, :, 1 - hv, :], ssv, ALU.mult),
                     reads=["MB0", "ss"], writes=["rt2A"])
            P.op("dve", lambda e: e.tensor_tensor(rt1A, rt1A, rt2A, ALU.add), reads=["rt1A", "rt2A"], writes=["rt1A"])
            scv = sc_[:, 2 * j:2 * j + 2].unsqueeze(1).unsqueeze(3).to_broadcast([128, NB, 2, 64])
            P.op("dve", lambda e, dstA=dstA, t14=t14, scv=scv: e.tensor_tensor(dstA.rearrange("p t (h d) -> p t h d", h=2), t14, scv, ALU.mult),
                 reads=["rt1A", "xi", "zi"], writes=[dtok])
        for g4 in range(NB // 4):
            def fnT(e, g4=g4):
                ins = None
                for t4 in range(4):
                    tb = g4 * 4 + t4
                    e.transpose(sm1[:, t4 * 128:(t4 + 1) * 128], QtA[:, tb, :], identb[:])
                    ins = e.transpose(sm1[:, 512 + t4 * 128:512 + (t4 + 1) * 128], KtA[:, tb, :], identb[:])
                return ins
            P.op("pe", fnT, reads=["QtA", "KtA", "identb"], writes=["sm1"])
            copy_op("act", QTr[:, g4 * 512:(g4 + 1) * 512], sm1[:, 0:512], ["sm1"], ["QTr"])
            copy_op("dve", KTr[:, g4 * 512:(g4 + 1) * 512], sm1[:, 512:1024], ["sm1"], ["KTr"])

            def fnKV(e, g4=g4):
                ins = None
                for t4 in range(4):
                    tb = g4 * 4 + t4
                    ins = e.matmul(sm0[:, t4 * 128:(t4 + 1) * 128], KtA[:, tb, :], Vb[:, tb, :], start=True, stop=True)
                return ins
            P.op("pe", fnKV, reads=["KtA", "Vb"], writes=["sm0"])
            for hh in range(2):
                rr_ = slice(hh * 64, (hh + 1) * 64)
                o_ = kvbuf[rr_, :, g4 * 4:(g4 + 1) * 4].rearrange("p e t -> p t e")
                i0_ = sm0[rr_, 0:512].rearrange("p (t c) -> p t c", t=4)[:, :, hh * 64:(hh + 1) * 64]
                i1_ = gt[rr_, j, :].unsqueeze(1).to_broadcast([64, 4, 64])
                P.op("dve", lambda e, o_=o_, i0_=i0_, i1_=i1_: e.tensor_tensor(o_, i0_, i1_, ALU.mult), reads=["sm0", "gt"], writes=["kvbuf"])
        P.op("dve", lambda e: e.tensor_tensor(ysb[:, 0:64], Scar[:, l, j, :], gt[:, j, :], ALU.mult), reads=[f"Scar{l}_{j}", "gt"], writes=["ysb"])
        P.op("dve", lambda e: e.tensor_tensor(kvbuf[:, :, 0], kvbuf[:, :, 0], ysb[:, 0:64], ALU.add), reads=["ysb", "kvbuf"], writes=["kvbuf"])
        P.op("dve", lambda e: e.tensor_tensor_scan(Sall[:].rearrange("p e n -> p (e n)"), Gpat[:].rearrange("p e n -> p (e n)"),
                                                   kvbuf[:].rearrange("p e n -> p (e n)"), 0.0, ALU.mult, ALU.add),
             reads=["kvbuf", "Gpat"], writes=["Sall"])
        copy_op("act", Sop[:, 0, :], Scar[:, l, j, :], [f"Scar{l}_{j}"], ["Sop"])
        copy_op("act", Sop[:, 1:NB, :], Sall[:, :, 0:NB - 1].rearrange("p e n -> p n e"), ["Sall"], ["Sop"])
        copy_op("dve", Scar[:, l, j, :], Sall[:, :, NB - 1], ["Sall", "Sop"], [f"Scar{l}_{j}"])
        if last:
            P.dma("sp", lambda e: e.dma_start(out=ro[l, b, 2 * j:2 * j + 2, :, :].rearrange("hh d e -> (hh d) e"), in_=Scar[:, l, j, :]),
                  reads=[f"Scar{l}_{j}"])
        for g4 in range(NB // 4):
            def fnA(e, g4=g4):
                ins = None
                for t4 in range(4):
                    tb = g4 * 4 + t4
                    for hh in range(2):
                        pb = 64 * hh
                        ins = e.matmul(big[hh][:, t4 * 128:(t4 + 1) * 128], KTr[pb:pb + 64, tb * 128:(tb + 1) * 128],
                                       QTr[pb:pb + 64, tb * 128:(tb + 1) * 128], start=True, stop=True)
                return ins
            P.op("pe", fnA, reads=["KTr", "QTr"], writes=["big0", "big1"])
            for hh in range(2):
                o_ = AmA[:, g4 * 4:(g4 + 1) * 4, hh, :]
                i0_ = big[hh][:, 0:512].rearrange("p (t i) -> p t i", t=4)
                i1_ = maskb[:].unsqueeze(1).to_broadcast([128, 4, 128])
                P.op("dve", lambda e, o_=o_, i0_=i0_, i1_=i1_: e.tensor_tensor(o_, i0_, i1_, ALU.mult), reads=[f"big{hh}", "maskb"], writes=["AmA"])
        Yh = [sm0[:, 0:512].rearrange("p (t e) -> p t e", t=NB), big[1][:, 512:1024].rearrange("p (t e) -> p t e", t=NB)]

        def fnY(e):
            ins = None
            for tb in range(NB):
                for hh in range(2):
                    pb = 64 * hh
                    e.matmul(Yh[hh][:, tb, :], AmA[:, tb, hh, :], Vb[:, tb, hh * 64:(hh + 1) * 64], start=True, stop=False)
                    ins = e.matmul(Yh[hh][:, tb, :], QTr[pb:pb + 64, tb * 128:(tb + 1) * 128], Sop[pb:pb + 64, tb, :], start=False, stop=True)
            return ins
        P.op("pe", fnY, reads=["AmA", "Vb", "QTr", "Sop"], writes=["sm0", "big1"])
        for hh in range(2):
            tk = "sm0" if hh == 0 else "big1"
            copy_op("act", ysbA[:, :, hh * 64:(hh + 1) * 64], Yh[hh], [tk, "MB0"], ["MB0"])
            P.op("act", lambda e, hh=hh: e.activation(ysqA[:, :, hh * 64:(hh + 1) * 64], Yh[hh], AF.Square), reads=[tk, "MB0"], writes=["MB0"])
        g = gstA
        y3 = ysbA.rearrange("p t (h d) -> p (t h) d", h=2)
        q3 = ysqA.rearrange("p t (h d) -> p (t h) d", h=2)
        P.op("dve", lambda e: e.reduce_sum(g[:, 0, :], y3, AX.X), reads=["MB0"], writes=["gstA"])
        P.op("dve", lambda e: e.reduce_sum(g[:, 1, :], q3, AX.X), reads=["MB0", "gstA"], writes=["gstA"])
        P.op("dve", lambda e: e.tensor_scalar(g[:, 0, :], g[:, 0, :], 1.0 / 64, None, ALU.mult), reads=["gstA"], writes=["gstA"])
        P.op("dve", lambda e: e.tensor_tensor(g[:, 2, :], g[:, 0, :], g[:, 0, :], ALU.mult), reads=["gstA"], writes=["gstA"])
        P.op("dve", lambda e: e.scalar_tensor_tensor(g[:, 3, :], g[:, 1, :], 1.0 / 64, g[:, 2, :], ALU.mult, ALU.subtract), reads=["gstA"], writes=["gstA"])
        P.op("dve", lambda e: e.tensor_scalar(g[:, 3, :], g[:, 3, :], LN_EPS, None, ALU.add), reads=["gstA"], writes=["gstA"])
        P.op("pool", lambda e: e.tensor_tensor(g[:, 4, :], g[:, 3, :], mhalf[:, 0:16], ALU.pow), reads=["gstA", "mhalf"], writes=["gstA"])
        P.op("dve", lambda e: e.tensor_tensor(y3, y3, g[:, 0, :].unsqueeze(2).to_broadcast([128, 16, 64]), ALU.subtract), reads=["gstA", "MB0"], writes=["MB0"])
        P.op("dve", lambda e: e.tensor_tensor(ynbA.rearrange("p t (h d) -> p (t h) d", h=2), y3,
                                              g[:, 4, :].unsqueeze(2).to_broadcast([128, 16, 64]), ALU.mult), reads=["gstA", "MB0"], writes=["ynbA"])

        def fnT2(e):
            ins = None
            for tb in range(NB):
                ins = e.transpose(sm1[:, tb * 128:(tb + 1) * 128], ynbA[:, tb, :], identb[:])
            return ins
        P.op("pe", fnT2, reads=["ynbA", "identb"], writes=["sm1"])
        P.op("dve", lambda e: e.scalar_tensor_tensor(yg[1][:, j, :], sm1[:, 0:NT], prm[:, l, j:j + 1], szb[:, 0:NT], ALU.mult, ALU.mult),
             reads=["sm1", "sz"] + prm_all(l), writes=["yg1_0", "yg1_1"])

    def lru_job(l, b, hf, c, wi, last, woff=0):
        wt = wtoks(wi, 1)
        wv = wb[wi][:, woff:woff + 2048].rearrange("p (kc g c) -> p kc g c", kc=8, g=2)
        szc = szc_buf
        pl = prm_all(l)
        copy_op("dve", xrbuf[:, 0:3], convcar[:, l, c, :], [f"convcar{l}_{c}"], ["xrbuf"])
        for tt in range(NTT):
            ai = nxt("acc", 2)
            fm_mm(acc[ai], ai, lambda kc: wv[:, kc, 0, :], tt, wt)
            copy_op("act", xrbuf[:, 3 + tt * 512:3 + (tt + 1) * 512], acc[ai][:], [f"acc{ai}"], ["xrbuf"])
            yield
            ai = nxt("acc", 2)
            fm_mm(acc[ai], ai, lambda kc: wv[:, kc, 1, :], tt, wt)
            act_fn(szc[:, tt * 512:(tt + 1) * 512], acc[ai][:], AF.Silu, [f"acc{ai}"], ["szc"])
            yield
        P.op("dve", lambda e: e.tensor_scalar(xc[:], xrbuf[:, 0:NT], prm[:, l, 8 + c:9 + c], prm[:, l, 4 + c:5 + c], ALU.mult, ALU.add),
             reads=["xrbuf"] + pl, writes=["xc"])
        yield
        for tap in range(1, 4):
            P.op("dve", lambda e, tap=tap: e.scalar_tensor_tensor(xc[:], xrbuf[:, tap:tap + NT], prm[:, l, 8 + 4 * tap + c:9 + 4 * tap + c],
                                                                  xc[:], ALU.mult, ALU.add),
                 reads=["xrbuf", "xc"], writes=["xc"])
            yield
        copy_op("act", xcb[:], xc[:], ["xc"], ["xcb"])
        yield
        for tt in range(NTT):
            sl = slice(tt * 512, (tt + 1) * 512)
            ai = nxt("acc", 2)
            mm_group(acc[ai][:], f"acc{ai}", [(WgA[:, l, c, :], xcb[:, sl])], ["xcb"] + WgTok[l])
            act_fn(bA[:, sl], acc[ai][:], AF.Sigmoid, [f"acc{ai}", "bA"] + pl, ["bA"], bias=prm[:, l, 24 + c:25 + c])
            ai = nxt("acc", 2)
            mm_group(acc[ai][:], f"acc{ai}", [(WgX[:, l, c, :], xcb[:, sl])], ["xcb"] + WgTok[l])
            act_fn(bC[:, sl], acc[ai][:], AF.Sigmoid, [f"acc{ai}", "bC"] + pl, ["bC"], bias=prm[:, l, 28 + c:29 + c])
            yield
        act_fn(bB[:], bA[:], AF.Exp, ["bA"], ["bB"], scale=prm[:, l, 56 + c:57 + c])
        act_fn(bA[:], bA[:], AF.Exp, ["bA", "bB"], ["bA"], scale=prm[:, l, 36 + c:37 + c])
        yield
        act_fn(bB[:], bB[:], AF.Sqrt, ["bB"], ["bB"], scale=-1.0, bias=onesf[:, 0:1])
        yield
        P.op("dve", lambda e: e.tensor_tensor(bC[:], bC[:], bB[:], ALU.mult), reads=["bB", "bC"], writes=["bC"])
        yield
        P.op("dve", lambda e: e.tensor_tensor(bC[:], bC[:], xc[:], ALU.mult), reads=["xc", "bC"], writes=["bC"])
        yield
        P.op("dve", lambda e: e.tensor_tensor_scan(bB[:], bA[:], bC[:], hcar[:, l, c:c + 1], ALU.mult, ALU.add),
             reads=["bA", "bC", f"hcar{l}_{c}", "bB"], writes=["bB"])
        yield
        copy_op("dve", hcar[:, l, c:c + 1], bB[:, NT - 1:NT], ["bB"], [f"hcar{l}_{c}"])
        P.op("dve", lambda e: e.tensor_tensor(yg[2][:, c, :], bB[:], szc[:, 0:NT], ALU.mult),
             reads=["bB", "szc"], writes=["yg2_0", "yg2_1"])
        copy_op("dve", convcar[:, l, c, :], xrbuf[:, NT:NT + 3], ["xrbuf"], [f"convcar{l}_{c}"])
        if last:
            P.dma("sp", lambda e: e.dma_start(out=co[l, b, :, c * 128:(c + 1) * 128].rearrange("t p -> p t"), in_=convcar[:, l, c, :]),
                  reads=[f"convcar{l}_{c}"])
            P.dma("sp", lambda e: e.dma_start(out=lo[l, b, c * 128:(c + 1) * 128].rearrange("(p o) -> p o", o=1), in_=hcar[:, l, c:c + 1]),
                  reads=[f"hcar{l}_{c}"])

    def interleave(*gens):
        gens = list(gens)
        while gens:
            for g_ in list(gens):
                try:
                    next(g_)
                except StopIteration:
                    gens.remove(g_)

    def d1_job(l, mc, wi):
        wt = wtoks(wi, 2)
        wg = wb[wi][:, 0:3072].rearrange("p (kc g c) -> p kc g c", kc=8, g=3)
        wbr = wb[wi][:, 3072:4608].rearrange("p (kc g c) -> p kc g c", kc=4, g=3)
        for tt in range(CFG["ntt"]):
            for br in range(3):
                ai = nxt("acc", 2)
                fm_mm(acc[ai], ai, lambda kc, br=br: wg[:, kc, br, :], tt, wt)
                act_fn(cw(gbuf[br]), cw(acc[ai]), AF.Sigmoid, [f"acc{ai}"], [f"gbuf{br}"])
                ai = nxt("acc", 2)
                fm_mm(acc[ai], ai, lambda kc, br=br: wbr[:, kc, br, :], tt, wt, nkc=4, rhs_src=yg[br], rhs_toks=[f"yg{br}_{tt}"])
                P.op("dve", lambda e, o_=cw(tb3[br]), a_=cw(acc[ai]), g_=cw(gbuf[br]): e.tensor_tensor(o_, a_, g_, ALU.mult),
                     reads=[f"acc{ai}", f"gbuf{br}"], writes=[f"tb3_{br}"])
            P.op("dve", lambda e, a_=cw(tb3[0]), b_=cw(tb3[1]): e.tensor_tensor(a_, a_, b_, ALU.add), reads=["tb3_0", "tb3_1"], writes=["tb3_0"])
            P.op("dve", lambda e, o_=merged[:, mc, csl(tt)], a_=cw(tb3[0]), b_=cw(tb3[2]): e.tensor_tensor(o_, a_, b_, ALU.add),
                 reads=["tb3_0", "tb3_2"], writes=[f"mg{mc}_{tt}"])

    def d2_job(l, rc, wi):
        wt = wtoks(wi, 1)
        wo = wb[wi][:, 0:1024].rearrange("p (kc c) -> p kc c", kc=8)
        for tt in range(CFG["ntt"]):
            ai = nxt("acc", 2)
            fm_mm(acc[ai], ai, lambda kc: wo[:, kc, :], tt, wt, rhs_src=merged, rhs_toks=[f"mg{k}_{tt}" for k in range(8)])
            sl = csl(tt)
            P.op("dve", lambda e, a_=cw(acc[ai]), sl=sl: e.scalar_tensor_tensor(xT[:, rc, sl], xT[:, rc, sl], ALPHA, a_, ALU.mult, ALU.add),
                 reads=[f"acc{ai}"], writes=[f"xT{rc}_{tt}"])
            copy_op("act", xB[:, rc, sl], xT[:, rc, sl], [f"xT{rc}_{tt}"], [f"xB{rc}_{tt}"])

    def d3_job(l, rc, wi):
        wt = wtoks(wi, 2)
        wpg = wb[wi][:, 0:1024].rearrange("p (kc c) -> p kc c", kc=8)
        wpl = wb[wi][:, 1024:1280].rearrange("p (kc c) -> p kc c", kc=2)
        for tt in range(CFG["ntt"]):
            sl = csl(tt)
            ai = nxt("acc", 2)
            fm_mm(acc[ai], ai, lambda kc: wpg[:, kc, :], tt, wt)
            act_fn(cw(gbuf[0]), cw(acc[ai]), AF.Sigmoid, [f"acc{ai}"], ["gbuf0"])
            ai = nxt("acc", 2)
            fm_mm(acc[ai], ai, lambda kc: wpl[:, kc, :], tt, wt, nkc=2, rhs_src=pT, rhs_toks=[f"pT_{tt}"])
            P.op("dve", lambda e, o_=cw(tb3[0]), a_=cw(acc[ai]), g_=cw(gbuf[0]): e.tensor_tensor(o_, a_, g_, ALU.mult), reads=[f"acc{ai}", "gbuf0"], writes=["tb3_0"])
            P.op("dve", lambda e, sl=sl, t_=cw(tb3[0]): e.tensor_tensor(xT[:, rc, sl], xT[:, rc, sl], t_, ALU.add), reads=["tb3_0"], writes=[f"xT{rc}_{tt}"])

    def ln_phase(l, b, hf, write_y):
        pl = prm_all(l)
        for tb in range(NB):
            tt = tb // 4
            bl = slice(tb * 128, (tb + 1) * 128)

            def fn(e, bl=bl):
                ins = None
                for rc in range(8):
                    e.matmul(sm0[:, 0:128], xT[:, rc, bl], xT[:, rc, bl], start=(rc == 0), stop=(rc == 7))
                for rc in range(8):
                    ins = e.matmul(sm0[:, 128:130], xT[:, rc, bl], onesf[:, 0:2], start=(rc == 0), stop=(rc == 7))
                return ins
            P.op("pe", fn, reads=xf_toks(tt) + ["onesf"], writes=["sm0"])
            P.op("dve", lambda e: e.tensor_tensor(lntmp[:], sm0[:, 0:128], identf[:], ALU.mult), reads=["sm0", "identf"], writes=["lntmp"])
            P.op("dve", lambda e, tb=tb: e.reduce_sum(lnst[:, NB + tb:NB + tb + 1], lntmp[:], AX.X), reads=["lntmp", "lnst"], writes=["lnst"])
            copy_op("dve", lnst[:, tb:tb + 1], sm0[:, 128:129], ["sm0", "lnst"], ["lnst"])
        s = lnst
        A0, A1, A2, A3, A4, A5 = [slice(k * NB, (k + 1) * NB) for k in range(6)]
        P.op("dve", lambda e: e.tensor_scalar(s[:, A0], s[:, A0], 1.0 / D, None, ALU.mult), reads=["lnst"], writes=["lnst"])
        P.op("dve", lambda e: e.tensor_tensor(s[:, A2], s[:, A0], s[:, A0], ALU.mult), reads=["lnst"], writes=["lnst"])
        P.op("dve", lambda e: e.scalar_tensor_tensor(s[:, A3], s[:, A1], 1.0 / D, s[:, A2], ALU.mult, ALU.subtract), reads=["lnst"], writes=["lnst"])
        P.op("dve", lambda e: e.tensor_scalar(s[:, A3], s[:, A3], LN_EPS, None, ALU.add), reads=["lnst"], writes=["lnst"])
        P.op("pool", lambda e: e.tensor_tensor(s[:, A4], s[:, A3], mhalf[:, 0:NB], ALU.pow), reads=["lnst", "mhalf"], writes=["lnst"])
        P.op("dve", lambda e: e.scalar_tensor_tensor(s[:, A5], s[:, A0], -1.0, s[:, A4], ALU.mult, ALU.mult), reads=["lnst"], writes=["lnst"])
        bcv = sm0[:, 0:256].rearrange("p (a t) -> p a t", a=2)
        for tb in range(NB):
            tt = tb // 4
            bl = slice(tb * 128, (tb + 1) * 128)
            P.op("dve", lambda e, tb=tb: e.tensor_scalar(dA[:], identf[:], s[:, 4 * NB + tb:4 * NB + tb + 1], None, ALU.mult), reads=["lnst", "identf"], writes=["dA"])
            P.op("dve", lambda e, tb=tb: e.tensor_scalar(dB[:], identf[:], s[:, 5 * NB + tb:5 * NB + tb + 1], None, ALU.mult), reads=["lnst", "identf"], writes=["dB"])

            def fn(e):
                e.matmul(bcv[:, 0, :], onesf[:], dA[:], start=True, stop=True)
                return e.matmul(bcv[:, 1, :], onesf[:], dB[:], start=True, stop=True)
            P.op("pe", fn, reads=["dA", "dB", "onesf"], writes=["sm0"])
            P.op("dve", lambda e, bl=bl: e.tensor_tensor(lnt[:], xT[:, :, bl], bc(bcv[:, 0:1, :], [128, 8, 128]), ALU.mult),
                 reads=["sm0"] + xf_toks(tt), writes=["lnt"])
            P.op("dve", lambda e: e.tensor_tensor(lnt[:], lnt[:], bc(bcv[:, 1:2, :], [128, 8, 128]), ALU.add), reads=["sm0", "lnt"], writes=["lnt"])
            P.op("dve", lambda e: e.tensor_tensor(lnt[:], lnt[:], bc(prm[:, l, 40:48].rearrange("p (c o) -> p c o", o=1), [128, 8, 128]), ALU.mult),
                 reads=["lnt"] + pl, writes=["lnt"])
            P.op("dve", lambda e, bl=bl: e.tensor_tensor(xT[:, :, bl], lnt[:], bc(prm[:, l, 48:56].rearrange("p (c o) -> p c o", o=1), [128, 8, 128]), ALU.add),
                 reads=["lnt"] + pl, writes=xf_toks(tt))
            P.op("act", lambda e, bl=bl: e.copy(xB[:, :, bl], xT[:, :, bl]), reads=xf_toks(tt), writes=xb_toks(tt))
            if write_y:
                yi_ = nxt("yst", 2)
                for hb in range(2):
                    ai = nxt("acc", 2)
                    av = acc[ai][:].rearrange("p (c t) -> p c t", c=4)

                    def fnT(e, bl=bl, av=av, hb=hb):
                        ins = None
                        for c in range(4):
                            ins = e.transpose(av[:, c, :], xT[:, hb * 4 + c, bl], identf[:])
                        return ins
                    P.op("pe", fnT, reads=xf_toks(tt) + ["identf"], writes=[f"acc{ai}"])
                    copy_op("act" if hb == 0 else "dve", yst[yi_][:, hb * 512:(hb + 1) * 512], acc[ai][:], [f"acc{ai}", f"yst{yi_}"], [f"yst{yi_}"])
                r0 = hf * NT + tb * 128
                P.dma("sp", lambda e, yi_=yi_, r0=r0: e.dma_start(out=yo[b, r0:r0 + 128, :], in_=yst[yi_][:]), reads=[f"yst{yi_}"])

    def s_load(l):
        if l == 0:
            P.dma("sp", lambda e: e.dma_start(out=xin[0][0:64, :], in_=xs_d.rearrange("s t d -> (s t) d")), writes=["xin0"])
            for hb in range(2):
                sv = sm0[:, 0:256].rearrange("p (c t) -> p c t", c=4)

                def fn(e, sv=sv, hb=hb):
                    ins = None
                    for c in range(4):
                        cg = hb * 4 + c
                        ins = e.transpose(sv[:, c, :], xin[0][0:64, cg * 128:(cg + 1) * 128], identf[0:64, 0:64])
                    return ins
                P.op("pe", fn, reads=["xin0", "identf"], writes=["sm0"])
                cs = slice(hb * 4, hb * 4 + 4)
                P.op("act", lambda e, sv=sv, cs=cs: e.copy(xT[:, cs, 0:64], sv), reads=["sm0"], writes=xf_toks(0))
                P.op("act", lambda e, sv=sv, cs=cs: e.copy(xB[:, cs, 0:64], sv), reads=["sm0"], writes=xb_toks(0))
        P.dma("sp", lambda e: e.dma_start(out=pin[0][0:64, :], in_=ps_d[l].rearrange("s t d -> (s t) d")), writes=["pin0"])
        sv2 = sm0[:, 0:128].rearrange("p (c t) -> p c t", c=2)

        def fn2(e):
            ins = None
            for c in range(2):
                ins = e.transpose(sv2[:, c, :], pin[0][0:64, c * 128:(c + 1) * 128], identf[0:64, 0:64])
            return ins
        P.op("pe", fn2, reads=["pin0", "identf"], writes=["sm0"])
        copy_op("act", pT[:, :, 0:64], sv2, ["sm0"], ["pT_0"])

    def s_att_job(l, j, wi):
        wt = wtoks(wi, 1)
        wv = wb[wi][:, 0:4096].rearrange("p (kc g c) -> p kc g c", kc=8, g=4)
        P.op("dve", lambda e: e.memset(Vv[:, :, :, 64:65], 1.0), writes=["Vv"])
        ai = nxt("acc", 2)
        fm_mm(acc[ai], ai, lambda kc: wv[:, kc, 0, :], 0, wt)
        copy_op("act", QT[:, 0:64], acc[ai][:, 0:64], [f"acc{ai}"], ["QT"], scale=0.125)
        ai = nxt("acc", 2)
        fm_mm(acc[ai], ai, lambda kc: wv[:, kc, 1, :], 0, wt)
        copy_op("dve", KT[:, 512:576], acc[ai][:, 0:64], [f"acc{ai}"], ["KTn"])
        ai = nxt("acc", 2)
        fm_mm(acc[ai], ai, lambda kc: wv[:, kc, 3, :], 0, wt)
        act_fn(sz[:, 0:64], acc[ai][:, 0:64], AF.Silu, [f"acc{ai}"], ["sz"])
        Ov = sm0[:, 0:130].rearrange("p (h d) -> p h d", h=2)
        for s in range(NSMP):
            cs_ = slice(s * 32, (s + 1) * 32)
            P.dma("pool", lambda e, s=s: e.dma_start(out=kctm[:], in_=ck_d[l, s, :, 2 * j:2 * j + 2, :].rearrange("(a p) h d -> p a (h d)", p=128)),
                  writes=["kctm"])

            def fnk(e):
                ins = None
                for a in range(4):
                    ins = e.transpose(sm1[:, a * 128:(a + 1) * 128], kctm[:, a, :], identb[:])
                return ins
            P.op("pe", fnk, reads=["kctm", "identb"], writes=["sm1"])
            copy_op("act", KT[:, 0:512], sm1[:, 0:512], ["sm1"], ["KT"])
            for hh in range(2):
                P.dma("pool", lambda e, s=s, hh=hh: e.dma_start(out=Vv[:, 0:4, hh, 0:64],
                                                                 in_=cv_d[l, s, :, 2 * j + hh, :].rearrange("(a p) d -> p a d", p=128)),
                      reads=["Vv"], writes=[f"Vvc{hh}"])
            ai = nxt("acc", 2)
            pairs = [(xB[:, kc, cs_], wv[:, kc, 2, :]) for kc in range(8)]
            mm_group(acc[ai][0:32, 0:128], f"acc{ai}", pairs, xb_toks(0) + wt)
            copy_op("dve", Vv[0:32, 4, :, 0:64], acc[ai][0:32, 0:128].rearrange("p (h d) -> p h d", h=2), [f"acc{ai}", "Vv"], ["Vvn"])
            si = nxt("kvst", 2)
            copy_op("act", kvst[si][0:32, :], acc[ai][0:32, 0:128], [f"acc{ai}"], [f"kvst{si}"])
            P.dma("sp", lambda e, si=si, s=s: e.dma_start(out=vso[l, s, :, 2 * j:2 * j + 2, :], in_=kvst[si][0:32, :].rearrange("p (h d) -> p h d", h=2)),
                  reads=[f"kvst{si}"])
            ai = nxt("acc", 2)
            pairs = [(xB[:, kc, cs_], wv[:, kc, 1, :]) for kc in range(8)]
            mm_group(acc[ai][0:32, 0:128], f"acc{ai}", pairs, xb_toks(0) + wt)
            si = nxt("kvst", 2)
            copy_op("act", kvst[si][0:32, :], acc[ai][0:32, 0:128], [f"acc{ai}"], [f"kvst{si}"])
            P.dma("sp", lambda e, si=si, s=s: e.dma_start(out=kso[l, s, :, 2 * j:2 * j + 2, :], in_=kvst[si][0:32, :].rearrange("p (h d) -> p h d", h=2)),
                  reads=[f"kvst{si}"])
            for hh in range(2):
                h = 2 * j + hh
                pb = 64 * hh
                STv = big[hh][:, 0:640].rearrange("p (a q) -> p a q", a=5)

                def fn(e, STv=STv, pb=pb, s=s):
                    e.matmul(STv[0:32, 0, 0:32], KT[pb:pb + 64, 512 + s * 32:512 + (s + 1) * 32], QT[pb:pb + 64, s * 32:(s + 1) * 32], start=True, stop=True)
                    ins = None
                    for jp in range(1, 5):
                        kb = 4 - jp
                        ins = e.matmul(STv[:, jp, 0:32], KT[pb:pb + 64, kb * 128:(kb + 1) * 128], QT[pb:pb + 64, s * 32:(s + 1) * 32], start=True, stop=True)
                    return ins
                P.op("pe", fn, reads=["KT", "KTn", "QT"], writes=[f"big{hh}"])
                act_fn(Eb[hh][0:32, 0, 0:32], STv[0:32, 0, 0:32], AF.Exp, [f"big{hh}"], [f"Eb{hh}"])
                act_fn(Eb[hh][:, 1:4, 0:32], STv[:, 1:4, 0:32], AF.Exp, [f"big{hh}", f"Eb{hh}"], [f"Eb{hh}"])
                act_fn(Eb[hh][:, 4:5, 0:32], STv[:, 4:5, 0:32], AF.Exp, [f"big{hh}", f"Eb{hh}"], [f"Eb{hh}"])
                P.op("dve", lambda e, hh=hh, h=h: e.tensor_tensor(PTb[hh][0:32, 0, 0:32], Eb[hh][0:32, 0, 0:32], EB[0:32, h, 0, 0:32], ALU.mult),
                     reads=[f"Eb{hh}", "EB"], writes=[f"PT{hh}"])
                P.op("dve", lambda e, hh=hh, h=h: e.tensor_tensor(PTb[hh][:, 1:5, 0:32], Eb[hh][:, 1:5, 0:32], EB[:, h, 1:5, 0:32], ALU.mult),
                     reads=[f"Eb{hh}", "EB", f"PT{hh}"], writes=[f"PT{hh}"])

                def fn2(e, hh=hh):
                    e.matmul(Ov[0:32, hh, :], PTb[hh][0:32, 0, 0:32], Vv[0:32, 4, hh, :], start=True, stop=False)
                    ins = None
                    for jp in range(1, 5):
                        ins = e.matmul(Ov[0:32, hh, :], PTb[hh][:, jp, 0:32], Vv[:, 4 - jp, hh, :], start=False, stop=(jp == 4))
                    return ins
                P.op("pe", fn2, reads=[f"PT{hh}", "Vv", "Vvc0", "Vvc1", "Vvn"], writes=["sm0"])
            P.op("dve", lambda e: e.reciprocal(rcp[0:32, :].rearrange("p (h o) -> p h o", o=1), Ov[0:32, :, 64:65]), reads=["sm0"], writes=["rcp"])
            P.op("dve", lambda e: e.tensor_tensor(ya[0:32, :].rearrange("p (h d) -> p h d", h=2), Ov[0:32, :, 0:64],
                                                  bc(rcp[0:32, :].rearrange("p (h o) -> p h o", o=1), [32, 2, 64]), ALU.mult),
                 reads=["sm0", "rcp"], writes=["ya"])
            P.op("pe", lambda e: e.transpose(sm1[:, 0:32], ya[0:32, :], identb[0:32, 0:32]), reads=["ya", "identb"], writes=["sm1"])
            P.op("dve", lambda e, cs_=cs_: e.tensor_tensor(yg[0][:, j, cs_], sm1[:, 0:32], sz[:, cs_], ALU.mult),
                 reads=["sm1", "sz"], writes=["yg0_0"])

    def s_ret_job(l, j, wi):
        wt = wtoks(wi, 1)
        wv = wb[wi][:, 0:4096].rearrange("p (kc g c) -> p kc g c", kc=8, g=4)
        R = slice(0, 32)
        ai = nxt("acc", 2)
        fm_mm(acc[ai], ai, lambda kc: wv[:, kc, 3, :], 0, wt)
        act_fn(sz[:, 0:64], acc[ai][:, 0:64], AF.Silu, [f"acc{ai}"], ["sz"])
        Ah = [big[0][0:32, 0:32], big[1][0:32, 0:32]]
        Yh = [sm0[0:32, 0:64], big[1][0:32, 512:576]]
        gblk = 8
        for s in range(NSMP):
            cs_ = slice(s * 32, (s + 1) * 32)
            P.dma("sp", lambda e, s=s: e.dma_start(out=S0f[:], in_=sr_d[l, s, 2 * j:2 * j + 2, :, :].rearrange("hh d e -> (hh d) e")), writes=["S0f"])
            copy_op("act", Sop[:, 0, :], S0f[:], ["S0f"], ["Sop"])
            ai = nxt("acc", 2)
            pairs = [(xB[:, kc, cs_], wv[:, kc, 0:3, :]) for kc in range(8)]
            mm_group(acc[ai][R, 0:384], f"acc{ai}", pairs, xb_toks(0) + wt)
            at = f"acc{ai}"
            for qi, (dst, sc_) in enumerate(((Qt, xi), (Kt, zi))):
                X = acc[ai][R, qi * 128:(qi + 1) * 128].rearrange("p (h two d) -> p h two d", h=2, two=2)
                ccv = bc(cc[R, gblk, :].rearrange("p (o d) -> p o d", o=1), [32, 2, 64])
                t1v = rt1[R, :].rearrange("p (h d) -> p h d", h=2)
                t2v = rt2[R, :].rearrange("p (h two d) -> p h two d", h=2, two=2)
                P.op("dve", lambda e, ai=ai, qi=qi, ccv=ccv, t1v=t1v: e.tensor_tensor(
                    t1v, acc[ai][R, qi * 128:(qi + 1) * 128].rearrange("p (h d) -> p h d", h=2), ccv, ALU.mult),
                    reads=[at, "cc"], writes=["rt1"])
                for hv in range(2):
                    ssv = bc(ss[R, gblk, hv * 32:(hv + 1) * 32].rearrange("p (o d) -> p o d", o=1), [32, 2, 32])
                    P.op("dve", lambda e, X=X, hv=hv, ssv=ssv, t2v=t2v: e.tensor_tensor(t2v[:, :, hv, :], X[:, :, 1 - hv, :], ssv, ALU.mult),
                         reads=[at, "ss"], writes=["rt2"])
                P.op("dve", lambda e: e.tensor_tensor(rt1[R, :], rt1[R, :], rt2[R, :], ALU.add), reads=["rt1", "rt2"], writes=["rt1"])
                scv = bc(sc_[R, 2 * j:2 * j + 2].rearrange("p (h o) -> p h o", o=1), [32, 2, 64])
                P.op("dve", lambda e, dst=dst, scv=scv, t1v=t1v: e.tensor_tensor(dst[R, :].rearrange("p (h d) -> p h d", h=2), t1v, scv, ALU.mult),
                     reads=["rt1", "xi", "zi"], writes=["Qt" if qi == 0 else "Kt"])
            copy_op("dve", Vb[R, 0, :], acc[ai][R, 256:384], [at], ["Vb"])
            P.op("pe", lambda e: e.transpose(sm1[:, 0:32], Qt[R, :], identb[0:32, 0:32]), reads=["Qt", "identb"], writes=["sm1"])
            copy_op("act", QTr[:, cs_], sm1[:, 0:32], ["sm1"], ["QTr"])
            P.op("pe", lambda e: e.transpose(sm1[:, 128:160], Kt[R, :], identb[0:32, 0:32]), reads=["Kt", "identb"], writes=["sm1"])
            copy_op("dve", KTr[:, cs_], sm1[:, 128:160], ["sm1"], ["KTr"])
            P.op("pe", lambda e: e.matmul(sm0[:, 0:128], Kt[R, :], Vb[R, 0, :], start=True, stop=True), reads=["Kt", "Vb"], writes=["sm0"])
            for hh in range(2):
                rr_ = slice(hh * 64, (hh + 1) * 64)
                P.op("dve", lambda e, hh=hh, rr_=rr_: e.tensor_tensor(Snew[rr_, :], sm0[rr_, hh * 64:(hh + 1) * 64], S0f[rr_, :], ALU.add),
                     reads=["sm0", "S0f", "Snew"], writes=["Snew"])
                P.op("dve", lambda e, rr_=rr_: e.tensor_tensor(Snew[rr_, :], Snew[rr_, :], gt32[rr_, j, :], ALU.mult),
                     reads=["Snew", "gt32"], writes=["Snew"])
            P.dma("sp", lambda e, s=s: e.dma_start(out=rso[l, s, 2 * j:2 * j + 2, :, :].rearrange("hh d e -> (hh d) e"), in_=Snew[:]), reads=["Snew"])

            def fnA(e, cs_=cs_):
                ins = None
                for hh in range(2):
                    pb = 64 * hh
                    ins = e.matmul(Ah[hh], KTr[pb:pb + 64, cs_], QTr[pb:pb + 64, cs_], start=True, stop=True)
                return ins
            P.op("pe", fnA, reads=["KTr", "QTr"], writes=["big0", "big1"])
            for hh in range(2):
                P.op("dve", lambda e, hh=hh: e.tensor_tensor(Am[R, hh, 0:32], Ah[hh], maskb[0:32, 0:32], ALU.mult),
                     reads=[f"big{hh}", "maskb"], writes=["Am"])

            def fnY(e, cs_=cs_):
                ins = None
                for hh in range(2):
                    pb = 64 * hh
                    e.matmul(Yh[hh], Am[R, hh, 0:32], Vb[R, 0, hh * 64:(hh + 1) * 64], start=True, stop=False)
                    ins = e.matmul(Yh[hh], QTr[pb:pb + 64, cs_], Sop[pb:pb + 64, 0, :], start=False, stop=True)
                return ins
            P.op("pe", fnY, reads=["Am", "Vb", "QTr", "Sop"], writes=["sm0", "big1"])
            for hh in range(2):
                tk = "sm0" if hh == 0 else "big1"
                copy_op("act", ysb[R, hh * 64:(hh + 1) * 64], Yh[hh], [tk, "ysb"], ["ysb"])
                P.op("act", lambda e, hh=hh: e.activation(ysq[R, hh * 64:(hh + 1) * 64], Yh[hh], AF.Square), reads=[tk, "ysq"], writes=["ysq"])
            g = gst
            yv3 = ysb[R, :].rearrange("p (h d) -> p h d", h=2)
            P.op("dve", lambda e: e.reduce_sum(g[R, 0:2], yv3, AX.X), reads=["ysb"], writes=["gst"])
            P.op("dve", lambda e: e.reduce_sum(g[R, 2:4], ysq[R, :].rearrange("p (h d) -> p h d", h=2), AX.X), reads=["ysq", "gst"], writes=["gst"])
            P.op("dve", lambda e: e.tensor_scalar(g[R, 0:2], g[R, 0:2], 1.0 / 64, None, ALU.mult), reads=["gst"], writes=["gst"])
            P.op("dve", lambda e: e.tensor_tensor(g[R, 4:6], g[R, 0:2], g[R, 0:2], ALU.mult), reads=["gst"], writes=["gst"])
            P.op("dve", lambda e: e.scalar_tensor_tensor(g[R, 6:8], g[R, 2:4], 1.0 / 64, g[R, 4:6], ALU.mult, ALU.subtract), reads=["gst"], writes=["gst"])
            P.op("dve", lambda e: e.tensor_scalar(g[R, 6:8], g[R, 6:8], LN_EPS, None, ALU.add), reads=["gst"], writes=["gst"])
            P.op("pool", lambda e: e.tensor_tensor(g[R, 8:10], g[R, 6:8], mhalf[R, 0:2], ALU.pow), reads=["gst", "mhalf"], writes=["gst"])
            P.op("dve", lambda e: e.tensor_tensor(yv3, yv3, bc(g[R, 0:2].rearrange("p (h o) -> p h o", o=1), [32, 2, 64]), ALU.subtract),
                 reads=["gst", "ysb"], writes=["ysb"])
            P.op("dve", lambda e: e.tensor_tensor(ynb[R, :].rearrange("p (h d) -> p h d", h=2), yv3,
                                                  bc(g[R, 8:10].rearrange("p (h o) -> p h o", o=1), [32, 2, 64]), ALU.mult),
                 reads=["gst", "ysb"], writes=["ynb"])
            P.op("pe", lambda e: e.transpose(sm1[:, 0:32], ynb[R, :], identb[0:32, 0:32]), reads=["ynb", "identb"], writes=["sm1"])
            P.op("dve", lambda e, cs_=cs_: e.scalar_tensor_tensor(yg[1][:, j, cs_], sm1[:, 0:32], prm[:, l, j:j + 1], sz[:, cs_], ALU.mult, ALU.mult),
                 reads=["sm1", "sz"] + prm_all(l), writes=["yg1_0"])

    def s_lru_job(l, c, wi):
        wt = wtoks(wi, 1)
        wv = wb[wi][:, 0:2048].rearrange("p (kc g c) -> p kc g c", kc=8, g=2)
        pl = prm_all(l)
        ai = nxt("acc", 2)
        fm_mm(acc[ai], ai, lambda kc: wv[:, kc, 1, :], 0, wt)
        act_fn(sz[:, 0:64], acc[ai][:, 0:64], AF.Silu, [f"acc{ai}"], ["sz"])
        Wd = slice(0, 32)
        for s in range(NSMP):
            cs_ = slice(s * 32, (s + 1) * 32)
            P.dma("sp", lambda e, s=s: e.dma_start(out=xrbuf[:, 0:3], in_=sc_d[l, s, :, c * 128:(c + 1) * 128].rearrange("t p -> p t")), writes=["xrbuf"])
            P.dma("sp", lambda e, s=s: e.dma_start(out=hcar[:, l, c:c + 1], in_=sl_d[l, s, c * 128:(c + 1) * 128].rearrange("(p o) -> p o", o=1)),
                  writes=[f"hcar{l}_{c}"])
            ai = nxt("acc", 2)
            pairs = [(wv[:, kc, 0, :], xB[:, kc, cs_]) for kc in range(8)]
            mm_group(acc[ai][:, 0:32], f"acc{ai}", pairs, xb_toks(0) + wt)
            copy_op("act", xrbuf[:, 3:35], acc[ai][:, 0:32], [f"acc{ai}", "xrbuf"], ["xrbuf"])
            P.op("dve", lambda e: e.tensor_scalar(xc[:, Wd], xrbuf[:, 0:32], prm[:, l, 8 + c:9 + c], prm[:, l, 4 + c:5 + c], ALU.mult, ALU.add),
                 reads=["xrbuf"] + pl, writes=["xc"])
            for tap in range(1, 4):
                P.op("dve", lambda e, tap=tap: e.scalar_tensor_tensor(xc[:, Wd], xrbuf[:, tap:tap + 32], prm[:, l, 8 + 4 * tap + c:9 + 4 * tap + c],
                                                                      xc[:, Wd], ALU.mult, ALU.add),
                     reads=["xrbuf", "xc"], writes=["xc"])
            copy_op("act", xcb[:, Wd], xc[:, Wd], ["xc"], ["xcb"])
            ai = nxt("acc", 2)
            mm_group(acc[ai][:, 0:32], f"acc{ai}", [(WgA[:, l, c, :], xcb[:, Wd])], ["xcb"] + WgTok[l])
            act_fn(bA[:, Wd], acc[ai][:, 0:32], AF.Sigmoid, [f"acc{ai}"] + pl, ["bA"], bias=prm[:, l, 24 + c:25 + c])
            ai = nxt("acc", 2)
            mm_group(acc[ai][:, 0:32], f"acc{ai}", [(WgX[:, l, c, :], xcb[:, Wd])], ["xcb"] + WgTok[l])
            act_fn(bC[:, Wd], acc[ai][:, 0:32], AF.Sigmoid, [f"acc{ai}"] + pl, ["bC"], bias=prm[:, l, 28 + c:29 + c])
            act_fn(bB[:, Wd], bA[:, Wd], AF.Exp, ["bA"], ["bB"], scale=prm[:, l, 56 + c:57 + c])
            act_fn(bA[:, Wd], bA[:, Wd], AF.Exp, ["bA", "bB"], ["bA"], scale=prm[:, l, 36 + c:37 + c])
            act_fn(bB[:, Wd], bB[:, Wd], AF.Sqrt, ["bB"], ["bB"], scale=-1.0, bias=onesf[:, 0:1])
            P.op("dve", lambda e: e.tensor_tensor(bC[:, Wd], bC[:, Wd], bB[:, Wd], ALU.mult), reads=["bB", "bC"], writes=["bC"])
            P.op("dve", lambda e: e.tensor_tensor(bC[:, Wd], bC[:, Wd], xc[:, Wd], ALU.mult), reads=["xc", "bC"], writes=["bC"])
            P.op("dve", lambda e: e.tensor_tensor_scan(bB[:, Wd], bA[:, Wd], bC[:, Wd], hcar[:, l, c:c + 1], ALU.mult, ALU.add),
                 reads=["bA", "bC", f"hcar{l}_{c}", "bB"], writes=["bB"])
            P.dma("sp", lambda e, s=s: e.dma_start(out=lso[l, s, c * 128:(c + 1) * 128].rearrange("(p o) -> p o", o=1), in_=bB[:, 31:32]), reads=["bB"])
            P.dma("sp", lambda e, s=s: e.dma_start(out=cso[l, s, :, c * 128:(c + 1) * 128].rearrange("t p -> p t"), in_=xrbuf[:, 32:35]), reads=["xrbuf"])
            P.op("dve", lambda e, cs_=cs_: e.tensor_tensor(yg[2][:, c, cs_], bB[:, Wd], sz[:, cs_], ALU.mult),
                 reads=["bB", "sz"], writes=["yg2_0"])

    def s_ln(l, write_y):
        pl = prm_all(l)
        R = slice(0, 64)
        bl = slice(0, 64)

        def fn(e):
            ins = None
            for rc in range(8):
                e.matmul(sm0[R, 0:64], xT[:, rc, bl], xT[:, rc, bl], start=(rc == 0), stop=(rc == 7))
            for rc in range(8):
                ins = e.matmul(sm0[R, 128:130], xT[:, rc, bl], onesf[:, 0:2], start=(rc == 0), stop=(rc == 7))
            return ins
        P.op("pe", fn, reads=xf_toks(0) + ["onesf"], writes=["sm0"])
        s_ = lnst
        P.op("dve", lambda e: e.tensor_tensor(lntmp[R, 0:64], sm0[R, 0:64], identf[R, 0:64], ALU.mult), reads=["sm0", "identf"], writes=["lntmp"])
        P.op("dve", lambda e: e.reduce_sum(s_[R, 1:2], lntmp[R, 0:64], AX.X), reads=["lntmp", "lnst"], writes=["lnst"])
        copy_op("dve", s_[R, 0:1], sm0[R, 128:129], ["sm0", "lnst"], ["lnst"])
        P.op("dve", lambda e: e.tensor_scalar(s_[R, 0:1], s_[R, 0:1], 1.0 / D, None, ALU.mult), reads=["lnst"], writes=["lnst"])
        P.op("dve", lambda e: e.tensor_tensor(s_[R, 2:3], s_[R, 0:1], s_[R, 0:1], ALU.mult), reads=["lnst"], writes=["lnst"])
        P.op("dve", lambda e: e.scalar_tensor_tensor(s_[R, 3:4], s_[R, 1:2], 1.0 / D, s_[R, 2:3], ALU.mult, ALU.subtract), reads=["lnst"], writes=["lnst"])
        P.op("dve", lambda e: e.tensor_scalar(s_[R, 3:4], s_[R, 3:4], LN_EPS, None, ALU.add), reads=["lnst"], writes=["lnst"])
        P.op("pool", lambda e: e.tensor_tensor(s_[R, 4:5], s_[R, 3:4], mhalf[R, 0:1], ALU.pow), reads=["lnst", "mhalf"], writes=["lnst"])
        P.op("dve", lambda e: e.scalar_tensor_tensor(s_[R, 5:6], s_[R, 0:1], -1.0, s_[R, 4:5], ALU.mult, ALU.mult), reads=["lnst"], writes=["lnst"])
        P.op("dve", lambda e: e.tensor_scalar(dA[R, 0:64], identf[R, 0:64], s_[R, 4:5], None, ALU.mult), reads=["lnst", "identf"], writes=["dA"])
        P.op("dve", lambda e: e.tensor_scalar(dB[R, 0:64], identf[R, 0:64], s_[R, 5:6], None, ALU.mult), reads=["lnst", "identf"], writes=["dB"])
        bcv = sm0[:, 0:128].rearrange("p (a t) -> p a t", a=2)

        def fnb(e):
            e.matmul(bcv[:, 0, :], onesf[R, :], dA[R, 0:64], start=True, stop=True)
            return e.matmul(bcv[:, 1, :], onesf[R, :], dB[R, 0:64], start=True, stop=True)
        P.op("pe", fnb, reads=["dA", "dB", "onesf"], writes=["sm0"])
        lv = lnt[:, :, 0:64]
        P.op("dve", lambda e: e.tensor_tensor(lv, xT[:, :, bl], bc(bcv[:, 0:1, :], [128, 8, 64]), ALU.mult), reads=["sm0"] + xf_toks(0), writes=["lnt"])
        P.op("dve", lambda e: e.tensor_tensor(lv, lv, bc(bcv[:, 1:2, :], [128, 8, 64]), ALU.add), reads=["sm0", "lnt"], writes=["lnt"])
        P.op("dve", lambda e: e.tensor_tensor(lv, lv, bc(prm[:, l, 40:48].rearrange("p (c o) -> p c o", o=1), [128, 8, 64]), ALU.mult),
             reads=["lnt"] + pl, writes=["lnt"])
        P.op("dve", lambda e: e.tensor_tensor(xT[:, :, bl], lv, bc(prm[:, l, 48:56].rearrange("p (c o) -> p c o", o=1), [128, 8, 64]), ALU.add),
             reads=["lnt"] + pl, writes=xf_toks(0))
        P.op("act", lambda e: e.copy(xB[:, :, bl], xT[:, :, bl]), reads=xf_toks(0), writes=xb_toks(0))
        if write_y:
            for hb in range(2):
                ai = nxt("acc", 2)
                av = acc[ai][R, :].rearrange("p (c t) -> p c t", c=4)

                def fnT(e, av=av, hb=hb):
                    ins = None
                    for c in range(4):
                        ins = e.transpose(av[:, c, :], xT[:, hb * 4 + c, bl], identf[:])
                    return ins
                P.op("pe", fnT, reads=xf_toks(0) + ["identf"], writes=[f"acc{ai}"])
                copy_op("act" if hb == 0 else "dve", yst[0][R, hb * 512:(hb + 1) * 512], acc[ai][R, :], [f"acc{ai}", "yst0"], ["yst0"])
            P.dma("sp", lambda e: e.dma_start(out=yso.rearrange("s t d -> (s t) d"), in_=yst[0][R, :]), reads=["yst0"])

    def wsrc_att(l, j):
        v = w_in[l].rearrange("(kc p) (g jj c) -> p kc g jj c", p=128, g=16, jj=4)
        return v[:, :, 0:4, j, :]

    def wsrc_ret(l, j):
        v = w_in[l].rearrange("(kc p) (g jj c) -> p kc g jj c", p=128, g=16, jj=4)
        return v[:, :, 4:8, j, :]

    def wsrc_lru(l, c):
        v = w_in[l].rearrange("(kc p) (g jj c) -> p kc g jj c", p=128, g=16, jj=4)
        return v[:, :, 8:10, c, :]

    def wsrc_gate(l, mc):
        v = w_in[l].rearrange("(kc p) (g mm c) -> p kc g mm c", p=128, g=8, mm=8)
        return v[:, :, 5:8, mc, :]

    def wsrc_br(l, mc):
        v = di["w_branch"][l].rearrange("br (kc p) (mm c) -> p kc br mm c", p=128, mm=8)
        return v[:, :, :, mc, :]

    def wsrc_sq(name, l, rc):
        v = di[name][l].rearrange("(kc p) (rr c) -> p kc rr c", p=128, rr=8)
        return v[:, :, rc, :]

    import os as _os
    KSTOP = int(_os.environ.get("KSTOP", "99"))
    for b in range(NSEQ if (KSTOP > 0 and not _os.environ.get("KSKIPP")) else 0):
        for hf in range(NHF):
            last = (hf == NHF - 1)
            for l in range(L):
                jobs = []
                for j in range(4):
                    jobs.append((lambda wi, j=j, l=l: load_w(wi, [
                        (wb[wi][:, 0:4096].rearrange("p (kc g c) -> p kc g c", kc=8, g=4), wsrc_att(l, j)),
                        (wb[wi][:, 4096:6144].rearrange("p (kc g c) -> p kc g c", kc=8, g=2), wsrc_lru(l, j))]),
                        lambda wi, j=j, l=l: interleave(att_job(l, b, hf, j, wi, last), lru_job(l, b, hf, j, wi, last, woff=4096))))
                for j in range(4):
                    jobs.append((lambda wi, j=j, l=l: load_w(wi, [(wb[wi][:, 0:4096].rearrange("p (kc g c) -> p kc g c", kc=8, g=4), wsrc_ret(l, j))]),
                                 lambda wi, j=j, l=l: ret_job(l, b, hf, j, wi, last)))
                for mc in range(8):
                    jobs.append((lambda wi, mc=mc, l=l: load_w(wi, [
                        (wb[wi][:, 0:3072].rearrange("p (kc g c) -> p kc g c", kc=8, g=3), wsrc_gate(l, mc)),
                        (wb[wi][:, 3072:4608].rearrange("p (kc g c) -> p kc g c", kc=4, g=3), wsrc_br(l, mc))]),
                        lambda wi, mc=mc, l=l: d1_job(l, mc, wi)))
                for rc in range(8):
                    jobs.append((lambda wi, rc=rc, l=l: load_w(wi, [(wb[wi][:, 0:1024].rearrange("p (kc c) -> p kc c", kc=8), wsrc_sq("w_out", l, rc))]),
                                 lambda wi, rc=rc, l=l: d2_job(l, rc, wi)))
                for rc in range(8):
                    jobs.append((lambda wi, rc=rc, l=l: load_w(wi, [
                        (wb[wi][:, 0:1024].rearrange("p (kc c) -> p kc c", kc=8), wsrc_sq("w_ple_gate", l, rc)),
                        (wb[wi][:, 1024:1280].rearrange("p (kc c) -> p kc c", kc=2),
                         di["w_ple"][l].rearrange("(kc p) (rr c) -> p kc rr c", p=128, rr=8)[:, :, rc, :])]),
                        lambda wi, rc=rc, l=l: d3_job(l, rc, wi)))
                wis = [nxt("wb", 3) for _ in jobs]
                P.barrier()
                KPRE = int(_os.environ.get("KPRE", "15"))
                if KPRE & 8:
                    jobs[0][0](wis[0])
                if l == 0 and (KPRE & 1):
                    load_x(b, hf)
                if KPRE & 2:
                    load_p(l, b, hf)
                if hf == 0:
                    for j in range(4):
                        P.op("dve", lambda e, j=j, l=l: e.memset(Scar[:, l, j, :], 0.0), writes=[f"Scar{l}_{j}"])
                        P.op("dve", lambda e, j=j, l=l: e.memset(convcar[:, l, j, :], 0.0), writes=[f"convcar{l}_{j}"])
                        P.op("dve", lambda e, j=j, l=l: e.memset(hcar[:, l, j:j + 1], 0.0), writes=[f"hcar{l}_{j}"])
                if KPRE & 4:
                    build_EB(l)
                if KSTOP < 99 and (b, hf, l) != (0, 0, 0):
                    continue
                if KSTOP <= 1:
                    continue
                if len(jobs) > 1:
                    jobs[1][0](wis[1])
                for k, (ld, cp) in enumerate(jobs):
                    if KSTOP == 2 and k >= 4 or KSTOP == 3 and k >= 8 or KSTOP == 5 and k >= 16:
                        break
                    if k in (0, 4, 8):
                        P.barrier()
                    if k + 2 < len(jobs):
                        jobs[k + 2][0](wis[k + 2])
                    cp(wis[k])
                if KSTOP >= 7:
                    ln_phase(l, b, hf, write_y=(l == L - 1))

    if NSMP > 0 and KSTOP >= 99:
        CFG["w"] = NSMP * 32
        CFG["ntt"] = 1
        for l in range(L):
            jobs = []
            for j in range(4):
                jobs.append((lambda wi, j=j, l=l: load_w(wi, [(wb[wi][:, 0:4096].rearrange("p (kc g c) -> p kc g c", kc=8, g=4), wsrc_att(l, j))]),
                             lambda wi, j=j, l=l: s_att_job(l, j, wi)))
            for j in range(4):
                jobs.append((lambda wi, j=j, l=l: load_w(wi, [(wb[wi][:, 0:4096].rearrange("p (kc g c) -> p kc g c", kc=8, g=4), wsrc_ret(l, j))]),
                             lambda wi, j=j, l=l: s_ret_job(l, j, wi)))
            for c in range(4):
                jobs.append((lambda wi, c=c, l=l: load_w(wi, [(wb[wi][:, 0:2048].rearrange("p (kc g c) -> p kc g c", kc=8, g=2), wsrc_lru(l, c))]),
                             lambda wi, c=c, l=l: s_lru_job(l, c, wi)))
            for mc in range(8):
                jobs.append((lambda wi, mc=mc, l=l: load_w(wi, [
                    (wb[wi][:, 0:3072].rearrange("p (kc g c) -> p kc g c", kc=8, g=3), wsrc_gate(l, mc)),
                    (wb[wi][:, 3072:4608].rearrange("p (kc g c) -> p kc g c", kc=4, g=3), wsrc_br(l, mc))]),
                    lambda wi, mc=mc, l=l: d1_job(l, mc, wi)))
            for rc in range(8):
                jobs.append((lambda wi, rc=rc, l=l: load_w(wi, [(wb[wi][:, 0:1024].rearrange("p (kc c) -> p kc c", kc=8), wsrc_sq("w_out", l, rc))]),
                             lambda wi, rc=rc, l=l: d2_job(l, rc, wi)))
            for rc in range(8):
                jobs.append((lambda wi, rc=rc, l=l: load_w(wi, [
                    (wb[wi][:, 0:1024].rearrange("p (kc c) -> p kc c", kc=8), wsrc_sq("w_ple_gate", l, rc)),
                    (wb[wi][:, 1024:1280].rearrange("p (kc c) -> p kc c", kc=2),
                     di["w_ple"][l].rearrange("(kc p) (rr c) -> p kc rr c", p=128, rr=8)[:, :, rc, :])]),
                    lambda wi, rc=rc, l=l: d3_job(l, rc, wi)))
            wis = [nxt("wb", 3) for _ in jobs]
            P.barrier()
            jobs[0][0](wis[0])
            s_load(l)
            build_EB(l)
            jobs[1][0](wis[1])
            for k, (ld, cp) in enumerate(jobs):
                if k in (0, 4, 8, 12):
                    P.barrier()
                if k + 2 < len(jobs):
                    jobs[k + 2][0](wis[k + 2])
                cp(wis[k])
            s_ln(l, write_y=(l == L - 1))

    P.wait_all("sp")
    print("PROG nrec", P.nrec, {e: len(v) for e, v in P.ops.items()})
    if _os.environ.get("KLOG"):
        with open(_os.environ["KLOG"], "w") as f:
            for r in P.log:
                f.write(repr(r) + "\n")
    with nc.allow_non_contiguous_dma(reason="small param / state vectors"):
        P.emit(sems)
    es.close()
    return nc


OUT_NAMES = ["y_prompt", "y_sample", "k_a_prompt", "v_a_prompt", "k_a_sample", "v_a_sample",
             "ret_prompt", "ret_sample", "conv_prompt", "conv_sample", "lru_prompt", "lru_sample"]


def kernel(**inputs):
    NSEQ = 4
    NSMP = 2
    nc = build(NSEQ=NSEQ, NSMP=NSMP)
    consts = host_consts()
    in_maps = []
    for c in range(N_CORES):
        m = {}
        m["x_prompt"] = np.ascontiguousarray(inputs["x_prompt"][c * NSEQ:(c + 1) * NSEQ])
        m["p_prompt"] = np.ascontiguousarray(inputs["p_prompt"][:, c * NSEQ:(c + 1) * NSEQ])
        m["x_sample"] = np.ascontiguousarray(inputs["x_sample"][c * NSMP:(c + 1) * NSMP])
        for k in ("p_sample", "cache_k_a", "cache_v_a", "state_ret", "state_conv", "state_lru"):
            m[k] = np.ascontiguousarray(inputs[k][:, c * NSMP:(c + 1) * NSMP])
        for k in W_NAMES:
            m[k] = np.ascontiguousarray(inputs[k])
        m.update(consts)
        in_maps.append(m)
    res = run_bass_kernel_spmd(nc, in_maps, core_ids=list(range(N_CORES)))
    R = res.results
    out = {}
    out["y_prompt"] = np.concatenate([r["y_prompt"] for r in R], 0)
    out["y_sample"] = np.concatenate([r["y_sample"] for r in R], 0)
    for nm in ("k_a_prompt", "v_a_prompt", "ret_prompt", "conv_prompt", "lru_prompt",
               "k_a_sample", "v_a_sample", "ret_sample", "conv_sample", "lru_sample"):
        out[nm] = np.concatenate([r[nm] for r in R], 1)
    return tuple(np.asarray(out[n], dtype=np.float32) for n in OUT_NAMES)
```

```python
import numpy as np
from contextlib import ExitStack
import concourse.bass as bass
import concourse.mybir as mybir
from concourse.bass_utils import run_bass_kernel_spmd

F32 = mybir.dt.float32
BF16 = mybir.dt.bfloat16
AF = mybir.ActivationFunctionType
ALU = mybir.AluOpType
AX = mybir.AxisListType

ENGS = ("pe", "act", "dve", "pool", "sp")
N_CORES = 8
D = 1024
SEQ = 2048
NT = 1024
NB = NT // 128
NTT = NT // 512
NHF = SEQ // NT
ALPHA = (2 * 2) ** 0.25
LN_EPS = 1e-5


class Prog:
    def __init__(self, nc):
        self.nc = nc
        self.ops = {e: [] for e in ENGS}
        self.cnt = {e: 0 for e in ENGS}
        self.known = {e: {} for e in ENGS}
        self.tw = {}
        self.tr = {}
        n_lanes = {"sp": 6, "act": 2, "pool": 4}
        self.lanes = {q: [[f"dma_{q}_{i}", 0] for i in range(n)] for q, n in n_lanes.items()}
        self.lane_rr = {q: 0 for q in n_lanes}
        self.semkeys = [f"c_{e}" for e in ENGS if e != "sp"] + [l[0] for q in self.lanes for l in self.lanes[q]]
        import os
        self.maxop = int(os.environ.get("KMAXOP", "1000000000"))
        self.nrec = 0
        self.log = []

    def _deps(self, eng, reads, writes):
        deps = []
        for t in list(reads) + list(writes):
            ev = self.tw.get(t)
            if ev is not None:
                deps.append(ev)
        for t in writes:
            deps.extend(self.tr.get(t, ()))
        kn = self.known[eng]
        best = {}
        for (sk, v) in deps:
            if eng == "pe" and sk == "c_pe":
                continue
            if kn.get(sk, 0) >= v:
                continue
            if best.get(sk, 0) < v:
                best[sk] = v
        waits = []
        for sk, v in best.items():
            kn[sk] = v
            waits.append((sk, v))
        return waits

    def _commit(self, ev, reads, writes):
        for t in reads:
            self.tr.setdefault(t, []).append(ev)
        for t in writes:
            self.tw[t] = ev
            self.tr[t] = []

    def op(self, eng, fn, reads=(), writes=()):
        self.nrec += 1
        if self.nrec > self.maxop:
            return None
        self.log.append((self.nrec, eng, fn.__code__.co_firstlineno, tuple(writes)))
        PS = ("acc", "big", "sm0", "sm1")
        writes = list(writes) + [t for t in reads if t.startswith(PS)]
        reads = [t for t in reads if not t.startswith(PS)]
        waits = self._deps(eng, reads, writes)
        self.cnt[eng] += 1
        ev = (f"c_{eng}", self.cnt[eng])
        self.ops[eng].append((waits, fn, ev[0], 1))
        self._commit(ev, reads, writes)
        return ev

    def dma(self, q, fn, reads=(), writes=()):
        self.nrec += 1
        if self.nrec > self.maxop:
            return None
        self.log.append((self.nrec, "dma_" + q, fn.__code__.co_firstlineno, tuple(writes)))
        lanes = self.lanes[q]
        i = self.lane_rr[q]
        self.lane_rr[q] = (i + 1) % len(lanes)
        lane = lanes[i]
        waits = self._deps(q, reads, writes)
        if lane[1] > 0 and self.known[q].get(lane[0], 0) < lane[1]:
            self.known[q][lane[0]] = lane[1]
            waits.append((lane[0], lane[1]))
        lane[1] += 16
        ev = (lane[0], lane[1])
        self.ops[q].append((waits, fn, lane[0], 16))
        self._commit(ev, reads, writes)
        return ev

    def barrier(self):
        evs = [(f"c_{e}", self.cnt[e]) for e in ENGS if e != "sp" and self.cnt[e] > 0]
        evs += [(lane[0], lane[1]) for lane in self.lanes["sp"] if lane[1] > 0]
        for eng in ENGS:
            waits = []
            kn = self.known[eng]
            for (sk, v) in evs:
                if sk == f"c_{eng}" and eng == "pe":
                    continue
                if kn.get(sk, 0) >= v:
                    continue
                kn[sk] = v
                waits.append((sk, v))
            if waits:
                self.ops[eng].append((waits, None, None, 0))

    def wait_all(self, eng):
        waits = []
        for e in ENGS:
            if e == "sp" or self.cnt[e] == 0:
                continue
            waits.append((f"c_{e}", self.cnt[e]))
        for q in self.lanes:
            for lane in self.lanes[q]:
                if lane[1] > 0:
                    waits.append((lane[0], lane[1]))
        self.ops[eng].append((waits, None, None, 0))

    def emit(self, sems):
        nc = self.nc

        def run(engine, lst):
            for waits, fn, sk, inc in lst:
                for (wk, wv) in waits:
                    engine.wait_ge(sems[wk], wv)
                if fn is not None:
                    ins = fn(engine)
                    ins.then_inc(sems[sk], inc)

        with nc.Block() as block:
            @block.tensor
            def _(e):
                run(e, self.ops["pe"])

            @block.scalar
            def _(e):
                run(e, self.ops["act"])

            @block.vector
            def _(e):
                run(e, self.ops["dve"])

            @block.gpsimd
            def _(e):
                run(e, self.ops["pool"])

            @block.sync
            def _(e):
                run(e, self.ops["sp"])


def host_consts():
    half = 32
    inv = (10000.0 ** (-np.arange(half, dtype=np.float32) / half)).astype(np.float32)
    pos = np.arange(SEQ, dtype=np.float32)
    ang = pos[:, None] * inv[None, :]
    c = np.cos(ang).astype(np.float32)
    s = np.sin(ang).astype(np.float32)
    cc = np.concatenate([c, c], -1).reshape(SEQ // 128, 128, 64).transpose(1, 0, 2)
    ss = np.concatenate([-s, s], -1).reshape(SEQ // 128, 128, 64).transpose(1, 0, 2)
    h = np.arange(8, dtype=np.float32)
    log_g = np.log1p(-np.exp2(-5.0 - h)).astype(np.float64)
    p = np.arange(128, dtype=np.float64)
    xi = np.exp(log_g[None, :] * (p[:, None] + 1.0))
    zi = np.exp(-log_g[None, :] * (p[:, None] + 1.0)) * (64 ** -0.5)
    gt = np.zeros((128, 4, 64), np.float64)
    gt32 = np.zeros((128, 4, 64), np.float64)
    for pp in range(128):
        for cch in range(4):
            hh = 2 * cch + pp // 64
            gt[pp, cch, :] = np.exp(log_g[hh] * 128.0)
            gt32[pp, cch, :] = np.exp(log_g[hh] * 32.0)
    ident = np.eye(128, dtype=np.float32)
    jj = np.arange(128)
    mask = (jj[None, :] >= jj[:, None]).astype(np.float32)
    return {
        "c_cc": np.ascontiguousarray(cc, dtype=np.float32),
        "c_ss": np.ascontiguousarray(ss, dtype=np.float32),
        "c_xi": xi.astype(np.float32),
        "c_zi": zi.astype(np.float32),
        "c_gt": gt.astype(np.float32),
        "c_gt32": gt32.astype(np.float32),
        "c_ident": ident,
        "c_mask": mask,
        "c_anti": np.ascontiguousarray(ident[::-1]),
    }


W_NAMES = {
    "w_in": [2, 1024, 8192], "rel_table": [2, 8, 257], "gn_gain": [2, 512], "conv_w": [2, 4, 512],
    "conv_b": [2, 512], "w_gate_a": [2, 8, 64, 64], "b_gate_a": [2, 512], "w_gate_x": [2, 8, 64, 64],
    "b_gate_x": [2, 512], "lru_lambda": [2, 512], "w_branch": [2, 3, 512, 1024], "w_out": [2, 1024, 1024],
    "ln_gain": [2, 1024], "ln_bias": [2, 1024], "w_ple": [2, 256, 1024], "w_ple_gate": [2, 1024, 1024],
}
C_SHAPES = {"c_cc": [128, 16, 64], "c_ss": [128, 16, 64], "c_xi": [128, 8], "c_zi": [128, 8],
            "c_gt": [128, 4, 64], "c_gt32": [128, 4, 64], "c_ident": [128, 128], "c_mask": [128, 128], "c_anti": [128, 128]}


def build(NSEQ=4, L=2, NSMP=2):
    nc = bass.Bass("TRN2", target_bir_lowering=False)
    di = {}

    def din(name, shape):
        di[name] = nc.dram_tensor(name, shape, F32, kind="ExternalInput").ap()
        return di[name]

    def dout(name, shape):
        di[name] = nc.dram_tensor(name, shape, F32, kind="ExternalOutput").ap()
        return di[name]

    xp = din("x_prompt", [NSEQ, SEQ, D])
    pp_ = din("p_prompt", [L, NSEQ, SEQ, 256])
    for k, shp in W_NAMES.items():
        din(k, shp)
    for k, shp in C_SHAPES.items():
        din(k, shp)
    yo = dout("y_prompt", [NSEQ, SEQ, D])
    ko = dout("k_a_prompt", [L, NSEQ, 512, 8, 64])
    vo = dout("v_a_prompt", [L, NSEQ, 512, 8, 64])
    ro = dout("ret_prompt", [L, NSEQ, 8, 64, 64])
    co = dout("conv_prompt", [L, NSEQ, 3, 512])
    lo = dout("lru_prompt", [L, NSEQ, 512])
    xs_d = din("x_sample", [NSMP, 32, D])
    ps_d = din("p_sample", [L, NSMP, 32, 256])
    ck_d = din("cache_k_a", [L, NSMP, 512, 8, 64])
    cv_d = din("cache_v_a", [L, NSMP, 512, 8, 64])
    sr_d = din("state_ret", [L, NSMP, 8, 64, 64])
    sc_d = din("state_conv", [L, NSMP, 3, 512])
    sl_d = din("state_lru", [L, NSMP, 512])
    yso = dout("y_sample", [NSMP, 32, D])
    kso = dout("k_a_sample", [L, NSMP, 32, 8, 64])
    vso = dout("v_a_sample", [L, NSMP, 32, 8, 64])
    rso = dout("ret_sample", [L, NSMP, 8, 64, 64])
    cso = dout("conv_sample", [L, NSMP, 3, 512])
    lso = dout("lru_sample", [L, NSMP, 512])
    gt32 = None
    ext = nc.dram_tensor("ext_scratch", [L, 8, 768], F32, kind="Internal").ap()

    es = ExitStack()

    def sb(name, shape, dt=F32):
        return es.enter_context(nc.sbuf_tensor(name, shape, dt))

    def ps(name, shape, dt=F32):
        return es.enter_context(nc.psum_tensor(name, shape, dt))

    xT = sb("xT", [128, 8, NT])
    xB = sb("xB", [128, 8, NT], BF16)
    yg = [sb(f"yg{i}", [128, 4, NT], BF16) for i in range(3)]
    merged = sb("merged", [128, 8, NT], BF16)
    pT = sb("pT", [128, 2, NT], BF16)
    WBN = 6144
    wb = [sb(f"wb{i}", [128, WBN], BF16) for i in range(3)]
    EB = sb("EB", [128, 8, 5, 128], BF16)
    biasst = [sb("biasst0", [128, 5, 128])]
    cc = sb("cc", [128, 16, 64]); ss = sb("ss", [128, 16, 64])
    xi = sb("xi", [128, 8]); zi = sb("zi", [128, 8])
    gt = sb("gt", [128, 4, 64])
    gt32 = sb("gt32", [128, 4, 64])
    identf = sb("identf", [128, 128]); identb = sb("identb", [128, 128], BF16)
    maskb = sb("maskb", [128, 128], BF16)
    onesf = sb("onesf", [128, 128])
    antif = sb("antif", [128, 128])
    mhalf = sb("mhalf", [128, 16])
    NP = 64
    prm = sb("prm", [128, L, NP])
    WgA = sb("WgA", [128, L, 4, 128], BF16); WgX = sb("WgX", [128, L, 4, 128], BF16)
    kcar = sb("kcar", [128, L, 4, 512], BF16)
    vcar = sb("vcar", [128, L, 4, 4 * 130], BF16)
    Scar = sb("Scar", [128, L, 4, 64])
    convcar = sb("convcar", [128, L, 4, 3])
    hcar = sb("hcar", [128, L, 4])
    sz = sb("sz", [128, NT], BF16)
    SCRN = 7168
    scr = sb("scr", [128, SCRN])
    scrb = scr.bitcast(BF16)

    def carve(items, start=0):
        out = {}
        off = start
        for name, n, dt in items:
            nbytes = n * (4 if dt == F32 else 2)
            if dt == F32:
                out[name] = scr[:, off // 4: off // 4 + n]
            else:
                out[name] = scrb[:, off // 2: off // 2 + n]
            off += (nbytes + 63) // 64 * 64
        assert off <= SCRN * 4, (off, SCRN * 4)
        return out

    cv = carve([("xin0", 1024, F32), ("xin1", 1024, F32), ("pin0", 256, F32), ("pin1", 256, F32),
                ("rts", 257, F32), ("exts", 768, F32)])
    xin = [cv["xin0"], cv["xin1"]]; pin = [cv["pin0"], cv["pin1"]]
    rts = cv["rts"][0:8, :]; exts = cv["exts"][0:8, :]
    cv = carve([("QT", NT, BF16), ("KT", 512 + NT, BF16), ("Vv", (4 + NB) * 130, BF16), ("Eb0", 640, BF16), ("Eb1", 640, BF16),
                ("PT0", 640, BF16), ("PT1", 640, BF16), ("PT2", 640, BF16), ("PT3", 640, BF16), ("rcp", 2, F32), ("ya", 128, BF16), ("kvst0", 128, F32), ("kvst1", 128, F32), ("kctm", 512, BF16)])
    QT = cv["QT"]; KT = cv["KT"]
    Vv = cv["Vv"].rearrange("p (a h d) -> p a h d", a=4 + NB, h=2)
    Eb = [cv["Eb0"].rearrange("p (a q) -> p a q", a=5), cv["Eb1"].rearrange("p (a q) -> p a q", a=5)]
    PTb = [cv["PT0"].rearrange("p (a q) -> p a q", a=5), cv["PT1"].rearrange("p (a q) -> p a q", a=5)]
    PTd = [[cv[f"PT{2 * par + hh}"].rearrange("p (a q) -> p a q", a=5) for hh in range(2)] for par in range(2)]
    rcp = cv["rcp"]; ya = cv["ya"]; kvst = [cv["kvst0"], cv["kvst1"]]
    kctm = cv["kctm"].rearrange("p (a c) -> p a c", a=4)
    ya2 = [ya, cv["kctm"][:, 0:128]]
    cv = carve([("QTr", NT, BF16), ("KTr", NT, BF16), ("Vb", NB * 128, BF16), ("rt1", 128, F32), ("rt2", 128, F32),
                ("Qt", 128, BF16), ("Kt", 128, BF16), ("kvbuf", 64 * NB, F32), ("Sall", 64 * NB, F32), ("Gpat", 64 * NB, F32),
                ("Sop", NB * 64, BF16), ("Am", 256, BF16), ("ysb", 128, F32), ("ysq", 128, F32), ("gst", 16, F32), ("ynb", 128, BF16), ("S0f", 64, F32), ("Snew", 64, F32),
                ("QtA", NB * 128, BF16), ("KtA", NB * 128, BF16), ("AmA", NB * 256, BF16), ("gstA", 96, F32), ("ynbA", NB * 128, BF16)])
    QTr = cv["QTr"]; KTr = cv["KTr"]; Vb = cv["Vb"].rearrange("p (n c) -> p n c", n=NB)
    rt1 = cv["rt1"]; rt2 = cv["rt2"]; Qt = cv["Qt"]; Kt = cv["Kt"]
    kvbuf = cv["kvbuf"].rearrange("p (e n) -> p e n", n=NB); Sall = cv["Sall"].rearrange("p (e n) -> p e n", n=NB)
    Gpat = cv["Gpat"].rearrange("p (e n) -> p e n", n=NB); Sop = cv["Sop"].rearrange("p (n e) -> p n e", n=NB)
    S0f = cv["S0f"]; Snew = cv["Snew"]
    QtA = cv["QtA"].rearrange("p (t c) -> p t c", t=NB); KtA = cv["KtA"].rearrange("p (t c) -> p t c", t=NB)
    AmA = cv["AmA"].rearrange("p (t h i) -> p t h i", t=NB, h=2); gstA = cv["gstA"].rearrange("p (k n) -> p k n", k=6)
    ynbA = cv["ynbA"].rearrange("p (t c) -> p t c", t=NB)
    mgf = merged.bitcast(F32)[:].rearrange("p a b -> p (a b)")
    qkraw = mgf[:, 0:2048].rearrange("p (t c) -> p t c", t=NB)
    rt1A = mgf[:, 2048:3072].rearrange("p (t c) -> p t c", t=NB)
    rt2A = mgf[:, 3072:4096].rearrange("p (t c) -> p t c", t=NB)
    ysbA = mgf[:, 0:1024].rearrange("p (t c) -> p t c", t=NB)
    ysqA = mgf[:, 1024:2048].rearrange("p (t c) -> p t c", t=NB)
    Am = cv["Am"].rearrange("p (h i) -> p h i", h=2); ysb = cv["ysb"]; ysq = cv["ysq"]; gst = cv["gst"]; ynb = cv["ynb"]
    cv = carve([("bB", NT, F32), ("bC", NT, F32), ("szc", NT, BF16)], start=18432)
    bB = cv["bB"]; bC = cv["bC"]; szc_buf = cv["szc"]
    _mg = merged.bitcast(F32)[:].rearrange("p a b -> p (a b)")
    xrbuf = _mg[:, 0:3 + NT]; xc = _mg[:, 1028:1028 + NT]; bA = _mg[:, 2052:2052 + NT]
    xcb = merged[:].rearrange("p a b -> p (a b)")[:, 2 * 3076:2 * 3076 + NT]
    cv = carve([("gbuf0", 512, BF16), ("gbuf1", 512, BF16), ("gbuf2", 512, BF16), ("tb3_0", 512, F32), ("tb3_1", 512, F32), ("tb3_2", 512, F32),
                ("lnst", 8 * NB, F32), ("lntmp", 128, F32), ("lnt2", 128, F32), ("dA", 128, F32), ("dB", 128, F32), ("lnt", 1024, F32),
                ("yst0", 1024, F32), ("yst1", 1024, F32)])
    gbuf = [cv["gbuf0"], cv["gbuf1"], cv["gbuf2"]]; tb3 = [cv["tb3_0"], cv["tb3_1"], cv["tb3_2"]]
    lnst = cv["lnst"]; lntmp = cv["lntmp"]; lnt2 = cv["lnt2"]; dA = cv["dA"]; dB = cv["dB"]
    lnt = cv["lnt"].rearrange("p (c t) -> p c t", c=8); yst = [cv["yst0"], cv["yst1"]]

    acc = [ps(f"acc{i}", [128, 512]) for i in range(2)]
    big = [ps(f"big{i}", [128, 1024]) for i in range(2)]
    sm0 = ps("sm0", [128, 512])
    sm1 = ps("sm1", [128, 1024], BF16)

    P = Prog(nc)
    sems = {k: es.enter_context(nc.semaphore(k)) for k in P.semkeys}
    rr = {"acc": 0, "ev": 0, "xin": 0, "pin": 0, "wb": 0, "kvst": 0, "bst": 0, "yst": 0}

    def nxt(key, n):
        v = rr[key]
        rr[key] = (v + 1) % n
        return v

    def bc(ap, shape):
        return ap.to_broadcast(shape)

    def mm_group(out_ap, out_tok, pairs, rtoks):
        def fn(e):
            ins = None
            n = len(pairs)
            for i, (l_, r_) in enumerate(pairs):
                ins = e.matmul(out_ap, l_, r_, start=(i == 0), stop=(i == n - 1))
            return ins
        P.op("pe", fn, reads=rtoks, writes=[out_tok])

    def xb_toks(tt):
        return [f"xB{kc}_{tt}" for kc in range(8)]

    def xf_toks(tt):
        return [f"xT{kc}_{tt}" for kc in range(8)]

    def evac_engine():
        return "act" if nxt("ev", 2) == 0 else "dve"

    def copy_op(eng, out_ap, in_ap, reads, writes, scale=None):
        if eng == "act":
            if scale is None:
                P.op("act", lambda e: e.copy(out_ap, in_ap), reads=reads, writes=writes)
            else:
                P.op("act", lambda e: e.mul(out_ap, in_ap, scale), reads=reads, writes=writes)
        else:
            if scale is None:
                P.op(eng, lambda e: e.tensor_scalar(out_ap, in_ap, 1.0, None, ALU.mult), reads=reads, writes=writes)
            else:
                P.op(eng, lambda e: e.tensor_scalar(out_ap, in_ap, scale, None, ALU.mult), reads=reads, writes=writes)

    def act_fn(out_ap, in_ap, func, reads, writes, bias=None, scale=None):
        kw = {}
        if bias is not None:
            kw["bias"] = bias
        if scale is not None:
            kw["scale"] = scale
        P.op("act", lambda e: e.activation(out_ap, in_ap, func, **kw), reads=reads, writes=writes)

    import os as _os
    def load_const(dst, name, toks):
        P.dma("sp", lambda e: e.dma_start(out=dst, in_=di[name]), writes=toks)

    load_const(cc[:], "c_cc", ["cc"]); load_const(ss[:], "c_ss", ["ss"])
    load_const(xi[:], "c_xi", ["xi"]); load_const(zi[:], "c_zi", ["zi"])
    load_const(gt[:], "c_gt", ["gt"]); load_const(identf[:], "c_ident", ["identf"]); load_const(antif[:], "c_anti", ["antif"]); load_const(gt32[:], "c_gt32", ["gt32"])
    P.dma("pool", lambda e: e.dma_start(out=identb[:], in_=di["c_ident"]), writes=["identb"])
    P.dma("pool", lambda e: e.dma_start(out=maskb[:], in_=di["c_mask"]), writes=["maskb"])
    P.op("dve", lambda e: e.memset(onesf[:], 1.0), writes=["onesf"])
    P.op("dve", lambda e: e.memset(mhalf[:], -0.5), writes=["mhalf"])
    P.op("dve", lambda e: e.memset(WgA[:], 0.0), writes=["WgA"])
    P.op("dve", lambda e: e.memset(WgX[:], 0.0), writes=["WgX"])
    P.op("dve", lambda e: e.memset(prm[:], 0.0), writes=["prm"])

    def pcol(l, a, b):
        return prm[:, l, a:b]

    for l in range(L):
        def vec_load(name, col, n, l=l):
            src = di[name][l].rearrange("(c p) -> p c", p=128)
            P.dma("sp", lambda e: e.dma_start(out=prm[:, l, col:col + n], in_=src), reads=["prm"], writes=[f"prm{l}_{col}"])
        vec_load("gn_gain", 0, 4)
        vec_load("conv_b", 4, 4)
        for tap in range(4):
            src = di["conv_w"][l, tap].rearrange("(c p) -> p c", p=128)
            P.dma("sp", lambda e, src=src, tap=tap, l=l: e.dma_start(out=prm[:, l, 8 + 4 * tap:12 + 4 * tap], in_=src),
                  reads=["prm"], writes=[f"prm{l}_cw{tap}"])
        vec_load("b_gate_a", 24, 4)
        vec_load("b_gate_x", 28, 4)
        vec_load("lru_lambda", 32, 4)
        vec_load("ln_gain", 40, 8)
        vec_load("ln_bias", 48, 8)
        act_fn(prm[:, l, 36:40], prm[:, l, 32:36], AF.Exp, [f"prm{l}_32"], [f"prm{l}_sp"], scale=-1.0)
        act_fn(prm[:, l, 36:40], prm[:, l, 36:40], AF.Ln, [f"prm{l}_sp"], [f"prm{l}_sp"], bias=onesf[:, 0:1])
        P.op("dve", lambda e, l=l: e.tensor_scalar(prm[:, l, 56:60], prm[:, l, 36:40], -16.0, None, ALU.mult),
             reads=[f"prm{l}_sp"], writes=[f"prm{l}_sp2"])
        P.op("dve", lambda e, l=l: e.tensor_scalar(prm[:, l, 36:40], prm[:, l, 36:40], -8.0, None, ALU.mult),
             reads=[f"prm{l}_sp", f"prm{l}_sp2"], writes=[f"prm{l}_sp"])
        for nm, Wt in (("w_gate_a", WgA), ("w_gate_x", WgX)):
            srcv = di[nm][l].rearrange("(c hh) i j -> hh i c j", hh=2)
            for hh in range(2):
                P.dma("pool", lambda e, Wt=Wt, srcv=srcv, hh=hh, l=l: e.dma_start(
                    out=Wt[hh * 64:(hh + 1) * 64, l, :, hh * 64:(hh + 1) * 64], in_=srcv[hh]),
                    reads=["WgA" if nm == "w_gate_a" else "WgX"], writes=[f"{nm}{l}_{hh}"])
        P.dma("sp", lambda e, l=l: e.dma_start(out=rts[:], in_=di["rel_table"][l]), writes=["rts"])
        P.op("dve", lambda e: e.tensor_copy(exts[:, 0:256], rts[:, 1:257]), reads=["rts"], writes=["exts"])
        P.op("dve", lambda e: e.tensor_copy(exts[:, 256:768], bc(rts[:, 256:257], [8, 512])),
             reads=["rts", "exts"], writes=["exts"])
        P.dma("sp", lambda e, l=l: e.dma_start(out=ext[l], in_=exts[:]), reads=["exts"], writes=[f"ext{l}"])
    WgTok = [[f"w_gate_a{l}_0", f"w_gate_a{l}_1", f"w_gate_x{l}_0", f"w_gate_x{l}_1", "WgA", "WgX"] for l in range(L)]
    prm_all = lambda l: ([f"prm{l}_{c}" for c in (0, 4, 24, 28, 40, 48)] + [f"prm{l}_cw{t}" for t in range(4)]
                         + [f"prm{l}_sp", f"prm{l}_sp2", "prm"])

    w_in = di["w_in"]
    P.barrier()

    def load_w(i, pieces):
        flat = []
        for dst, src in pieces:
            if len(dst.shape) == 4:
                for gi in range(dst.shape[2]):
                    flat.append((dst[:, :, gi, :], src[:, :, gi, :]))
            else:
                flat.append((dst, src))
        assert len(flat) <= 6
        for k, (dst, src) in enumerate(flat):
            P.dma("pool", lambda e, dst=dst, src=src: e.dma_start(out=dst, in_=src), writes=[f"wb{i}"] if k == 0 else [f"wb{i}_p{k}"])

    def wtoks(i, npieces):
        return [f"wb{i}"] + [f"wb{i}_p{k}" for k in range(1, 6)]

    def build_EB(l):
        for h in range(8):
            bst = biasst[0]
            base = ext[l, h, 0:1]
            src = bass.AP(base.tensor, base.offset, [[1, 128], [128, 5], [1, 128]])
            P.dma("sp", lambda e, bst=bst, src=src: e.dma_start(out=bst[:], in_=src), reads=[f"ext{l}"], writes=["bst0"])
            bv = big[0][:, 0:640]

            def fn(e, bst=bst, bv=bv):
                bf = bst[:].rearrange("p a q -> p (a q)")
                e.matmul(bv[:, 0:512], antif[:], bf[:, 0:512], start=True, stop=True)
                return e.matmul(bv[:, 512:640], antif[:], bf[:, 512:640], start=True, stop=True)
            P.op("pe", fn, reads=["bst0", "antif"], writes=["big0"])
            bv5 = bv.rearrange("p (a q) -> p a q", a=5)
            act_fn(EB[:, h, 0:4, :], bv5[:, 0:4, :], AF.Exp, ["big0"], ["EB"])
            act_fn(EB[:, h, 4:5, :], bv5[:, 4:5, :], AF.Exp, ["big0", "EB"], ["EB"])
        P.op("dve", lambda e: e.memset(EB[0:64, :, 4, 64:128], 0.0), reads=["EB"], writes=["EB"])
        P.op("dve", lambda e: e.memset(EB[64:128, :, 0, 0:64], 0.0), reads=["EB"], writes=["EB"])

    def load_x(b, hf):
        for tb in range(NB):
            xi_ = nxt("xin", 2)
            r0 = hf * NT + tb * 128
            P.dma("sp", lambda e, xi_=xi_, r0=r0: e.dma_start(out=xin[xi_][:], in_=xp[b, r0:r0 + 128, :]), writes=[f"xin{xi_}"])
            tt = tb // 4
            for hb in range(4):
                sv = sm0[:, 0:256].rearrange("p (c t) -> p c t", c=2)

                def fn(e, xi_=xi_, sv=sv, hb=hb):
                    ins = None
                    for c in range(2):
                        cg = hb * 2 + c
                        ins = e.transpose(sv[:, c, :], xin[xi_][:, cg * 128:(cg + 1) * 128], identf[:])
                    return ins
                P.op("pe", fn, reads=[f"xin{xi_}", "identf"], writes=["sm0"])
                cs = slice(hb * 2, hb * 2 + 2)
                if int(_os.environ.get("KX", "3")) & 1:
                    P.op("act", lambda e, tb=tb, sv=sv, cs=cs: e.copy(xT[:, cs, tb * 128:(tb + 1) * 128], sv),
                         reads=["sm0"], writes=[f"xT{c}_{tt}" for c in range(8)])
                if int(_os.environ.get("KX", "3")) & 2:
                    P.op("act", lambda e, tb=tb, sv=sv, cs=cs: e.copy(xB[:, cs, tb * 128:(tb + 1) * 128], sv),
                         reads=["sm0"], writes=[f"xB{c}_{tt}" for c in range(8)])

    def load_p(l, b, hf):
        for tb in range(NB):
            pi_ = nxt("pin", 2)
            r0 = hf * NT + tb * 128
            P.dma("sp", lambda e, pi_=pi_, r0=r0: e.dma_start(out=pin[pi_][:], in_=pp_[l, b, r0:r0 + 128, :]), writes=[f"pin{pi_}"])
            sv = sm0[:, 0:256].rearrange("p (c t) -> p c t", c=2)

            def fn(e, pi_=pi_, sv=sv):
                ins = None
                for c in range(2):
                    ins = e.transpose(sv[:, c, :], pin[pi_][:, c * 128:(c + 1) * 128], identf[:])
                return ins
            P.op("pe", fn, reads=[f"pin{pi_}", "identf"], writes=["sm0"])
            copy_op(evac_engine(), pT[:, :, tb * 128:(tb + 1) * 128], sv, ["sm0"], [f"pT_{tb // 4}"])

    CFG = {"w": 512, "ntt": NTT}

    def csl(tt):
        return slice(tt * CFG["w"], (tt + 1) * CFG["w"])

    def cw(ap):
        return ap[:, 0:CFG["w"]]

    def fm_mm(out_acc, ai, lhs_fn, tt, wt, nkc=8, rhs_src=None, rhs_toks=None):
        src = xB if rhs_src is None else rhs_src
        pairs = [(lhs_fn(kc), src[:, kc, csl(tt)]) for kc in range(nkc)]
        mm_group(cw(out_acc), f"acc{ai}", pairs, (xb_toks(tt) if rhs_toks is None else rhs_toks) + wt)

    def att_job(l, b, hf, j, wi, last):
        wt = wtoks(wi, 1)
        wv = wb[wi][:, 0:4096].rearrange("p (kc g c) -> p kc g c", kc=8, g=4)
        P.op("dve", lambda e: e.memset(Vv[:, :, :, 64:65], 1.0), writes=["Vv"])
        if hf > 0:
            copy_op("dve", KT[:, 0:512], kcar[:, l, j, :], [f"kcar{l}_{j}"], ["KT"])
            copy_op("act", Vv[:, 0:4, :, :], vcar[:, l, j, :].rearrange("p (a h d) -> p a h d", a=4, h=2), [f"vcar{l}_{j}"], ["Vv"])
        for tt in range(NTT):
            ai = nxt("acc", 2)
            fm_mm(acc[ai], ai, lambda kc: wv[:, kc, 0, :], tt, wt)
            copy_op("act", QT[:, tt * 512:(tt + 1) * 512], acc[ai][:], [f"acc{ai}"], ["QT"], scale=0.125)
            ai = nxt("acc", 2)
            fm_mm(acc[ai], ai, lambda kc: wv[:, kc, 1, :], tt, wt)
            copy_op("dve", KT[:, 512 + tt * 512:512 + (tt + 1) * 512], acc[ai][:], [f"acc{ai}"], ["KT"])
            ai = nxt("acc", 2)
            fm_mm(acc[ai], ai, lambda kc: wv[:, kc, 3, :], tt, wt)
            act_fn(sz[:, tt * 512:(tt + 1) * 512], acc[ai][:], AF.Silu, [f"acc{ai}"], ["sz"])
            yield
        for tb in range(NB):
            yield
            tt = tb // 4
            ai = nxt("acc", 2)
            pairs = [(xB[:, kc, tb * 128:(tb + 1) * 128], wv[:, kc, 2, :]) for kc in range(8)]
            mm_group(acc[ai][:, 0:128], f"acc{ai}", pairs, xb_toks(tt) + wt)
            av = acc[ai][:, 0:128].rearrange("p (h d) -> p h d", h=2)
            copy_op("dve", Vv[:, 4 + tb, :, 0:64], av, [f"acc{ai}"], ["Vv"])
            if last and tb >= NB - 4:
                si = nxt("kvst", 2)
                copy_op("act", kvst[si][:], acc[ai][:, 0:128], [f"acc{ai}"], [f"kvst{si}"])
                r0 = (tb - (NB - 4)) * 128
                P.dma("sp", lambda e, si=si, r0=r0: e.dma_start(
                    out=vo[l, b, r0:r0 + 128, 2 * j:2 * j + 2, :], in_=kvst[si][:].rearrange("p (h d) -> p h d", h=2)),
                    reads=[f"kvst{si}"])
                ai2 = nxt("acc", 2)
                pairs = [(xB[:, kc, tb * 128:(tb + 1) * 128], wv[:, kc, 1, :]) for kc in range(8)]
                mm_group(acc[ai2][:, 0:128], f"acc{ai2}", pairs, xb_toks(tt) + wt)
                si = nxt("kvst", 2)
                copy_op("act", kvst[si][:], acc[ai2][:, 0:128], [f"acc{ai2}"], [f"kvst{si}"])
                P.dma("sp", lambda e, si=si, r0=r0: e.dma_start(
                    out=ko[l, b, r0:r0 + 128, 2 * j:2 * j + 2, :], in_=kvst[si][:].rearrange("p (h d) -> p h d", h=2)),
                    reads=[f"kvst{si}"])
        Ov = sm0[:, 0:130].rearrange("p (h d) -> p h d", h=2)

        def stageA(tb):
            gblk = hf * NB + tb
            njp = 5 - max(0, 4 - gblk)
            par = tb % 2
            for hh in range(2):
                h = 2 * j + hh
                pb = 64 * hh
                STv = big[hh][:, 0:640].rearrange("p (a q) -> p a q", a=5)

                def fn(e, STv=STv, pb=pb, tb=tb, njp=njp):
                    ins = None
                    for jp in range(njp):
                        kb = tb + 4 - jp
                        ins = e.matmul(STv[:, jp, :], KT[pb:pb + 64, kb * 128:(kb + 1) * 128],
                                       QT[pb:pb + 64, tb * 128:(tb + 1) * 128], start=True, stop=True)
                    return ins
                P.op("pe", fn, reads=["KT", "QT"], writes=[f"big{hh}"])
                act_fn(Eb[hh][:, 0:min(njp, 4), :], STv[:, 0:min(njp, 4), :], AF.Exp, [f"big{hh}"], [f"Eb{hh}"])
                if njp == 5:
                    act_fn(Eb[hh][:, 4:5, :], STv[:, 4:5, :], AF.Exp, [f"big{hh}", f"Eb{hh}"], [f"Eb{hh}"])
                P.op("dve", lambda e, hh=hh, h=h, njp=njp, par=par: e.tensor_tensor(PTd[par][hh][:, 0:njp, :], Eb[hh][:, 0:njp, :], EB[:, h, 0:njp, :], ALU.mult),
                     reads=[f"Eb{hh}", "EB"], writes=[f"PT{par}{hh}"])

        def stageB(tb):
            gblk = hf * NB + tb
            njp = 5 - max(0, 4 - gblk)
            par = tb % 2
            ya_ = ya2[par]
            for hh in range(2):
                def fn2(e, hh=hh, tb=tb, njp=njp, par=par):
                    ins = None
                    for jp in range(njp):
                        ins = e.matmul(Ov[:, hh, :], PTd[par][hh][:, jp, :], Vv[:, tb + 4 - jp, hh, :], start=(jp == 0), stop=(jp == njp - 1))
                    return ins
                P.op("pe", fn2, reads=[f"PT{par}{hh}", "Vv"], writes=["sm0"])
            P.op("dve", lambda e: e.reciprocal(rcp[:].rearrange("p (h o) -> p h o", o=1), Ov[:, :, 64:65]), reads=["sm0"], writes=["rcp"])
            P.op("dve", lambda e, ya_=ya_: e.tensor_tensor(ya_[:].rearrange("p (h d) -> p h d", h=2), Ov[:, :, 0:64],
                                                           bc(rcp[:].rearrange("p (h o) -> p h o", o=1), [128, 2, 64]), ALU.mult),
                 reads=["sm0", "rcp"], writes=[f"ya{par}"])

        def stageC(tb):
            par = tb % 2
            ya_ = ya2[par]
            P.op("pe", lambda e, ya_=ya_: e.transpose(sm1[:, 0:128], ya_[:], identb[:]), reads=[f"ya{par}", "identb"], writes=["sm1"])
            P.op("dve", lambda e, tb=tb: e.tensor_tensor(yg[0][:, j, tb * 128:(tb + 1) * 128], sm1[:, 0:128], sz[:, tb * 128:(tb + 1) * 128], ALU.mult),
                 reads=["sm1", "sz"], writes=[f"yg0_{tb // 4}"])

        for t in range(NB + 2):
            if t < NB:
                stageA(t)
                yield
            if 1 <= t <= NB:
                stageB(t - 1)
                yield
            if t >= 2:
                stageC(t - 2)
                yield
        copy_op("dve", kcar[:, l, j, :], KT[:, NT:NT + 512], ["KT"], [f"kcar{l}_{j}"])
        copy_op("act", vcar[:, l, j, :].rearrange("p (a h d) -> p a h d", a=4, h=2), Vv[:, NB:NB + 4, :, :], ["Vv"], [f"vcar{l}_{j}"])

    def ret_job(l, b, hf, j, wi, last):
        wt = wtoks(wi, 1)
        wv = wb[wi][:, 0:4096].rearrange("p (kc g c) -> p kc g c", kc=8, g=4)
        szb = sz
        g0 = hf * NB
        for tt in range(NTT):
            ai = nxt("acc", 2)
            fm_mm(acc[ai], ai, lambda kc: wv[:, kc, 3, :], tt, wt)
            act_fn(szb[:, tt * 512:(tt + 1) * 512], acc[ai][:], AF.Silu, [f"acc{ai}"], ["sz"])
        P.op("dve", lambda e: e.tensor_copy(Gpat[:], bc(gt[:, j, :].rearrange("p (e o) -> p e o", o=1), [128, 64, NB])),
             reads=["gt"], writes=["Gpat"])
        P.op("dve", lambda e: e.memset(Gpat[:, :, 0:1], 0.0), reads=["Gpat"], writes=["Gpat"])
        for tb in range(NB):
            tt = tb // 4
            ai = nxt("acc", 2)
            pairs = [(xB[:, kc, tb * 128:(tb + 1) * 128], wv[:, kc, 0:3, :]) for kc in range(8)]
            mm_group(acc[ai][:, 0:384], f"acc{ai}", pairs, xb_toks(tt) + wt)
            copy_op("act", qkraw[:, tb, :], acc[ai][:, 0:256], [f"acc{ai}"], ["MB0"])
            copy_op("act", Vb[:, tb, :], acc[ai][:, 256:384], [f"acc{ai}"], ["Vb"])
        for qi, (dstA, sc_, dtok) in enumerate(((QtA, xi, "QtA"), (KtA, zi, "KtA"))):
            raw = qkraw[:, :, qi * 128:(qi + 1) * 128]
            raw4 = raw.rearrange("p t (h d) -> p t h d", h=2)
            raw5 = raw.rearrange("p t (h two d) -> p t h two d", h=2, two=2)
            t14 = rt1A.rearrange("p t (h d) -> p t h d", h=2)
            t25 = rt2A.rearrange("p t (h two d) -> p t h two d", h=2, two=2)
            ccv = cc[:, g0:g0 + NB, :].unsqueeze(2).to_broadcast([128, NB, 2, 64])
            P.op("dve", lambda e, t14=t14, raw4=raw4, ccv=ccv: e.tensor_tensor(t14, raw4, ccv, ALU.mult), reads=["MB0", "cc"], writes=["rt1A"])
            for hv in range(2):
                ssv = ss[:, g0:g0 + NB, hv * 32:(hv + 1) * 32].unsqueeze(2).to_broadcast([128, NB, 2, 32])
                P.op("dve", lambda e, t25=t25, raw5=raw5, ssv=ssv, hv=hv: e.tensor_tensor(t25[:, :, :, hv, :], raw5[:, :, :, 1 - hv, :], ssv, ALU.mult),
                     reads=["MB0", "ss"], writes=["rt2A"])
            P.op("dve", lambda e: e.tensor_tensor(rt1A, rt1A, rt2A, ALU.add), reads=["rt1A", "rt2A"], writes=["rt1A"])
            scv = sc_[:, 2 * j:2 * j + 2].unsqueeze(1).unsqueeze(3).to_broadcast([128, NB, 2, 64])
            P.op("dve", lambda e, dstA=dstA, t14=t14, scv=scv: e.tensor_tensor(dstA.rearrange("p t (h d) -> p t h d", h=2), t14, scv, ALU.mult),
                 reads=["rt1A", "xi", "zi"], writes=[dtok])
        for g4 in range(NB // 4):
            def fnT(e, g4=g4):
                ins = None
                for t4 in range(4):
                    tb = g4 * 4 + t4
                    e.transpose(sm1[:, t4 * 128:(t4 + 1) * 128], QtA[:, tb, :], identb[:])
                    ins = e.transpose(sm1[:, 512 + t4 * 128:512 + (t4 + 1) * 128], KtA[:, tb, :], identb[:])
                return ins
            P.op("pe", fnT, reads=["QtA", "KtA", "identb"], writes=["sm1"])
            copy_op("act", QTr[:, g4 * 512:(g4 + 1) * 512], sm1[:, 0:512], ["sm1"], ["QTr"])
            copy_op("dve", KTr[:, g4 * 512:(g4 + 1) * 512], sm1[:, 512:1024], ["sm1"], ["KTr"])

            def fnKV(e, g4=g4):
                ins = None
                for t4 in range(4):
                    tb = g4 * 4 + t4
                    ins = e.matmul(sm0[:, t4 * 128:(t4 + 1) * 128], KtA[:, tb, :], Vb[:, tb, :], start=True, stop=True)
                return ins
            P.op("pe", fnKV, reads=["KtA", "Vb"], writes=["sm0"])
            for hh in range(2):
                rr_ = slice(hh * 64, (hh + 1) * 64)
                o_ = kvbuf[rr_, :, g4 * 4:(g4 + 1) * 4].rearrange("p e t -> p t e")
                i0_ = sm0[rr_, 0:512].rearrange("p (t c) -> p t c", t=4)[:, :, hh * 64:(hh + 1) * 64]
                i1_ = gt[rr_, j, :].unsqueeze(1).to_broadcast([64, 4, 64])
                P.op("dve", lambda e, o_=o_, i0_=i0_, i1_=i1_: e.tensor_tensor(o_, i0_, i1_, ALU.mult), reads=["sm0", "gt"], writes=["kvbuf"])
        P.op("dve", lambda e: e.tensor_tensor(ysb[:, 0:64], Scar[:, l, j, :], gt[:, j, :], ALU.mult), reads=[f"Scar{l}_{j}", "gt"], writes=["ysb"])
        P.op("dve", lambda e: e.tensor_tensor(kvbuf[:, :, 0], kvbuf[:, :, 0], ysb[:, 0:64], ALU.add), reads=["ysb", "kvbuf"], writes=["kvbuf"])
        P.op("dve", lambda e: e.tensor_tensor_scan(Sall[:].rearrange("p e n -> p (e n)"), Gpat[:].rearrange("p e n -> p (e n)"),
                                                   kvbuf[:].rearrange("p e n -> p (e n)"), 0.0, ALU.mult, ALU.add),
             reads=["kvbuf", "Gpat"], writes=["Sall"])
        copy_op("act", Sop[:, 0, :], Scar[:, l, j, :], [f"Scar{l}_{j}"], ["Sop"])
        copy_op("act", Sop[:, 1:NB, :], Sall[:, :, 0:NB - 1].rearrange("p e n -> p n e"), ["Sall"], ["Sop"])
        copy_op("dve", Scar[:, l, j, :], Sall[:, :, NB - 1], ["Sall", "Sop"], [f"Scar{l}_{j}"])
        if last:
            P.dma("sp", lambda e: e.dma_start(out=ro[l, b, 2 * j:2 * j + 2, :, :].rearrange("hh d e -> (hh d) e"), in_=Scar[:, l, j, :]),
                  reads=[f"Scar{l}_{j}"])
        for g4 in range(NB // 4):
            def fnA(e, g4=g4):
                ins = None
                for t4 in range(4):
                    tb = g4 * 4 + t4
                    for hh in range(2):
                        pb = 64 * hh
                        ins = e.matmul(big[hh][:, t4 * 128:(t4 + 1) * 128], KTr[pb:pb + 64, tb * 128:(tb + 1) * 128],
                                       QTr[pb:pb + 64, tb * 128:(tb + 1) * 128], start=True, stop=True)
                return ins
            P.op("pe", fnA, reads=["KTr", "QTr"], writes=["big0", "big1"])
            for hh in range(2):
                o_ = AmA[:, g4 * 4:(g4 + 1) * 4, hh, :]
                i0_ = big[hh][:, 0:512].rearrange("p (t i) -> p t i", t=4)
                i1_ = maskb[:].unsqueeze(1).to_broadcast([128, 4, 128])
                P.op("dve", lambda e, o_=o_, i0_=i0_, i1_=i1_: e.tensor_tensor(o_, i0_, i1_, ALU.mult), reads=[f"big{hh}", "maskb"], writes=["AmA"])
        Yh = [sm0[:, 0:512].rearrange("p (t e) -> p t e", t=NB), big[1][:, 512:1024].rearrange("p (t e) -> p t e", t=NB)]

        def fnY(e):
            ins = None
            for tb in range(NB):
                for hh in range(2):
                    pb = 64 * hh
                    e.matmul(Yh[hh][:, tb, :], AmA[:, tb, hh, :], Vb[:, tb, hh * 64:(hh + 1) * 64], start=True, stop=False)
                    ins = e.matmul(Yh[hh][:, tb, :], QTr[pb:pb + 64, tb * 128:(tb + 1) * 128], Sop[pb:pb + 64, tb, :], start=False, stop=True)
            return ins
        P.op("pe", fnY, reads=["AmA", "Vb", "QTr", "Sop"], writes=["sm0", "big1"])
        for hh in range(2):
            tk = "sm0" if hh == 0 else "big1"
            copy_op("act", ysbA[:, :, hh * 64:(hh + 1) * 64], Yh[hh], [tk, "MB0"], ["MB0"])
            P.op("act", lambda e, hh=hh: e.activation(ysqA[:, :, hh * 64:(hh + 1) * 64], Yh[hh], AF.Square), reads=[tk, "MB0"], writes=["MB0"])
        g = gstA
        y3 = ysbA.rearrange("p t (h d) -> p (t h) d", h=2)
        q3 = ysqA.rearrange("p t (h d) -> p (t h) d", h=2)
        P.op("dve", lambda e: e.reduce_sum(g[:, 0, :], y3, AX.X), reads=["MB0"], writes=["gstA"])
        P.op("dve", lambda e: e.reduce_sum(g[:, 1, :], q3, AX.X), reads=["MB0", "gstA"], writes=["gstA"])
        P.op("dve", lambda e: e.tensor_scalar(g[:, 0, :], g[:, 0, :], 1.0 / 64, None, ALU.mult), reads=["gstA"], writes=["gstA"])
        P.op("dve", lambda e: e.tensor_tensor(g[:, 2, :], g[:, 0, :], g[:, 0, :], ALU.mult), reads=["gstA"], writes=["gstA"])
        P.op("dve", lambda e: e.scalar_tensor_tensor(g[:, 3, :], g[:, 1, :], 1.0 / 64, g[:, 2, :], ALU.mult, ALU.subtract), reads=["gstA"], writes=["gstA"])
        P.op("dve", lambda e: e.tensor_scalar(g[:, 3, :], g[:, 3, :], LN_EPS, None, ALU.add), reads=["gstA"], writes=["gstA"])
        P.op("pool", lambda e: e.tensor_tensor(g[:, 4, :], g[:, 3, :], mhalf[:, 0:16], ALU.pow), reads=["gstA", "mhalf"], writes=["gstA"])
        P.op("dve", lambda e: e.tensor_tensor(y3, y3, g[:, 0, :].unsqueeze(2).to_broadcast([128, 16, 64]), ALU.subtract), reads=["gstA", "MB0"], writes=["MB0"])
        P.op("dve", lambda e: e.tensor_tensor(ynbA.rearrange("p t (h d) -> p (t h) d", h=2), y3,
                                              g[:, 4, :].unsqueeze(2).to_broadcast([128, 16, 64]), ALU.mult), reads=["gstA", "MB0"], writes=["ynbA"])

        def fnT2(e):
            ins = None
            for tb in range(NB):
                ins = e.transpose(sm1[:, tb * 128:(tb + 1) * 128], ynbA[:, tb, :], identb[:])
            return ins
        P.op("pe", fnT2, reads=["ynbA", "identb"], writes=["sm1"])
        P.op("dve", lambda e: e.scalar_tensor_tensor(yg[1][:, j, :], sm1[:, 0:NT], prm[:, l, j:j + 1], szb[:, 0:NT], ALU.mult, ALU.mult),
             reads=["sm1", "sz"] + prm_all(l), writes=["yg1_0", "yg1_1"])

    def lru_job(l, b, hf, c, wi, last, woff=0):
        wt = wtoks(wi, 1)
        wv = wb[wi][:, woff:woff + 2048].rearrange("p (kc g c) -> p kc g c", kc=8, g=2)
        szc = szc_buf
        pl = prm_all(l)
        copy_op("dve", xrbuf[:, 0:3], convcar[:, l, c, :], [f"convcar{l}_{c}"], ["xrbuf"])
        for tt in range(NTT):
            ai = nxt("acc", 2)
            fm_mm(acc[ai], ai, lambda kc: wv[:, kc, 0, :], tt, wt)
            copy_op("act", xrbuf[:, 3 + tt * 512:3 + (tt + 1) * 512], acc[ai][:], [f"acc{ai}"], ["xrbuf"])
            yield
            ai = nxt("acc", 2)
            fm_mm(acc[ai], ai, lambda kc: wv[:, kc, 1, :], tt, wt)
            act_fn(szc[:, tt * 512:(tt + 1) * 512], acc[ai][:], AF.Silu, [f"acc{ai}"], ["szc"])
            yield
        P.op("dve", lambda e: e.tensor_scalar(xc[:], xrbuf[:, 0:NT], prm[:, l, 8 + c:9 + c], prm[:, l, 4 + c:5 + c], ALU.mult, ALU.add),
             reads=["xrbuf"] + pl, writes=["xc"])
        yield
        for tap in range(1, 4):
            P.op("dve", lambda e, tap=tap: e.scalar_tensor_tensor(xc[:], xrbuf[:, tap:tap + NT], prm[:, l, 8 + 4 * tap + c:9 + 4 * tap + c],
                                                                  xc[:], ALU.mult, ALU.add),
                 reads=["xrbuf", "xc"], writes=["xc"])
            yield
        copy_op("act", xcb[:], xc[:], ["xc"], ["xcb"])
        yield
        for tt in range(NTT):
            sl = slice(tt * 512, (tt + 1) * 512)
            ai = nxt("acc", 2)
            mm_group(acc[ai][:], f"acc{ai}", [(WgA[:, l, c, :], xcb[:, sl])], ["xcb"] + WgTok[l])
            act_fn(bA[:, sl], acc[ai][:], AF.Sigmoid, [f"acc{ai}", "bA"] + pl, ["bA"], bias=prm[:, l, 24 + c:25 + c])
            ai = nxt("acc", 2)
            mm_group(acc[ai][:], f"acc{ai}", [(WgX[:, l, c, :], xcb[:, sl])], ["xcb"] + WgTok[l])
            act_fn(bC[:, sl], acc[ai][:], AF.Sigmoid, [f"acc{ai}", "bC"] + pl, ["bC"], bias=prm[:, l, 28 + c:29 + c])
            yield
        act_fn(bB[:], bA[:], AF.Exp, ["bA"], ["bB"], scale=prm[:, l, 56 + c:57 + c])
        act_fn(bA[:], bA[:], AF.Exp, ["bA", "bB"], ["bA"], scale=prm[:, l, 36 + c:37 + c])
        yield
        act_fn(bB[:], bB[:], AF.Sqrt, ["bB"], ["bB"], scale=-1.0, bias=onesf[:, 0:1])
        yield
        P.op("dve", lambda e: e.tensor_tensor(bC[:], bC[:], bB[:], ALU.mult), reads=["bB", "bC"], writes=["bC"])
        yield
        P.op("dve", lambda e: e.tensor_tensor(bC[:], bC[:], xc[:], ALU.mult), reads=["xc", "bC"], writes=["bC"])
        yield
        P.op("dve", lambda e: e.tensor_tensor_scan(bB[:], bA[:], bC[:], hcar[:, l, c:c + 1], ALU.mult, ALU.add),
             reads=["bA", "bC", f"hcar{l}_{c}", "bB"], writes=["bB"])
        yield
        copy_op("dve", hcar[:, l, c:c + 1], bB[:, NT - 1:NT], ["bB"], [f"hcar{l}_{c}"])
        P.op("dve", lambda e: e.tensor_tensor(yg[2][:, c, :], bB[:], szc[:, 0:NT], ALU.mult),
             reads=["bB", "szc"], writes=["yg2_0", "yg2_1"])
        copy_op("dve", convcar[:, l, c, :], xrbuf[:, NT:NT + 3], ["xrbuf"], [f"convcar{l}_{c}"])
        if last:
            P.dma("sp", lambda e: e.dma_start(out=co[l, b, :, c * 128:(c + 1) * 128].rearrange("t p -> p t"), in_=convcar[:, l, c, :]),
                  reads=[f"convcar{l}_{c}"])
            P.dma("sp", lambda e: e.dma_start(out=lo[l, b, c * 128:(c + 1) * 128].rearrange("(p o) -> p o", o=1), in_=hcar[:, l, c:c + 1]),
                  reads=[f"hcar{l}_{c}"])

    def interleave(*gens):
        gens = list(gens)
        while gens:
            for g_ in list(gens):
                try:
                    next(g_)
                except StopIteration:
                    gens.remove(g_)

    def d1_job(l, mc, wi):
        wt = wtoks(wi, 2)
        wg = wb[wi][:, 0:3072].rearrange("p (kc g c) -> p kc g c", kc=8, g=3)
        wbr = wb[wi][:, 3072:4608].rearrange("p (kc g c) -> p kc g c", kc=4, g=3)
        for tt in range(CFG["ntt"]):
            for br in range(3):
                ai = nxt("acc", 2)
                fm_mm(acc[ai], ai, lambda kc, br=br: wg[:, kc, br, :], tt, wt)
                act_fn(cw(gbuf[br]), cw(acc[ai]), AF.Sigmoid, [f"acc{ai}"], [f"gbuf{br}"])
                ai = nxt("acc", 2)
                fm_mm(acc[ai], ai, lambda kc, br=br: wbr[:, kc, br, :], tt, wt, nkc=4, rhs_src=yg[br], rhs_toks=[f"yg{br}_{tt}"])
                P.op("dve", lambda e, o_=cw(tb3[br]), a_=cw(acc[ai]), g_=cw(gbuf[br]): e.tensor_tensor(o_, a_, g_, ALU.mult),
                     reads=[f"acc{ai}", f"gbuf{br}"], writes=[f"tb3_{br}"])
            P.op("dve", lambda e, a_=cw(tb3[0]), b_=cw(tb3[1]): e.tensor_tensor(a_, a_, b_, ALU.add), reads=["tb3_0", "tb3_1"], writes=["tb3_0"])
            P.op("dve", lambda e, o_=merged[:, mc, csl(tt)], a_=cw(tb3[0]), b_=cw(tb3[2]): e.tensor_tensor(o_, a_, b_, ALU.add),
                 reads=["tb3_0", "tb3_2"], writes=[f"mg{mc}_{tt}"])

    def d2_job(l, rc, wi):
        wt = wtoks(wi, 1)
        wo = wb[wi][:, 0:1024].rearrange("p (kc c) -> p kc c", kc=8)
        for tt in range(CFG["ntt"]):
            ai = nxt("acc", 2)
            fm_mm(acc[ai], ai, lambda kc: wo[:, kc, :], tt, wt, rhs_src=merged, rhs_toks=[f"mg{k}_{tt}" for k in range(8)])
            sl = csl(tt)
            P.op("dve", lambda e, a_=cw(acc[ai]), sl=sl: e.scalar_tensor_tensor(xT[:, rc, sl], xT[:, rc, sl], ALPHA, a_, ALU.mult, ALU.add),
                 reads=[f"acc{ai}"], writes=[f"xT{rc}_{tt}"])
            copy_op("act", xB[:, rc, sl], xT[:, rc, sl], [f"xT{rc}_{tt}"], [f"xB{rc}_{tt}"])

    def d3_job(l, rc, wi):
        wt = wtoks(wi, 2)
        wpg = wb[wi][:, 0:1024].rearrange("p (kc c) -> p kc c", kc=8)
        wpl = wb[wi][:, 1024:1280].rearrange("p (kc c) -> p kc c", kc=2)
        for tt in range(CFG["ntt"]):
            sl = csl(tt)
            ai = nxt("acc", 2)
            fm_mm(acc[ai], ai, lambda kc: wpg[:, kc, :], tt, wt)
            act_fn(cw(gbuf[0]), cw(acc[ai]), AF.Sigmoid, [f"acc{ai}"], ["gbuf0"])
            ai = nxt("acc", 2)
            fm_mm(acc[ai], ai, lambda kc: wpl[:, kc, :], tt, wt, nkc=2, rhs_src=pT, rhs_toks=[f"pT_{tt}"])
            P.op("dve", lambda e, o_=cw(tb3[0]), a_=cw(acc[ai]), g_=cw(gbuf[0]): e.tensor_tensor(o_, a_, g_, ALU.mult), reads=[f"acc{ai}", "gbuf0"], writes=["tb3_0"])
            P.op("dve", lambda e, sl=sl, t_=cw(tb3[0]): e.tensor_tensor(xT[:, rc, sl], xT[:, rc, sl], t_, ALU.add), reads=["tb3_0"], writes=[f"xT{rc}_{tt}"])

    def ln_phase(l, b, hf, write_y):
        pl = prm_all(l)
        for tb in range(NB):
            tt = tb // 4
            bl = slice(tb * 128, (tb + 1) * 128)
            ai = nxt("acc", 2)
            pa = acc[ai]

            def fn(e, bl=bl, pa=pa):
                ins = None
                for rc in range(8):
                    e.matmul(pa[:, 0:128], xT[:, rc, bl], xT[:, rc, bl], start=(rc == 0), stop=(rc == 7))
                for rc in range(8):
                    ins = e.matmul(pa[:, 128:130], xT[:, rc, bl], onesf[:, 0:2], start=(rc == 0), stop=(rc == 7))
                return ins
            P.op("pe", fn, reads=xf_toks(tt) + ["onesf"], writes=[f"acc{ai}"])
            P.op("dve", lambda e, pa=pa: e.tensor_tensor(lntmp[:], pa[:, 0:128], identf[:], ALU.mult), reads=[f"acc{ai}", "identf"], writes=["lntmp"])
            P.op("dve", lambda e, tb=tb: e.reduce_sum(lnst[:, NB + tb:NB + tb + 1], lntmp[:], AX.X), reads=["lntmp", "lnst"], writes=["lnst"])
            copy_op("dve", lnst[:, tb:tb + 1], pa[:, 128:129], [f"acc{ai}", "lnst"], ["lnst"])
        s = lnst
        A0, A1, A2, A3, A4, A5 = [slice(k * NB, (k + 1) * NB) for k in range(6)]
        P.op("dve", lambda e: e.tensor_scalar(s[:, A0], s[:, A0], 1.0 / D, None, ALU.mult), reads=["lnst"], writes=["lnst"])
        P.op("dve", lambda e: e.tensor_tensor(s[:, A2], s[:, A0], s[:, A0], ALU.mult), reads=["lnst"], writes=["lnst"])
        P.op("dve", lambda e: e.scalar_tensor_tensor(s[:, A3], s[:, A1], 1.0 / D, s[:, A2], ALU.mult, ALU.subtract), reads=["lnst"], writes=["lnst"])
        P.op("dve", lambda e: e.tensor_scalar(s[:, A3], s[:, A3], LN_EPS, None, ALU.add), reads=["lnst"], writes=["lnst"])
        P.op("pool", lambda e: e.tensor_tensor(s[:, A4], s[:, A3], mhalf[:, 0:NB], ALU.pow), reads=["lnst", "mhalf"], writes=["lnst"])
        P.op("dve", lambda e: e.scalar_tensor_tensor(s[:, A5], s[:, A0], -1.0, s[:, A4], ALU.mult, ALU.mult), reads=["lnst"], writes=["lnst"])
        dAB = [(dA, dB), (lntmp, lnt2)]
        for tb in range(NB):
            tt = tb // 4
            bl = slice(tb * 128, (tb + 1) * 128)
            ai = nxt("acc", 2)
            bcv = acc[ai][:, 0:256].rearrange("p (a t) -> p a t", a=2)
            dA_, dB_ = dAB[tb % 2]
            tkA, tkB = ("dA", "dB") if tb % 2 == 0 else ("lntmp", "lnt2")
            P.op("dve", lambda e, tb=tb, dA_=dA_: e.tensor_scalar(dA_[:], identf[:], s[:, 4 * NB + tb:4 * NB + tb + 1], None, ALU.mult), reads=["lnst", "identf"], writes=[tkA])
            P.op("dve", lambda e, tb=tb, dB_=dB_: e.tensor_scalar(dB_[:], identf[:], s[:, 5 * NB + tb:5 * NB + tb + 1], None, ALU.mult), reads=["lnst", "identf"], writes=[tkB])

            def fn(e, bcv=bcv, dA_=dA_, dB_=dB_):
                e.matmul(bcv[:, 0, :], onesf[:], dA_[:], start=True, stop=True)
                return e.matmul(bcv[:, 1, :], onesf[:], dB_[:], start=True, stop=True)
            P.op("pe", fn, reads=[tkA, tkB, "onesf"], writes=[f"acc{ai}"])
            P.op("dve", lambda e, bl=bl, bcv=bcv: e.tensor_tensor(lnt[:], xT[:, :, bl], bc(bcv[:, 0:1, :], [128, 8, 128]), ALU.mult),
                 reads=[f"acc{ai}"] + xf_toks(tt), writes=["lnt"])
            P.op("dve", lambda e, bcv=bcv: e.tensor_tensor(lnt[:], lnt[:], bc(bcv[:, 1:2, :], [128, 8, 128]), ALU.add), reads=[f"acc{ai}", "lnt"], writes=["lnt"])
            P.op("dve", lambda e: e.tensor_tensor(lnt[:], lnt[:], bc(prm[:, l, 40:48].rearrange("p (c o) -> p c o", o=1), [128, 8, 128]), ALU.mult),
                 reads=["lnt"] + pl, writes=["lnt"])
            P.op("dve", lambda e, bl=bl: e.tensor_tensor(xT[:, :, bl], lnt[:], bc(prm[:, l, 48:56].rearrange("p (c o) -> p c o", o=1), [128, 8, 128]), ALU.add),
                 reads=["lnt"] + pl, writes=xf_toks(tt))
            P.op("act", lambda e, bl=bl: e.copy(xB[:, :, bl], xT[:, :, bl]), reads=xf_toks(tt), writes=xb_toks(tt))
            if write_y:
                yi_ = nxt("yst", 2)
                for hb in range(2):
                    ai = nxt("acc", 2)
                    av = acc[ai][:].rearrange("p (c t) -> p c t", c=4)

                    def fnT(e, bl=bl, av=av, hb=hb):
                        ins = None
                        for c in range(4):
                            ins = e.transpose(av[:, c, :], xT[:, hb * 4 + c, bl], identf[:])
                        return ins
                    P.op("pe", fnT, reads=xf_toks(tt) + ["identf"], writes=[f"acc{ai}"])
                    copy_op("act" if hb == 0 else "dve", yst[yi_][:, hb * 512:(hb + 1) * 512], acc[ai][:], [f"acc{ai}", f"yst{yi_}"], [f"yst{yi_}"])
                r0 = hf * NT + tb * 128
                P.dma("sp", lambda e, yi_=yi_, r0=r0: e.dma_start(out=yo[b, r0:r0 + 128, :], in_=yst[yi_][:]), reads=[f"yst{yi_}"])

    def s_load(l):
        if l == 0:
            P.dma("sp", lambda e: e.dma_start(out=xin[0][0:64, :], in_=xs_d.rearrange("s t d -> (s t) d")), writes=["xin0"])
            for hb in range(2):
                sv = sm0[:, 0:256].rearrange("p (c t) -> p c t", c=4)

                def fn(e, sv=sv, hb=hb):
                    ins = None
                    for c in range(4):
                        cg = hb * 4 + c
                        ins = e.transpose(sv[:, c, :], xin[0][0:64, cg * 128:(cg + 1) * 128], identf[0:64, 0:64])
                    return ins
                P.op("pe", fn, reads=["xin0", "identf"], writes=["sm0"])
                cs = slice(hb * 4, hb * 4 + 4)
                P.op("act", lambda e, sv=sv, cs=cs: e.copy(xT[:, cs, 0:64], sv), reads=["sm0"], writes=xf_toks(0))
                P.op("act", lambda e, sv=sv, cs=cs: e.copy(xB[:, cs, 0:64], sv), reads=["sm0"], writes=xb_toks(0))
        P.dma("sp", lambda e: e.dma_start(out=pin[0][0:64, :], in_=ps_d[l].rearrange("s t d -> (s t) d")), writes=["pin0"])
        sv2 = sm0[:, 0:128].rearrange("p (c t) -> p c t", c=2)

        def fn2(e):
            ins = None
            for c in range(2):
                ins = e.transpose(sv2[:, c, :], pin[0][0:64, c * 128:(c + 1) * 128], identf[0:64, 0:64])
            return ins
        P.op("pe", fn2, reads=["pin0", "identf"], writes=["sm0"])
        copy_op("act", pT[:, :, 0:64], sv2, ["sm0"], ["pT_0"])

    def s_att_job(l, j, wi):
        wt = wtoks(wi, 1)
        wv = wb[wi][:, 0:4096].rearrange("p (kc g c) -> p kc g c", kc=8, g=4)
        P.op("dve", lambda e: e.memset(Vv[:, :, :, 64:65], 1.0), writes=["Vv"])
        ai = nxt("acc", 2)
        fm_mm(acc[ai], ai, lambda kc: wv[:, kc, 0, :], 0, wt)
        copy_op("act", QT[:, 0:64], acc[ai][:, 0:64], [f"acc{ai}"], ["QT"], scale=0.125)
        ai = nxt("acc", 2)
        fm_mm(acc[ai], ai, lambda kc: wv[:, kc, 1, :], 0, wt)
        copy_op("dve", KT[:, 512:576], acc[ai][:, 0:64], [f"acc{ai}"], ["KTn"])
        ai = nxt("acc", 2)
        fm_mm(acc[ai], ai, lambda kc: wv[:, kc, 3, :], 0, wt)
        act_fn(sz[:, 0:64], acc[ai][:, 0:64], AF.Silu, [f"acc{ai}"], ["sz"])
        Ov = sm0[:, 0:130].rearrange("p (h d) -> p h d", h=2)
        for s in range(NSMP):
            cs_ = slice(s * 32, (s + 1) * 32)
            P.dma("pool", lambda e, s=s: e.dma_start(out=kctm[:], in_=ck_d[l, s, :, 2 * j:2 * j + 2, :].rearrange("(a p) h d -> p a (h d)", p=128)),
                  writes=["kctm"])

            def fnk(e):
                ins = None
                for a in range(4):
                    ins = e.transpose(sm1[:, a * 128:(a + 1) * 128], kctm[:, a, :], identb[:])
                return ins
            P.op("pe", fnk, reads=["kctm", "identb"], writes=["sm1"])
            copy_op("act", KT[:, 0:512], sm1[:, 0:512], ["sm1"], ["KT"])
            for hh in range(2):
                P.dma("pool", lambda e, s=s, hh=hh: e.dma_start(out=Vv[:, 0:4, hh, 0:64],
                                                                 in_=cv_d[l, s, :, 2 * j + hh, :].rearrange("(a p) d -> p a d", p=128)),
                      reads=["Vv"], writes=[f"Vvc{hh}"])
            ai = nxt("acc", 2)
            pairs = [(xB[:, kc, cs_], wv[:, kc, 2, :]) for kc in range(8)]
            mm_group(acc[ai][0:32, 0:128], f"acc{ai}", pairs, xb_toks(0) + wt)
            copy_op("dve", Vv[0:32, 4, :, 0:64], acc[ai][0:32, 0:128].rearrange("p (h d) -> p h d", h=2), [f"acc{ai}", "Vv"], ["Vvn"])
            si = nxt("kvst", 2)
            copy_op("act", kvst[si][0:32, :], acc[ai][0:32, 0:128], [f"acc{ai}"], [f"kvst{si}"])
            P.dma("sp", lambda e, si=si, s=s: e.dma_start(out=vso[l, s, :, 2 * j:2 * j + 2, :], in_=kvst[si][0:32, :].rearrange("p (h d) -> p h d", h=2)),
                  reads=[f"kvst{si}"])
            ai = nxt("acc", 2)
            pairs = [(xB[:, kc, cs_], wv[:, kc, 1, :]) for kc in range(8)]
            mm_group(acc[ai][0:32, 0:128], f"acc{ai}", pairs, xb_toks(0) + wt)
            si = nxt("kvst", 2)
            copy_op("act", kvst[si][0:32, :], acc[ai][0:32, 0:128], [f"acc{ai}"], [f"kvst{si}"])
            P.dma("sp", lambda e, si=si, s=s: e.dma_start(out=kso[l, s, :, 2 * j:2 * j + 2, :], in_=kvst[si][0:32, :].rearrange("p (h d) -> p h d", h=2)),
                  reads=[f"kvst{si}"])
            for hh in range(2):
                h = 2 * j + hh
                pb = 64 * hh
                STv = big[hh][:, 0:640].rearrange("p (a q) -> p a q", a=5)

                def fn(e, STv=STv, pb=pb, s=s):
                    e.matmul(STv[0:32, 0, 0:32], KT[pb:pb + 64, 512 + s * 32:512 + (s + 1) * 32], QT[pb:pb + 64, s * 32:(s + 1) * 32], start=True, stop=True)
                    ins = None
                    for jp in range(1, 5):
                        kb = 4 - jp
                        ins = e.matmul(STv[:, jp, 0:32], KT[pb:pb + 64, kb * 128:(kb + 1) * 128], QT[pb:pb + 64, s * 32:(s + 1) * 32], start=True, stop=True)
                    return ins
                P.op("pe", fn, reads=["KT", "KTn", "QT"], writes=[f"big{hh}"])
                act_fn(Eb[hh][0:32, 0, 0:32], STv[0:32, 0, 0:32], AF.Exp, [f"big{hh}"], [f"Eb{hh}"])
                act_fn(Eb[hh][:, 1:4, 0:32], STv[:, 1:4, 0:32], AF.Exp, [f"big{hh}", f"Eb{hh}"], [f"Eb{hh}"])
                act_fn(Eb[hh][:, 4:5, 0:32], STv[:, 4:5, 0:32], AF.Exp, [f"big{hh}", f"Eb{hh}"], [f"Eb{hh}"])
                P.op("dve", lambda e, hh=hh, h=h: e.tensor_tensor(PTb[hh][0:32, 0, 0:32], Eb[hh][0:32, 0, 0:32], EB[0:32, h, 0, 0:32], ALU.mult),
                     reads=[f"Eb{hh}", "EB"], writes=[f"PT{hh}"])
                P.op("dve", lambda e, hh=hh, h=h: e.tensor_tensor(PTb[hh][:, 1:5, 0:32], Eb[hh][:, 1:5, 0:32], EB[:, h, 1:5, 0:32], ALU.mult),
                     reads=[f"Eb{hh}", "EB", f"PT{hh}"], writes=[f"PT{hh}"])

                def fn2(e, hh=hh):
                    e.matmul(Ov[0:32, hh, :], PTb[hh][0:32, 0, 0:32], Vv[0:32, 4, hh, :], start=True, stop=False)
                    ins = None
                    for jp in range(1, 5):
                        ins = e.matmul(Ov[0:32, hh, :], PTb[hh][:, jp, 0:32], Vv[:, 4 - jp, hh, :], start=False, stop=(jp == 4))
                    return ins
                P.op("pe", fn2, reads=[f"PT{hh}", "Vv", "Vvc0", "Vvc1", "Vvn"], writes=["sm0"])
            P.op("dve", lambda e: e.reciprocal(rcp[0:32, :].rearrange("p (h o) -> p h o", o=1), Ov[0:32, :, 64:65]), reads=["sm0"], writes=["rcp"])
            P.op("dve", lambda e: e.tensor_tensor(ya[0:32, :].rearrange("p (h d) -> p h d", h=2), Ov[0:32, :, 0:64],
                                                  bc(rcp[0:32, :].rearrange("p (h o) -> p h o", o=1), [32, 2, 64]), ALU.mult),
                 reads=["sm0", "rcp"], writes=["ya"])
            P.op("pe", lambda e: e.transpose(sm1[:, 0:32], ya[0:32, :], identb[0:32, 0:32]), reads=["ya", "identb"], writes=["sm1"])
            P.op("dve", lambda e, cs_=cs_: e.tensor_tensor(yg[0][:, j, cs_], sm1[:, 0:32], sz[:, cs_], ALU.mult),
                 reads=["sm1", "sz"], writes=["yg0_0"])

    def s_ret_job(l, j, wi):
        wt = wtoks(wi, 1)
        wv = wb[wi][:, 0:4096].rearrange("p (kc g c) -> p kc g c", kc=8, g=4)
        R = slice(0, 32)
        ai = nxt("acc", 2)
        fm_mm(acc[ai], ai, lambda kc: wv[:, kc, 3, :], 0, wt)
        act_fn(sz[:, 0:64], acc[ai][:, 0:64], AF.Silu, [f"acc{ai}"], ["sz"])
        Ah = [big[0][0:32, 0:32], big[1][0:32, 0:32]]
        Yh = [sm0[0:32, 0:64], big[1][0:32, 512:576]]
        gblk = 8
        for s in range(NSMP):
            cs_ = slice(s * 32, (s + 1) * 32)
            P.dma("sp", lambda e, s=s: e.dma_start(out=S0f[:], in_=sr_d[l, s, 2 * j:2 * j + 2, :, :].rearrange("hh d e -> (hh d) e")), writes=["S0f"])
            copy_op("act", Sop[:, 0, :], S0f[:], ["S0f"], ["Sop"])
            ai = nxt("acc", 2)
            pairs = [(xB[:, kc, cs_], wv[:, kc, 0:3, :]) for kc in range(8)]
            mm_group(acc[ai][R, 0:384], f"acc{ai}", pairs, xb_toks(0) + wt)
            at = f"acc{ai}"
            for qi, (dst, sc_) in enumerate(((Qt, xi), (Kt, zi))):
                X = acc[ai][R, qi * 128:(qi + 1) * 128].rearrange("p (h two d) -> p h two d", h=2, two=2)
                ccv = bc(cc[R, gblk, :].rearrange("p (o d) -> p o d", o=1), [32, 2, 64])
                t1v = rt1[R, :].rearrange("p (h d) -> p h d", h=2)
                t2v = rt2[R, :].rearrange("p (h two d) -> p h two d", h=2, two=2)
                P.op("dve", lambda e, ai=ai, qi=qi, ccv=ccv, t1v=t1v: e.tensor_tensor(
                    t1v, acc[ai][R, qi * 128:(qi + 1) * 128].rearrange("p (h d) -> p h d", h=2), ccv, ALU.mult),
                    reads=[at, "cc"], writes=["rt1"])
                for hv in range(2):
                    ssv = bc(ss[R, gblk, hv * 32:(hv + 1) * 32].rearrange("p (o d) -> p o d", o=1), [32, 2, 32])
                    P.op("dve", lambda e, X=X, hv=hv, ssv=ssv, t2v=t2v: e.tensor_tensor(t2v[:, :, hv, :], X[:, :, 1 - hv, :], ssv, ALU.mult),
                         reads=[at, "ss"], writes=["rt2"])
                P.op("dve", lambda e: e.tensor_tensor(rt1[R, :], rt1[R, :], rt2[R, :], ALU.add), reads=["rt1", "rt2"], writes=["rt1"])
                scv = bc(sc_[R, 2 * j:2 * j + 2].rearrange("p (h o) -> p h o", o=1), [32, 2, 64])
                P.op("dve", lambda e, dst=dst, scv=scv, t1v=t1v: e.tensor_tensor(dst[R, :].rearrange("p (h d) -> p h d", h=2), t1v, scv, ALU.mult),
                     reads=["rt1", "xi", "zi"], writes=["Qt" if qi == 0 else "Kt"])
            copy_op("dve", Vb[R, 0, :], acc[ai][R, 256:384], [at], ["Vb"])
            P.op("pe", lambda e: e.transpose(sm1[:, 0:32], Qt[R, :], identb[0:32, 0:32]), reads=["Qt", "identb"], writes=["sm1"])
            copy_op("act", QTr[:, cs_], sm1[:, 0:32], ["sm1"], ["QTr"])
            P.op("pe", lambda e: e.transpose(sm1[:, 128:160], Kt[R, :], identb[0:32, 0:32]), reads=["Kt", "identb"], writes=["sm1"])
            copy_op("dve", KTr[:, cs_], sm1[:, 128:160], ["sm1"], ["KTr"])
            P.op("pe", lambda e: e.matmul(sm0[:, 0:128], Kt[R, :], Vb[R, 0, :], start=True, stop=True), reads=["Kt", "Vb"], writes=["sm0"])
            for hh in range(2):
                rr_ = slice(hh * 64, (hh + 1) * 64)
                P.op("dve", lambda e, hh=hh, rr_=rr_: e.tensor_tensor(Snew[rr_, :], sm0[rr_, hh * 64:(hh + 1) * 64], S0f[rr_, :], ALU.add),
                     reads=["sm0", "S0f", "Snew"], writes=["Snew"])
                P.op("dve", lambda e, rr_=rr_: e.tensor_tensor(Snew[rr_, :], Snew[rr_, :], gt32[rr_, j, :], ALU.mult),
                     reads=["Snew", "gt32"], writes=["Snew"])
            P.dma("sp", lambda e, s=s: e.dma_start(out=rso[l, s, 2 * j:2 * j + 2, :, :].rearrange("hh d e -> (hh d) e"), in_=Snew[:]), reads=["Snew"])

            def fnA(e, cs_=cs_):
                ins = None
                for hh in range(2):
                    pb = 64 * hh
                    ins = e.matmul(Ah[hh], KTr[pb:pb + 64, cs_], QTr[pb:pb + 64, cs_], start=True, stop=True)
                return ins
            P.op("pe", fnA, reads=["KTr", "QTr"], writes=["big0", "big1"])
            for hh in range(2):
                P.op("dve", lambda e, hh=hh: e.tensor_tensor(Am[R, hh, 0:32], Ah[hh], maskb[0:32, 0:32], ALU.mult),
                     reads=[f"big{hh}", "maskb"], writes=["Am"])

            def fnY(e, cs_=cs_):
                ins = None
                for hh in range(2):
                    pb = 64 * hh
                    e.matmul(Yh[hh], Am[R, hh, 0:32], Vb[R, 0, hh * 64:(hh + 1) * 64], start=True, stop=False)
                    ins = e.matmul(Yh[hh], QTr[pb:pb + 64, cs_], Sop[pb:pb + 64, 0, :], start=False, stop=True)
                return ins
            P.op("pe", fnY, reads=["Am", "Vb", "QTr", "Sop"], writes=["sm0", "big1"])
            for hh in range(2):
                tk = "sm0" if hh == 0 else "big1"
                copy_op("act", ysb[R, hh * 64:(hh + 1) * 64], Yh[hh], [tk, "ysb"], ["ysb"])
                P.op("act", lambda e, hh=hh: e.activation(ysq[R, hh * 64:(hh + 1) * 64], Yh[hh], AF.Square), reads=[tk, "ysq"], writes=["ysq"])
            g = gst
            yv3 = ysb[R, :].rearrange("p (h d) -> p h d", h=2)
            P.op("dve", lambda e: e.reduce_sum(g[R, 0:2], yv3, AX.X), reads=["ysb"], writes=["gst"])
            P.op("dve", lambda e: e.reduce_sum(g[R, 2:4], ysq[R, :].rearrange("p (h d) -> p h d", h=2), AX.X), reads=["ysq", "gst"], writes=["gst"])
            P.op("dve", lambda e: e.tensor_scalar(g[R, 0:2], g[R, 0:2], 1.0 / 64, None, ALU.mult), reads=["gst"], writes=["gst"])
            P.op("dve", lambda e: e.tensor_tensor(g[R, 4:6], g[R, 0:2], g[R, 0:2], ALU.mult), reads=["gst"], writes=["gst"])
            P.op("dve", lambda e: e.scalar_tensor_tensor(g[R, 6:8], g[R, 2:4], 1.0 / 64, g[R, 4:6], ALU.mult, ALU.subtract), reads=["gst"], writes=["gst"])
            P.op("dve", lambda e: e.tensor_scalar(g[R, 6:8], g[R, 6:8], LN_EPS, None, ALU.add), reads=["gst"], writes=["gst"])
            P.op("pool", lambda e: e.tensor_tensor(g[R, 8:10], g[R, 6:8], mhalf[R, 0:2], ALU.pow), reads=["gst", "mhalf"], writes=["gst"])
            P.op("dve", lambda e: e.tensor_tensor(yv3, yv3, bc(g[R, 0:2].rearrange("p (h o) -> p h o", o=1), [32, 2, 64]), ALU.subtract),
                 reads=["gst", "ysb"], writes=["ysb"])
            P.op("dve", lambda e: e.tensor_tensor(ynb[R, :].rearrange("p (h d) -> p h d", h=2), yv3,
                                                  bc(g[R, 8:10].rearrange("p (h o) -> p h o", o=1), [32, 2, 64]), ALU.mult),
                 reads=["gst", "ysb"], writes=["ynb"])
            P.op("pe", lambda e: e.transpose(sm1[:, 0:32], ynb[R, :], identb[0:32, 0:32]), reads=["ynb", "identb"], writes=["sm1"])
            P.op("dve", lambda e, cs_=cs_: e.scalar_tensor_tensor(yg[1][:, j, cs_], sm1[:, 0:32], prm[:, l, j:j + 1], sz[:, cs_], ALU.mult, ALU.mult),
                 reads=["sm1", "sz"] + prm_all(l), writes=["yg1_0"])

    def s_lru_job(l, c, wi):
        wt = wtoks(wi, 1)
        wv = wb[wi][:, 0:2048].rearrange("p (kc g c) -> p kc g c", kc=8, g=2)
        pl = prm_all(l)
        ai = nxt("acc", 2)
        fm_mm(acc[ai], ai, lambda kc: wv[:, kc, 1, :], 0, wt)
        act_fn(sz[:, 0:64], acc[ai][:, 0:64], AF.Silu, [f"acc{ai}"], ["sz"])
        Wd = slice(0, 32)
        for s in range(NSMP):
            cs_ = slice(s * 32, (s + 1) * 32)
            P.dma("sp", lambda e, s=s: e.dma_start(out=xrbuf[:, 0:3], in_=sc_d[l, s, :, c * 128:(c + 1) * 128].rearrange("t p -> p t")), writes=["xrbuf"])
            P.dma("sp", lambda e, s=s: e.dma_start(out=hcar[:, l, c:c + 1], in_=sl_d[l, s, c * 128:(c + 1) * 128].rearrange("(p o) -> p o", o=1)),
                  writes=[f"hcar{l}_{c}"])
            ai = nxt("acc", 2)
            pairs = [(wv[:, kc, 0, :], xB[:, kc, cs_]) for kc in range(8)]
            mm_group(acc[ai][:, 0:32], f"acc{ai}", pairs, xb_toks(0) + wt)
            copy_op("act", xrbuf[:, 3:35], acc[ai][:, 0:32], [f"acc{ai}", "xrbuf"], ["xrbuf"])
            P.op("dve", lambda e: e.tensor_scalar(xc[:, Wd], xrbuf[:, 0:32], prm[:, l, 8 + c:9 + c], prm[:, l, 4 + c:5 + c], ALU.mult, ALU.add),
                 reads=["xrbuf"] + pl, writes=["xc"])
            for tap in range(1, 4):
                P.op("dve", lambda e, tap=tap: e.scalar_tensor_tensor(xc[:, Wd], xrbuf[:, tap:tap + 32], prm[:, l, 8 + 4 * tap + c:9 + 4 * tap + c],
                                                                      xc[:, Wd], ALU.mult, ALU.add),
                     reads=["xrbuf", "xc"], writes=["xc"])
            copy_op("act", xcb[:, Wd], xc[:, Wd], ["xc"], ["xcb"])
            ai = nxt("acc", 2)
            mm_group(acc[ai][:, 0:32], f"acc{ai}", [(WgA[:, l, c, :], xcb[:, Wd])], ["xcb"] + WgTok[l])
            act_fn(bA[:, Wd], acc[ai][:, 0:32], AF.Sigmoid, [f"acc{ai}"] + pl, ["bA"], bias=prm[:, l, 24 + c:25 + c])
            ai = nxt("acc", 2)
            mm_group(acc[ai][:, 0:32], f"acc{ai}", [(WgX[:, l, c, :], xcb[:, Wd])], ["xcb"] + WgTok[l])
            act_fn(bC[:, Wd], acc[ai][:, 0:32], AF.Sigmoid, [f"acc{ai}"] + pl, ["bC"], bias=prm[:, l, 28 + c:29 + c])
            act_fn(bB[:, Wd], bA[:, Wd], AF.Exp, ["bA"], ["bB"], scale=prm[:, l, 56 + c:57 + c])
            act_fn(bA[:, Wd], bA[:, Wd], AF.Exp, ["bA", "bB"], ["bA"], scale=prm[:, l, 36 + c:37 + c])
            act_fn(bB[:, Wd], bB[:, Wd], AF.Sqrt, ["bB"], ["bB"], scale=-1.0, bias=onesf[:, 0:1])
            P.op("dve", lambda e: e.tensor_tensor(bC[:, Wd], bC[:, Wd], bB[:, Wd], ALU.mult), reads=["bB", "bC"], writes=["bC"])
            P.op("dve", lambda e: e.tensor_tensor(bC[:, Wd], bC[:, Wd], xc[:, Wd], ALU.mult), reads=["xc", "bC"], writes=["bC"])
            P.op("dve", lambda e: e.tensor_tensor_scan(bB[:, Wd], bA[:, Wd], bC[:, Wd], hcar[:, l, c:c + 1], ALU.mult, ALU.add),
                 reads=["bA", "bC", f"hcar{l}_{c}", "bB"], writes=["bB"])
            P.dma("sp", lambda e, s=s: e.dma_start(out=lso[l, s, c * 128:(c + 1) * 128].rearrange("(p o) -> p o", o=1), in_=bB[:, 31:32]), reads=["bB"])
            P.dma("sp", lambda e, s=s: e.dma_start(out=cso[l, s, :, c * 128:(c + 1) * 128].rearrange("t p -> p t"), in_=xrbuf[:, 32:35]), reads=["xrbuf"])
            P.op("dve", lambda e, cs_=cs_: e.tensor_tensor(yg[2][:, c, cs_], bB[:, Wd], sz[:, cs_], ALU.mult),
                 reads=["bB", "sz"], writes=["yg2_0"])

    def s_ln(l, write_y):
        pl = prm_all(l)
        R = slice(0, 64)
        bl = slice(0, 64)

        def fn(e):
            ins = None
            for rc in range(8):
                e.matmul(sm0[R, 0:64], xT[:, rc, bl], xT[:, rc, bl], start=(rc == 0), stop=(rc == 7))
            for rc in range(8):
                ins = e.matmul(sm0[R, 128:130], xT[:, rc, bl], onesf[:, 0:2], start=(rc == 0), stop=(rc == 7))
            return ins
        P.op("pe", fn, reads=xf_toks(0) + ["onesf"], writes=["sm0"])
        s_ = lnst
        P.op("dve", lambda e: e.tensor_tensor(lntmp[R, 0:64], sm0[R, 0:64], identf[R, 0:64], ALU.mult), reads=["sm0", "identf"], writes=["lntmp"])
        P.op("dve", lambda e: e.reduce_sum(s_[R, 1:2], lntmp[R, 0:64], AX.X), reads=["lntmp", "lnst"], writes=["lnst"])
        copy_op("dve", s_[R, 0:1], sm0[R, 128:129], ["sm0", "lnst"], ["lnst"])
        P.op("dve", lambda e: e.tensor_scalar(s_[R, 0:1], s_[R, 0:1], 1.0 / D, None, ALU.mult), reads=["lnst"], writes=["lnst"])
        P.op("dve", lambda e: e.tensor_tensor(s_[R, 2:3], s_[R, 0:1], s_[R, 0:1], ALU.mult), reads=["lnst"], writes=["lnst"])
        P.op("dve", lambda e: e.scalar_tensor_tensor(s_[R, 3:4], s_[R, 1:2], 1.0 / D, s_[R, 2:3], ALU.mult, ALU.subtract), reads=["lnst"], writes=["lnst"])
        P.op("dve", lambda e: e.tensor_scalar(s_[R, 3:4], s_[R, 3:4], LN_EPS, None, ALU.add), reads=["lnst"], writes=["lnst"])
        P.op("pool", lambda e: e.tensor_tensor(s_[R, 4:5], s_[R, 3:4], mhalf[R, 0:1], ALU.pow), reads=["lnst", "mhalf"], writes=["lnst"])
        P.op("dve", lambda e: e.scalar_tensor_tensor(s_[R, 5:6], s_[R, 0:1], -1.0, s_[R, 4:5], ALU.mult, ALU.mult), reads=["lnst"], writes=["lnst"])
        P.op("dve", lambda e: e.tensor_scalar(dA[R, 0:64], identf[R, 0:64], s_[R, 4:5], None, ALU.mult), reads=["lnst", "identf"], writes=["dA"])
        P.op("dve", lambda e: e.tensor_scalar(dB[R, 0:64], identf[R, 0:64], s_[R, 5:6], None, ALU.mult), reads=["lnst", "identf"], writes=["dB"])
        bcv = sm0[:, 0:128].rearrange("p (a t) -> p a t", a=2)

        def fnb(e):
            e.matmul(bcv[:, 0, :], onesf[R, :], dA[R, 0:64], start=True, stop=True)
            return e.matmul(bcv[:, 1, :], onesf[R, :], dB[R, 0:64], start=True, stop=True)
        P.op("pe", fnb, reads=["dA", "dB", "onesf"], writes=["sm0"])
        lv = lnt[:, :, 0:64]
        P.op("dve", lambda e: e.tensor_tensor(lv, xT[:, :, bl], bc(bcv[:, 0:1, :], [128, 8, 64]), ALU.mult), reads=["sm0"] + xf_toks(0), writes=["lnt"])
        P.op("dve", lambda e: e.tensor_tensor(lv, lv, bc(bcv[:, 1:2, :], [128, 8, 64]), ALU.add), reads=["sm0", "lnt"], writes=["lnt"])
        P.op("dve", lambda e: e.tensor_tensor(lv, lv, bc(prm[:, l, 40:48].rearrange("p (c o) -> p c o", o=1), [128, 8, 64]), ALU.mult),
             reads=["lnt"] + pl, writes=["lnt"])
        P.op("dve", lambda e: e.tensor_tensor(xT[:, :, bl], lv, bc(prm[:, l, 48:56].rearrange("p (c o) -> p c o", o=1), [128, 8, 64]), ALU.add),
             reads=["lnt"] + pl, writes=xf_toks(0))
        P.op("act", lambda e: e.copy(xB[:, :, bl], xT[:, :, bl]), reads=xf_toks(0), writes=xb_toks(0))
        if write_y:
            for hb in range(2):
                ai = nxt("acc", 2)
                av = acc[ai][R, :].rearrange("p (c t) -> p c t", c=4)

                def fnT(e, av=av, hb=hb):
                    ins = None
                    for c in range(4):
                        ins = e.transpose(av[:, c, :], xT[:, hb * 4 + c, bl], identf[:])
                    return ins
                P.op("pe", fnT, reads=xf_toks(0) + ["identf"], writes=[f"acc{ai}"])
                copy_op("act" if hb == 0 else "dve", yst[0][R, hb * 512:(hb + 1) * 512], acc[ai][R, :], [f"acc{ai}", "yst0"], ["yst0"])
            P.dma("sp", lambda e: e.dma_start(out=yso.rearrange("s t d -> (s t) d"), in_=yst[0][R, :]), reads=["yst0"])

    def wsrc_att(l, j):
        v = w_in[l].rearrange("(kc p) (g jj c) -> p kc g jj c", p=128, g=16, jj=4)
        return v[:, :, 0:4, j, :]

    def wsrc_ret(l, j):
        v = w_in[l].rearrange("(kc p) (g jj c) -> p kc g jj c", p=128, g=16, jj=4)
        return v[:, :, 4:8, j, :]

    def wsrc_lru(l, c):
        v = w_in[l].rearrange("(kc p) (g jj c) -> p kc g jj c", p=128, g=16, jj=4)
        return v[:, :, 8:10, c, :]

    def wsrc_gate(l, mc):
        v = w_in[l].rearrange("(kc p) (g mm c) -> p kc g mm c", p=128, g=8, mm=8)
        return v[:, :, 5:8, mc, :]

    def wsrc_br(l, mc):
        v = di["w_branch"][l].rearrange("br (kc p) (mm c) -> p kc br mm c", p=128, mm=8)
        return v[:, :, :, mc, :]

    def wsrc_sq(name, l, rc):
        v = di[name][l].rearrange("(kc p) (rr c) -> p kc rr c", p=128, rr=8)
        return v[:, :, rc, :]

    import os as _os
    KSTOP = int(_os.environ.get("KSTOP", "99"))
    for b in range(NSEQ if (KSTOP > 0 and not _os.environ.get("KSKIPP")) else 0):
        for hf in range(NHF):
            last = (hf == NHF - 1)
            for l in range(L):
                jobs = []
                for j in range(4):
                    jobs.append((lambda wi, j=j, l=l: load_w(wi, [
                        (wb[wi][:, 0:4096].rearrange("p (kc g c) -> p kc g c", kc=8, g=4), wsrc_att(l, j)),
                        (wb[wi][:, 4096:6144].rearrange("p (kc g c) -> p kc g c", kc=8, g=2), wsrc_lru(l, j))]),
                        lambda wi, j=j, l=l: interleave(att_job(l, b, hf, j, wi, last), lru_job(l, b, hf, j, wi, last, woff=4096))))
                for j in range(4):
                    jobs.append((lambda wi, j=j, l=l: load_w(wi, [(wb[wi][:, 0:4096].rearrange("p (kc g c) -> p kc g c", kc=8, g=4), wsrc_ret(l, j))]),
                                 lambda wi, j=j, l=l: ret_job(l, b, hf, j, wi, last)))
                for mc in range(8):
                    jobs.append((lambda wi, mc=mc, l=l: load_w(wi, [
                        (wb[wi][:, 0:3072].rearrange("p (kc g c) -> p kc g c", kc=8, g=3), wsrc_gate(l, mc)),
                        (wb[wi][:, 3072:4608].rearrange("p (kc g c) -> p kc g c", kc=4, g=3), wsrc_br(l, mc))]),
                        lambda wi, mc=mc, l=l: d1_job(l, mc, wi)))
                for rc in range(8):
                    jobs.append((lambda wi, rc=rc, l=l: load_w(wi, [(wb[wi][:, 0:1024].rearrange("p (kc c) -> p kc c", kc=8), wsrc_sq("w_out", l, rc))]),
                                 lambda wi, rc=rc, l=l: d2_job(l, rc, wi)))
                for rc in range(8):
                    jobs.append((lambda wi, rc=rc, l=l: load_w(wi, [
                        (wb[wi][:, 0:1024].rearrange("p (kc c) -> p kc c", kc=8), wsrc_sq("w_ple_gate", l, rc)),
                        (wb[wi][:, 1024:1280].rearrange("p (kc c) -> p kc c", kc=2),
                         di["w_ple"][l].rearrange("(kc p) (rr c) -> p kc rr c", p=128, rr=8)[:, :, rc, :])]),
                        lambda wi, rc=rc, l=l: d3_job(l, rc, wi)))
                wis = [nxt("wb", 3) for _ in jobs]
                P.barrier()
                KPRE = int(_os.environ.get("KPRE", "15"))
                if KPRE & 8:
                    jobs[0][0](wis[0])
                if l == 0 and (KPRE & 1):
                    load_x(b, hf)
                if KPRE & 2:
                    load_p(l, b, hf)
                if hf == 0:
                    for j in range(4):
                        P.op("dve", lambda e, j=j, l=l: e.memset(Scar[:, l, j, :], 0.0), writes=[f"Scar{l}_{j}"])
                        P.op("dve", lambda e, j=j, l=l: e.memset(convcar[:, l, j, :], 0.0), writes=[f"convcar{l}_{j}"])
                        P.op("dve", lambda e, j=j, l=l: e.memset(hcar[:, l, j:j + 1], 0.0), writes=[f"hcar{l}_{j}"])
                if KPRE & 4:
                    build_EB(l)
                if KSTOP < 99 and (b, hf, l) != (0, 0, 0):
                    continue
                if KSTOP <= 1:
                    continue
                if len(jobs) > 1:
                    jobs[1][0](wis[1])
                for k, (ld, cp) in enumerate(jobs):
                    if KSTOP == 2 and k >= 4 or KSTOP == 3 and k >= 8 or KSTOP == 5 and k >= 16:
                        break
                    if k in (0, 4, 8):
                        P.barrier()
                    if k + 2 < len(jobs):
                        jobs[k + 2][0](wis[k + 2])
                    cp(wis[k])
                if KSTOP >= 7:
                    ln_phase(l, b, hf, write_y=(l == L - 1))

    if NSMP > 0 and KSTOP >= 99:
        CFG["w"] = NSMP * 32
        CFG["ntt"] = 1
        for l in range(L):
            jobs = []
            for j in range(4):
                jobs.append((lambda wi, j=j, l=l: load_w(wi, [(wb[wi][:, 0:4096].rearrange("p (kc g c) -> p kc g c", kc=8, g=4), wsrc_att(l, j))]),
                             lambda wi, j=j, l=l: s_att_job(l, j, wi)))
            for j in range(4):
                jobs.append((lambda wi, j=j, l=l: load_w(wi, [(wb[wi][:, 0:4096].rearrange("p (kc g c) -> p kc g c", kc=8, g=4), wsrc_ret(l, j))]),
                             lambda wi, j=j, l=l: s_ret_job(l, j, wi)))
            for c in range(4):
                jobs.append((lambda wi, c=c, l=l: load_w(wi, [(wb[wi][:, 0:2048].rearrange("p (kc g c) -> p kc g c", kc=8, g=2), wsrc_lru(l, c))]),
                             lambda wi, c=c, l=l: s_lru_job(l, c, wi)))
            for mc in range(8):
                jobs.append((lambda wi, mc=mc, l=l: load_w(wi, [
                    (wb[wi][:, 0:3072].rearrange("p (kc g c) -> p kc g c", kc=8, g=3), wsrc_gate(l, mc)),
                    (wb[wi][:, 3072:4608].rearrange("p (kc g c) -> p kc g c", kc=4, g=3), wsrc_br(l, mc))]),
                    lambda wi, mc=mc, l=l: d1_job(l, mc, wi)))
            for rc in range(8):
                jobs.append((lambda wi, rc=rc, l=l: load_w(wi, [(wb[wi][:, 0:1024].rearrange("p (kc c) -> p kc c", kc=8), wsrc_sq("w_out", l, rc))]),
                             lambda wi, rc=rc, l=l: d2_job(l, rc, wi)))
            for rc in range(8):
                jobs.append((lambda wi, rc=rc, l=l: load_w(wi, [
                    (wb[wi][:, 0:1024].rearrange("p (kc c) -> p kc c", kc=8), wsrc_sq("w_ple_gate", l, rc)),
                    (wb[wi][:, 1024:1280].rearrange("p (kc c) -> p kc c", kc=2),
                     di["w_ple"][l].rearrange("(kc p) (rr c) -> p kc rr c", p=128, rr=8)[:, :, rc, :])]),
                    lambda wi, rc=rc, l=l: d3_job(l, rc, wi)))
            wis = [nxt("wb", 3) for _ in jobs]
            P.barrier()
            jobs[0][0](wis[0])
            s_load(l)
            build_EB(l)
            jobs[1][0](wis[1])
            for k, (ld, cp) in enumerate(jobs):
                if k in (0, 4, 8, 12):
                    P.barrier()
                if k + 2 < len(jobs):
                    jobs[k + 2][0](wis[k + 2])
                cp(wis[k])
            s_ln(l, write_y=(l == L - 1))

    P.wait_all("sp")
    print("PROG nrec", P.nrec, {e: len(v) for e, v in P.ops.items()})
    if _os.environ.get("KLOG"):
        with open(_os.environ["KLOG"], "w") as f:
            for r in P.log:
                f.write(repr(r) + "\n")
    with nc.allow_non_contiguous_dma(reason="small param / state vectors"):
        P.emit(sems)
    es.close()
    return nc


OUT_NAMES = ["y_prompt", "y_sample", "k_a_prompt", "v_a_prompt", "k_a_sample", "v_a_sample",
             "ret_prompt", "ret_sample", "conv_prompt", "conv_sample", "lru_prompt", "lru_sample"]


def kernel(**inputs):
    NSEQ = 4
    NSMP = 2
    nc = build(NSEQ=NSEQ, NSMP=NSMP)
    consts = host_consts()
    in_maps = []
    for c in range(N_CORES):
        m = {}
        m["x_prompt"] = np.ascontiguousarray(inputs["x_prompt"][c * NSEQ:(c + 1) * NSEQ])
        m["p_prompt"] = np.ascontiguousarray(inputs["p_prompt"][:, c * NSEQ:(c + 1) * NSEQ])
        m["x_sample"] = np.ascontiguousarray(inputs["x_sample"][c * NSMP:(c + 1) * NSMP])
        for k in ("p_sample", "cache_k_a", "cache_v_a", "state_ret", "state_conv", "state_lru"):
            m[k] = np.ascontiguousarray(inputs[k][:, c * NSMP:(c + 1) * NSMP])
        for k in W_NAMES:
            m[k] = np.ascontiguousarray(inputs[k])
        m.update(consts)
        in_maps.append(m)
    res = run_bass_kernel_spmd(nc, in_maps, core_ids=list(range(N_CORES)))
    R = res.results
    out = {}
    out["y_prompt"] = np.concatenate([r["y_prompt"] for r in R], 0)
    out["y_sample"] = np.concatenate([r["y_sample"] for r in R], 0)
    for nm in ("k_a_prompt", "v_a_prompt", "ret_prompt", "conv_prompt", "lru_prompt",
               "k_a_sample", "v_a_sample", "ret_sample", "conv_sample", "lru_sample"):
        out[nm] = np.concatenate([r[nm] for r in R], 1)
    return tuple(np.asarray(out[n], dtype=np.float32) for n in OUT_NAMES)
```

```python
import numpy as np
from contextlib import ExitStack
import concourse.bass as bass
import concourse.mybir as mybir
from concourse.bass_utils import run_bass_kernel_spmd

F32 = mybir.dt.float32
BF16 = mybir.dt.bfloat16
AF = mybir.ActivationFunctionType
ALU = mybir.AluOpType
AX = mybir.AxisListType

ENGS = ("pe", "act", "dve", "pool", "sp")
N_CORES = 8
D = 1024
SEQ = 2048
NT = 1024
NB = NT // 128
NTT = NT // 512
NHF = SEQ // NT
ALPHA = (2 * 2) ** 0.25
LN_EPS = 1e-5


class Prog:
    def __init__(self, nc):
        self.nc = nc
        self.ops = {e: [] for e in ENGS}
        self.cnt = {e: 0 for e in ENGS}
        self.known = {e: {} for e in ENGS}
        self.tw = {}
        self.tr = {}
        n_lanes = {"sp": 6, "act": 2, "pool": 4}
        self.lanes = {q: [[f"dma_{q}_{i}", 0] for i in range(n)] for q, n in n_lanes.items()}
        self.lane_rr = {q: 0 for q in n_lanes}
        self.semkeys = [f"c_{e}" for e in ENGS if e != "sp"] + [l[0] for q in self.lanes for l in self.lanes[q]]
        import os
        self.maxop = int(os.environ.get("KMAXOP", "1000000000"))
        self.nrec = 0
        self.log = []

    def _deps(self, eng, reads, writes):
        deps = []
        for t in list(reads) + list(writes):
            ev = self.tw.get(t)
            if ev is not None:
                deps.append(ev)
        for t in writes:
            deps.extend(self.tr.get(t, ()))
        kn = self.known[eng]
        best = {}
        for (sk, v) in deps:
            if eng == "pe" and sk == "c_pe":
                continue
            if kn.get(sk, 0) >= v:
                continue
            if best.get(sk, 0) < v:
                best[sk] = v
        waits = []
        for sk, v in best.items():
            kn[sk] = v
            waits.append((sk, v))
        return waits

    def _commit(self, ev, reads, writes):
        for t in reads:
            self.tr.setdefault(t, []).append(ev)
        for t in writes:
            self.tw[t] = ev
            self.tr[t] = []

    def op(self, eng, fn, reads=(), writes=()):
        self.nrec += 1
        if self.nrec > self.maxop:
            return None
        self.log.append((self.nrec, eng, fn.__code__.co_firstlineno, tuple(writes)))
        PS = ("acc", "big", "sm0", "sm1")
        writes = list(writes) + [t for t in reads if t.startswith(PS)]
        reads = [t for t in reads if not t.startswith(PS)]
        waits = self._deps(eng, reads, writes)
        self.cnt[eng] += 1
        ev = (f"c_{eng}", self.cnt[eng])
        self.ops[eng].append((waits, fn, ev[0], 1))
        self._commit(ev, reads, writes)
        return ev

    def dma(self, q, fn, reads=(), writes=()):
        self.nrec += 1
        if self.nrec > self.maxop:
            return None
        self.log.append((self.nrec, "dma_" + q, fn.__code__.co_firstlineno, tuple(writes)))
        lanes = self.lanes[q]
        i = self.lane_rr[q]
        self.lane_rr[q] = (i + 1) % len(lanes)
        lane = lanes[i]
        waits = self._deps(q, reads, writes)
        if lane[1] > 0 and self.known[q].get(lane[0], 0) < lane[1]:
            self.known[q][lane[0]] = lane[1]
            waits.append((lane[0], lane[1]))
        lane[1] += 16
        ev = (lane[0], lane[1])
        self.ops[q].append((waits, fn, lane[0], 16))
        self._commit(ev, reads, writes)
        return ev

    def barrier(self):
        evs = [(f"c_{e}", self.cnt[e]) for e in ENGS if e != "sp" and self.cnt[e] > 0]
        evs += [(lane[0], lane[1]) for lane in self.lanes["sp"] if lane[1] > 0]
        for eng in ENGS:
            waits = []
            kn = self.known[eng]
            for (sk, v) in evs:
                if sk == f"c_{eng}" and eng == "pe":
                    continue
                if kn.get(sk, 0) >= v:
                    continue
                kn[sk] = v
                waits.append((sk, v))
            if waits:
                self.ops[eng].append((waits, None, None, 0))

    def wait_all(self, eng):
        waits = []
        for e in ENGS:
            if e == "sp" or self.cnt[e] == 0:
                continue
            waits.append((f"c_{e}", self.cnt[e]))
        for q in self.lanes:
            for lane in self.lanes[q]:
                if lane[1] > 0:
                    waits.append((lane[0], lane[1]))
        self.ops[eng].append((waits, None, None, 0))

    def emit(self, sems):
        nc = self.nc

        def run(engine, lst):
            for waits, fn, sk, inc in lst:
                for (wk, wv) in waits:
                    engine.wait_ge(sems[wk], wv)
                if fn is not None:
                    ins = fn(engine)
                    ins.then_inc(sems[sk], inc)

        with nc.Block() as block:
            @block.tensor
            def _(e):
                run(e, self.ops["pe"])

            @block.scalar
            def _(e):
                run(e, self.ops["act"])

            @block.vector
            def _(e):
                run(e, self.ops["dve"])

            @block.gpsimd
            def _(e):
                run(e, self.ops["pool"])

            @block.sync
            def _(e):
                run(e, self.ops["sp"])


def host_consts():
    half = 32
    inv = (10000.0 ** (-np.arange(half, dtype=np.float32) / half)).astype(np.float32)
    pos = np.arange(SEQ, dtype=np.float32)
    ang = pos[:, None] * inv[None, :]
    c = np.cos(ang).astype(np.float32)
    s = np.sin(ang).astype(np.float32)
    cc = np.concatenate([c, c], -1).reshape(SEQ // 128, 128, 64).transpose(1, 0, 2)
    ss = np.concatenate([-s, s], -1).reshape(SEQ // 128, 128, 64).transpose(1, 0, 2)
    h = np.arange(8, dtype=np.float32)
    log_g = np.log1p(-np.exp2(-5.0 - h)).astype(np.float64)
    p = np.arange(128, dtype=np.float64)
    xi = np.exp(log_g[None, :] * (p[:, None] + 1.0))
    zi = np.exp(-log_g[None, :] * (p[:, None] + 1.0)) * (64 ** -0.5)
    gt = np.zeros((128, 4, 64), np.float64)
    gt32 = np.zeros((128, 4, 64), np.float64)
    for pp in range(128):
        for cch in range(4):
            hh = 2 * cch + pp // 64
            gt[pp, cch, :] = np.exp(log_g[hh] * 128.0)
            gt32[pp, cch, :] = np.exp(log_g[hh] * 32.0)
    ident = np.eye(128, dtype=np.float32)
    jj = np.arange(128)
    mask = (jj[None, :] >= jj[:, None]).astype(np.float32)
    return {
        "c_cc": np.ascontiguousarray(cc, dtype=np.float32),
        "c_ss": np.ascontiguousarray(ss, dtype=np.float32),
        "c_xi": xi.astype(np.float32),
        "c_zi": zi.astype(np.float32),
        "c_gt": gt.astype(np.float32),
        "c_gt32": gt32.astype(np.float32),
        "c_ident": ident,
        "c_mask": mask,
        "c_anti": np.ascontiguousarray(ident[::-1]),
    }


W_NAMES = {
    "w_in": [2, 1024, 8192], "rel_table": [2, 8, 257], "gn_gain": [2, 512], "conv_w": [2, 4, 512],
    "conv_b": [2, 512], "w_gate_a": [2, 8, 64, 64], "b_gate_a": [2, 512], "w_gate_x": [2, 8, 64, 64],
    "b_gate_x": [2, 512], "lru_lambda": [2, 512], "w_branch": [2, 3, 512, 1024], "w_out": [2, 1024, 1024],
    "ln_gain": [2, 1024], "ln_bias": [2, 1024], "w_ple": [2, 256, 1024], "w_ple_gate": [2, 1024, 1024],
}
C_SHAPES = {"c_cc": [128, 16, 64], "c_ss": [128, 16, 64], "c_xi": [128, 8], "c_zi": [128, 8],
            "c_gt": [128, 4, 64], "c_gt32": [128, 4, 64], "c_ident": [128, 128], "c_mask": [128, 128], "c_anti": [128, 128]}


def build(NSEQ=4, L=2, NSMP=2):
    nc = bass.Bass("TRN2", target_bir_lowering=False)
    di = {}

    def din(name, shape):
        di[name] = nc.dram_tensor(name, shape, F32, kind="ExternalInput").ap()
        return di[name]

    def dout(name, shape):
        di[name] = nc.dram_tensor(name, shape, F32, kind="ExternalOutput").ap()
        return di[name]

    xp = din("x_prompt", [NSEQ, SEQ, D])
    pp_ = din("p_prompt", [L, NSEQ, SEQ, 256])
    for k, shp in W_NAMES.items():
        din(k, shp)
    for k, shp in C_SHAPES.items():
        din(k, shp)
    yo = dout("y_prompt", [NSEQ, SEQ, D])
    ko = dout("k_a_prompt", [L, NSEQ, 512, 8, 64])
    vo = dout("v_a_prompt", [L, NSEQ, 512, 8, 64])
    ro = dout("ret_prompt", [L, NSEQ, 8, 64, 64])
    co = dout("conv_prompt", [L, NSEQ, 3, 512])
    lo = dout("lru_prompt", [L, NSEQ, 512])
    xs_d = din("x_sample", [NSMP, 32, D])
    ps_d = din("p_sample", [L, NSMP, 32, 256])
    ck_d = din("cache_k_a", [L, NSMP, 512, 8, 64])
    cv_d = din("cache_v_a", [L, NSMP, 512, 8, 64])
    sr_d = din("state_ret", [L, NSMP, 8, 64, 64])
    sc_d = din("state_conv", [L, NSMP, 3, 512])
    sl_d = din("state_lru", [L, NSMP, 512])
    yso = dout("y_sample", [NSMP, 32, D])
    kso = dout("k_a_sample", [L, NSMP, 32, 8, 64])
    vso = dout("v_a_sample", [L, NSMP, 32, 8, 64])
    rso = dout("ret_sample", [L, NSMP, 8, 64, 64])
    cso = dout("conv_sample", [L, NSMP, 3, 512])
    lso = dout("lru_sample", [L, NSMP, 512])
    gt32 = None
    ext = nc.dram_tensor("ext_scratch", [L, 8, 768], F32, kind="Internal").ap()

    es = ExitStack()

    def sb(name, shape, dt=F32):
        return es.enter_context(nc.sbuf_tensor(name, shape, dt))

    def ps(name, shape, dt=F32):
        return es.enter_context(nc.psum_tensor(name, shape, dt))

    xT = sb("xT", [128, 8, NT])
    xB = sb("xB", [128, 8, NT], BF16)
    yg = [sb(f"yg{i}", [128, 4, NT], BF16) for i in range(3)]
    merged = sb("merged", [128, 8, NT], BF16)
    pT = sb("pT", [128, 2, NT], BF16)
    WBN = 6144
    wb = [sb(f"wb{i}", [128, WBN], BF16) for i in range(3)]
    EB = sb("EB", [128, 8, 5, 128], BF16)
    biasst = [sb("biasst0", [128, 5, 128])]
    cc = sb("cc", [128, 16, 64]); ss = sb("ss", [128, 16, 64])
    xi = sb("xi", [128, 8]); zi = sb("zi", [128, 8])
    gt = sb("gt", [128, 4, 64])
    gt32 = sb("gt32", [128, 4, 64])
    identf = sb("identf", [128, 128]); identb = sb("identb", [128, 128], BF16)
    maskb = sb("maskb", [128, 128], BF16)
    onesf = sb("onesf", [128, 128])
    antif = sb("antif", [128, 128])
    mhalf = sb("mhalf", [128, 16])
    NP = 64
    prm = sb("prm", [128, L, NP])
    WgA = sb("WgA", [128, L, 4, 128], BF16); WgX = sb("WgX", [128, L, 4, 128], BF16)
    kcar = sb("kcar", [128, L, 4, 512], BF16)
    vcar = sb("vcar", [128, L, 4, 4 * 130], BF16)
    Scar = sb("Scar", [128, L, 4, 64])
    convcar = sb("convcar", [128, L, 4, 3])
    hcar = sb("hcar", [128, L, 4])
    sz = sb("sz", [128, NT], BF16)
    SCRN = 7168
    scr = sb("scr", [128, SCRN])
    scrb = scr.bitcast(BF16)

    def carve(items, start=0):
        out = {}
        off = start
        for name, n, dt in items:
            nbytes = n * (4 if dt == F32 else 2)
            if dt == F32:
                out[name] = scr[:, off // 4: off // 4 + n]
            else:
                out[name] = scrb[:, off // 2: off // 2 + n]
            off += (nbytes + 63) // 64 * 64
        assert off <= SCRN * 4, (off, SCRN * 4)
        return out

    cv = carve([("xin0", 1024, F32), ("xin1", 1024, F32), ("pin0", 256, F32), ("pin1", 256, F32),
                ("rts", 257, F32), ("exts", 768, F32)])
    xin = [cv["xin0"], cv["xin1"]]; pin = [cv["pin0"], cv["pin1"]]
    rts = cv["rts"][0:8, :]; exts = cv["exts"][0:8, :]
    cv = carve([("QT", NT, BF16), ("KT", 512 + NT, BF16), ("Vv", (4 + NB) * 130, BF16), ("Eb0", 640, BF16), ("Eb1", 640, BF16),
                ("PT0", 640, BF16), ("PT1", 640, BF16), ("PT2", 640, BF16), ("PT3", 640, BF16), ("rcp", 2, F32), ("ya", 128, BF16), ("kvst0", 128, F32), ("kvst1", 128, F32), ("kctm", 512, BF16)])
    QT = cv["QT"]; KT = cv["KT"]
    Vv = cv["Vv"].rearrange("p (a h d) -> p a h d", a=4 + NB, h=2)
    Eb = [cv["Eb0"].rearrange("p (a q) -> p a q", a=5), cv["Eb1"].rearrange("p (a q) -> p a q", a=5)]
    PTb = [cv["PT0"].rearrange("p (a q) -> p a q", a=5), cv["PT1"].rearrange("p (a q) -> p a q", a=5)]
    PTd = [[cv[f"PT{2 * par + hh}"].rearrange("p (a q) -> p a q", a=5) for hh in range(2)] for par in range(2)]
    rcp = cv["rcp"]; ya = cv["ya"]; kvst = [cv["kvst0"], cv["kvst1"]]
    kctm = cv["kctm"].rearrange("p (a c) -> p a c", a=4)
    ya2 = [ya, cv["kctm"][:, 0:128]]
    cv = carve([("QTr", NT, BF16), ("KTr", NT, BF16), ("Vb", NB * 128, BF16), ("rt1", 128, F32), ("rt2", 128, F32),
                ("Qt", 128, BF16), ("Kt", 128, BF16), ("kvbuf", 64 * NB, F32), ("Sall", 64 * NB, F32), ("Gpat", 64 * NB, F32),
                ("Sop", NB * 64, BF16), ("Am", 256, BF16), ("ysb", 128, F32), ("ysq", 128, F32), ("gst", 16, F32), ("ynb", 128, BF16), ("S0f", 64, F32), ("Snew", 64, F32),
                ("QtA", NB * 128, BF16), ("KtA", NB * 128, BF16), ("AmA", 4 * 256, BF16), ("gstA", 96, F32), ("ynbA", NB * 128, BF16),
                ("sz2", NT, BF16)])
    QTr = cv["QTr"]; KTr = cv["KTr"]; Vb = cv["Vb"].rearrange("p (n c) -> p n c", n=NB)
    rt1 = cv["rt1"]; rt2 = cv["rt2"]; Qt = cv["Qt"]; Kt = cv["Kt"]
    kvbuf = cv["kvbuf"].rearrange("p (e n) -> p e n", n=NB); Sall = cv["Sall"].rearrange("p (e n) -> p e n", n=NB)
    Gpat = cv["Gpat"].rearrange("p (e n) -> p e n", n=NB); Sop = cv["Sop"].rearrange("p (n e) -> p n e", n=NB)
    S0f = cv["S0f"]; Snew = cv["Snew"]
    QtA = cv["QtA"].rearrange("p (t c) -> p t c", t=NB); KtA = cv["KtA"].rearrange("p (t c) -> p t c", t=NB)
    AmA = cv["AmA"].rearrange("p (t h i) -> p t h i", t=4, h=2); gstA = cv["gstA"].rearrange("p (k n) -> p k n", k=6)
    sz2 = cv["sz2"]
    Vb2t = sb("Vb2", [128, NB, 128], BF16)
    Vbd = [Vb, Vb2t[:]]
    szd = [sz, sz2]
    ynbA = cv["ynbA"].rearrange("p (t c) -> p t c", t=NB)
    mgf = merged.bitcast(F32)[:].rearrange("p a b -> p (a b)")
    qkraw = mgf[:, 0:2048].rearrange("p (t c) -> p t c", t=NB)
    rt1A = mgf[:, 2048:3072].rearrange("p (t c) -> p t c", t=NB)
    rt2A = mgf[:, 3072:4096].rearrange("p (t c) -> p t c", t=NB)
    ysbA = mgf[:, 2048:3072].rearrange("p (t c) -> p t c", t=NB)
    ysqA = mgf[:, 3072:4096].rearrange("p (t c) -> p t c", t=NB)
    Am = cv["Am"].rearrange("p (h i) -> p h i", h=2); ysb = cv["ysb"]; ysq = cv["ysq"]; gst = cv["gst"]; ynb = cv["ynb"]
    cv = carve([("bB", NT, F32), ("bC", NT, F32), ("szc", NT, BF16)], start=18432)
    bB = cv["bB"]; bC = cv["bC"]; szc_buf = cv["szc"]
    _mg = merged.bitcast(F32)[:].rearrange("p a b -> p (a b)")
    xrbuf = _mg[:, 0:3 + NT]; xc = _mg[:, 1028:1028 + NT]; bA = _mg[:, 2052:2052 + NT]
    xcb = merged[:].rearrange("p a b -> p (a b)")[:, 2 * 3076:2 * 3076 + NT]
    cv = carve([("gbuf0", 512, BF16), ("gbuf1", 512, BF16), ("gbuf2", 512, BF16), ("tb3_0", 512, F32), ("tb3_1", 512, F32), ("tb3_2", 512, F32),
                ("lnst", 8 * NB, F32), ("lntmp", 128, F32), ("lnt2", 128, F32), ("dA", 128, F32), ("dB", 128, F32), ("lnt", 1024, F32),
                ("yst0", 1024, F32), ("yst1", 1024, F32)])
    gbuf = [cv["gbuf0"], cv["gbuf1"], cv["gbuf2"]]; tb3 = [cv["tb3_0"], cv["tb3_1"], cv["tb3_2"]]
    lnst = cv["lnst"]; lntmp = cv["lntmp"]; lnt2 = cv["lnt2"]; dA = cv["dA"]; dB = cv["dB"]
    lnt = cv["lnt"].rearrange("p (c t) -> p c t", c=8); yst = [cv["yst0"], cv["yst1"]]

    acc = [ps(f"acc{i}", [128, 512]) for i in range(2)]
    big = [ps(f"big{i}", [128, 1024]) for i in range(2)]
    sm0 = ps("sm0", [128, 512])
    sm1 = ps("sm1", [128, 1024], BF16)

    P = Prog(nc)
    sems = {k: es.enter_context(nc.semaphore(k)) for k in P.semkeys}
    rr = {"acc": 0, "ev": 0, "xin": 0, "pin": 0, "wb": 0, "kvst": 0, "bst": 0, "yst": 0}

    def nxt(key, n):
        v = rr[key]
        rr[key] = (v + 1) % n
        return v

    def bc(ap, shape):
        return ap.to_broadcast(shape)

    def mm_group(out_ap, out_tok, pairs, rtoks):
        def fn(e):
            ins = None
            n = len(pairs)
            for i, (l_, r_) in enumerate(pairs):
                ins = e.matmul(out_ap, l_, r_, start=(i == 0), stop=(i == n - 1))
            return ins
        P.op("pe", fn, reads=rtoks, writes=[out_tok])

    def xb_toks(tt):
        return [f"xB{kc}_{tt}" for kc in range(8)]

    def xf_toks(tt):
        return [f"xT{kc}_{tt}" for kc in range(8)]

    def evac_engine():
        return "act" if nxt("ev", 2) == 0 else "dve"

    def copy_op(eng, out_ap, in_ap, reads, writes, scale=None):
        if eng == "act":
            if scale is None:
                P.op("act", lambda e: e.copy(out_ap, in_ap), reads=reads, writes=writes)
            else:
                P.op("act", lambda e: e.mul(out_ap, in_ap, scale), reads=reads, writes=writes)
        else:
            if scale is None:
                P.op(eng, lambda e: e.tensor_scalar(out_ap, in_ap, 1.0, None, ALU.mult), reads=reads, writes=writes)
            else:
                P.op(eng, lambda e: e.tensor_scalar(out_ap, in_ap, scale, None, ALU.mult), reads=reads, writes=writes)

    def act_fn(out_ap, in_ap, func, reads, writes, bias=None, scale=None):
        kw = {}
        if bias is not None:
            kw["bias"] = bias
        if scale is not None:
            kw["scale"] = scale
        P.op("act", lambda e: e.activation(out_ap, in_ap, func, **kw), reads=reads, writes=writes)

    import os as _os
    def load_const(dst, name, toks):
        P.dma("sp", lambda e: e.dma_start(out=dst, in_=di[name]), writes=toks)

    load_const(cc[:], "c_cc", ["cc"]); load_const(ss[:], "c_ss", ["ss"])
    load_const(xi[:], "c_xi", ["xi"]); load_const(zi[:], "c_zi", ["zi"])
    load_const(gt[:], "c_gt", ["gt"]); load_const(identf[:], "c_ident", ["identf"]); load_const(antif[:], "c_anti", ["antif"]); load_const(gt32[:], "c_gt32", ["gt32"])
    P.dma("pool", lambda e: e.dma_start(out=identb[:], in_=di["c_ident"]), writes=["identb"])
    P.dma("pool", lambda e: e.dma_start(out=maskb[:], in_=di["c_mask"]), writes=["maskb"])
    P.op("dve", lambda e: e.memset(onesf[:], 1.0), writes=["onesf"])
    P.op("dve", lambda e: e.memset(mhalf[:], -0.5), writes=["mhalf"])
    P.op("dve", lambda e: e.memset(WgA[:], 0.0), writes=["WgA"])
    P.op("dve", lambda e: e.memset(WgX[:], 0.0), writes=["WgX"])
    P.op("dve", lambda e: e.memset(prm[:], 0.0), writes=["prm"])

    def pcol(l, a, b):
        return prm[:, l, a:b]

    for l in range(L):
        def vec_load(name, col, n, l=l):
            src = di[name][l].rearrange("(c p) -> p c", p=128)
            P.dma("sp", lambda e: e.dma_start(out=prm[:, l, col:col + n], in_=src), reads=["prm"], writes=[f"prm{l}_{col}"])
        vec_load("gn_gain", 0, 4)
        vec_load("conv_b", 4, 4)
        for tap in range(4):
            src = di["conv_w"][l, tap].rearrange("(c p) -> p c", p=128)
            P.dma("sp", lambda e, src=src, tap=tap, l=l: e.dma_start(out=prm[:, l, 8 + 4 * tap:12 + 4 * tap], in_=src),
                  reads=["prm"], writes=[f"prm{l}_cw{tap}"])
        vec_load("b_gate_a", 24, 4)
        vec_load("b_gate_x", 28, 4)
        vec_load("lru_lambda", 32, 4)
        vec_load("ln_gain", 40, 8)
        vec_load("ln_bias", 48, 8)
        act_fn(prm[:, l, 36:40], prm[:, l, 32:36], AF.Exp, [f"prm{l}_32"], [f"prm{l}_sp"], scale=-1.0)
        act_fn(prm[:, l, 36:40], prm[:, l, 36:40], AF.Ln, [f"prm{l}_sp"], [f"prm{l}_sp"], bias=onesf[:, 0:1])
        P.op("dve", lambda e, l=l: e.tensor_scalar(prm[:, l, 56:60], prm[:, l, 36:40], -16.0, None, ALU.mult),
             reads=[f"prm{l}_sp"], writes=[f"prm{l}_sp2"])
        P.op("dve", lambda e, l=l: e.tensor_scalar(prm[:, l, 36:40], prm[:, l, 36:40], -8.0, None, ALU.mult),
             reads=[f"prm{l}_sp", f"prm{l}_sp2"], writes=[f"prm{l}_sp"])
        for nm, Wt in (("w_gate_a", WgA), ("w_gate_x", WgX)):
            srcv = di[nm][l].rearrange("(c hh) i j -> hh i c j", hh=2)
            for hh in range(2):
                P.dma("pool", lambda e, Wt=Wt, srcv=srcv, hh=hh, l=l: e.dma_start(
                    out=Wt[hh * 64:(hh + 1) * 64, l, :, hh * 64:(hh + 1) * 64], in_=srcv[hh]),
                    reads=["WgA" if nm == "w_gate_a" else "WgX"], writes=[f"{nm}{l}_{hh}"])
        P.dma("sp", lambda e, l=l: e.dma_start(out=rts[:], in_=di["rel_table"][l]), writes=["rts"])
        P.op("dve", lambda e: e.tensor_copy(exts[:, 0:256], rts[:, 1:257]), reads=["rts"], writes=["exts"])
        P.op("dve", lambda e: e.tensor_copy(exts[:, 256:768], bc(rts[:, 256:257], [8, 512])),
             reads=["rts", "exts"], writes=["exts"])
        P.dma("sp", lambda e, l=l: e.dma_start(out=ext[l], in_=exts[:]), reads=["exts"], writes=[f"ext{l}"])
    WgTok = [[f"w_gate_a{l}_0", f"w_gate_a{l}_1", f"w_gate_x{l}_0", f"w_gate_x{l}_1", "WgA", "WgX"] for l in range(L)]
    prm_all = lambda l: ([f"prm{l}_{c}" for c in (0, 4, 24, 28, 40, 48)] + [f"prm{l}_cw{t}" for t in range(4)]
                         + [f"prm{l}_sp", f"prm{l}_sp2", "prm"])

    w_in = di["w_in"]
    P.barrier()

    def load_w(i, pieces):
        flat = []
        for dst, src in pieces:
            if len(dst.shape) == 4:
                for gi in range(dst.shape[2]):
                    flat.append((dst[:, :, gi, :], src[:, :, gi, :]))
            else:
                flat.append((dst, src))
        assert len(flat) <= 6
        for k, (dst, src) in enumerate(flat):
            P.dma("pool", lambda e, dst=dst, src=src: e.dma_start(out=dst, in_=src), writes=[f"wb{i}"] if k == 0 else [f"wb{i}_p{k}"])

    def wtoks(i, npieces):
        return [f"wb{i}"] + [f"wb{i}_p{k}" for k in range(1, 6)]

    def build_EB(l):
        for h in range(8):
            bst = biasst[0]
            base = ext[l, h, 0:1]
            src = bass.AP(base.tensor, base.offset, [[1, 128], [128, 5], [1, 128]])
            P.dma("sp", lambda e, bst=bst, src=src: e.dma_start(out=bst[:], in_=src), reads=[f"ext{l}"], writes=["bst0"])
            bv = big[0][:, 0:640]

            def fn(e, bst=bst, bv=bv):
                bf = bst[:].rearrange("p a q -> p (a q)")
                e.matmul(bv[:, 0:512], antif[:], bf[:, 0:512], start=True, stop=True)
                return e.matmul(bv[:, 512:640], antif[:], bf[:, 512:640], start=True, stop=True)
            P.op("pe", fn, reads=["bst0", "antif"], writes=["big0"])
            bv5 = bv.rearrange("p (a q) -> p a q", a=5)
            act_fn(EB[:, h, 0:4, :], bv5[:, 0:4, :], AF.Exp, ["big0"], ["EB"])
            act_fn(EB[:, h, 4:5, :], bv5[:, 4:5, :], AF.Exp, ["big0", "EB"], ["EB"])
        P.op("dve", lambda e: e.memset(EB[0:64, :, 4, 64:128], 0.0), reads=["EB"], writes=["EB"])
        P.op("dve", lambda e: e.memset(EB[64:128, :, 0, 0:64], 0.0), reads=["EB"], writes=["EB"])

    def load_x(b, hf):
        for tb in range(NB):
            xi_ = nxt("xin", 2)
            r0 = hf * NT + tb * 128
            P.dma("sp", lambda e, xi_=xi_, r0=r0: e.dma_start(out=xin[xi_][:], in_=xp[b, r0:r0 + 128, :]), writes=[f"xin{xi_}"])
            tt = tb // 4
            for hb in range(4):
                sv = sm0[:, 0:256].rearrange("p (c t) -> p c t", c=2)

                def fn(e, xi_=xi_, sv=sv, hb=hb):
                    ins = None
                    for c in range(2):
                        cg = hb * 2 + c
                        ins = e.transpose(sv[:, c, :], xin[xi_][:, cg * 128:(cg + 1) * 128], identf[:])
                    return ins
                P.op("pe", fn, reads=[f"xin{xi_}", "identf"], writes=["sm0"])
                cs = slice(hb * 2, hb * 2 + 2)
                if int(_os.environ.get("KX", "3")) & 1:
                    P.op("act", lambda e, tb=tb, sv=sv, cs=cs: e.copy(xT[:, cs, tb * 128:(tb + 1) * 128], sv),
                         reads=["sm0"], writes=[f"xT{c}_{tt}" for c in range(8)])
                if int(_os.environ.get("KX", "3")) & 2:
                    P.op("act", lambda e, tb=tb, sv=sv, cs=cs: e.copy(xB[:, cs, tb * 128:(tb + 1) * 128], sv),
                         reads=["sm0"], writes=[f"xB{c}_{tt}" for c in range(8)])

    def load_p(l, b, hf):
        for tb in range(NB):
            pi_ = nxt("pin", 2)
            r0 = hf * NT + tb * 128
            P.dma("sp", lambda e, pi_=pi_, r0=r0: e.dma_start(out=pin[pi_][:], in_=pp_[l, b, r0:r0 + 128, :]), writes=[f"pin{pi_}"])
            sv = sm0[:, 0:256].rearrange("p (c t) -> p c t", c=2)

            def fn(e, pi_=pi_, sv=sv):
                ins = None
                for c in range(2):
                    ins = e.transpose(sv[:, c, :], pin[pi_][:, c * 128:(c + 1) * 128], identf[:])
                return ins
            P.op("pe", fn, reads=[f"pin{pi_}", "identf"], writes=["sm0"])
            copy_op(evac_engine(), pT[:, :, tb * 128:(tb + 1) * 128], sv, ["sm0"], [f"pT_{tb // 4}"])

    CFG = {"w": 512, "ntt": NTT}

    def csl(tt):
        return slice(tt * CFG["w"], (tt + 1) * CFG["w"])

    def cw(ap):
        return ap[:, 0:CFG["w"]]

    def fm_mm(out_acc, ai, lhs_fn, tt, wt, nkc=8, rhs_src=None, rhs_toks=None):
        src = xB if rhs_src is None else rhs_src
        pairs = [(lhs_fn(kc), src[:, kc, csl(tt)]) for kc in range(nkc)]
        mm_group(cw(out_acc), f"acc{ai}", pairs, (xb_toks(tt) if rhs_toks is None else rhs_toks) + wt)

    def att_job(l, b, hf, j, wi, last):
        wt = wtoks(wi, 1)
        wv = wb[wi][:, 0:4096].rearrange("p (kc g c) -> p kc g c", kc=8, g=4)
        P.op("dve", lambda e: e.memset(Vv[:, :, :, 64:65], 1.0), writes=["Vv"])
        if hf > 0:
            copy_op("dve", KT[:, 0:512], kcar[:, l, j, :], [f"kcar{l}_{j}"], ["KT"])
            copy_op("act", Vv[:, 0:4, :, :], vcar[:, l, j, :].rearrange("p (a h d) -> p a h d", a=4, h=2), [f"vcar{l}_{j}"], ["Vv"])
        for tt in range(NTT):
            ai = nxt("acc", 2)
            fm_mm(acc[ai], ai, lambda kc: wv[:, kc, 0, :], tt, wt)
            copy_op("act", QT[:, tt * 512:(tt + 1) * 512], acc[ai][:], [f"acc{ai}"], ["QT"], scale=0.125)
            ai = nxt("acc", 2)
            fm_mm(acc[ai], ai, lambda kc: wv[:, kc, 1, :], tt, wt)
            copy_op("dve", KT[:, 512 + tt * 512:512 + (tt + 1) * 512], acc[ai][:], [f"acc{ai}"], ["KT"])
            ai = nxt("acc", 2)
            fm_mm(acc[ai], ai, lambda kc: wv[:, kc, 3, :], tt, wt)
            act_fn(sz[:, tt * 512:(tt + 1) * 512], acc[ai][:], AF.Silu, [f"acc{ai}"], ["sz"])
            yield
        for tb in range(NB):
            yield
            tt = tb // 4
            ai = nxt("acc", 2)
            pairs = [(xB[:, kc, tb * 128:(tb + 1) * 128], wv[:, kc, 2, :]) for kc in range(8)]
            mm_group(acc[ai][:, 0:128], f"acc{ai}", pairs, xb_toks(tt) + wt)
            av = acc[ai][:, 0:128].rearrange("p (h d) -> p h d", h=2)
            copy_op("dve", Vv[:, 4 + tb, :, 0:64], av, [f"acc{ai}"], ["Vv"])
            if last and tb >= NB - 4:
                si = nxt("kvst", 2)
                copy_op("act", kvst[si][:], acc[ai][:, 0:128], [f"acc{ai}"], [f"kvst{si}"])
                r0 = (tb - (NB - 4)) * 128
                P.dma("sp", lambda e, si=si, r0=r0: e.dma_start(
                    out=vo[l, b, r0:r0 + 128, 2 * j:2 * j + 2, :], in_=kvst[si][:].rearrange("p (h d) -> p h d", h=2)),
                    reads=[f"kvst{si}"])
                ai2 = nxt("acc", 2)
                pairs = [(xB[:, kc, tb * 128:(tb + 1) * 128], wv[:, kc, 1, :]) for kc in range(8)]
                mm_group(acc[ai2][:, 0:128], f"acc{ai2}", pairs, xb_toks(tt) + wt)
                si = nxt("kvst", 2)
                copy_op("act", kvst[si][:], acc[ai2][:, 0:128], [f"acc{ai2}"], [f"kvst{si}"])
                P.dma("sp", lambda e, si=si, r0=r0: e.dma_start(
                    out=ko[l, b, r0:r0 + 128, 2 * j:2 * j + 2, :], in_=kvst[si][:].rearrange("p (h d) -> p h d", h=2)),
                    reads=[f"kvst{si}"])
        Ov = sm0[:, 0:130].rearrange("p (h d) -> p h d", h=2)

        def stageA(tb):
            gblk = hf * NB + tb
            njp = 5 - max(0, 4 - gblk)
            par = tb % 2
            for hh in range(2):
                h = 2 * j + hh
                pb = 64 * hh
                STv = big[hh][:, 0:640].rearrange("p (a q) -> p a q", a=5)

                def fn(e, STv=STv, pb=pb, tb=tb, njp=njp):
                    ins = None
                    for jp in range(njp):
                        kb = tb + 4 - jp
                        ins = e.matmul(STv[:, jp, :], KT[pb:pb + 64, kb * 128:(kb + 1) * 128],
                                       QT[pb:pb + 64, tb * 128:(tb + 1) * 128], start=True, stop=True)
                    return ins
                P.op("pe", fn, reads=["KT", "QT"], writes=[f"big{hh}"])
                act_fn(Eb[hh][:, 0:min(njp, 4), :], STv[:, 0:min(njp, 4), :], AF.Exp, [f"big{hh}"], [f"Eb{hh}"])
                if njp == 5:
                    act_fn(Eb[hh][:, 4:5, :], STv[:, 4:5, :], AF.Exp, [f"big{hh}", f"Eb{hh}"], [f"Eb{hh}"])
                P.op("dve", lambda e, hh=hh, h=h, njp=njp, par=par: e.tensor_tensor(PTd[par][hh][:, 0:njp, :], Eb[hh][:, 0:njp, :], EB[:, h, 0:njp, :], ALU.mult),
                     reads=[f"Eb{hh}", "EB"], writes=[f"PT{par}{hh}"])

        def stageB(tb):
            gblk = hf * NB + tb
            njp = 5 - max(0, 4 - gblk)
            par = tb % 2
            ya_ = ya2[par]
            for hh in range(2):
                def fn2(e, hh=hh, tb=tb, njp=njp, par=par):
                    ins = None
                    for jp in range(njp):
                        ins = e.matmul(Ov[:, hh, :], PTd[par][hh][:, jp, :], Vv[:, tb + 4 - jp, hh, :], start=(jp == 0), stop=(jp == njp - 1))
                    return ins
                P.op("pe", fn2, reads=[f"PT{par}{hh}", "Vv"], writes=["sm0"])
            P.op("dve", lambda e: e.reciprocal(rcp[:].rearrange("p (h o) -> p h o", o=1), Ov[:, :, 64:65]), reads=["sm0"], writes=["rcp"])
            P.op("dve", lambda e, ya_=ya_: e.tensor_tensor(ya_[:].rearrange("p (h d) -> p h d", h=2), Ov[:, :, 0:64],
                                                           bc(rcp[:].rearrange("p (h o) -> p h o", o=1), [128, 2, 64]), ALU.mult),
                 reads=["sm0", "rcp"], writes=[f"ya{par}"])

        def stageC(tb):
            par = tb % 2
            ya_ = ya2[par]
            P.op("pe", lambda e, ya_=ya_: e.transpose(sm1[:, 0:128], ya_[:], identb[:]), reads=[f"ya{par}", "identb"], writes=["sm1"])
            P.op("dve", lambda e, tb=tb: e.tensor_tensor(yg[0][:, j, tb * 128:(tb + 1) * 128], sm1[:, 0:128], sz[:, tb * 128:(tb + 1) * 128], ALU.mult),
                 reads=["sm1", "sz"], writes=[f"yg0_{tb // 4}"])

        for t in range(NB + 2):
            if t < NB:
                stageA(t)
                yield
            if 1 <= t <= NB:
                stageB(t - 1)
                yield
            if t >= 2:
                stageC(t - 2)
                yield
        copy_op("dve", kcar[:, l, j, :], KT[:, NT:NT + 512], ["KT"], [f"kcar{l}_{j}"])
        copy_op("act", vcar[:, l, j, :].rearrange("p (a h d) -> p a h d", a=4, h=2), Vv[:, NB:NB + 4, :, :], ["Vv"], [f"vcar{l}_{j}"])

    def retA(l, b, hf, j, wi):
        par = j % 2
        wt = wtoks(wi, 1)
        wv = wb[wi][:, 0:4096].rearrange("p (kc g c) -> p kc g c", kc=8, g=4)
        for tt in range(NTT):
            ai = nxt("acc", 2)
            fm_mm(acc[ai], ai, lambda kc: wv[:, kc, 3, :], tt, wt)
            act_fn(szd[par][:, tt * 512:(tt + 1) * 512], acc[ai][:], AF.Silu, [f"acc{ai}"], [f"szr{par}"])
        for tb in range(NB):
            tt = tb // 4
            ai = nxt("acc", 2)
            pairs = [(xB[:, kc, tb * 128:(tb + 1) * 128], wv[:, kc, 0:3, :]) for kc in range(8)]
            mm_group(acc[ai][:, 0:384], f"acc{ai}", pairs, xb_toks(tt) + wt)
            copy_op("act", qkraw[:, tb, :], acc[ai][:, 0:256], [f"acc{ai}"], ["MB0"])
            copy_op("act", Vbd[par][:, tb, :], acc[ai][:, 256:384], [f"acc{ai}"], [f"Vb{par}"])

    def retB1(l, b, hf, j):
        g0 = hf * NB
        for qi, (dstA, sc_, dtok) in enumerate(((QtA, xi, "QtA"), (KtA, zi, "KtA"))):
            raw = qkraw[:, :, qi * 128:(qi + 1) * 128]
            raw4 = raw.rearrange("p t (h d) -> p t h d", h=2)
            raw5 = raw.rearrange("p t (h two d) -> p t h two d", h=2, two=2)
            t14 = rt1A.rearrange("p t (h d) -> p t h d", h=2)
            t25 = rt2A.rearrange("p t (h two d) -> p t h two d", h=2, two=2)
            ccv = cc[:, g0:g0 + NB, :].unsqueeze(2).to_broadcast([128, NB, 2, 64])
            P.op("dve", lambda e, t14=t14, raw4=raw4, ccv=ccv: e.tensor_tensor(t14, raw4, ccv, ALU.mult), reads=["MB0", "cc"], writes=["rt1A"])
            for hv in range(2):
                ssv = ss[:, g0:g0 + NB, hv * 32:(hv + 1) * 32].unsqueeze(2).to_broadcast([128, NB, 2, 32])
                P.op("dve", lambda e, t25=t25, raw5=raw5, ssv=ssv, hv=hv: e.tensor_tensor(t25[:, :, :, hv, :], raw5[:, :, :, 1 - hv, :], ssv, ALU.mult),
                     reads=["MB0", "ss"], writes=["rt2A"])
            P.op("dve", lambda e: e.tensor_tensor(rt1A, rt1A, rt2A, ALU.add), reads=["rt1A", "rt2A"], writes=["rt1A"])
            scv = sc_[:, 2 * j:2 * j + 2].unsqueeze(1).unsqueeze(3).to_broadcast([128, NB, 2, 64])
            P.op("dve", lambda e, dstA=dstA, t14=t14, scv=scv: e.tensor_tensor(dstA.rearrange("p t (h d) -> p t h d", h=2), t14, scv, ALU.mult),
                 reads=["rt1A", "xi", "zi"], writes=[dtok])

    def retB2(l, b, hf, j, last):
        par = j % 2
        Vb_ = Vbd[par]
        vtok = f"Vb{par}"
        szb = szd[par]
        P.op("dve", lambda e: e.tensor_copy(Gpat[:], bc(gt[:, j, :].rearrange("p (e o) -> p e o", o=1), [128, 64, NB])),
             reads=["gt"], writes=["Gpat"])
        P.op("dve", lambda e: e.memset(Gpat[:, :, 0:1], 0.0), reads=["Gpat"], writes=["Gpat"])
        for g4 in range(NB // 4):
            def fnT(e, g4=g4):
                ins = None
                for t4 in range(4):
                    tb = g4 * 4 + t4
                    e.transpose(sm1[:, t4 * 128:(t4 + 1) * 128], QtA[:, tb, :], identb[:])
                    ins = e.transpose(sm1[:, 512 + t4 * 128:512 + (t4 + 1) * 128], KtA[:, tb, :], identb[:])
                return ins
            P.op("pe", fnT, reads=["QtA", "KtA", "identb"], writes=["sm1"])
            copy_op("act", QTr[:, g4 * 512:(g4 + 1) * 512], sm1[:, 0:512], ["sm1"], ["QTr"])
            copy_op("dve", KTr[:, g4 * 512:(g4 + 1) * 512], sm1[:, 512:1024], ["sm1"], ["KTr"])

            def fnKV(e, g4=g4):
                ins = None
                for t4 in range(4):
                    tb = g4 * 4 + t4
                    ins = e.matmul(sm0[:, t4 * 128:(t4 + 1) * 128], KtA[:, tb, :], Vb_[:, tb, :], start=True, stop=True)
                return ins
            P.op("pe", fnKV, reads=["KtA", vtok], writes=["sm0"])
            for hh in range(2):
                rr_ = slice(hh * 64, (hh + 1) * 64)
                o_ = kvbuf[rr_, :, g4 * 4:(g4 + 1) * 4].rearrange("p e t -> p t e")
                i0_ = sm0[rr_, 0:512].rearrange("p (t c) -> p t c", t=4)[:, :, hh * 64:(hh + 1) * 64]
                i1_ = gt[rr_, j, :].unsqueeze(1).to_broadcast([64, 4, 64])
                P.op("dve", lambda e, o_=o_, i0_=i0_, i1_=i1_: e.tensor_tensor(o_, i0_, i1_, ALU.mult), reads=["sm0", "gt"], writes=["kvbuf"])
        P.op("dve", lambda e: e.tensor_tensor(ysb[:, 0:64], Scar[:, l, j, :], gt[:, j, :], ALU.mult), reads=[f"Scar{l}_{j}", "gt"], writes=["ysb"])
        P.op("dve", lambda e: e.tensor_tensor(kvbuf[:, :, 0], kvbuf[:, :, 0], ysb[:, 0:64], ALU.add), reads=["ysb", "kvbuf"], writes=["kvbuf"])
        P.op("dve", lambda e: e.tensor_tensor_scan(Sall[:].rearrange("p e n -> p (e n)"), Gpat[:].rearrange("p e n -> p (e n)"),
                                                   kvbuf[:].rearrange("p e n -> p (e n)"), 0.0, ALU.mult, ALU.add),
             reads=["kvbuf", "Gpat"], writes=["Sall"])
        copy_op("act", Sop[:, 0, :], Scar[:, l, j, :], [f"Scar{l}_{j}"], ["Sop"])
        copy_op("act", Sop[:, 1:NB, :], Sall[:, :, 0:NB - 1].rearrange("p e n -> p n e"), ["Sall"], ["Sop"])
        copy_op("dve", Scar[:, l, j, :], Sall[:, :, NB - 1], ["Sall", "Sop"], [f"Scar{l}_{j}"])
        if last:
            P.dma("sp", lambda e: e.dma_start(out=ro[l, b, 2 * j:2 * j + 2, :, :].rearrange("hh d e -> (hh d) e"), in_=Scar[:, l, j, :]),
                  reads=[f"Scar{l}_{j}"])
        Yh = [sm0[:, 0:512].rearrange("p (t e) -> p t e", t=NB), big[1][:, 512:1024].rearrange("p (t e) -> p t e", t=NB)]
        for g4 in range(NB // 4):
            def fnA(e, g4=g4):
                ins = None
                for t4 in range(4):
                    tb = g4 * 4 + t4
                    for hh in range(2):
                        pb = 64 * hh
                        ins = e.matmul(big[hh][:, t4 * 128:(t4 + 1) * 128], KTr[pb:pb + 64, tb * 128:(tb + 1) * 128],
                                       QTr[pb:pb + 64, tb * 128:(tb + 1) * 128], start=True, stop=True)
                return ins
            P.op("pe", fnA, reads=["KTr", "QTr"], writes=["big0", "big1"])
            for hh in range(2):
                o_ = AmA[:, :, hh, :]
                i0_ = big[hh][:, 0:512].rearrange("p (t i) -> p t i", t=4)
                i1_ = maskb[:].unsqueeze(1).to_broadcast([128, 4, 128])
                P.op("dve", lambda e, o_=o_, i0_=i0_, i1_=i1_: e.tensor_tensor(o_, i0_, i1_, ALU.mult), reads=[f"big{hh}", "maskb"], writes=["AmA"])

            def fnY(e, g4=g4):
                ins = None
                for t4 in range(4):
                    tb = g4 * 4 + t4
                    for hh in range(2):
                        pb = 64 * hh
                        e.matmul(Yh[hh][:, tb, :], AmA[:, t4, hh, :], Vb_[:, tb, hh * 64:(hh + 1) * 64], start=True, stop=False)
                        ins = e.matmul(Yh[hh][:, tb, :], QTr[pb:pb + 64, tb * 128:(tb + 1) * 128], Sop[pb:pb + 64, tb, :], start=False, stop=True)
                return ins
            P.op("pe", fnY, reads=["AmA", vtok, "QTr", "Sop"], writes=["sm0", "big1"])
        for hh in range(2):
            tk = "sm0" if hh == 0 else "big1"
            copy_op("act", ysbA[:, :, hh * 64:(hh + 1) * 64], Yh[hh], [tk, "rt1A"], ["rt1A"])
            P.op("act", lambda e, hh=hh: e.activation(ysqA[:, :, hh * 64:(hh + 1) * 64], Yh[hh], AF.Square), reads=[tk, "rt2A"], writes=["rt2A"])
        g = gstA
        y3 = ysbA.rearrange("p t (h d) -> p (t h) d", h=2)
        q3 = ysqA.rearrange("p t (h d) -> p (t h) d", h=2)
        P.op("dve", lambda e: e.reduce_sum(g[:, 0, :], y3, AX.X), reads=["rt1A"], writes=["gstA"])
        P.op("dve", lambda e: e.reduce_sum(g[:, 1, :], q3, AX.X), reads=["rt2A", "gstA"], writes=["gstA"])
        P.op("dve", lambda e: e.tensor_scalar(g[:, 0, :], g[:, 0, :], 1.0 / 64, None, ALU.mult), reads=["gstA"], writes=["gstA"])
        P.op("dve", lambda e: e.tensor_tensor(g[:, 2, :], g[:, 0, :], g[:, 0, :], ALU.mult), reads=["gstA"], writes=["gstA"])
        P.op("dve", lambda e: e.scalar_tensor_tensor(g[:, 3, :], g[:, 1, :], 1.0 / 64, g[:, 2, :], ALU.mult, ALU.subtract), reads=["gstA"], writes=["gstA"])
        P.op("dve", lambda e: e.tensor_scalar(g[:, 3, :], g[:, 3, :], LN_EPS, None, ALU.add), reads=["gstA"], writes=["gstA"])
        P.op("pool", lambda e: e.tensor_tensor(g[:, 4, :], g[:, 3, :], mhalf[:, 0:16], ALU.pow), reads=["gstA", "mhalf"], writes=["gstA"])
        P.op("dve", lambda e: e.tensor_tensor(y3, y3, g[:, 0, :].unsqueeze(2).to_broadcast([128, 16, 64]), ALU.subtract), reads=["gstA", "rt1A"], writes=["rt1A"])
        P.op("dve", lambda e: e.tensor_tensor(ynbA.rearrange("p t (h d) -> p (t h) d", h=2), y3,
                                              g[:, 4, :].unsqueeze(2).to_broadcast([128, 16, 64]), ALU.mult), reads=["gstA", "rt1A"], writes=["ynbA"])

        def fnT2(e):
            ins = None
            for tb in range(NB):
                ins = e.transpose(sm1[:, tb * 128:(tb + 1) * 128], ynbA[:, tb, :], identb[:])
            return ins
        P.op("pe", fnT2, reads=["ynbA", "identb"], writes=["sm1"])
        P.op("dve", lambda e: e.scalar_tensor_tensor(yg[1][:, j, :], sm1[:, 0:NT], prm[:, l, j:j + 1], szb[:, 0:NT], ALU.mult, ALU.mult),
             reads=["sm1", f"szr{par}"] + prm_all(l), writes=["yg1_0", "yg1_1"])

    def ret_job(l, b, hf, j, wis_ret, last):
        if j == 0:
            retA(l, b, hf, 0, wis_ret[0])
        retB1(l, b, hf, j)
        if j + 1 < 4:
            retA(l, b, hf, j + 1, wis_ret[j + 1])
        retB2(l, b, hf, j, last)

    def lru_job(l, b, hf, c, wi, last, woff=0):
        wt = wtoks(wi, 1)
        wv = wb[wi][:, woff:woff + 2048].rearrange("p (kc g c) -> p kc g c", kc=8, g=2)
        szc = szc_buf
        pl = prm_all(l)
        copy_op("dve", xrbuf[:, 0:3], convcar[:, l, c, :], [f"convcar{l}_{c}"], ["xrbuf"])
        for tt in range(NTT):
            ai = nxt("acc", 2)
            fm_mm(acc[ai], ai, lambda kc: wv[:, kc, 0, :], tt, wt)
            copy_op("act", xrbuf[:, 3 + tt * 512:3 + (tt + 1) * 512], acc[ai][:], [f"acc{ai}"], ["xrbuf"])
            yield
            ai = nxt("acc", 2)
            fm_mm(acc[ai], ai, lambda kc: wv[:, kc, 1, :], tt, wt)
            act_fn(szc[:, tt * 512:(tt + 1) * 512], acc[ai][:], AF.Silu, [f"acc{ai}"], ["szc"])
            yield
        P.op("dve", lambda e: e.tensor_scalar(xc[:], xrbuf[:, 0:NT], prm[:, l, 8 + c:9 + c], prm[:, l, 4 + c:5 + c], ALU.mult, ALU.add),
             reads=["xrbuf"] + pl, writes=["xc"])
        yield
        for tap in range(1, 4):
            P.op("dve", lambda e, tap=tap: e.scalar_tensor_tensor(xc[:], xrbuf[:, tap:tap + NT], prm[:, l, 8 + 4 * tap + c:9 + 4 * tap + c],
                                                                  xc[:], ALU.mult, ALU.add),
                 reads=["xrbuf", "xc"], writes=["xc"])
            yield
        copy_op("act", xcb[:], xc[:], ["xc"], ["xcb"])
        yield
        for tt in range(NTT):
            sl = slice(tt * 512, (tt + 1) * 512)
            ai = nxt("acc", 2)
            mm_group(acc[ai][:], f"acc{ai}", [(WgA[:, l, c, :], xcb[:, sl])], ["xcb"] + WgTok[l])
            act_fn(bA[:, sl], acc[ai][:], AF.Sigmoid, [f"acc{ai}", "bA"] + pl, ["bA"], bias=prm[:, l, 24 + c:25 + c])
            ai = nxt("acc", 2)
            mm_group(acc[ai][:], f"acc{ai}", [(WgX[:, l, c, :], xcb[:, sl])], ["xcb"] + WgTok[l])
            act_fn(bC[:, sl], acc[ai][:], AF.Sigmoid, [f"acc{ai}", "bC"] + pl, ["bC"], bias=prm[:, l, 28 + c:29 + c])
            yield
        act_fn(bB[:], bA[:], AF.Exp, ["bA"], ["bB"], scale=prm[:, l, 56 + c:57 + c])
        act_fn(bA[:], bA[:], AF.Exp, ["bA", "bB"], ["bA"], scale=prm[:, l, 36 + c:37 + c])
        yield
        act_fn(bB[:], bB[:], AF.Sqrt, ["bB"], ["bB"], scale=-1.0, bias=onesf[:, 0:1])
        yield
        P.op("dve", lambda e: e.tensor_tensor(bC[:], bC[:], bB[:], ALU.mult), reads=["bB", "bC"], writes=["bC"])
        yield
        P.op("dve", lambda e: e.tensor_tensor(bC[:], bC[:], xc[:], ALU.mult), reads=["xc", "bC"], writes=["bC"])
        yield
        P.op("dve", lambda e: e.tensor_tensor_scan(bB[:], bA[:], bC[:], hcar[:, l, c:c + 1], ALU.mult, ALU.add),
             reads=["bA", "bC", f"hcar{l}_{c}", "bB"], writes=["bB"])
        yield
        copy_op("dve", hcar[:, l, c:c + 1], bB[:, NT - 1:NT], ["bB"], [f"hcar{l}_{c}"])
        P.op("dve", lambda e: e.tensor_tensor(yg[2][:, c, :], bB[:], szc[:, 0:NT], ALU.mult),
             reads=["bB", "szc"], writes=["yg2_0", "yg2_1"])
        copy_op("dve", convcar[:, l, c, :], xrbuf[:, NT:NT + 3], ["xrbuf"], [f"convcar{l}_{c}"])
        if last:
            P.dma("sp", lambda e: e.dma_start(out=co[l, b, :, c * 128:(c + 1) * 128].rearrange("t p -> p t"), in_=convcar[:, l, c, :]),
                  reads=[f"convcar{l}_{c}"])
            P.dma("sp", lambda e: e.dma_start(out=lo[l, b, c * 128:(c + 1) * 128].rearrange("(p o) -> p o", o=1), in_=hcar[:, l, c:c + 1]),
                  reads=[f"hcar{l}_{c}"])

    def interleave(*gens):
        gens = list(gens)
        while gens:
            for g_ in list(gens):
                try:
                    next(g_)
                except StopIteration:
                    gens.remove(g_)

    def d1_job(l, mc, wi):
        wt = wtoks(wi, 2)
        wg = wb[wi][:, 0:3072].rearrange("p (kc g c) -> p kc g c", kc=8, g=3)
        wbr = wb[wi][:, 3072:4608].rearrange("p (kc g c) -> p kc g c", kc=4, g=3)
        for tt in range(CFG["ntt"]):
            for br in range(3):
                ai = nxt("acc", 2)
                fm_mm(acc[ai], ai, lambda kc, br=br: wg[:, kc, br, :], tt, wt)
                act_fn(cw(gbuf[br]), cw(acc[ai]), AF.Sigmoid, [f"acc{ai}"], [f"gbuf{br}"])
                ai = nxt("acc", 2)
                fm_mm(acc[ai], ai, lambda kc, br=br: wbr[:, kc, br, :], tt, wt, nkc=4, rhs_src=yg[br], rhs_toks=[f"yg{br}_{tt}"])
                P.op("dve", lambda e, o_=cw(tb3[br]), a_=cw(acc[ai]), g_=cw(gbuf[br]): e.tensor_tensor(o_, a_, g_, ALU.mult),
                     reads=[f"acc{ai}", f"gbuf{br}"], writes=[f"tb3_{br}"])
            P.op("dve", lambda e, a_=cw(tb3[0]), b_=cw(tb3[1]): e.tensor_tensor(a_, a_, b_, ALU.add), reads=["tb3_0", "tb3_1"], writes=["tb3_0"])
            P.op("dve", lambda e, o_=merged[:, mc, csl(tt)], a_=cw(tb3[0]), b_=cw(tb3[2]): e.tensor_tensor(o_, a_, b_, ALU.add),
                 reads=["tb3_0", "tb3_2"], writes=[f"mg{mc}_{tt}"])

    def d2_job(l, rc, wi):
        wt = wtoks(wi, 1)
        wo = wb[wi][:, 0:1024].rearrange("p (kc c) -> p kc c", kc=8)
        for tt in range(CFG["ntt"]):
            ai = nxt("acc", 2)
            fm_mm(acc[ai], ai, lambda kc: wo[:, kc, :], tt, wt, rhs_src=merged, rhs_toks=[f"mg{k}_{tt}" for k in range(8)])
            sl = csl(tt)
            P.op("dve", lambda e, a_=cw(acc[ai]), sl=sl: e.scalar_tensor_tensor(xT[:, rc, sl], xT[:, rc, sl], ALPHA, a_, ALU.mult, ALU.add),
                 reads=[f"acc{ai}"], writes=[f"xT{rc}_{tt}"])
            copy_op("act", xB[:, rc, sl], xT[:, rc, sl], [f"xT{rc}_{tt}"], [f"xB{rc}_{tt}"])

    def d3_job(l, rc, wi):
        wt = wtoks(wi, 2)
        wpg = wb[wi][:, 0:1024].rearrange("p (kc c) -> p kc c", kc=8)
        wpl = wb[wi][:, 1024:1280].rearrange("p (kc c) -> p kc c", kc=2)
        for tt in range(CFG["ntt"]):
            sl = csl(tt)
            ai = nxt("acc", 2)
            fm_mm(acc[ai], ai, lambda kc: wpg[:, kc, :], tt, wt)
            act_fn(cw(gbuf[0]), cw(acc[ai]), AF.Sigmoid, [f"acc{ai}"], ["gbuf0"])
            ai = nxt("acc", 2)
            fm_mm(acc[ai], ai, lambda kc: wpl[:, kc, :], tt, wt, nkc=2, rhs_src=pT, rhs_toks=[f"pT_{tt}"])
            P.op("dve", lambda e, o_=cw(tb3[0]), a_=cw(acc[ai]), g_=cw(gbuf[0]): e.tensor_tensor(o_, a_, g_, ALU.mult), reads=[f"acc{ai}", "gbuf0"], writes=["tb3_0"])
            P.op("dve", lambda e, sl=sl, t_=cw(tb3[0]): e.tensor_tensor(xT[:, rc, sl], xT[:, rc, sl], t_, ALU.add), reads=["tb3_0"], writes=[f"xT{rc}_{tt}"])

    def ln_phase(l, b, hf, write_y):
        pl = prm_all(l)
        for tb in range(NB):
            tt = tb // 4
            bl = slice(tb * 128, (tb + 1) * 128)
            ai = nxt("acc", 2)
            pa = acc[ai]

            def fn(e, bl=bl, pa=pa):
                ins = None
                for rc in range(8):
                    e.matmul(pa[:, 0:128], xT[:, rc, bl], xT[:, rc, bl], start=(rc == 0), stop=(rc == 7))
                for rc in range(8):
                    ins = e.matmul(pa[:, 128:130], xT[:, rc, bl], onesf[:, 0:2], start=(rc == 0), stop=(rc == 7))
                return ins
            P.op("pe", fn, reads=xf_toks(tt) + ["onesf"], writes=[f"acc{ai}"])
            P.op("dve", lambda e, pa=pa: e.tensor_tensor(lntmp[:], pa[:, 0:128], identf[:], ALU.mult), reads=[f"acc{ai}", "identf"], writes=["lntmp"])
            P.op("dve", lambda e, tb=tb: e.reduce_sum(lnst[:, NB + tb:NB + tb + 1], lntmp[:], AX.X), reads=["lntmp", "lnst"], writes=["lnst"])
            copy_op("dve", lnst[:, tb:tb + 1], pa[:, 128:129], [f"acc{ai}", "lnst"], ["lnst"])
        s = lnst
        A0, A1, A2, A3, A4, A5 = [slice(k * NB, (k + 1) * NB) for k in range(6)]
        P.op("dve", lambda e: e.tensor_scalar(s[:, A0], s[:, A0], 1.0 / D, None, ALU.mult), reads=["lnst"], writes=["lnst"])
        P.op("dve", lambda e: e.tensor_tensor(s[:, A2], s[:, A0], s[:, A0], ALU.mult), reads=["lnst"], writes=["lnst"])
        P.op("dve", lambda e: e.scalar_tensor_tensor(s[:, A3], s[:, A1], 1.0 / D, s[:, A2], ALU.mult, ALU.subtract), reads=["lnst"], writes=["lnst"])
        P.op("dve", lambda e: e.tensor_scalar(s[:, A3], s[:, A3], LN_EPS, None, ALU.add), reads=["lnst"], writes=["lnst"])
        P.op("pool", lambda e: e.tensor_tensor(s[:, A4], s[:, A3], mhalf[:, 0:NB], ALU.pow), reads=["lnst", "mhalf"], writes=["lnst"])
        P.op("dve", lambda e: e.scalar_tensor_tensor(s[:, A5], s[:, A0], -1.0, s[:, A4], ALU.mult, ALU.mult), reads=["lnst"], writes=["lnst"])
        dAB = [(dA, dB), (lntmp, lnt2)]
        for tb in range(NB):
            tt = tb // 4
            bl = slice(tb * 128, (tb + 1) * 128)
            ai = nxt("acc", 2)
            bcv = acc[ai][:, 0:256].rearrange("p (a t) -> p a t", a=2)
            dA_, dB_ = dAB[tb % 2]
            tkA, tkB = ("dA", "dB") if tb % 2 == 0 else ("lntmp", "lnt2")
            P.op("dve", lambda e, tb=tb, dA_=dA_: e.tensor_scalar(dA_[:], identf[:], s[:, 4 * NB + tb:4 * NB + tb + 1], None, ALU.mult), reads=["lnst", "identf"], writes=[tkA])
            P.op("dve", lambda e, tb=tb, dB_=dB_: e.tensor_scalar(dB_[:], identf[:], s[:, 5 * NB + tb:5 * NB + tb + 1], None, ALU.mult), reads=["lnst", "identf"], writes=[tkB])

            def fn(e, bcv=bcv, dA_=dA_, dB_=dB_):
                e.matmul(bcv[:, 0, :], onesf[:], dA_[:], start=True, stop=True)
                return e.matmul(bcv[:, 1, :], onesf[:], dB_[:], start=True, stop=True)
            P.op("pe", fn, reads=[tkA, tkB, "onesf"], writes=[f"acc{ai}"])
            P.op("dve", lambda e, bl=bl, bcv=bcv: e.tensor_tensor(lnt[:], xT[:, :, bl], bc(bcv[:, 0:1, :], [128, 8, 128]), ALU.mult),
                 reads=[f"acc{ai}"] + xf_toks(tt), writes=["lnt"])
            P.op("dve", lambda e, bcv=bcv: e.tensor_tensor(lnt[:], lnt[:], bc(bcv[:, 1:2, :], [128, 8, 128]), ALU.add), reads=[f"acc{ai}", "lnt"], writes=["lnt"])
            P.op("dve", lambda e: e.tensor_tensor(lnt[:], lnt[:], bc(prm[:, l, 40:48].rearrange("p (c o) -> p c o", o=1), [128, 8, 128]), ALU.mult),
                 reads=["lnt"] + pl, writes=["lnt"])
            P.op("dve", lambda e, bl=bl: e.tensor_tensor(xT[:, :, bl], lnt[:], bc(prm[:, l, 48:56].rearrange("p (c o) -> p c o", o=1), [128, 8, 128]), ALU.add),
                 reads=["lnt"] + pl, writes=xf_toks(tt))
            P.op("act", lambda e, bl=bl: e.copy(xB[:, :, bl], xT[:, :, bl]), reads=xf_toks(tt), writes=xb_toks(tt))
            if write_y:
                yi_ = nxt("yst", 2)
                for hb in range(2):
                    ai = nxt("acc", 2)
                    av = acc[ai][:].rearrange("p (c t) -> p c t", c=4)

                    def fnT(e, bl=bl, av=av, hb=hb):
                        ins = None
                        for c in range(4):
                            ins = e.transpose(av[:, c, :], xT[:, hb * 4 + c, bl], identf[:])
                        return ins
                    P.op("pe", fnT, reads=xf_toks(tt) + ["identf"], writes=[f"acc{ai}"])
                    copy_op("act" if hb == 0 else "dve", yst[yi_][:, hb * 512:(hb + 1) * 512], acc[ai][:], [f"acc{ai}", f"yst{yi_}"], [f"yst{yi_}"])
                r0 = hf * NT + tb * 128
                P.dma("sp", lambda e, yi_=yi_, r0=r0: e.dma_start(out=yo[b, r0:r0 + 128, :], in_=yst[yi_][:]), reads=[f"yst{yi_}"])

    def s_load(l):
        if l == 0:
            P.dma("sp", lambda e: e.dma_start(out=xin[0][0:64, :], in_=xs_d.rearrange("s t d -> (s t) d")), writes=["xin0"])
            for hb in range(2):
                sv = sm0[:, 0:256].rearrange("p (c t) -> p c t", c=4)

                def fn(e, sv=sv, hb=hb):
                    ins = None
                    for c in range(4):
                        cg = hb * 4 + c
                        ins = e.transpose(sv[:, c, :], xin[0][0:64, cg * 128:(cg + 1) * 128], identf[0:64, 0:64])
                    return ins
                P.op("pe", fn, reads=["xin0", "identf"], writes=["sm0"])
                cs = slice(hb * 4, hb * 4 + 4)
                P.op("act", lambda e, sv=sv, cs=cs: e.copy(xT[:, cs, 0:64], sv), reads=["sm0"], writes=xf_toks(0))
                P.op("act", lambda e, sv=sv, cs=cs: e.copy(xB[:, cs, 0:64], sv), reads=["sm0"], writes=xb_toks(0))
        P.dma("sp", lambda e: e.dma_start(out=pin[0][0:64, :], in_=ps_d[l].rearrange("s t d -> (s t) d")), writes=["pin0"])
        sv2 = sm0[:, 0:128].rearrange("p (c t) -> p c t", c=2)

        def fn2(e):
            ins = None
            for c in range(2):
                ins = e.transpose(sv2[:, c, :], pin[0][0:64, c * 128:(c + 1) * 128], identf[0:64, 0:64])
            return ins
        P.op("pe", fn2, reads=["pin0", "identf"], writes=["sm0"])
        copy_op("act", pT[:, :, 0:64], sv2, ["sm0"], ["pT_0"])

    def s_att_job(l, j, wi):
        wt = wtoks(wi, 1)
        wv = wb[wi][:, 0:4096].rearrange("p (kc g c) -> p kc g c", kc=8, g=4)
        P.op("dve", lambda e: e.memset(Vv[:, :, :, 64:65], 1.0), writes=["Vv"])
        ai = nxt("acc", 2)
        fm_mm(acc[ai], ai, lambda kc: wv[:, kc, 0, :], 0, wt)
        copy_op("act", QT[:, 0:64], acc[ai][:, 0:64], [f"acc{ai}"], ["QT"], scale=0.125)
        ai = nxt("acc", 2)
        fm_mm(acc[ai], ai, lambda kc: wv[:, kc, 1, :], 0, wt)
        copy_op("dve", KT[:, 512:576], acc[ai][:, 0:64], [f"acc{ai}"], ["KTn"])
        ai = nxt("acc", 2)
        fm_mm(acc[ai], ai, lambda kc: wv[:, kc, 3, :], 0, wt)
        act_fn(sz[:, 0:64], acc[ai][:, 0:64], AF.Silu, [f"acc{ai}"], ["sz"])
        Ov = sm0[:, 0:130].rearrange("p (h d) -> p h d", h=2)
        for s in range(NSMP):
            cs_ = slice(s * 32, (s + 1) * 32)
            P.dma("pool", lambda e, s=s: e.dma_start(out=kctm[:], in_=ck_d[l, s, :, 2 * j:2 * j + 2, :].rearrange("(a p) h d -> p a (h d)", p=128)),
                  writes=["kctm"])

            def fnk(e):
                ins = None
                for a in range(4):
                    ins = e.transpose(sm1[:, a * 128:(a + 1) * 128], kctm[:, a, :], identb[:])
                return ins
            P.op("pe", fnk, reads=["kctm", "identb"], writes=["sm1"])
            copy_op("act", KT[:, 0:512], sm1[:, 0:512], ["sm1"], ["KT"])
            for hh in range(2):
                P.dma("pool", lambda e, s=s, hh=hh: e.dma_start(out=Vv[:, 0:4, hh, 0:64],
                                                                 in_=cv_d[l, s, :, 2 * j + hh, :].rearrange("(a p) d -> p a d", p=128)),
                      reads=["Vv"], writes=[f"Vvc{hh}"])
            ai = nxt("acc", 2)
            pairs = [(xB[:, kc, cs_], wv[:, kc, 2, :]) for kc in range(8)]
            mm_group(acc[ai][0:32, 0:128], f"acc{ai}", pairs, xb_toks(0) + wt)
            copy_op("dve", Vv[0:32, 4, :, 0:64], acc[ai][0:32, 0:128].rearrange("p (h d) -> p h d", h=2), [f"acc{ai}", "Vv"], ["Vvn"])
            si = nxt("kvst", 2)
            copy_op("act", kvst[si][0:32, :], acc[ai][0:32, 0:128], [f"acc{ai}"], [f"kvst{si}"])
            P.dma("sp", lambda e, si=si, s=s: e.dma_start(out=vso[l, s, :, 2 * j:2 * j + 2, :], in_=kvst[si][0:32, :].rearrange("p (h d) -> p h d", h=2)),
                  reads=[f"kvst{si}"])
            ai = nxt("acc", 2)
            pairs = [(xB[:, kc, cs_], wv[:, kc, 1, :]) for kc in range(8)]
            mm_group(acc[ai][0:32, 0:128], f"acc{ai}", pairs, xb_toks(0) + wt)
            si = nxt("kvst", 2)
            copy_op("act", kvst[si][0:32, :], acc[ai][0:32, 0:128], [f"acc{ai}"], [f"kvst{si}"])
            P.dma("sp", lambda e, si=si, s=s: e.dma_start(out=kso[l, s, :, 2 * j:2 * j + 2, :], in_=kvst[si][0:32, :].rearrange("p (h d) -> p h d", h=2)),
                  reads=[f"kvst{si}"])
            for hh in range(2):
                h = 2 * j + hh
                pb = 64 * hh
                STv = big[hh][:, 0:640].rearrange("p (a q) -> p a q", a=5)

                def fn(e, STv=STv, pb=pb, s=s):
                    e.matmul(STv[0:32, 0, 0:32], KT[pb:pb + 64, 512 + s * 32:512 + (s + 1) * 32], QT[pb:pb + 64, s * 32:(s + 1) * 32], start=True, stop=True)
                    ins = None
                    for jp in range(1, 5):
                        kb = 4 - jp
                        ins = e.matmul(STv[:, jp, 0:32], KT[pb:pb + 64, kb * 128:(kb + 1) * 128], QT[pb:pb + 64, s * 32:(s + 1) * 32], start=True, stop=True)
                    return ins
                P.op("pe", fn, reads=["KT", "KTn", "QT"], writes=[f"big{hh}"])
                act_fn(Eb[hh][0:32, 0, 0:32], STv[0:32, 0, 0:32], AF.Exp, [f"big{hh}"], [f"Eb{hh}"])
                act_fn(Eb[hh][:, 1:4, 0:32], STv[:, 1:4, 0:32], AF.Exp, [f"big{hh}", f"Eb{hh}"], [f"Eb{hh}"])
                act_fn(Eb[hh][:, 4:5, 0:32], STv[:, 4:5, 0:32], AF.Exp, [f"big{hh}", f"Eb{hh}"], [f"Eb{hh}"])
                P.op("dve", lambda e, hh=hh, h=h: e.tensor_tensor(PTb[hh][0:32, 0, 0:32], Eb[hh][0:32, 0, 0:32], EB[0:32, h, 0, 0:32], ALU.mult),
                     reads=[f"Eb{hh}", "EB"], writes=[f"PT{hh}"])
                P.op("dve", lambda e, hh=hh, h=h: e.tensor_tensor(PTb[hh][:, 1:5, 0:32], Eb[hh][:, 1:5, 0:32], EB[:, h, 1:5, 0:32], ALU.mult),
                     reads=[f"Eb{hh}", "EB", f"PT{hh}"], writes=[f"PT{hh}"])

                def fn2(e, hh=hh):
                    e.matmul(Ov[0:32, hh, :], PTb[hh][0:32, 0, 0:32], Vv[0:32, 4, hh, :], start=True, stop=False)
                    ins = None
                    for jp in range(1, 5):
                        ins = e.matmul(Ov[0:32, hh, :], PTb[hh][:, jp, 0:32], Vv[:, 4 - jp, hh, :], start=False, stop=(jp == 4))
                    return ins
                P.op("pe", fn2, reads=[f"PT{hh}", "Vv", "Vvc0", "Vvc1", "Vvn"], writes=["sm0"])
            P.op("dve", lambda e: e.reciprocal(rcp[0:32, :].rearrange("p (h o) -> p h o", o=1), Ov[0:32, :, 64:65]), reads=["sm0"], writes=["rcp"])
            P.op("dve", lambda e: e.tensor_tensor(ya[0:32, :].rearrange("p (h d) -> p h d", h=2), Ov[0:32, :, 0:64],
                                                  bc(rcp[0:32, :].rearrange("p (h o) -> p h o", o=1), [32, 2, 64]), ALU.mult),
                 reads=["sm0", "rcp"], writes=["ya"])
            P.op("pe", lambda e: e.transpose(sm1[:, 0:32], ya[0:32, :], identb[0:32, 0:32]), reads=["ya", "identb"], writes=["sm1"])
            P.op("dve", lambda e, cs_=cs_: e.tensor_tensor(yg[0][:, j, cs_], sm1[:, 0:32], sz[:, cs_], ALU.mult),
                 reads=["sm1", "sz"], writes=["yg0_0"])

    def s_ret_job(l, j, wi):
        wt = wtoks(wi, 1)
        wv = wb[wi][:, 0:4096].rearrange("p (kc g c) -> p kc g c", kc=8, g=4)
        R = slice(0, 32)
        ai = nxt("acc", 2)
        fm_mm(acc[ai], ai, lambda kc: wv[:, kc, 3, :], 0, wt)
        act_fn(sz[:, 0:64], acc[ai][:, 0:64], AF.Silu, [f"acc{ai}"], ["sz"])
        Ah = [big[0][0:32, 0:32], big[1][0:32, 0:32]]
        Yh = [sm0[0:32, 0:64], big[1][0:32, 512:576]]
        gblk = 8
        for s in range(NSMP):
            cs_ = slice(s * 32, (s + 1) * 32)
            P.dma("sp", lambda e, s=s: e.dma_start(out=S0f[:], in_=sr_d[l, s, 2 * j:2 * j + 2, :, :].rearrange("hh d e -> (hh d) e")), writes=["S0f"])
            copy_op("act", Sop[:, 0, :], S0f[:], ["S0f"], ["Sop"])
            ai = nxt("acc", 2)
            pairs = [(xB[:, kc, cs_], wv[:, kc, 0:3, :]) for kc in range(8)]
            mm_group(acc[ai][R, 0:384], f"acc{ai}", pairs, xb_toks(0) + wt)
            at = f"acc{ai}"
            for qi, (dst, sc_) in enumerate(((Qt, xi), (Kt, zi))):
                X = acc[ai][R, qi * 128:(qi + 1) * 128].rearrange("p (h two d) -> p h two d", h=2, two=2)
                ccv = bc(cc[R, gblk, :].rearrange("p (o d) -> p o d", o=1), [32, 2, 64])
                t1v = rt1[R, :].rearrange("p (h d) -> p h d", h=2)
                t2v = rt2[R, :].rearrange("p (h two d) -> p h two d", h=2, two=2)
                P.op("dve", lambda e, ai=ai, qi=qi, ccv=ccv, t1v=t1v: e.tensor_tensor(
                    t1v, acc[ai][R, qi * 128:(qi + 1) * 128].rearrange("p (h d) -> p h d", h=2), ccv, ALU.mult),
                    reads=[at, "cc"], writes=["rt1"])
                for hv in range(2):
                    ssv = bc(ss[R, gblk, hv * 32:(hv + 1) * 32].rearrange("p (o d) -> p o d", o=1), [32, 2, 32])
                    P.op("dve", lambda e, X=X, hv=hv, ssv=ssv, t2v=t2v: e.tensor_tensor(t2v[:, :, hv, :], X[:, :, 1 - hv, :], ssv, ALU.mult),
                         reads=[at, "ss"], writes=["rt2"])
                P.op("dve", lambda e: e.tensor_tensor(rt1[R, :], rt1[R, :], rt2[R, :], ALU.add), reads=["rt1", "rt2"], writes=["rt1"])
                scv = bc(sc_[R, 2 * j:2 * j + 2].rearrange("p (h o) -> p h o", o=1), [32, 2, 64])
                P.op("dve", lambda e, dst=dst, scv=scv, t1v=t1v: e.tensor_tensor(dst[R, :].rearrange("p (h d) -> p h d", h=2), t1v, scv, ALU.mult),
                     reads=["rt1", "xi", "zi"], writes=["Qt" if qi == 0 else "Kt"])
            copy_op("dve", Vb[R, 0, :], acc[ai][R, 256:384], [at], ["Vb"])
            P.op("pe", lambda e: e.transpose(sm1[:, 0:32], Qt[R, :], identb[0:32, 0:32]), reads=["Qt", "identb"], writes=["sm1"])
            copy_op("act", QTr[:, cs_], sm1[:, 0:32], ["sm1"], ["QTr"])
            P.op("pe", lambda e: e.transpose(sm1[:, 128:160], Kt[R, :], identb[0:32, 0:32]), reads=["Kt", "identb"], writes=["sm1"])
            copy_op("dve", KTr[:, cs_], sm1[:, 128:160], ["sm1"], ["KTr"])
            P.op("pe", lambda e: e.matmul(sm0[:, 0:128], Kt[R, :], Vb[R, 0, :], start=True, stop=True), reads=["Kt", "Vb"], writes=["sm0"])
            for hh in range(2):
                rr_ = slice(hh * 64, (hh + 1) * 64)
                P.op("dve", lambda e, hh=hh, rr_=rr_: e.tensor_tensor(Snew[rr_, :], sm0[rr_, hh * 64:(hh + 1) * 64], S0f[rr_, :], ALU.add),
                     reads=["sm0", "S0f", "Snew"], writes=["Snew"])
                P.op("dve", lambda e, rr_=rr_: e.tensor_tensor(Snew[rr_, :], Snew[rr_, :], gt32[rr_, j, :], ALU.mult),
                     reads=["Snew", "gt32"], writes=["Snew"])
            P.dma("sp", lambda e, s=s: e.dma_start(out=rso[l, s, 2 * j:2 * j + 2, :, :].rearrange("hh d e -> (hh d) e"), in_=Snew[:]), reads=["Snew"])

            def fnA(e, cs_=cs_):
                ins = None
                for hh in range(2):
                    pb = 64 * hh
                    ins = e.matmul(Ah[hh], KTr[pb:pb + 64, cs_], QTr[pb:pb + 64, cs_], start=True, stop=True)
                return ins
            P.op("pe", fnA, reads=["KTr", "QTr"], writes=["big0", "big1"])
            for hh in range(2):
                P.op("dve", lambda e, hh=hh: e.tensor_tensor(Am[R, hh, 0:32], Ah[hh], maskb[0:32, 0:32], ALU.mult),
                     reads=[f"big{hh}", "maskb"], writes=["Am"])

            def fnY(e, cs_=cs_):
                ins = None
                for hh in range(2):
                    pb = 64 * hh
                    e.matmul(Yh[hh], Am[R, hh, 0:32], Vb[R, 0, hh * 64:(hh + 1) * 64], start=True, stop=False)
                    ins = e.matmul(Yh[hh], QTr[pb:pb + 64, cs_], Sop[pb:pb + 64, 0, :], start=False, stop=True)
                return ins
            P.op("pe", fnY, reads=["Am", "Vb", "QTr", "Sop"], writes=["sm0", "big1"])
            for hh in range(2):
                tk = "sm0" if hh == 0 else "big1"
                copy_op("act", ysb[R, hh * 64:(hh + 1) * 64], Yh[hh], [tk, "ysb"], ["ysb"])
                P.op("act", lambda e, hh=hh: e.activation(ysq[R, hh * 64:(hh + 1) * 64], Yh[hh], AF.Square), reads=[tk, "ysq"], writes=["ysq"])
            g = gst
            yv3 = ysb[R, :].rearrange("p (h d) -> p h d", h=2)
            P.op("dve", lambda e: e.reduce_sum(g[R, 0:2], yv3, AX.X), reads=["ysb"], writes=["gst"])
            P.op("dve", lambda e: e.reduce_sum(g[R, 2:4], ysq[R, :].rearrange("p (h d) -> p h d", h=2), AX.X), reads=["ysq", "gst"], writes=["gst"])
            P.op("dve", lambda e: e.tensor_scalar(g[R, 0:2], g[R, 0:2], 1.0 / 64, None, ALU.mult), reads=["gst"], writes=["gst"])
            P.op("dve", lambda e: e.tensor_tensor(g[R, 4:6], g[R, 0:2], g[R, 0:2], ALU.mult), reads=["gst"], writes=["gst"])
            P.op("dve", lambda e: e.scalar_tensor_tensor(g[R, 6:8], g[R, 2:4], 1.0 / 64, g[R, 4:6], ALU.mult, ALU.subtract), reads=["gst"], writes=["gst"])
            P.op("dve", lambda e: e.tensor_scalar(g[R, 6:8], g[R, 6:8], LN_EPS, None, ALU.add), reads=["gst"], writes=["gst"])
            P.op("pool", lambda e: e.tensor_tensor(g[R, 8:10], g[R, 6:8], mhalf[R, 0:2], ALU.pow), reads=["gst", "mhalf"], writes=["gst"])
            P.op("dve", lambda e: e.tensor_tensor(yv3, yv3, bc(g[R, 0:2].rearrange("p (h o) -> p h o", o=1), [32, 2, 64]), ALU.subtract),
                 reads=["gst", "ysb"], writes=["ysb"])
            P.op("dve", lambda e: e.tensor_tensor(ynb[R, :].rearrange("p (h d) -> p h d", h=2), yv3,
                                                  bc(g[R, 8:10].rearrange("p (h o) -> p h o", o=1), [32, 2, 64]), ALU.mult),
                 reads=["gst", "ysb"], writes=["ynb"])
            P.op("pe", lambda e: e.transpose(sm1[:, 0:32], ynb[R, :], identb[0:32, 0:32]), reads=["ynb", "identb"], writes=["sm1"])
            P.op("dve", lambda e, cs_=cs_: e.scalar_tensor_tensor(yg[1][:, j, cs_], sm1[:, 0:32], prm[:, l, j:j + 1], sz[:, cs_], ALU.mult, ALU.mult),
                 reads=["sm1", "sz"] + prm_all(l), writes=["yg1_0"])

    def s_lru_job(l, c, wi):
        wt = wtoks(wi, 1)
        wv = wb[wi][:, 0:2048].rearrange("p (kc g c) -> p kc g c", kc=8, g=2)
        pl = prm_all(l)
        ai = nxt("acc", 2)
        fm_mm(acc[ai], ai, lambda kc: wv[:, kc, 1, :], 0, wt)
        act_fn(sz[:, 0:64], acc[ai][:, 0:64], AF.Silu, [f"acc{ai}"], ["sz"])
        Wd = slice(0, 32)
        for s in range(NSMP):
            cs_ = slice(s * 32, (s + 1) * 32)
            P.dma("sp", lambda e, s=s: e.dma_start(out=xrbuf[:, 0:3], in_=sc_d[l, s, :, c * 128:(c + 1) * 128].rearrange("t p -> p t")), writes=["xrbuf"])
            P.dma("sp", lambda e, s=s: e.dma_start(out=hcar[:, l, c:c + 1], in_=sl_d[l, s, c * 128:(c + 1) * 128].rearrange("(p o) -> p o", o=1)),
                  writes=[f"hcar{l}_{c}"])
            ai = nxt("acc", 2)
            pairs = [(wv[:, kc, 0, :], xB[:, kc, cs_]) for kc in range(8)]
            mm_group(acc[ai][:, 0:32], f"acc{ai}", pairs, xb_toks(0) + wt)
            copy_op("act", xrbuf[:, 3:35], acc[ai][:, 0:32], [f"acc{ai}", "xrbuf"], ["xrbuf"])
            P.op("dve", lambda e: e.tensor_scalar(xc[:, Wd], xrbuf[:, 0:32], prm[:, l, 8 + c:9 + c], prm[:, l, 4 + c:5 + c], ALU.mult, ALU.add),
                 reads=["xrbuf"] + pl, writes=["xc"])
            for tap in range(1, 4):
                P.op("dve", lambda e, tap=tap: e.scalar_tensor_tensor(xc[:, Wd], xrbuf[:, tap:tap + 32], prm[:, l, 8 + 4 * tap + c:9 + 4 * tap + c],
                                                                      xc[:, Wd], ALU.mult, ALU.add),
                     reads=["xrbuf", "xc"], writes=["xc"])
            copy_op("act", xcb[:, Wd], xc[:, Wd], ["xc"], ["xcb"])
            ai = nxt("acc", 2)
            mm_group(acc[ai][:, 0:32], f"acc{ai}", [(WgA[:, l, c, :], xcb[:, Wd])], ["xcb"] + WgTok[l])
            act_fn(bA[:, Wd], acc[ai][:, 0:32], AF.Sigmoid, [f"acc{ai}"] + pl, ["bA"], bias=prm[:, l, 24 + c:25 + c])
            ai = nxt("acc", 2)
            mm_group(acc[ai][:, 0:32], f"acc{ai}", [(WgX[:, l, c, :], xcb[:, Wd])], ["xcb"] + WgTok[l])
            act_fn(bC[:, Wd], acc[ai][:, 0:32], AF.Sigmoid, [f"acc{ai}"] + pl, ["bC"], bias=prm[:, l, 28 + c:29 + c])
            act_fn(bB[:, Wd], bA[:, Wd], AF.Exp, ["bA"], ["bB"], scale=prm[:, l, 56 + c:57 + c])
            act_fn(bA[:, Wd], bA[:, Wd], AF.Exp, ["bA", "bB"], ["bA"], scale=prm[:, l, 36 + c:37 + c])
            act_fn(bB[:, Wd], bB[:, Wd], AF.Sqrt, ["bB"], ["bB"], scale=-1.0, bias=onesf[:, 0:1])
            P.op("dve", lambda e: e.tensor_tensor(bC[:, Wd], bC[:, Wd], bB[:, Wd], ALU.mult), reads=["bB", "bC"], writes=["bC"])
            P.op("dve", lambda e: e.tensor_tensor(bC[:, Wd], bC[:, Wd], xc[:, Wd], ALU.mult), reads=["xc", "bC"], writes=["bC"])
            P.op("dve", lambda e: e.tensor_tensor_scan(bB[:, Wd], bA[:, Wd], bC[:, Wd], hcar[:, l, c:c + 1], ALU.mult, ALU.add),
                 reads=["bA", "bC", f"hcar{l}_{c}", "bB"], writes=["bB"])
            P.dma("sp", lambda e, s=s: e.dma_start(out=lso[l, s, c * 128:(c + 1) * 128].rearrange("(p o) -> p o", o=1), in_=bB[:, 31:32]), reads=["bB"])
            P.dma("sp", lambda e, s=s: e.dma_start(out=cso[l, s, :, c * 128:(c + 1) * 128].rearrange("t p -> p t"), in_=xrbuf[:, 32:35]), reads=["xrbuf"])
            P.op("dve", lambda e, cs_=cs_: e.tensor_tensor(yg[2][:, c, cs_], bB[:, Wd], sz[:, cs_], ALU.mult),
                 reads=["bB", "sz"], writes=["yg2_0"])

    def s_ln(l, write_y):
        pl = prm_all(l)
        R = slice(0, 64)
        bl = slice(0, 64)

        def fn(e):
            ins = None
            for rc in range(8):
                e.matmul(sm0[R, 0:64], xT[:, rc, bl], xT[:, rc, bl], start=(rc == 0), stop=(rc == 7))
            for rc in range(8):
                ins = e.matmul(sm0[R, 128:130], xT[:, rc, bl], onesf[:, 0:2], start=(rc == 0), stop=(rc == 7))
            return ins
        P.op("pe", fn, reads=xf_toks(0) + ["onesf"], writes=["sm0"])
        s_ = lnst
        P.op("dve", lambda e: e.tensor_tensor(lntmp[R, 0:64], sm0[R, 0:64], identf[R, 0:64], ALU.mult), reads=["sm0", "identf"], writes=["lntmp"])
        P.op("dve", lambda e: e.reduce_sum(s_[R, 1:2], lntmp[R, 0:64], AX.X), reads=["lntmp", "lnst"], writes=["lnst"])
        copy_op("dve", s_[R, 0:1], sm0[R, 128:129], ["sm0", "lnst"], ["lnst"])
        P.op("dve", lambda e: e.tensor_scalar(s_[R, 0:1], s_[R, 0:1], 1.0 / D, None, ALU.mult), reads=["lnst"], writes=["lnst"])
        P.op("dve", lambda e: e.tensor_tensor(s_[R, 2:3], s_[R, 0:1], s_[R, 0:1], ALU.mult), reads=["lnst"], writes=["lnst"])
        P.op("dve", lambda e: e.scalar_tensor_tensor(s_[R, 3:4], s_[R, 1:2], 1.0 / D, s_[R, 2:3], ALU.mult, ALU.subtract), reads=["lnst"], writes=["lnst"])
        P.op("dve", lambda e: e.tensor_scalar(s_[R, 3:4], s_[R, 3:4], LN_EPS, None, ALU.add), reads=["lnst"], writes=["lnst"])
        P.op("pool", lambda e: e.tensor_tensor(s_[R, 4:5], s_[R, 3:4], mhalf[R, 0:1], ALU.pow), reads=["lnst", "mhalf"], writes=["lnst"])
        P.op("dve", lambda e: e.scalar_tensor_tensor(s_[R, 5:6], s_[R, 0:1], -1.0, s_[R, 4:5], ALU.mult, ALU.mult), reads=["lnst"], writes=["lnst"])
        P.op("dve", lambda e: e.tensor_scalar(dA[R, 0:64], identf[R, 0:64], s_[R, 4:5], None, ALU.mult), reads=["lnst", "identf"], writes=["dA"])
        P.op("dve", lambda e: e.tensor_scalar(dB[R, 0:64], identf[R, 0:64], s_[R, 5:6], None, ALU.mult), reads=["lnst", "identf"], writes=["dB"])
        bcv = sm0[:, 0:128].rearrange("p (a t) -> p a t", a=2)

        def fnb(e):
            e.matmul(bcv[:, 0, :], onesf[R, :], dA[R, 0:64], start=True, stop=True)
            return e.matmul(bcv[:, 1, :], onesf[R, :], dB[R, 0:64], start=True, stop=True)
        P.op("pe", fnb, reads=["dA", "dB", "onesf"], writes=["sm0"])
        lv = lnt[:, :, 0:64]
        P.op("dve", lambda e: e.tensor_tensor(lv, xT[:, :, bl], bc(bcv[:, 0:1, :], [128, 8, 64]), ALU.mult), reads=["sm0"] + xf_toks(0), writes=["lnt"])
        P.op("dve", lambda e: e.tensor_tensor(lv, lv, bc(bcv[:, 1:2, :], [128, 8, 64]), ALU.add), reads=["sm0", "lnt"], writes=["lnt"])
        P.op("dve", lambda e: e.tensor_tensor(lv, lv, bc(prm[:, l, 40:48].rearrange("p (c o) -> p c o", o=1), [128, 8, 64]), ALU.mult),
             reads=["lnt"] + pl, writes=["lnt"])
        P.op("dve", lambda e: e.tensor_tensor(xT[:, :, bl], lv, bc(prm[:, l, 48:56].rearrange("p (c o) -> p c o", o=1), [128, 8, 64]), ALU.add),
             reads=["lnt"] + pl, writes=xf_toks(0))
        P.op("act", lambda e: e.copy(xB[:, :, bl], xT[:, :, bl]), reads=xf_toks(0), writes=xb_toks(0))
        if write_y:
            for hb in range(2):
                ai = nxt("acc", 2)
                av = acc[ai][R, :].rearrange("p (c t) -> p c t", c=4)

                def fnT(e, av=av, hb=hb):
                    ins = None
                    for c in range(4):
                        ins = e.transpose(av[:, c, :], xT[:, hb * 4 + c, bl], identf[:])
                    return ins
                P.op("pe", fnT, reads=xf_toks(0) + ["identf"], writes=[f"acc{ai}"])
                copy_op("act" if hb == 0 else "dve", yst[0][R, hb * 512:(hb + 1) * 512], acc[ai][R, :], [f"acc{ai}", "yst0"], ["yst0"])
            P.dma("sp", lambda e: e.dma_start(out=yso.rearrange("s t d -> (s t) d"), in_=yst[0][R, :]), reads=["yst0"])

    def wsrc_att(l, j):
        v = w_in[l].rearrange("(kc p) (g jj c) -> p kc g jj c", p=128, g=16, jj=4)
        return v[:, :, 0:4, j, :]

    def wsrc_ret(l, j):
        v = w_in[l].rearrange("(kc p) (g jj c) -> p kc g jj c", p=128, g=16, jj=4)
        return v[:, :, 4:8, j, :]

    def wsrc_lru(l, c):
        v = w_in[l].rearrange("(kc p) (g jj c) -> p kc g jj c", p=128, g=16, jj=4)
        return v[:, :, 8:10, c, :]

    def wsrc_gate(l, mc):
        v = w_in[l].rearrange("(kc p) (g mm c) -> p kc g mm c", p=128, g=8, mm=8)
        return v[:, :, 5:8, mc, :]

    def wsrc_br(l, mc):
        v = di["w_branch"][l].rearrange("br (kc p) (mm c) -> p kc br mm c", p=128, mm=8)
        return v[:, :, :, mc, :]

    def wsrc_sq(name, l, rc):
        v = di[name][l].rearrange("(kc p) (rr c) -> p kc rr c", p=128, rr=8)
        return v[:, :, rc, :]

    import os as _os
    KSTOP = int(_os.environ.get("KSTOP", "99"))
    for b in range(NSEQ if (KSTOP > 0 and not _os.environ.get("KSKIPP")) else 0):
        for hf in range(NHF):
            last = (hf == NHF - 1)
            for l in range(L):
                jobs = []
                for j in range(4):
                    jobs.append((lambda wi, j=j, l=l: load_w(wi, [
                        (wb[wi][:, 0:4096].rearrange("p (kc g c) -> p kc g c", kc=8, g=4), wsrc_att(l, j)),
                        (wb[wi][:, 4096:6144].rearrange("p (kc g c) -> p kc g c", kc=8, g=2), wsrc_lru(l, j))]),
                        lambda wi, j=j, l=l: interleave(att_job(l, b, hf, j, wi, last), lru_job(l, b, hf, j, wi, last, woff=4096))))
                for j in range(4):
                    jobs.append((lambda wi, j=j, l=l: load_w(wi, [(wb[wi][:, 0:4096].rearrange("p (kc g c) -> p kc g c", kc=8, g=4), wsrc_ret(l, j))]),
                                 lambda wi, j=j, l=l: ret_job(l, b, hf, j, wis[4:8], last)))
                for mc in range(8):
                    jobs.append((lambda wi, mc=mc, l=l: load_w(wi, [
                        (wb[wi][:, 0:3072].rearrange("p (kc g c) -> p kc g c", kc=8, g=3), wsrc_gate(l, mc)),
                        (wb[wi][:, 3072:4608].rearrange("p (kc g c) -> p kc g c", kc=4, g=3), wsrc_br(l, mc))]),
                        lambda wi, mc=mc, l=l: d1_job(l, mc, wi)))
                for rc in range(8):
                    jobs.append((lambda wi, rc=rc, l=l: load_w(wi, [(wb[wi][:, 0:1024].rearrange("p (kc c) -> p kc c", kc=8), wsrc_sq("w_out", l, rc))]),
                                 lambda wi, rc=rc, l=l: d2_job(l, rc, wi)))
                for rc in range(8):
                    jobs.append((lambda wi, rc=rc, l=l: load_w(wi, [
                        (wb[wi][:, 0:1024].rearrange("p (kc c) -> p kc c", kc=8), wsrc_sq("w_ple_gate", l, rc)),
                        (wb[wi][:, 1024:1280].rearrange("p (kc c) -> p kc c", kc=2),
                         di["w_ple"][l].rearrange("(kc p) (rr c) -> p kc rr c", p=128, rr=8)[:, :, rc, :])]),
                        lambda wi, rc=rc, l=l: d3_job(l, rc, wi)))
                wis = [nxt("wb", 3) for _ in jobs]
                P.barrier()
                KPRE = int(_os.environ.get("KPRE", "15"))
                if KPRE & 8:
                    jobs[0][0](wis[0])
                if l == 0 and (KPRE & 1):
                    load_x(b, hf)
                if KPRE & 2:
                    load_p(l, b, hf)
                if hf == 0:
                    for j in range(4):
                        P.op("dve", lambda e, j=j, l=l: e.memset(Scar[:, l, j, :], 0.0), writes=[f"Scar{l}_{j}"])
                        P.op("dve", lambda e, j=j, l=l: e.memset(convcar[:, l, j, :], 0.0), writes=[f"convcar{l}_{j}"])
                        P.op("dve", lambda e, j=j, l=l: e.memset(hcar[:, l, j:j + 1], 0.0), writes=[f"hcar{l}_{j}"])
                if KPRE & 4:
                    build_EB(l)
                if KSTOP < 99 and (b, hf, l) != (0, 0, 0):
                    continue
                if KSTOP <= 1:
                    continue
                if len(jobs) > 1:
                    jobs[1][0](wis[1])
                for k, (ld, cp) in enumerate(jobs):
                    if KSTOP == 2 and k >= 4 or KSTOP == 3 and k >= 8 or KSTOP == 5 and k >= 16:
                        break
                    if k in (0, 4, 8):
                        P.barrier()
                    if k + 2 < len(jobs):
                        jobs[k + 2][0](wis[k + 2])
                    cp(wis[k])
                if KSTOP >= 7:
                    ln_phase(l, b, hf, write_y=(l == L - 1))

    if NSMP > 0 and KSTOP >= 99:
        CFG["w"] = NSMP * 32
        CFG["ntt"] = 1
        for l in range(L):
            jobs = []
            for j in range(4):
                jobs.append((lambda wi, j=j, l=l: load_w(wi, [(wb[wi][:, 0:4096].rearrange("p (kc g c) -> p kc g c", kc=8, g=4), wsrc_att(l, j))]),
                             lambda wi, j=j, l=l: s_att_job(l, j, wi)))
            for j in range(4):
                jobs.append((lambda wi, j=j, l=l: load_w(wi, [(wb[wi][:, 0:4096].rearrange("p (kc g c) -> p kc g c", kc=8, g=4), wsrc_ret(l, j))]),
                             lambda wi, j=j, l=l: s_ret_job(l, j, wi)))
            for c in range(4):
                jobs.append((lambda wi, c=c, l=l: load_w(wi, [(wb[wi][:, 0:2048].rearrange("p (kc g c) -> p kc g c", kc=8, g=2), wsrc_lru(l, c))]),
                             lambda wi, c=c, l=l: s_lru_job(l, c, wi)))
            for mc in range(8):
                jobs.append((lambda wi, mc=mc, l=l: load_w(wi, [
                    (wb[wi][:, 0:3072].rearrange("p (kc g c) -> p kc g c", kc=8, g=3), wsrc_gate(l, mc)),
                    (wb[wi][:, 3072:4608].rearrange("p (kc g c) -> p kc g c", kc=4, g=3), wsrc_br(l, mc))]),
                    lambda wi, mc=mc, l=l: d1_job(l, mc, wi)))
            for rc in range(8):
                jobs.append((lambda wi, rc=rc, l=l: load_w(wi, [(wb[wi][:, 0:1024].rearrange("p (kc c) -> p kc c", kc=8), wsrc_sq("w_out", l, rc))]),
                             lambda wi, rc=rc, l=l: d2_job(l, rc, wi)))
            for rc in range(8):
                jobs.append((lambda wi, rc=rc, l=l: load_w(wi, [
                    (wb[wi][:, 0:1024].rearrange("p (kc c) -> p kc c", kc=8), wsrc_sq("w_ple_gate", l, rc)),
                    (wb[wi][:, 1024:1280].rearrange("p (kc c) -> p kc c", kc=2),
                     di["w_ple"][l].rearrange("(kc p) (rr c) -> p kc rr c", p=128, rr=8)[:, :, rc, :])]),
                    lambda wi, rc=rc, l=l: d3_job(l, rc, wi)))
            wis = [nxt("wb", 3) for _ in jobs]
            P.barrier()
            jobs[0][0](wis[0])
            s_load(l)
            build_EB(l)
            jobs[1][0](wis[1])
            for k, (ld, cp) in enumerate(jobs):
                if k in (0, 4, 8, 12):
                    P.barrier()
                if k + 2 < len(jobs):
                    jobs[k + 2][0](wis[k + 2])
                cp(wis[k])
            s_ln(l, write_y=(l == L - 1))

    P.wait_all("sp")
    print("PROG nrec", P.nrec, {e: len(v) for e, v in P.ops.items()})
    if _os.environ.get("KLOG"):
        with open(_os.environ["KLOG"], "w") as f:
            for r in P.log:
                f.write(repr(r) + "\n")
    with nc.allow_non_contiguous_dma(reason="small param / state vectors"):
        P.emit(sems)
    es.close()
    return nc


OUT_NAMES = ["y_prompt", "y_sample", "k_a_prompt", "v_a_prompt", "k_a_sample", "v_a_sample",
             "ret_prompt", "ret_sample", "conv_prompt", "conv_sample", "lru_prompt", "lru_sample"]


def kernel(**inputs):
    NSEQ = 4
    NSMP = 2
    nc = build(NSEQ=NSEQ, NSMP=NSMP)
    consts = host_consts()
    in_maps = []
    for c in range(N_CORES):
        m = {}
        m["x_prompt"] = np.ascontiguousarray(inputs["x_prompt"][c * NSEQ:(c + 1) * NSEQ])
        m["p_prompt"] = np.ascontiguousarray(inputs["p_prompt"][:, c * NSEQ:(c + 1) * NSEQ])
        m["x_sample"] = np.ascontiguousarray(inputs["x_sample"][c * NSMP:(c + 1) * NSMP])
        for k in ("p_sample", "cache_k_a", "cache_v_a", "state_ret", "state_conv", "state_lru"):
            m[k] = np.ascontiguousarray(inputs[k][:, c * NSMP:(c + 1) * NSMP])
        for k in W_NAMES:
            m[k] = np.ascontiguousarray(inputs[k])
        m.update(consts)
        in_maps.append(m)
    res = run_bass_kernel_spmd(nc, in_maps, core_ids=list(range(N_CORES)))
    R = res.results
    out = {}
    out["y_prompt"] = np.concatenate([r["y_prompt"] for r in R], 0)
    out["y_sample"] = np.concatenate([r["y_sample"] for r in R], 0)
    for nm in ("k_a_prompt", "v_a_prompt", "ret_prompt", "conv_prompt", "lru_prompt",
               "k_a_sample", "v_a_sample", "ret_sample", "conv_sample", "lru_sample"):
        out[nm] = np.concatenate([r[nm] for r in R], 1)
    return tuple(np.asarray(out[n], dtype=np.float32) for n in OUT_NAMES)
```

```python
import numpy as np
from contextlib import ExitStack
import concourse.bass as bass
import concourse.mybir as mybir
from concourse.bass_utils import run_bass_kernel_spmd

F32 = mybir.dt.float32
BF16 = mybir.dt.bfloat16
AF = mybir.ActivationFunctionType
ALU = mybir.AluOpType
AX = mybir.AxisListType

ENGS = ("pe", "act", "dve", "pool", "sp")
N_CORES = 8
D = 1024
SEQ = 2048
NT = 1024
NB = NT // 128
NTT = NT // 512
NHF = SEQ // NT
ALPHA = (2 * 2) ** 0.25
LN_EPS = 1e-5


class Prog:
    def __init__(self, nc):
        self.nc = nc
        self.ops = {e: [] for e in ENGS}
        self.cnt = {e: 0 for e in ENGS}
        self.known = {e: {} for e in ENGS}
        self.tw = {}
        self.tr = {}
        n_lanes = {"sp": 6, "act": 2, "pool": 4}
        self.lanes = {q: [[f"dma_{q}_{i}", 0] for i in range(n)] for q, n in n_lanes.items()}
        self.lane_rr = {q: 0 for q in n_lanes}
        self.semkeys = [f"c_{e}" for e in ENGS if e != "sp"] + [l[0] for q in self.lanes for l in self.lanes[q]]
        import os
        self.maxop = int(os.environ.get("KMAXOP", "1000000000"))
        self.nrec = 0
        self.log = []

    def _deps(self, eng, reads, writes):
        deps = []
        for t in list(reads) + list(writes):
            ev = self.tw.get(t)
            if ev is not None:
                deps.append(ev)
        for t in writes:
            deps.extend(self.tr.get(t, ()))
        kn = self.known[eng]
        best = {}
        for (sk, v) in deps:
            if eng == "pe" and sk == "c_pe":
                continue
            if kn.get(sk, 0) >= v:
                continue
            if best.get(sk, 0) < v:
                best[sk] = v
        waits = []
        for sk, v in best.items():
            kn[sk] = v
            waits.append((sk, v))
        return waits

    def _commit(self, ev, reads, writes):
        for t in reads:
            self.tr.setdefault(t, []).append(ev)
        for t in writes:
            self.tw[t] = ev
            self.tr[t] = []

    def op(self, eng, fn, reads=(), writes=()):
        self.nrec += 1
        if self.nrec > self.maxop:
            return None
        self.log.append((self.nrec, eng, fn.__code__.co_firstlineno, tuple(writes)))
        PS = ("acc", "big", "sm0", "sm1")
        writes = list(writes) + [t for t in reads if t.startswith(PS)]
        reads = [t for t in reads if not t.startswith(PS)]
        waits = self._deps(eng, reads, writes)
        self.cnt[eng] += 1
        ev = (f"c_{eng}", self.cnt[eng])
        self.ops[eng].append((waits, fn, ev[0], 1))
        self._commit(ev, reads, writes)
        return ev

    def dma(self, q, fn, reads=(), writes=()):
        self.nrec += 1
        if self.nrec > self.maxop:
            return None
        self.log.append((self.nrec, "dma_" + q, fn.__code__.co_firstlineno, tuple(writes)))
        lanes = self.lanes[q]
        i = self.lane_rr[q]
        self.lane_rr[q] = (i + 1) % len(lanes)
        lane = lanes[i]
        waits = self._deps(q, reads, writes)
        if lane[1] > 0 and self.known[q].get(lane[0], 0) < lane[1]:
            self.known[q][lane[0]] = lane[1]
            waits.append((lane[0], lane[1]))
        lane[1] += 16
        ev = (lane[0], lane[1])
        self.ops[q].append((waits, fn, lane[0], 16))
        self._commit(ev, reads, writes)
        return ev

    def barrier(self):
        evs = [(f"c_{e}", self.cnt[e]) for e in ENGS if e != "sp" and self.cnt[e] > 0]
        evs += [(lane[0], lane[1]) for lane in self.lanes["sp"] if lane[1] > 0]
        for eng in ENGS:
            waits = []
            kn = self.known[eng]
            for (sk, v) in evs:
                if sk == f"c_{eng}" and eng == "pe":
                    continue
                if kn.get(sk, 0) >= v:
                    continue
                kn[sk] = v
                waits.append((sk, v))
            if waits:
                self.ops[eng].append((waits, None, None, 0))

    def wait_all(self, eng):
        waits = []
        for e in ENGS:
            if e == "sp" or self.cnt[e] == 0:
                continue
            waits.append((f"c_{e}", self.cnt[e]))
        for q in self.lanes:
            for lane in self.lanes[q]:
                if lane[1] > 0:
                    waits.append((lane[0], lane[1]))
        self.ops[eng].append((waits, None, None, 0))

    def emit(self, sems):
        nc = self.nc

        def run(engine, lst):
            for waits, fn, sk, inc in lst:
                for (wk, wv) in waits:
                    engine.wait_ge(sems[wk], wv)
                if fn is not None:
                    ins = fn(engine)
                    ins.then_inc(sems[sk], inc)

        with nc.Block() as block:
            @block.tensor
            def _(e):
                run(e, self.ops["pe"])

            @block.scalar
            def _(e):
                run(e, self.ops["act"])

            @block.vector
            def _(e):
                run(e, self.ops["dve"])

            @block.gpsimd
            def _(e):
                run(e, self.ops["pool"])

            @block.sync
            def _(e):
                run(e, self.ops["sp"])


def host_consts():
    half = 32
    inv = (10000.0 ** (-np.arange(half, dtype=np.float32) / half)).astype(np.float32)
    pos = np.arange(SEQ, dtype=np.float32)
    ang = pos[:, None] * inv[None, :]
    c = np.cos(ang).astype(np.float32)
    s = np.sin(ang).astype(np.float32)
    cc = np.concatenate([c, c], -1).reshape(SEQ // 128, 128, 64).transpose(1, 0, 2)
    ss = np.concatenate([-s, s], -1).reshape(SEQ // 128, 128, 64).transpose(1, 0, 2)
    h = np.arange(8, dtype=np.float32)
    log_g = np.log1p(-np.exp2(-5.0 - h)).astype(np.float64)
    p = np.arange(128, dtype=np.float64)
    xi = np.exp(log_g[None, :] * (p[:, None] + 1.0))
    zi = np.exp(-log_g[None, :] * (p[:, None] + 1.0)) * (64 ** -0.5)
    gt = np.zeros((128, 4, 64), np.float64)
    gt32 = np.zeros((128, 4, 64), np.float64)
    for pp in range(128):
        for cch in range(4):
            hh = 2 * cch + pp // 64
            gt[pp, cch, :] = np.exp(log_g[hh] * 128.0)
            gt32[pp, cch, :] = np.exp(log_g[hh] * 32.0)
    ident = np.eye(128, dtype=np.float32)
    jj = np.arange(128)
    mask = (jj[None, :] >= jj[:, None]).astype(np.float32)
    return {
        "c_cc": np.ascontiguousarray(cc, dtype=np.float32),
        "c_ss": np.ascontiguousarray(ss, dtype=np.float32),
        "c_xi": xi.astype(np.float32),
        "c_zi": zi.astype(np.float32),
        "c_gt": gt.astype(np.float32),
        "c_gt32": gt32.astype(np.float32),
        "c_ident": ident,
        "c_mask": mask,
        "c_anti": np.ascontiguousarray(ident[::-1]),
    }


W_NAMES = {
    "w_in": [2, 1024, 8192], "rel_table": [2, 8, 257], "gn_gain": [2, 512], "conv_w": [2, 4, 512],
    "conv_b": [2, 512], "w_gate_a": [2, 8, 64, 64], "b_gate_a": [2, 512], "w_gate_x": [2, 8, 64, 64],
    "b_gate_x": [2, 512], "lru_lambda": [2, 512], "w_branch": [2, 3, 512, 1024], "w_out": [2, 1024, 1024],
    "ln_gain": [2, 1024], "ln_bias": [2, 1024], "w_ple": [2, 256, 1024], "w_ple_gate": [2, 1024, 1024],
}
C_SHAPES = {"c_cc": [128, 16, 64], "c_ss": [128, 16, 64], "c_xi": [128, 8], "c_zi": [128, 8],
            "c_gt": [128, 4, 64], "c_gt32": [128, 4, 64], "c_ident": [128, 128], "c_mask": [128, 128], "c_anti": [128, 128]}


def build(NSEQ=4, L=2, NSMP=2):
    nc = bass.Bass("TRN2", target_bir_lowering=False)
    di = {}

    def din(name, shape):
        di[name] = nc.dram_tensor(name, shape, F32, kind="ExternalInput").ap()
        return di[name]

    def dout(name, shape):
        di[name] = nc.dram_tensor(name, shape, F32, kind="ExternalOutput").ap()
        return di[name]

    xp = din("x_prompt", [NSEQ, SEQ, D])
    pp_ = din("p_prompt", [L, NSEQ, SEQ, 256])
    for k, shp in W_NAMES.items():
        din(k, shp)
    for k, shp in C_SHAPES.items():
        din(k, shp)
    yo = dout("y_prompt", [NSEQ, SEQ, D])
    ko = dout("k_a_prompt", [L, NSEQ, 512, 8, 64])
    vo = dout("v_a_prompt", [L, NSEQ, 512, 8, 64])
    ro = dout("ret_prompt", [L, NSEQ, 8, 64, 64])
    co = dout("conv_prompt", [L, NSEQ, 3, 512])
    lo = dout("lru_prompt", [L, NSEQ, 512])
    xs_d = din("x_sample", [NSMP, 32, D])
    ps_d = din("p_sample", [L, NSMP, 32, 256])
    ck_d = din("cache_k_a", [L, NSMP, 512, 8, 64])
    cv_d = din("cache_v_a", [L, NSMP, 512, 8, 64])
    sr_d = din("state_ret", [L, NSMP, 8, 64, 64])
    sc_d = din("state_conv", [L, NSMP, 3, 512])
    sl_d = din("state_lru", [L, NSMP, 512])
    yso = dout("y_sample", [NSMP, 32, D])
    kso = dout("k_a_sample", [L, NSMP, 32, 8, 64])
    vso = dout("v_a_sample", [L, NSMP, 32, 8, 64])
    rso = dout("ret_sample", [L, NSMP, 8, 64, 64])
    cso = dout("conv_sample", [L, NSMP, 3, 512])
    lso = dout("lru_sample", [L, NSMP, 512])
    gt32 = None
    ext = nc.dram_tensor("ext_scratch", [L, 8, 768], F32, kind="Internal").ap()

    es = ExitStack()

    def sb(name, shape, dt=F32):
        return es.enter_context(nc.sbuf_tensor(name, shape, dt))

    def ps(name, shape, dt=F32):
        return es.enter_context(nc.psum_tensor(name, shape, dt))

    xT = sb("xT", [128, 8, NT])
    xB = sb("xB", [128, 8, NT], BF16)
    yg = [sb(f"yg{i}", [128, 4, NT], BF16) for i in range(3)]
    merged = sb("merged", [128, 8, NT], BF16)
    pT = sb("pT", [128, 2, NT], BF16)
    pTb = sb("pTb", [128, 2, NT], BF16)
    pT2 = [pT, pTb]
    pinx = [sb("pinx0", [128, 256]), sb("pinx1", [128, 256])]
    WBN = 6144
    wb = [sb(f"wb{i}", [128, WBN], BF16) for i in range(3)]
    EBall = sb("EBall", [128, L, 8, 5, 128], BF16)
    biasst = [sb("biasst0", [128, 5, 128])]
    cc = sb("cc", [128, 16, 64]); ss = sb("ss", [128, 16, 64])
    xi = sb("xi", [128, 8]); zi = sb("zi", [128, 8])
    gt = sb("gt", [128, 4, 64])
    gt32 = sb("gt32", [128, 4, 64])
    identf = sb("identf", [128, 128]); identb = sb("identb", [128, 128], BF16)
    maskb = sb("maskb", [128, 128], BF16)
    onesf = sb("onesf", [128, 128])
    antif = sb("antif", [128, 128])
    mhalf = sb("mhalf", [128, 16])
    NP = 64
    prm = sb("prm", [128, L, NP])
    WgA = sb("WgA", [128, L, 4, 128], BF16); WgX = sb("WgX", [128, L, 4, 128], BF16)
    kcar = nc.dram_tensor("kcar_scratch", [128, L, 4, 512], BF16, kind="Internal").ap()
    vcar = nc.dram_tensor("vcar_scratch", [128, L, 4, 4 * 130], BF16, kind="Internal").ap()
    Scar = sb("Scar", [128, L, 4, 64])
    convcar = sb("convcar", [128, L, 4, 3])
    hcar = sb("hcar", [128, L, 4])
    sz = sb("sz", [128, NT], BF16)
    SCRN = 7168
    scr = sb("scr", [128, SCRN])
    scrb = scr.bitcast(BF16)

    def carve(items, start=0):
        out = {}
        off = start
        for name, n, dt in items:
            nbytes = n * (4 if dt == F32 else 2)
            if dt == F32:
                out[name] = scr[:, off // 4: off // 4 + n]
            else:
                out[name] = scrb[:, off // 2: off // 2 + n]
            off += (nbytes + 63) // 64 * 64
        assert off <= SCRN * 4, (off, SCRN * 4)
        return out

    cv = carve([("xin0", 1024, F32), ("xin1", 1024, F32), ("pin0", 256, F32), ("pin1", 256, F32),
                ("rts", 257, F32), ("exts", 768, F32)])
    xin = [cv["xin0"], cv["xin1"]]; pin = [cv["pin0"], cv["pin1"]]
    rts = cv["rts"][0:8, :]; exts = cv["exts"][0:8, :]
    cv = carve([("QT", NT, BF16), ("KT", 512 + NT, BF16), ("Vv", (4 + NB) * 130, BF16), ("Eb0", 640, BF16), ("Eb1", 640, BF16),
                ("PT0", 640, BF16), ("PT1", 640, BF16), ("PT2", 640, BF16), ("PT3", 640, BF16), ("rcp", 2, F32), ("ya", 128, BF16), ("kvst0", 128, F32), ("kvst1", 128, F32), ("kctm", 512, BF16)])
    QT = cv["QT"]; KT = cv["KT"]
    Vv = cv["Vv"].rearrange("p (a h d) -> p a h d", a=4 + NB, h=2)
    Eb = [cv["Eb0"].rearrange("p (a q) -> p a q", a=5), cv["Eb1"].rearrange("p (a q) -> p a q", a=5)]
    PTb = [cv["PT0"].rearrange("p (a q) -> p a q", a=5), cv["PT1"].rearrange("p (a q) -> p a q", a=5)]
    PTd = [[cv[f"PT{2 * par + hh}"].rearrange("p (a q) -> p a q", a=5) for hh in range(2)] for par in range(2)]
    rcp = cv["rcp"]; ya = cv["ya"]; kvst = [cv["kvst0"], cv["kvst1"]]
    kctm = cv["kctm"].rearrange("p (a c) -> p a c", a=4)
    ya2 = [ya, cv["kctm"][:, 0:128]]
    cv = carve([("QTr", NT, BF16), ("KTr", NT, BF16), ("Vb", NB * 128, BF16), ("rt1", 128, F32), ("rt2", 128, F32),
                ("Qt", 128, BF16), ("Kt", 128, BF16), ("kvbuf", 64 * NB, F32), ("Sall", 64 * NB, F32), ("Gpat", 64 * NB, F32),
                ("Sop", NB * 64, BF16), ("Am", 256, BF16), ("ysb", 128, F32), ("ysq", 128, F32), ("gst", 16, F32), ("ynb", 128, BF16), ("S0f", 64, F32), ("Snew", 64, F32),
                ("QtA", NB * 128, BF16), ("KtA", NB * 128, BF16), ("AmA", 4 * 256, BF16), ("gstA", 96, F32), ("ynbA", NB * 128, BF16),
                ("sz2", NT, BF16)])
    QTr = cv["QTr"]; KTr = cv["KTr"]; Vb = cv["Vb"].rearrange("p (n c) -> p n c", n=NB)
    rt1 = cv["rt1"]; rt2 = cv["rt2"]; Qt = cv["Qt"]; Kt = cv["Kt"]
    kvbuf = cv["kvbuf"].rearrange("p (e n) -> p e n", n=NB); Sall = cv["Sall"].rearrange("p (e n) -> p e n", n=NB)
    Gpat = cv["Gpat"].rearrange("p (e n) -> p e n", n=NB); Sop = cv["Sop"].rearrange("p (n e) -> p n e", n=NB)
    S0f = cv["S0f"]; Snew = cv["Snew"]
    QtA = cv["QtA"].rearrange("p (t c) -> p t c", t=NB); KtA = cv["KtA"].rearrange("p (t c) -> p t c", t=NB)
    AmA = cv["AmA"].rearrange("p (t h i) -> p t h i", t=4, h=2); gstA = cv["gstA"].rearrange("p (k n) -> p k n", k=6)
    sz2 = cv["sz2"]
    Vb2t = sb("Vb2", [128, NB, 128], BF16)
    Vbd = [Vb, Vb2t[:]]
    szd = [sz, sz2]
    ynbA = cv["ynbA"].rearrange("p (t c) -> p t c", t=NB)
    mgf = merged.bitcast(F32)[:].rearrange("p a b -> p (a b)")
    qkraw = mgf[:, 0:2048].rearrange("p (t c) -> p t c", t=NB)
    rt1A = mgf[:, 2048:3072].rearrange("p (t c) -> p t c", t=NB)
    rt2A = mgf[:, 3072:4096].rearrange("p (t c) -> p t c", t=NB)
    ysbA = mgf[:, 2048:3072].rearrange("p (t c) -> p t c", t=NB)
    ysqA = mgf[:, 3072:4096].rearrange("p (t c) -> p t c", t=NB)
    Am = cv["Am"].rearrange("p (h i) -> p h i", h=2); ysb = cv["ysb"]; ysq = cv["ysq"]; gst = cv["gst"]; ynb = cv["ynb"]
    cv = carve([("bB", NT, F32), ("bC", NT, F32), ("szc", NT, BF16)], start=18432)
    bB = cv["bB"]; bC = cv["bC"]; szc_buf = cv["szc"]
    _mg = merged.bitcast(F32)[:].rearrange("p a b -> p (a b)")
    xrbuf = _mg[:, 0:3 + NT]; xc = _mg[:, 1028:1028 + NT]; bA = _mg[:, 2052:2052 + NT]
    xcb = merged[:].rearrange("p a b -> p (a b)")[:, 2 * 3076:2 * 3076 + NT]
    cv = carve([("gbuf0", 512, BF16), ("gbuf1", 512, BF16), ("gbuf2", 512, BF16), ("tb3_0", 512, F32), ("tb3_1", 512, F32), ("tb3_2", 512, F32),
                ("lnst", 8 * NB, F32), ("lntmp", 128, F32), ("lnt2", 128, F32), ("dA", 128, F32), ("dB", 128, F32), ("lnt", 1024, F32),
                ("yst0", 1024, F32), ("yst1", 1024, F32)])
    gbuf = [cv["gbuf0"], cv["gbuf1"], cv["gbuf2"]]; tb3 = [cv["tb3_0"], cv["tb3_1"], cv["tb3_2"]]
    lnst = cv["lnst"]; lntmp = cv["lntmp"]; lnt2 = cv["lnt2"]; dA = cv["dA"]; dB = cv["dB"]
    lnt = cv["lnt"].rearrange("p (c t) -> p c t", c=8); yst = [cv["yst0"], cv["yst1"]]

    acc = [ps(f"acc{i}", [128, 512]) for i in range(2)]
    big = [ps(f"big{i}", [128, 1024]) for i in range(2)]
    sm0 = ps("sm0", [128, 512])
    sm1 = ps("sm1", [128, 1024], BF16)

    P = Prog(nc)
    sems = {k: es.enter_context(nc.semaphore(k)) for k in P.semkeys}
    rr = {"acc": 0, "ev": 0, "xin": 0, "pin": 0, "wb": 0, "kvst": 0, "bst": 0, "yst": 0}

    def nxt(key, n):
        v = rr[key]
        rr[key] = (v + 1) % n
        return v

    def bc(ap, shape):
        return ap.to_broadcast(shape)

    def mm_group(out_ap, out_tok, pairs, rtoks):
        def fn(e):
            ins = None
            n = len(pairs)
            for i, (l_, r_) in enumerate(pairs):
                ins = e.matmul(out_ap, l_, r_, start=(i == 0), stop=(i == n - 1))
            return ins
        P.op("pe", fn, reads=rtoks, writes=[out_tok])

    def xb_toks(tt):
        return [f"xB{kc}_{tt}" for kc in range(8)]

    def xf_toks(tt):
        return [f"xT{kc}_{tt}" for kc in range(8)]

    def evac_engine():
        return "act" if nxt("ev", 2) == 0 else "dve"

    def copy_op(eng, out_ap, in_ap, reads, writes, scale=None):
        if eng == "act":
            if scale is None:
                P.op("act", lambda e: e.copy(out_ap, in_ap), reads=reads, writes=writes)
            else:
                P.op("act", lambda e: e.mul(out_ap, in_ap, scale), reads=reads, writes=writes)
        else:
            if scale is None:
                P.op(eng, lambda e: e.tensor_scalar(out_ap, in_ap, 1.0, None, ALU.mult), reads=reads, writes=writes)
            else:
                P.op(eng, lambda e: e.tensor_scalar(out_ap, in_ap, scale, None, ALU.mult), reads=reads, writes=writes)

    def act_fn(out_ap, in_ap, func, reads, writes, bias=None, scale=None):
        kw = {}
        if bias is not None:
            kw["bias"] = bias
        if scale is not None:
            kw["scale"] = scale
        P.op("act", lambda e: e.activation(out_ap, in_ap, func, **kw), reads=reads, writes=writes)

    import os as _os
    def load_const(dst, name, toks):
        P.dma("sp", lambda e: e.dma_start(out=dst, in_=di[name]), writes=toks)

    load_const(cc[:], "c_cc", ["cc"]); load_const(ss[:], "c_ss", ["ss"])
    load_const(xi[:], "c_xi", ["xi"]); load_const(zi[:], "c_zi", ["zi"])
    load_const(gt[:], "c_gt", ["gt"]); load_const(identf[:], "c_ident", ["identf"]); load_const(antif[:], "c_anti", ["antif"]); load_const(gt32[:], "c_gt32", ["gt32"])
    P.dma("pool", lambda e: e.dma_start(out=identb[:], in_=di["c_ident"]), writes=["identb"])
    P.dma("pool", lambda e: e.dma_start(out=maskb[:], in_=di["c_mask"]), writes=["maskb"])
    P.op("dve", lambda e: e.memset(onesf[:], 1.0), writes=["onesf"])
    P.op("dve", lambda e: e.memset(mhalf[:], -0.5), writes=["mhalf"])
    P.op("dve", lambda e: e.memset(WgA[:], 0.0), writes=["WgA"])
    P.op("dve", lambda e: e.memset(WgX[:], 0.0), writes=["WgX"])
    P.op("dve", lambda e: e.memset(prm[:], 0.0), writes=["prm"])

    def pcol(l, a, b):
        return prm[:, l, a:b]

    for l in range(L):
        def vec_load(name, col, n, l=l):
            src = di[name][l].rearrange("(c p) -> p c", p=128)
            P.dma("sp", lambda e: e.dma_start(out=prm[:, l, col:col + n], in_=src), reads=["prm"], writes=[f"prm{l}_{col}"])
        vec_load("gn_gain", 0, 4)
        vec_load("conv_b", 4, 4)
        for tap in range(4):
            src = di["conv_w"][l, tap].rearrange("(c p) -> p c", p=128)
            P.dma("sp", lambda e, src=src, tap=tap, l=l: e.dma_start(out=prm[:, l, 8 + 4 * tap:12 + 4 * tap], in_=src),
                  reads=["prm"], writes=[f"prm{l}_cw{tap}"])
        vec_load("b_gate_a", 24, 4)
        vec_load("b_gate_x", 28, 4)
        vec_load("lru_lambda", 32, 4)
        vec_load("ln_gain", 40, 8)
        vec_load("ln_bias", 48, 8)
        act_fn(prm[:, l, 36:40], prm[:, l, 32:36], AF.Exp, [f"prm{l}_32"], [f"prm{l}_sp"], scale=-1.0)
        act_fn(prm[:, l, 36:40], prm[:, l, 36:40], AF.Ln, [f"prm{l}_sp"], [f"prm{l}_sp"], bias=onesf[:, 0:1])
        P.op("dve", lambda e, l=l: e.tensor_scalar(prm[:, l, 56:60], prm[:, l, 36:40], -16.0, None, ALU.mult),
             reads=[f"prm{l}_sp"], writes=[f"prm{l}_sp2"])
        P.op("dve", lambda e, l=l: e.tensor_scalar(prm[:, l, 36:40], prm[:, l, 36:40], -8.0, None, ALU.mult),
             reads=[f"prm{l}_sp", f"prm{l}_sp2"], writes=[f"prm{l}_sp"])
        for nm, Wt in (("w_gate_a", WgA), ("w_gate_x", WgX)):
            srcv = di[nm][l].rearrange("(c hh) i j -> hh i c j", hh=2)
            for hh in range(2):
                P.dma("pool", lambda e, Wt=Wt, srcv=srcv, hh=hh, l=l: e.dma_start(
                    out=Wt[hh * 64:(hh + 1) * 64, l, :, hh * 64:(hh + 1) * 64], in_=srcv[hh]),
                    reads=["WgA" if nm == "w_gate_a" else "WgX"], writes=[f"{nm}{l}_{hh}"])
        P.dma("sp", lambda e, l=l: e.dma_start(out=rts[:], in_=di["rel_table"][l]), writes=["rts"])
        P.op("dve", lambda e: e.tensor_copy(exts[:, 0:256], rts[:, 1:257]), reads=["rts"], writes=["exts"])
        P.op("dve", lambda e: e.tensor_copy(exts[:, 256:768], bc(rts[:, 256:257], [8, 512])),
             reads=["rts", "exts"], writes=["exts"])
        P.dma("sp", lambda e, l=l: e.dma_start(out=ext[l], in_=exts[:]), reads=["exts"], writes=[f"ext{l}"])
    WgTok = [[f"w_gate_a{l}_0", f"w_gate_a{l}_1", f"w_gate_x{l}_0", f"w_gate_x{l}_1", "WgA", "WgX"] for l in range(L)]
    prm_all = lambda l: ([f"prm{l}_{c}" for c in (0, 4, 24, 28, 40, 48)] + [f"prm{l}_cw{t}" for t in range(4)]
                         + [f"prm{l}_sp", f"prm{l}_sp2", "prm"])

    w_in = di["w_in"]
    P.barrier()

    def load_w(i, pieces):
        flat = []
        for dst, src in pieces:
            if len(dst.shape) == 4:
                for gi in range(dst.shape[2]):
                    flat.append((dst[:, :, gi, :], src[:, :, gi, :]))
            else:
                flat.append((dst, src))
        assert len(flat) <= 6
        for k, (dst, src) in enumerate(flat):
            P.dma("pool", lambda e, dst=dst, src=src: e.dma_start(out=dst, in_=src), writes=[f"wb{i}"] if k == 0 else [f"wb{i}_p{k}"])

    def wtoks(i, npieces):
        return [f"wb{i}"] + [f"wb{i}_p{k}" for k in range(1, 6)]

    EBl = [EBall[:, l_] for l_ in range(L)]

    def build_EB(l):
        EB = EBl[l]
        for h in range(8):
            bst = biasst[0]
            base = ext[l, h, 0:1]
            src = bass.AP(base.tensor, base.offset, [[1, 128], [128, 5], [1, 128]])
            P.dma("sp", lambda e, bst=bst, src=src: e.dma_start(out=bst[:], in_=src), reads=[f"ext{l}"], writes=["bst0"])
            bv = big[0][:, 0:640]

            def fn(e, bst=bst, bv=bv):
                bf = bst[:].rearrange("p a q -> p (a q)")
                e.matmul(bv[:, 0:512], antif[:], bf[:, 0:512], start=True, stop=True)
                return e.matmul(bv[:, 512:640], antif[:], bf[:, 512:640], start=True, stop=True)
            P.op("pe", fn, reads=["bst0", "antif"], writes=["big0"])
            bv5 = bv.rearrange("p (a q) -> p a q", a=5)
            act_fn(EB[:, h, 0:4, :], bv5[:, 0:4, :], AF.Exp, ["big0"], ["EB"])
            act_fn(EB[:, h, 4:5, :], bv5[:, 4:5, :], AF.Exp, ["big0", "EB"], ["EB"])
        P.op("dve", lambda e: e.memset(EB[0:64, :, 4, 64:128], 0.0), reads=["EB"], writes=["EB"])
        P.op("dve", lambda e: e.memset(EB[64:128, :, 0, 0:64], 0.0), reads=["EB"], writes=["EB"])

    def load_x(b, hf):
        for tb in range(NB):
            xi_ = nxt("xin", 2)
            r0 = hf * NT + tb * 128
            P.dma("sp", lambda e, xi_=xi_, r0=r0: e.dma_start(out=xin[xi_][:], in_=xp[b, r0:r0 + 128, :]), writes=[f"xin{xi_}"])
            tt = tb // 4
            for hb in range(4):
                sv = sm0[:, 0:256].rearrange("p (c t) -> p c t", c=2)

                def fn(e, xi_=xi_, sv=sv, hb=hb):
                    ins = None
                    for c in range(2):
                        cg = hb * 2 + c
                        ins = e.transpose(sv[:, c, :], xin[xi_][:, cg * 128:(cg + 1) * 128], identf[:])
                    return ins
                P.op("pe", fn, reads=[f"xin{xi_}", "identf"], writes=["sm0"])
                cs = slice(hb * 2, hb * 2 + 2)
                if int(_os.environ.get("KX", "3")) & 1:
                    P.op("act", lambda e, tb=tb, sv=sv, cs=cs: e.copy(xT[:, cs, tb * 128:(tb + 1) * 128], sv),
                         reads=["sm0"], writes=[f"xT{c}_{tt}" for c in range(8)])
                if int(_os.environ.get("KX", "3")) & 2:
                    P.op("act", lambda e, tb=tb, sv=sv, cs=cs: e.copy(xB[:, cs, tb * 128:(tb + 1) * 128], sv),
                         reads=["sm0"], writes=[f"xB{c}_{tt}" for c in range(8)])

    def load_p(l, b, hf, par):
        for tb in range(NB):
            pi_ = nxt("pin", 2)
            r0 = hf * NT + tb * 128
            P.dma("sp", lambda e, pi_=pi_, r0=r0: e.dma_start(out=pinx[pi_][:], in_=pp_[l, b, r0:r0 + 128, :]), writes=[f"pinx{pi_}"])
            sv = sm0[:, 0:256].rearrange("p (c t) -> p c t", c=2)

            def fn(e, pi_=pi_, sv=sv):
                ins = None
                for c in range(2):
                    ins = e.transpose(sv[:, c, :], pinx[pi_][:, c * 128:(c + 1) * 128], identf[:])
                return ins
            P.op("pe", fn, reads=[f"pinx{pi_}", "identf"], writes=["sm0"])
            copy_op(evac_engine(), pT2[par][:, :, tb * 128:(tb + 1) * 128], sv, ["sm0"], [f"pT{par}_{tb // 4}"])

    CFG = {"w": 512, "ntt": NTT, "pTpar": 0}

    def csl(tt):
        return slice(tt * CFG["w"], (tt + 1) * CFG["w"])

    def cw(ap):
        return ap[:, 0:CFG["w"]]

    def fm_mm(out_acc, ai, lhs_fn, tt, wt, nkc=8, rhs_src=None, rhs_toks=None):
        src = xB if rhs_src is None else rhs_src
        pairs = [(lhs_fn(kc), src[:, kc, csl(tt)]) for kc in range(nkc)]
        mm_group(cw(out_acc), f"acc{ai}", pairs, (xb_toks(tt) if rhs_toks is None else rhs_toks) + wt)

    def att_job(l, b, hf, j, wi, last):
        wt = wtoks(wi, 1)
        wv = wb[wi][:, 0:4096].rearrange("p (kc g c) -> p kc g c", kc=8, g=4)
        P.op("dve", lambda e: e.memset(Vv[:, :, :, 64:65], 1.0), writes=["Vv"])
        if hf > 0:
            P.dma("sp", lambda e: e.dma_start(out=KT[:, 0:512], in_=kcar[:, l, j, :]), reads=[f"kcar{l}_{j}"], writes=["KT"])
            P.dma("sp", lambda e: e.dma_start(out=Vv[:, 0:4, :, :].rearrange("p a h d -> p (a h d)"), in_=vcar[:, l, j, :]),
                  reads=[f"vcar{l}_{j}", "Vv"], writes=["Vv"])
        for tt in range(NTT):
            ai = nxt("acc", 2)
            fm_mm(acc[ai], ai, lambda kc: wv[:, kc, 0, :], tt, wt)
            copy_op("act", QT[:, tt * 512:(tt + 1) * 512], acc[ai][:], [f"acc{ai}"], ["QT"], scale=0.125)
            ai = nxt("acc", 2)
            fm_mm(acc[ai], ai, lambda kc: wv[:, kc, 1, :], tt, wt)
            copy_op("dve", KT[:, 512 + tt * 512:512 + (tt + 1) * 512], acc[ai][:], [f"acc{ai}"], ["KT"])
            ai = nxt("acc", 2)
            fm_mm(acc[ai], ai, lambda kc: wv[:, kc, 3, :], tt, wt)
            act_fn(sz[:, tt * 512:(tt + 1) * 512], acc[ai][:], AF.Silu, [f"acc{ai}"], ["sz"])
            yield
        for tb in range(NB):
            yield
            tt = tb // 4
            ai = nxt("acc", 2)
            pairs = [(xB[:, kc, tb * 128:(tb + 1) * 128], wv[:, kc, 2, :]) for kc in range(8)]
            mm_group(acc[ai][:, 0:128], f"acc{ai}", pairs, xb_toks(tt) + wt)
            av = acc[ai][:, 0:128].rearrange("p (h d) -> p h d", h=2)
            copy_op("dve", Vv[:, 4 + tb, :, 0:64], av, [f"acc{ai}"], ["Vv"])
            if last and tb >= NB - 4:
                si = nxt("kvst", 2)
                copy_op("act", kvst[si][:], acc[ai][:, 0:128], [f"acc{ai}"], [f"kvst{si}"])
                r0 = (tb - (NB - 4)) * 128
                P.dma("sp", lambda e, si=si, r0=r0: e.dma_start(
                    out=vo[l, b, r0:r0 + 128, 2 * j:2 * j + 2, :], in_=kvst[si][:].rearrange("p (h d) -> p h d", h=2)),
                    reads=[f"kvst{si}"])
                ai2 = nxt("acc", 2)
                pairs = [(xB[:, kc, tb * 128:(tb + 1) * 128], wv[:, kc, 1, :]) for kc in range(8)]
                mm_group(acc[ai2][:, 0:128], f"acc{ai2}", pairs, xb_toks(tt) + wt)
                si = nxt("kvst", 2)
                copy_op("act", kvst[si][:], acc[ai2][:, 0:128], [f"acc{ai2}"], [f"kvst{si}"])
                P.dma("sp", lambda e, si=si, r0=r0: e.dma_start(
                    out=ko[l, b, r0:r0 + 128, 2 * j:2 * j + 2, :], in_=kvst[si][:].rearrange("p (h d) -> p h d", h=2)),
                    reads=[f"kvst{si}"])
        Ov = sm0[:, 0:130].rearrange("p (h d) -> p h d", h=2)

        def stageA(tb):
            gblk = hf * NB + tb
            njp = 5 - max(0, 4 - gblk)
            par = tb % 2
            for hh in range(2):
                h = 2 * j + hh
                pb = 64 * hh
                STv = big[hh][:, 0:640].rearrange("p (a q) -> p a q", a=5)

                def fn(e, STv=STv, pb=pb, tb=tb, njp=njp):
                    ins = None
                    for jp in range(njp):
                        kb = tb + 4 - jp
                        ins = e.matmul(STv[:, jp, :], KT[pb:pb + 64, kb * 128:(kb + 1) * 128],
                                       QT[pb:pb + 64, tb * 128:(tb + 1) * 128], start=True, stop=True)
                    return ins
                P.op("pe", fn, reads=["KT", "QT"], writes=[f"big{hh}"])
                act_fn(Eb[hh][:, 0:min(njp, 4), :], STv[:, 0:min(njp, 4), :], AF.Exp, [f"big{hh}"], [f"Eb{hh}"])
                if njp == 5:
                    act_fn(Eb[hh][:, 4:5, :], STv[:, 4:5, :], AF.Exp, [f"big{hh}", f"Eb{hh}"], [f"Eb{hh}"])
                P.op("dve", lambda e, hh=hh, h=h, njp=njp, par=par: e.tensor_tensor(PTd[par][hh][:, 0:njp, :], Eb[hh][:, 0:njp, :], EBl[l][:, h, 0:njp, :], ALU.mult),
                     reads=[f"Eb{hh}", "EB"], writes=[f"PT{par}{hh}"])

        def stageB(tb):
            gblk = hf * NB + tb
            njp = 5 - max(0, 4 - gblk)
            par = tb % 2
            ya_ = ya2[par]
            for hh in range(2):
                def fn2(e, hh=hh, tb=tb, njp=njp, par=par):
                    ins = None
                    for jp in range(njp):
                        ins = e.matmul(Ov[:, hh, :], PTd[par][hh][:, jp, :], Vv[:, tb + 4 - jp, hh, :], start=(jp == 0), stop=(jp == njp - 1))
                    return ins
                P.op("pe", fn2, reads=[f"PT{par}{hh}", "Vv"], writes=["sm0"])
            P.op("dve", lambda e: e.reciprocal(rcp[:].rearrange("p (h o) -> p h o", o=1), Ov[:, :, 64:65]), reads=["sm0"], writes=["rcp"])
            P.op("dve", lambda e, ya_=ya_: e.tensor_tensor(ya_[:].rearrange("p (h d) -> p h d", h=2), Ov[:, :, 0:64],
                                                           bc(rcp[:].rearrange("p (h o) -> p h o", o=1), [128, 2, 64]), ALU.mult),
                 reads=["sm0", "rcp"], writes=[f"ya{par}"])

        def stageC(tb):
            par = tb % 2
            ya_ = ya2[par]
            P.op("pe", lambda e, ya_=ya_: e.transpose(sm1[:, 0:128], ya_[:], identb[:]), reads=[f"ya{par}", "identb"], writes=["sm1"])
            P.op("dve", lambda e, tb=tb: e.tensor_tensor(yg[0][:, j, tb * 128:(tb + 1) * 128], sm1[:, 0:128], sz[:, tb * 128:(tb + 1) * 128], ALU.mult),
                 reads=["sm1", "sz"], writes=[f"yg0_{tb // 4}"])

        for t in range(NB + 2):
            if t < NB:
                stageA(t)
                yield
            if 1 <= t <= NB:
                stageB(t - 1)
                yield
            if t >= 2:
                stageC(t - 2)
                yield
        P.dma("sp", lambda e: e.dma_start(out=kcar[:, l, j, :], in_=KT[:, NT:NT + 512]), reads=["KT"], writes=[f"kcar{l}_{j}"])
        P.dma("sp", lambda e: e.dma_start(out=vcar[:, l, j, :], in_=Vv[:, NB:NB + 4, :, :].rearrange("p a h d -> p (a h d)")),
              reads=["Vv"], writes=[f"vcar{l}_{j}"])

    def retA(l, b, hf, j, wi):
        par = j % 2
        wt = wtoks(wi, 1)
        wv = wb[wi][:, 0:4096].rearrange("p (kc g c) -> p kc g c", kc=8, g=4)
        for tt in range(NTT):
            ai = nxt("acc", 2)
            fm_mm(acc[ai], ai, lambda kc: wv[:, kc, 3, :], tt, wt)
            act_fn(szd[par][:, tt * 512:(tt + 1) * 512], acc[ai][:], AF.Silu, [f"acc{ai}"], [f"szr{par}"])
        for tb in range(NB):
            tt = tb // 4
            ai = nxt("acc", 2)
            pairs = [(xB[:, kc, tb * 128:(tb + 1) * 128], wv[:, kc, 0:3, :]) for kc in range(8)]
            mm_group(acc[ai][:, 0:384], f"acc{ai}", pairs, xb_toks(tt) + wt)
            copy_op("act", qkraw[:, tb, :], acc[ai][:, 0:256], [f"acc{ai}"], ["MB0"])
            copy_op("act", Vbd[par][:, tb, :], acc[ai][:, 256:384], [f"acc{ai}"], [f"Vb{par}"])

    def retB1(l, b, hf, j):
        g0 = hf * NB
        for qi, (dstA, sc_, dtok) in enumerate(((QtA, xi, "QtA"), (KtA, zi, "KtA"))):
            raw = qkraw[:, :, qi * 128:(qi + 1) * 128]
            raw4 = raw.rearrange("p t (h d) -> p t h d", h=2)
            raw5 = raw.rearrange("p t (h two d) -> p t h two d", h=2, two=2)
            t14 = rt1A.rearrange("p t (h d) -> p t h d", h=2)
            t25 = rt2A.rearrange("p t (h two d) -> p t h two d", h=2, two=2)
            ccv = cc[:, g0:g0 + NB, :].unsqueeze(2).to_broadcast([128, NB, 2, 64])
            P.op("dve", lambda e, t14=t14, raw4=raw4, ccv=ccv: e.tensor_tensor(t14, raw4, ccv, ALU.mult), reads=["MB0", "cc"], writes=["rt1A"])
            for hv in range(2):
                ssv = ss[:, g0:g0 + NB, hv * 32:(hv + 1) * 32].unsqueeze(2).to_broadcast([128, NB, 2, 32])
                P.op("dve", lambda e, t25=t25, raw5=raw5, ssv=ssv, hv=hv: e.tensor_tensor(t25[:, :, :, hv, :], raw5[:, :, :, 1 - hv, :], ssv, ALU.mult),
                     reads=["MB0", "ss"], writes=["rt2A"])
            P.op("dve", lambda e: e.tensor_tensor(rt1A, rt1A, rt2A, ALU.add), reads=["rt1A", "rt2A"], writes=["rt1A"])
            scv = sc_[:, 2 * j:2 * j + 2].unsqueeze(1).unsqueeze(3).to_broadcast([128, NB, 2, 64])
            P.op("dve", lambda e, dstA=dstA, t14=t14, scv=scv: e.tensor_tensor(dstA.rearrange("p t (h d) -> p t h d", h=2), t14, scv, ALU.mult),
                 reads=["rt1A", "xi", "zi"], writes=[dtok])

    def retB2(l, b, hf, j, last):
        par = j % 2
        Vb_ = Vbd[par]
        vtok = f"Vb{par}"
        szb = szd[par]
        P.op("dve", lambda e: e.tensor_copy(Gpat[:], bc(gt[:, j, :].rearrange("p (e o) -> p e o", o=1), [128, 64, NB])),
             reads=["gt"], writes=["Gpat"])
        P.op("dve", lambda e: e.memset(Gpat[:, :, 0:1], 0.0), reads=["Gpat"], writes=["Gpat"])
        for g4 in range(NB // 4):
            def fnT(e, g4=g4):
                ins = None
                for t4 in range(4):
                    tb = g4 * 4 + t4
                    e.transpose(sm1[:, t4 * 128:(t4 + 1) * 128], QtA[:, tb, :], identb[:])
                    ins = e.transpose(sm1[:, 512 + t4 * 128:512 + (t4 + 1) * 128], KtA[:, tb, :], identb[:])
                return ins
            P.op("pe", fnT, reads=["QtA", "KtA", "identb"], writes=["sm1"])
            copy_op("act", QTr[:, g4 * 512:(g4 + 1) * 512], sm1[:, 0:512], ["sm1"], ["QTr"])
            copy_op("dve", KTr[:, g4 * 512:(g4 + 1) * 512], sm1[:, 512:1024], ["sm1"], ["KTr"])

            def fnKV(e, g4=g4):
                ins = None
                for t4 in range(4):
                    tb = g4 * 4 + t4
                    ins = e.matmul(sm0[:, t4 * 128:(t4 + 1) * 128], KtA[:, tb, :], Vb_[:, tb, :], start=True, stop=True)
                return ins
            P.op("pe", fnKV, reads=["KtA", vtok], writes=["sm0"])
            for hh in range(2):
                rr_ = slice(hh * 64, (hh + 1) * 64)
                o_ = kvbuf[rr_, :, g4 * 4:(g4 + 1) * 4].rearrange("p e t -> p t e")
                i0_ = sm0[rr_, 0:512].rearrange("p (t c) -> p t c", t=4)[:, :, hh * 64:(hh + 1) * 64]
                i1_ = gt[rr_, j, :].unsqueeze(1).to_broadcast([64, 4, 64])
                P.op("dve", lambda e, o_=o_, i0_=i0_, i1_=i1_: e.tensor_tensor(o_, i0_, i1_, ALU.mult), reads=["sm0", "gt"], writes=["kvbuf"])
        P.op("dve", lambda e: e.tensor_tensor(ysb[:, 0:64], Scar[:, l, j, :], gt[:, j, :], ALU.mult), reads=[f"Scar{l}_{j}", "gt"], writes=["ysb"])
        P.op("dve", lambda e: e.tensor_tensor(kvbuf[:, :, 0], kvbuf[:, :, 0], ysb[:, 0:64], ALU.add), reads=["ysb", "kvbuf"], writes=["kvbuf"])
        P.op("dve", lambda e: e.tensor_tensor_scan(Sall[:].rearrange("p e n -> p (e n)"), Gpat[:].rearrange("p e n -> p (e n)"),
                                                   kvbuf[:].rearrange("p e n -> p (e n)"), 0.0, ALU.mult, ALU.add),
             reads=["kvbuf", "Gpat"], writes=["Sall"])
        copy_op("act", Sop[:, 0, :], Scar[:, l, j, :], [f"Scar{l}_{j}"], ["Sop"])
        copy_op("act", Sop[:, 1:NB, :], Sall[:, :, 0:NB - 1].rearrange("p e n -> p n e"), ["Sall"], ["Sop"])
        copy_op("dve", Scar[:, l, j, :], Sall[:, :, NB - 1], ["Sall", "Sop"], [f"Scar{l}_{j}"])
        if last:
            P.dma("sp", lambda e: e.dma_start(out=ro[l, b, 2 * j:2 * j + 2, :, :].rearrange("hh d e -> (hh d) e"), in_=Scar[:, l, j, :]),
                  reads=[f"Scar{l}_{j}"])
        Yh = [sm0[:, 0:512].rearrange("p (t e) -> p t e", t=NB), big[1][:, 512:1024].rearrange("p (t e) -> p t e", t=NB)]
        for g4 in range(NB // 4):
            def fnA(e, g4=g4):
                ins = None
                for t4 in range(4):
                    tb = g4 * 4 + t4
                    for hh in range(2):
                        pb = 64 * hh
                        ins = e.matmul(big[hh][:, t4 * 128:(t4 + 1) * 128], KTr[pb:pb + 64, tb * 128:(tb + 1) * 128],
                                       QTr[pb:pb + 64, tb * 128:(tb + 1) * 128], start=True, stop=True)
                return ins
            P.op("pe", fnA, reads=["KTr", "QTr"], writes=["big0", "big1"])
            for hh in range(2):
                o_ = AmA[:, :, hh, :]
                i0_ = big[hh][:, 0:512].rearrange("p (t i) -> p t i", t=4)
                i1_ = maskb[:].unsqueeze(1).to_broadcast([128, 4, 128])
                P.op("dve", lambda e, o_=o_, i0_=i0_, i1_=i1_: e.tensor_tensor(o_, i0_, i1_, ALU.mult), reads=[f"big{hh}", "maskb"], writes=["AmA"])

            def fnY(e, g4=g4):
                ins = None
                for t4 in range(4):
                    tb = g4 * 4 + t4
                    for hh in range(2):
                        pb = 64 * hh
                        e.matmul(Yh[hh][:, tb, :], AmA[:, t4, hh, :], Vb_[:, tb, hh * 64:(hh + 1) * 64], start=True, stop=False)
                        ins = e.matmul(Yh[hh][:, tb, :], QTr[pb:pb + 64, tb * 128:(tb + 1) * 128], Sop[pb:pb + 64, tb, :], start=False, stop=True)
                return ins
            P.op("pe", fnY, reads=["AmA", vtok, "QTr", "Sop"], writes=["sm0", "big1"])
        for hh in range(2):
            tk = "sm0" if hh == 0 else "big1"
            copy_op("act", ysbA[:, :, hh * 64:(hh + 1) * 64], Yh[hh], [tk, "rt1A"], ["rt1A"])
            P.op("act", lambda e, hh=hh: e.activation(ysqA[:, :, hh * 64:(hh + 1) * 64], Yh[hh], AF.Square), reads=[tk, "rt2A"], writes=["rt2A"])
        g = gstA
        y3 = ysbA.rearrange("p t (h d) -> p (t h) d", h=2)
        q3 = ysqA.rearrange("p t (h d) -> p (t h) d", h=2)
        P.op("dve", lambda e: e.reduce_sum(g[:, 0, :], y3, AX.X), reads=["rt1A"], writes=["gstA"])
        P.op("dve", lambda e: e.reduce_sum(g[:, 1, :], q3, AX.X), reads=["rt2A", "gstA"], writes=["gstA"])
        P.op("dve", lambda e: e.tensor_scalar(g[:, 0, :], g[:, 0, :], 1.0 / 64, None, ALU.mult), reads=["gstA"], writes=["gstA"])
        P.op("dve", lambda e: e.tensor_tensor(g[:, 2, :], g[:, 0, :], g[:, 0, :], ALU.mult), reads=["gstA"], writes=["gstA"])
        P.op("dve", lambda e: e.scalar_tensor_tensor(g[:, 3, :], g[:, 1, :], 1.0 / 64, g[:, 2, :], ALU.mult, ALU.subtract), reads=["gstA"], writes=["gstA"])
        P.op("dve", lambda e: e.tensor_scalar(g[:, 3, :], g[:, 3, :], LN_EPS, None, ALU.add), reads=["gstA"], writes=["gstA"])
        P.op("pool", lambda e: e.tensor_tensor(g[:, 4, :], g[:, 3, :], mhalf[:, 0:16], ALU.pow), reads=["gstA", "mhalf"], writes=["gstA"])
        P.op("dve", lambda e: e.tensor_tensor(y3, y3, g[:, 0, :].unsqueeze(2).to_broadcast([128, 16, 64]), ALU.subtract), reads=["gstA", "rt1A"], writes=["rt1A"])
        P.op("dve", lambda e: e.tensor_tensor(ynbA.rearrange("p t (h d) -> p (t h) d", h=2), y3,
                                              g[:, 4, :].unsqueeze(2).to_broadcast([128, 16, 64]), ALU.mult), reads=["gstA", "rt1A"], writes=["ynbA"])

        def fnT2(e):
            ins = None
            for tb in range(NB):
                ins = e.transpose(sm1[:, tb * 128:(tb + 1) * 128], ynbA[:, tb, :], identb[:])
            return ins
        P.op("pe", fnT2, reads=["ynbA", "identb"], writes=["sm1"])
        P.op("dve", lambda e: e.scalar_tensor_tensor(yg[1][:, j, :], sm1[:, 0:NT], prm[:, l, j:j + 1], szb[:, 0:NT], ALU.mult, ALU.mult),
             reads=["sm1", f"szr{par}"] + prm_all(l), writes=["yg1_0", "yg1_1"])

    def ret_job(l, b, hf, j, wis_ret, last):
        if j == 0:
            retA(l, b, hf, 0, wis_ret[0])
        retB1(l, b, hf, j)
        if j + 1 < 4:
            retA(l, b, hf, j + 1, wis_ret[j + 1])
        retB2(l, b, hf, j, last)

    def lru_job(l, b, hf, c, wi, last, woff=0):
        wt = wtoks(wi, 1)
        wv = wb[wi][:, woff:woff + 2048].rearrange("p (kc g c) -> p kc g c", kc=8, g=2)
        szc = szc_buf
        pl = prm_all(l)
        copy_op("dve", xrbuf[:, 0:3], convcar[:, l, c, :], [f"convcar{l}_{c}"], ["xrbuf"])
        for tt in range(NTT):
            ai = nxt("acc", 2)
            fm_mm(acc[ai], ai, lambda kc: wv[:, kc, 0, :], tt, wt)
            copy_op("act", xrbuf[:, 3 + tt * 512:3 + (tt + 1) * 512], acc[ai][:], [f"acc{ai}"], ["xrbuf"])
            yield
            ai = nxt("acc", 2)
            fm_mm(acc[ai], ai, lambda kc: wv[:, kc, 1, :], tt, wt)
            act_fn(szc[:, tt * 512:(tt + 1) * 512], acc[ai][:], AF.Silu, [f"acc{ai}"], ["szc"])
            yield
        P.op("dve", lambda e: e.tensor_scalar(xc[:], xrbuf[:, 0:NT], prm[:, l, 8 + c:9 + c], prm[:, l, 4 + c:5 + c], ALU.mult, ALU.add),
             reads=["xrbuf"] + pl, writes=["xc"])
        yield
        for tap in range(1, 4):
            P.op("dve", lambda e, tap=tap: e.scalar_tensor_tensor(xc[:], xrbuf[:, tap:tap + NT], prm[:, l, 8 + 4 * tap + c:9 + 4 * tap + c],
                                                                  xc[:], ALU.mult, ALU.add),
                 reads=["xrbuf", "xc"], writes=["xc"])
            yield
        copy_op("act", xcb[:], xc[:], ["xc"], ["xcb"])
        yield
        for tt in range(NTT):
            sl = slice(tt * 512, (tt + 1) * 512)
            ai = nxt("acc", 2)
            mm_group(acc[ai][:], f"acc{ai}", [(WgA[:, l, c, :], xcb[:, sl])], ["xcb"] + WgTok[l])
            act_fn(bA[:, sl], acc[ai][:], AF.Sigmoid, [f"acc{ai}", "bA"] + pl, ["bA"], bias=prm[:, l, 24 + c:25 + c])
            ai = nxt("acc", 2)
            mm_group(acc[ai][:], f"acc{ai}", [(WgX[:, l, c, :], xcb[:, sl])], ["xcb"] + WgTok[l])
            act_fn(bC[:, sl], acc[ai][:], AF.Sigmoid, [f"acc{ai}", "bC"] + pl, ["bC"], bias=prm[:, l, 28 + c:29 + c])
            yield
        act_fn(bB[:], bA[:], AF.Exp, ["bA"], ["bB"], scale=prm[:, l, 56 + c:57 + c])
        act_fn(bA[:], bA[:], AF.Exp, ["bA", "bB"], ["bA"], scale=prm[:, l, 36 + c:37 + c])
        yield
        act_fn(bB[:], bB[:], AF.Sqrt, ["bB"], ["bB"], scale=-1.0, bias=onesf[:, 0:1])
        yield
        P.op("dve", lambda e: e.tensor_tensor(bC[:], bC[:], bB[:], ALU.mult), reads=["bB", "bC"], writes=["bC"])
        yield
        P.op("dve", lambda e: e.tensor_tensor(bC[:], bC[:], xc[:], ALU.mult), reads=["xc", "bC"], writes=["bC"])
        yield
        P.op("dve", lambda e: e.tensor_tensor_scan(bB[:], bA[:], bC[:], hcar[:, l, c:c + 1], ALU.mult, ALU.add),
             reads=["bA", "bC", f"hcar{l}_{c}", "bB"], writes=["bB"])
        yield
        copy_op("dve", hcar[:, l, c:c + 1], bB[:, NT - 1:NT], ["bB"], [f"hcar{l}_{c}"])
        P.op("dve", lambda e: e.tensor_tensor(yg[2][:, c, :], bB[:], szc[:, 0:NT], ALU.mult),
             reads=["bB", "szc"], writes=["yg2_0", "yg2_1"])
        copy_op("dve", convcar[:, l, c, :], xrbuf[:, NT:NT + 3], ["xrbuf"], [f"convcar{l}_{c}"])
        if last:
            P.dma("sp", lambda e: e.dma_start(out=co[l, b, :, c * 128:(c + 1) * 128].rearrange("t p -> p t"), in_=convcar[:, l, c, :]),
                  reads=[f"convcar{l}_{c}"])
            P.dma("sp", lambda e: e.dma_start(out=lo[l, b, c * 128:(c + 1) * 128].rearrange("(p o) -> p o", o=1), in_=hcar[:, l, c:c + 1]),
                  reads=[f"hcar{l}_{c}"])

    def interleave(*gens):
        gens = list(gens)
        while gens:
            for g_ in list(gens):
                try:
                    next(g_)
                except StopIteration:
                    gens.remove(g_)

    def d1_job(l, mc, wi):
        wt = wtoks(wi, 2)
        wg = wb[wi][:, 0:3072].rearrange("p (kc g c) -> p kc g c", kc=8, g=3)
        wbr = wb[wi][:, 3072:4608].rearrange("p (kc g c) -> p kc g c", kc=4, g=3)
        for tt in range(CFG["ntt"]):
            for br in range(3):
                ai = nxt("acc", 2)
                fm_mm(acc[ai], ai, lambda kc, br=br: wg[:, kc, br, :], tt, wt)
                act_fn(cw(gbuf[br]), cw(acc[ai]), AF.Sigmoid, [f"acc{ai}"], [f"gbuf{br}"])
                ai = nxt("acc", 2)
                fm_mm(acc[ai], ai, lambda kc, br=br: wbr[:, kc, br, :], tt, wt, nkc=4, rhs_src=yg[br], rhs_toks=[f"yg{br}_{tt}"])
                P.op("dve", lambda e, o_=cw(tb3[br]), a_=cw(acc[ai]), g_=cw(gbuf[br]): e.tensor_tensor(o_, a_, g_, ALU.mult),
                     reads=[f"acc{ai}", f"gbuf{br}"], writes=[f"tb3_{br}"])
            P.op("dve", lambda e, a_=cw(tb3[0]), b_=cw(tb3[1]): e.tensor_tensor(a_, a_, b_, ALU.add), reads=["tb3_0", "tb3_1"], writes=["tb3_0"])
            P.op("dve", lambda e, o_=merged[:, mc, csl(tt)], a_=cw(tb3[0]), b_=cw(tb3[2]): e.tensor_tensor(o_, a_, b_, ALU.add),
                 reads=["tb3_0", "tb3_2"], writes=[f"mg{mc}_{tt}"])

    def d2_job(l, rc, wi):
        wt = wtoks(wi, 1)
        wo = wb[wi][:, 0:1024].rearrange("p (kc c) -> p kc c", kc=8)
        for tt in range(CFG["ntt"]):
            ai = nxt("acc", 2)
            fm_mm(acc[ai], ai, lambda kc: wo[:, kc, :], tt, wt, rhs_src=merged, rhs_toks=[f"mg{k}_{tt}" for k in range(8)])
            sl = csl(tt)
            P.op("dve", lambda e, a_=cw(acc[ai]), sl=sl: e.scalar_tensor_tensor(xT[:, rc, sl], xT[:, rc, sl], ALPHA, a_, ALU.mult, ALU.add),
                 reads=[f"acc{ai}"], writes=[f"xT{rc}_{tt}"])
            copy_op("act", xB[:, rc, sl], xT[:, rc, sl], [f"xT{rc}_{tt}"], [f"xB{rc}_{tt}"])

    def d3_job(l, rc, wi):
        wt = wtoks(wi, 2)
        wpg = wb[wi][:, 0:1024].rearrange("p (kc c) -> p kc c", kc=8)
        wpl = wb[wi][:, 1024:1280].rearrange("p (kc c) -> p kc c", kc=2)
        for tt in range(CFG["ntt"]):
            sl = csl(tt)
            ai = nxt("acc", 2)
            fm_mm(acc[ai], ai, lambda kc: wpg[:, kc, :], tt, wt)
            act_fn(cw(gbuf[0]), cw(acc[ai]), AF.Sigmoid, [f"acc{ai}"], ["gbuf0"])
            ai = nxt("acc", 2)
            fm_mm(acc[ai], ai, lambda kc: wpl[:, kc, :], tt, wt, nkc=2, rhs_src=pT2[CFG["pTpar"]], rhs_toks=[f"pT{CFG['pTpar']}_{tt}"])
            P.op("dve", lambda e, o_=cw(tb3[0]), a_=cw(acc[ai]), g_=cw(gbuf[0]): e.tensor_tensor(o_, a_, g_, ALU.mult), reads=[f"acc{ai}", "gbuf0"], writes=["tb3_0"])
            P.op("dve", lambda e, sl=sl, t_=cw(tb3[0]): e.tensor_tensor(xT[:, rc, sl], xT[:, rc, sl], t_, ALU.add), reads=["tb3_0"], writes=[f"xT{rc}_{tt}"])

    def ln_phase(l, b, hf, write_y):
        pl = prm_all(l)
        for tb in range(NB):
            tt = tb // 4
            bl = slice(tb * 128, (tb + 1) * 128)
            ai = nxt("acc", 2)
            pa = acc[ai]

            def fn(e, bl=bl, pa=pa):
                ins = None
                for rc in range(8):
                    e.matmul(pa[:, 0:128], xT[:, rc, bl], xT[:, rc, bl], start=(rc == 0), stop=(rc == 7))
                for rc in range(8):
                    ins = e.matmul(pa[:, 128:130], xT[:, rc, bl], onesf[:, 0:2], start=(rc == 0), stop=(rc == 7))
                return ins
            P.op("pe", fn, reads=xf_toks(tt) + ["onesf"], writes=[f"acc{ai}"])
            P.op("dve", lambda e, pa=pa: e.tensor_tensor(lntmp[:], pa[:, 0:128], identf[:], ALU.mult), reads=[f"acc{ai}", "identf"], writes=["lntmp"])
            P.op("dve", lambda e, tb=tb: e.reduce_sum(lnst[:, NB + tb:NB + tb + 1], lntmp[:], AX.X), reads=["lntmp", "lnst"], writes=["lnst"])
            copy_op("dve", lnst[:, tb:tb + 1], pa[:, 128:129], [f"acc{ai}", "lnst"], ["lnst"])
        s = lnst
        A0, A1, A2, A3, A4, A5 = [slice(k * NB, (k + 1) * NB) for k in range(6)]
        P.op("dve", lambda e: e.tensor_scalar(s[:, A0], s[:, A0], 1.0 / D, None, ALU.mult), reads=["lnst"], writes=["lnst"])
        P.op("dve", lambda e: e.tensor_tensor(s[:, A2], s[:, A0], s[:, A0], ALU.mult), reads=["lnst"], writes=["lnst"])
        P.op("dve", lambda e: e.scalar_tensor_tensor(s[:, A3], s[:, A1], 1.0 / D, s[:, A2], ALU.mult, ALU.subtract), reads=["lnst"], writes=["lnst"])
        P.op("dve", lambda e: e.tensor_scalar(s[:, A3], s[:, A3], LN_EPS, None, ALU.add), reads=["lnst"], writes=["lnst"])
        P.op("pool", lambda e: e.tensor_tensor(s[:, A4], s[:, A3], mhalf[:, 0:NB], ALU.pow), reads=["lnst", "mhalf"], writes=["lnst"])
        P.op("dve", lambda e: e.scalar_tensor_tensor(s[:, A5], s[:, A0], -1.0, s[:, A4], ALU.mult, ALU.mult), reads=["lnst"], writes=["lnst"])
        dAB = [(dA, dB), (lntmp, lnt2)]
        for tb in range(NB):
            tt = tb // 4
            bl = slice(tb * 128, (tb + 1) * 128)
            ai = nxt("acc", 2)
            bcv = acc[ai][:, 0:256].rearrange("p (a t) -> p a t", a=2)
            dA_, dB_ = dAB[tb % 2]
            tkA, tkB = ("dA", "dB") if tb % 2 == 0 else ("lntmp", "lnt2")
            P.op("dve", lambda e, tb=tb, dA_=dA_: e.tensor_scalar(dA_[:], identf[:], s[:, 4 * NB + tb:4 * NB + tb + 1], None, ALU.mult), reads=["lnst", "identf"], writes=[tkA])
            P.op("dve", lambda e, tb=tb, dB_=dB_: e.tensor_scalar(dB_[:], identf[:], s[:, 5 * NB + tb:5 * NB + tb + 1], None, ALU.mult), reads=["lnst", "identf"], writes=[tkB])

            def fn(e, bcv=bcv, dA_=dA_, dB_=dB_):
                e.matmul(bcv[:, 0, :], onesf[:], dA_[:], start=True, stop=True)
                return e.matmul(bcv[:, 1, :], onesf[:], dB_[:], start=True, stop=True)
            P.op("pe", fn, reads=[tkA, tkB, "onesf"], writes=[f"acc{ai}"])
            P.op("dve", lambda e, bl=bl, bcv=bcv: e.tensor_tensor(lnt[:], xT[:, :, bl], bc(bcv[:, 0:1, :], [128, 8, 128]), ALU.mult),
                 reads=[f"acc{ai}"] + xf_toks(tt), writes=["lnt"])
            P.op("dve", lambda e, bcv=bcv: e.tensor_tensor(lnt[:], lnt[:], bc(bcv[:, 1:2, :], [128, 8, 128]), ALU.add), reads=[f"acc{ai}", "lnt"], writes=["lnt"])
            P.op("dve", lambda e: e.tensor_tensor(lnt[:], lnt[:], bc(prm[:, l, 40:48].rearrange("p (c o) -> p c o", o=1), [128, 8, 128]), ALU.mult),
                 reads=["lnt"] + pl, writes=["lnt"])
            P.op("dve", lambda e, bl=bl: e.tensor_tensor(xT[:, :, bl], lnt[:], bc(prm[:, l, 48:56].rearrange("p (c o) -> p c o", o=1), [128, 8, 128]), ALU.add),
                 reads=["lnt"] + pl, writes=xf_toks(tt))
            P.op("act", lambda e, bl=bl: e.copy(xB[:, :, bl], xT[:, :, bl]), reads=xf_toks(tt), writes=xb_toks(tt))
            if write_y:
                yi_ = nxt("yst", 2)
                for hb in range(2):
                    ai = nxt("acc", 2)
                    av = acc[ai][:].rearrange("p (c t) -> p c t", c=4)

                    def fnT(e, bl=bl, av=av, hb=hb):
                        ins = None
                        for c in range(4):
                            ins = e.transpose(av[:, c, :], xT[:, hb * 4 + c, bl], identf[:])
                        return ins
                    P.op("pe", fnT, reads=xf_toks(tt) + ["identf"], writes=[f"acc{ai}"])
                    copy_op("act" if hb == 0 else "dve", yst[yi_][:, hb * 512:(hb + 1) * 512], acc[ai][:], [f"acc{ai}", f"yst{yi_}"], [f"yst{yi_}"])
                r0 = hf * NT + tb * 128
                P.dma("sp", lambda e, yi_=yi_, r0=r0: e.dma_start(out=yo[b, r0:r0 + 128, :], in_=yst[yi_][:]), reads=[f"yst{yi_}"])

    def s_load(l):
        if l == 0:
            P.dma("sp", lambda e: e.dma_start(out=xin[0][0:64, :], in_=xs_d.rearrange("s t d -> (s t) d")), writes=["xin0"])
            for hb in range(2):
                sv = sm0[:, 0:256].rearrange("p (c t) -> p c t", c=4)

                def fn(e, sv=sv, hb=hb):
                    ins = None
                    for c in range(4):
                        cg = hb * 4 + c
                        ins = e.transpose(sv[:, c, :], xin[0][0:64, cg * 128:(cg + 1) * 128], identf[0:64, 0:64])
                    return ins
                P.op("pe", fn, reads=["xin0", "identf"], writes=["sm0"])
                cs = slice(hb * 4, hb * 4 + 4)
                P.op("act", lambda e, sv=sv, cs=cs: e.copy(xT[:, cs, 0:64], sv), reads=["sm0"], writes=xf_toks(0))
                P.op("act", lambda e, sv=sv, cs=cs: e.copy(xB[:, cs, 0:64], sv), reads=["sm0"], writes=xb_toks(0))
        P.dma("sp", lambda e: e.dma_start(out=pin[0][0:64, :], in_=ps_d[l].rearrange("s t d -> (s t) d")), writes=["pin0"])
        sv2 = sm0[:, 0:128].rearrange("p (c t) -> p c t", c=2)

        def fn2(e):
            ins = None
            for c in range(2):
                ins = e.transpose(sv2[:, c, :], pin[0][0:64, c * 128:(c + 1) * 128], identf[0:64, 0:64])
            return ins
        P.op("pe", fn2, reads=["pin0", "identf"], writes=["sm0"])
        copy_op("act", pT2[0][:, :, 0:64], sv2, ["sm0"], ["pT0_0"])

    def s_att_job(l, j, wi):
        wt = wtoks(wi, 1)
        wv = wb[wi][:, 0:4096].rearrange("p (kc g c) -> p kc g c", kc=8, g=4)
        P.op("dve", lambda e: e.memset(Vv[:, :, :, 64:65], 1.0), writes=["Vv"])
        ai = nxt("acc", 2)
        fm_mm(acc[ai], ai, lambda kc: wv[:, kc, 0, :], 0, wt)
        copy_op("act", QT[:, 0:64], acc[ai][:, 0:64], [f"acc{ai}"], ["QT"], scale=0.125)
        ai = nxt("acc", 2)
        fm_mm(acc[ai], ai, lambda kc: wv[:, kc, 1, :], 0, wt)
        copy_op("dve", KT[:, 512:576], acc[ai][:, 0:64], [f"acc{ai}"], ["KTn"])
        ai = nxt("acc", 2)
        fm_mm(acc[ai], ai, lambda kc: wv[:, kc, 3, :], 0, wt)
        act_fn(sz[:, 0:64], acc[ai][:, 0:64], AF.Silu, [f"acc{ai}"], ["sz"])
        Ov = sm0[:, 0:130].rearrange("p (h d) -> p h d", h=2)
        for s in range(NSMP):
            cs_ = slice(s * 32, (s + 1) * 32)
            P.dma("pool", lambda e, s=s: e.dma_start(out=kctm[:], in_=ck_d[l, s, :, 2 * j:2 * j + 2, :].rearrange("(a p) h d -> p a (h d)", p=128)),
                  writes=["kctm"])

            def fnk(e):
                ins = None
                for a in range(4):
                    ins = e.transpose(sm1[:, a * 128:(a + 1) * 128], kctm[:, a, :], identb[:])
                return ins
            P.op("pe", fnk, reads=["kctm", "identb"], writes=["sm1"])
            copy_op("act", KT[:, 0:512], sm1[:, 0:512], ["sm1"], ["KT"])
            for hh in range(2):
                P.dma("pool", lambda e, s=s, hh=hh: e.dma_start(out=Vv[:, 0:4, hh, 0:64],
                                                                 in_=cv_d[l, s, :, 2 * j + hh, :].rearrange("(a p) d -> p a d", p=128)),
                      reads=["Vv"], writes=[f"Vvc{hh}"])
            ai = nxt("acc", 2)
            pairs = [(xB[:, kc, cs_], wv[:, kc, 2, :]) for kc in range(8)]
            mm_group(acc[ai][0:32, 0:128], f"acc{ai}", pairs, xb_toks(0) + wt)
            copy_op("dve", Vv[0:32, 4, :, 0:64], acc[ai][0:32, 0:128].rearrange("p (h d) -> p h d", h=2), [f"acc{ai}", "Vv"], ["Vvn"])
            si = nxt("kvst", 2)
            copy_op("act", kvst[si][0:32, :], acc[ai][0:32, 0:128], [f"acc{ai}"], [f"kvst{si}"])
            P.dma("sp", lambda e, si=si, s=s: e.dma_start(out=vso[l, s, :, 2 * j:2 * j + 2, :], in_=kvst[si][0:32, :].rearrange("p (h d) -> p h d", h=2)),
                  reads=[f"kvst{si}"])
            ai = nxt("acc", 2)
            pairs = [(xB[:, kc, cs_], wv[:, kc, 1, :]) for kc in range(8)]
            mm_group(acc[ai][0:32, 0:128], f"acc{ai}", pairs, xb_toks(0) + wt)
            si = nxt("kvst", 2)
            copy_op("act", kvst[si][0:32, :], acc[ai][0:32, 0:128], [f"acc{ai}"], [f"kvst{si}"])
            P.dma("sp", lambda e, si=si, s=s: e.dma_start(out=kso[l, s, :, 2 * j:2 * j + 2, :], in_=kvst[si][0:32, :].rearrange("p (h d) -> p h d", h=2)),
                  reads=[f"kvst{si}"])
            for hh in range(2):
                h = 2 * j + hh
                pb = 64 * hh
                STv = big[hh][:, 0:640].rearrange("p (a q) -> p a q", a=5)

                def fn(e, STv=STv, pb=pb, s=s):
                    e.matmul(STv[0:32, 0, 0:32], KT[pb:pb + 64, 512 + s * 32:512 + (s + 1) * 32], QT[pb:pb + 64, s * 32:(s + 1) * 32], start=True, stop=True)
                    ins = None
                    for jp in range(1, 5):
                        kb = 4 - jp
                        ins = e.matmul(STv[:, jp, 0:32], KT[pb:pb + 64, kb * 128:(kb + 1) * 128], QT[pb:pb + 64, s * 32:(s + 1) * 32], start=True, stop=True)
                    return ins
                P.op("pe", fn, reads=["KT", "KTn", "QT"], writes=[f"big{hh}"])
                act_fn(Eb[hh][0:32, 0, 0:32], STv[0:32, 0, 0:32], AF.Exp, [f"big{hh}"], [f"Eb{hh}"])
                act_fn(Eb[hh][:, 1:4, 0:32], STv[:, 1:4, 0:32], AF.Exp, [f"big{hh}", f"Eb{hh}"], [f"Eb{hh}"])
                act_fn(Eb[hh][:, 4:5, 0:32], STv[:, 4:5, 0:32], AF.Exp, [f"big{hh}", f"Eb{hh}"], [f"Eb{hh}"])
                P.op("dve", lambda e, hh=hh, h=h: e.tensor_tensor(PTb[hh][0:32, 0, 0:32], Eb[hh][0:32, 0, 0:32], EBl[l][0:32, h, 0, 0:32], ALU.mult),
                     reads=[f"Eb{hh}", "EB"], writes=[f"PT{hh}"])
                P.op("dve", lambda e, hh=hh, h=h: e.tensor_tensor(PTb[hh][:, 1:5, 0:32], Eb[hh][:, 1:5, 0:32], EBl[l][:, h, 1:5, 0:32], ALU.mult),
                     reads=[f"Eb{hh}", "EB", f"PT{hh}"], writes=[f"PT{hh}"])

                def fn2(e, hh=hh):
                    e.matmul(Ov[0:32, hh, :], PTb[hh][0:32, 0, 0:32], Vv[0:32, 4, hh, :], start=True, stop=False)
                    ins = None
                    for jp in range(1, 5):
                        ins = e.matmul(Ov[0:32, hh, :], PTb[hh][:, jp, 0:32], Vv[:, 4 - jp, hh, :], start=False, stop=(jp == 4))
                    return ins
                P.op("pe", fn2, reads=[f"PT{hh}", "Vv", "Vvc0", "Vvc1", "Vvn"], writes=["sm0"])
            P.op("dve", lambda e: e.reciprocal(rcp[0:32, :].rearrange("p (h o) -> p h o", o=1), Ov[0:32, :, 64:65]), reads=["sm0"], writes=["rcp"])
            P.op("dve", lambda e: e.tensor_tensor(ya[0:32, :].rearrange("p (h d) -> p h d", h=2), Ov[0:32, :, 0:64],
                                                  bc(rcp[0:32, :].rearrange("p (h o) -> p h o", o=1), [32, 2, 64]), ALU.mult),
                 reads=["sm0", "rcp"], writes=["ya"])
            P.op("pe", lambda e: e.transpose(sm1[:, 0:32], ya[0:32, :], identb[0:32, 0:32]), reads=["ya", "identb"], writes=["sm1"])
            P.op("dve", lambda e, cs_=cs_: e.tensor_tensor(yg[0][:, j, cs_], sm1[:, 0:32], sz[:, cs_], ALU.mult),
                 reads=["sm1", "sz"], writes=["yg0_0"])

    def s_ret_job(l, j, wi):
        wt = wtoks(wi, 1)
        wv = wb[wi][:, 0:4096].rearrange("p (kc g c) -> p kc g c", kc=8, g=4)
        R = slice(0, 32)
        ai = nxt("acc", 2)
        fm_mm(acc[ai], ai, lambda kc: wv[:, kc, 3, :], 0, wt)
        act_fn(sz[:, 0:64], acc[ai][:, 0:64], AF.Silu, [f"acc{ai}"], ["sz"])
        Ah = [big[0][0:32, 0:32], big[1][0:32, 0:32]]
        Yh = [sm0[0:32, 0:64], big[1][0:32, 512:576]]
        gblk = 8
        for s in range(NSMP):
            cs_ = slice(s * 32, (s + 1) * 32)
            P.dma("sp", lambda e, s=s: e.dma_start(out=S0f[:], in_=sr_d[l, s, 2 * j:2 * j + 2, :, :].rearrange("hh d e -> (hh d) e")), writes=["S0f"])
            copy_op("act", Sop[:, 0, :], S0f[:], ["S0f"], ["Sop"])
            ai = nxt("acc", 2)
            pairs = [(xB[:, kc, cs_], wv[:, kc, 0:3, :]) for kc in range(8)]
            mm_group(acc[ai][R, 0:384], f"acc{ai}", pairs, xb_toks(0) + wt)
            at = f"acc{ai}"
            for qi, (dst, sc_) in enumerate(((Qt, xi), (Kt, zi))):
                X = acc[ai][R, qi * 128:(qi + 1) * 128].rearrange("p (h two d) -> p h two d", h=2, two=2)
                ccv = bc(cc[R, gblk, :].rearrange("p (o d) -> p o d", o=1), [32, 2, 64])
                t1v = rt1[R, :].rearrange("p (h d) -> p h d", h=2)
                t2v = rt2[R, :].rearrange("p (h two d) -> p h two d", h=2, two=2)
                P.op("dve", lambda e, ai=ai, qi=qi, ccv=ccv, t1v=t1v: e.tensor_tensor(
                    t1v, acc[ai][R, qi * 128:(qi + 1) * 128].rearrange("p (h d) -> p h d", h=2), ccv, ALU.mult),
                    reads=[at, "cc"], writes=["rt1"])
                for hv in range(2):
                    ssv = bc(ss[R, gblk, hv * 32:(hv + 1) * 32].rearrange("p (o d) -> p o d", o=1), [32, 2, 32])
                    P.op("dve", lambda e, X=X, hv=hv, ssv=ssv, t2v=t2v: e.tensor_tensor(t2v[:, :, hv, :], X[:, :, 1 - hv, :], ssv, ALU.mult),
                         reads=[at, "ss"], writes=["rt2"])
                P.op("dve", lambda e: e.tensor_tensor(rt1[R, :], rt1[R, :], rt2[R, :], ALU.add), reads=["rt1", "rt2"], writes=["rt1"])
                scv = bc(sc_[R, 2 * j:2 * j + 2].rearrange("p (h o) -> p h o", o=1), [32, 2, 64])
                P.op("dve", lambda e, dst=dst, scv=scv, t1v=t1v: e.tensor_tensor(dst[R, :].rearrange("p (h d) -> p h d", h=2), t1v, scv, ALU.mult),
                     reads=["rt1", "xi", "zi"], writes=["Qt" if qi == 0 else "Kt"])
            copy_op("dve", Vb[R, 0, :], acc[ai][R, 256:384], [at], ["Vb"])
            P.op("pe", lambda e: e.transpose(sm1[:, 0:32], Qt[R, :], identb[0:32, 0:32]), reads=["Qt", "identb"], writes=["sm1"])
            copy_op("act", QTr[:, cs_], sm1[:, 0:32], ["sm1"], ["QTr"])
            P.op("pe", lambda e: e.transpose(sm1[:, 128:160], Kt[R, :], identb[0:32, 0:32]), reads=["Kt", "identb"], writes=["sm1"])
            copy_op("dve", KTr[:, cs_], sm1[:, 128:160], ["sm1"], ["KTr"])
            P.op("pe", lambda e: e.matmul(sm0[:, 0:128], Kt[R, :], Vb[R, 0, :], start=True, stop=True), reads=["Kt", "Vb"], writes=["sm0"])
            for hh in range(2):
                rr_ = slice(hh * 64, (hh + 1) * 64)
                P.op("dve", lambda e, hh=hh, rr_=rr_: e.tensor_tensor(Snew[rr_, :], sm0[rr_, hh * 64:(hh + 1) * 64], S0f[rr_, :], ALU.add),
                     reads=["sm0", "S0f", "Snew"], writes=["Snew"])
                P.op("dve", lambda e, rr_=rr_: e.tensor_tensor(Snew[rr_, :], Snew[rr_, :], gt32[rr_, j, :], ALU.mult),
                     reads=["Snew", "gt32"], writes=["Snew"])
            P.dma("sp", lambda e, s=s: e.dma_start(out=rso[l, s, 2 * j:2 * j + 2, :, :].rearrange("hh d e -> (hh d) e"), in_=Snew[:]), reads=["Snew"])

            def fnA(e, cs_=cs_):
                ins = None
                for hh in range(2):
                    pb = 64 * hh
                    ins = e.matmul(Ah[hh], KTr[pb:pb + 64, cs_], QTr[pb:pb + 64, cs_], start=True, stop=True)
                return ins
            P.op("pe", fnA, reads=["KTr", "QTr"], writes=["big0", "big1"])
            for hh in range(2):
                P.op("dve", lambda e, hh=hh: e.tensor_tensor(Am[R, hh, 0:32], Ah[hh], maskb[0:32, 0:32], ALU.mult),
                     reads=[f"big{hh}", "maskb"], writes=["Am"])

            def fnY(e, cs_=cs_):
                ins = None
                for hh in range(2):
                    pb = 64 * hh
                    e.matmul(Yh[hh], Am[R, hh, 0:32], Vb[R, 0, hh * 64:(hh + 1) * 64], start=True, stop=False)
                    ins = e.matmul(Yh[hh], QTr[pb:pb + 64, cs_], Sop[pb:pb + 64, 0, :], start=False, stop=True)
                return ins
            P.op("pe", fnY, reads=["Am", "Vb", "QTr", "Sop"], writes=["sm0", "big1"])
            for hh in range(2):
                tk = "sm0" if hh == 0 else "big1"
                copy_op("act", ysb[R, hh * 64:(hh + 1) * 64], Yh[hh], [tk, "ysb"], ["ysb"])
                P.op("act", lambda e, hh=hh: e.activation(ysq[R, hh * 64:(hh + 1) * 64], Yh[hh], AF.Square), reads=[tk, "ysq"], writes=["ysq"])
            g = gst
            yv3 = ysb[R, :].rearrange("p (h d) -> p h d", h=2)
            P.op("dve", lambda e: e.reduce_sum(g[R, 0:2], yv3, AX.X), reads=["ysb"], writes=["gst"])
            P.op("dve", lambda e: e.reduce_sum(g[R, 2:4], ysq[R, :].rearrange("p (h d) -> p h d", h=2), AX.X), reads=["ysq", "gst"], writes=["gst"])
            P.op("dve", lambda e: e.tensor_scalar(g[R, 0:2], g[R, 0:2], 1.0 / 64, None, ALU.mult), reads=["gst"], writes=["gst"])
            P.op("dve", lambda e: e.tensor_tensor(g[R, 4:6], g[R, 0:2], g[R, 0:2], ALU.mult), reads=["gst"], writes=["gst"])
            P.op("dve", lambda e: e.scalar_tensor_tensor(g[R, 6:8], g[R, 2:4], 1.0 / 64, g[R, 4:6], ALU.mult, ALU.subtract), reads=["gst"], writes=["gst"])
            P.op("dve", lambda e: e.tensor_scalar(g[R, 6:8], g[R, 6:8], LN_EPS, None, ALU.add), reads=["gst"], writes=["gst"])
            P.op("pool", lambda e: e.tensor_tensor(g[R, 8:10], g[R, 6:8], mhalf[R, 0:2], ALU.pow), reads=["gst", "mhalf"], writes=["gst"])
            P.op("dve", lambda e: e.tensor_tensor(yv3, yv3, bc(g[R, 0:2].rearrange("p (h o) -> p h o", o=1), [32, 2, 64]), ALU.subtract),
                 reads=["gst", "ysb"], writes=["ysb"])
            P.op("dve", lambda e: e.tensor_tensor(ynb[R, :].rearrange("p (h d) -> p h d", h=2), yv3,
                                                  bc(g[R, 8:10].rearrange("p (h o) -> p h o", o=1), [32, 2, 64]), ALU.mult),
                 reads=["gst", "ysb"], writes=["ynb"])
            P.op("pe", lambda e: e.transpose(sm1[:, 0:32], ynb[R, :], identb[0:32, 0:32]), reads=["ynb", "identb"], writes=["sm1"])
            P.op("dve", lambda e, cs_=cs_: e.scalar_tensor_tensor(yg[1][:, j, cs_], sm1[:, 0:32], prm[:, l, j:j + 1], sz[:, cs_], ALU.mult, ALU.mult),
                 reads=["sm1", "sz"] + prm_all(l), writes=["yg1_0"])

    def s_lru_job(l, c, wi):
        wt = wtoks(wi, 1)
        wv = wb[wi][:, 0:2048].rearrange("p (kc g c) -> p kc g c", kc=8, g=2)
        pl = prm_all(l)
        ai = nxt("acc", 2)
        fm_mm(acc[ai], ai, lambda kc: wv[:, kc, 1, :], 0, wt)
        act_fn(sz[:, 0:64], acc[ai][:, 0:64], AF.Silu, [f"acc{ai}"], ["sz"])
        Wd = slice(0, 32)
        for s in range(NSMP):
            cs_ = slice(s * 32, (s + 1) * 32)
            P.dma("sp", lambda e, s=s: e.dma_start(out=xrbuf[:, 0:3], in_=sc_d[l, s, :, c * 128:(c + 1) * 128].rearrange("t p -> p t")), writes=["xrbuf"])
            P.dma("sp", lambda e, s=s: e.dma_start(out=hcar[:, l, c:c + 1], in_=sl_d[l, s, c * 128:(c + 1) * 128].rearrange("(p o) -> p o", o=1)),
                  writes=[f"hcar{l}_{c}"])
            ai = nxt("acc", 2)
            pairs = [(wv[:, kc, 0, :], xB[:, kc, cs_]) for kc in range(8)]
            mm_group(acc[ai][:, 0:32], f"acc{ai}", pairs, xb_toks(0) + wt)
            copy_op("act", xrbuf[:, 3:35], acc[ai][:, 0:32], [f"acc{ai}", "xrbuf"], ["xrbuf"])
            P.op("dve", lambda e: e.tensor_scalar(xc[:, Wd], xrbuf[:, 0:32], prm[:, l, 8 + c:9 + c], prm[:, l, 4 + c:5 + c], ALU.mult, ALU.add),
                 reads=["xrbuf"] + pl, writes=["xc"])
            for tap in range(1, 4):
                P.op("dve", lambda e, tap=tap: e.scalar_tensor_tensor(xc[:, Wd], xrbuf[:, tap:tap + 32], prm[:, l, 8 + 4 * tap + c:9 + 4 * tap + c],
                                                                      xc[:, Wd], ALU.mult, ALU.add),
                     reads=["xrbuf", "xc"], writes=["xc"])
            copy_op("act", xcb[:, Wd], xc[:, Wd], ["xc"], ["xcb"])
            ai = nxt("acc", 2)
            mm_group(acc[ai][:, 0:32], f"acc{ai}", [(WgA[:, l, c, :], xcb[:, Wd])], ["xcb"] + WgTok[l])
            act_fn(bA[:, Wd], acc[ai][:, 0:32], AF.Sigmoid, [f"acc{ai}"] + pl, ["bA"], bias=prm[:, l, 24 + c:25 + c])
            ai = nxt("acc", 2)
            mm_group(acc[ai][:, 0:32], f"acc{ai}", [(WgX[:, l, c, :], xcb[:, Wd])], ["xcb"] + WgTok[l])
            act_fn(bC[:, Wd], acc[ai][:, 0:32], AF.Sigmoid, [f"acc{ai}"] + pl, ["bC"], bias=prm[:, l, 28 + c:29 + c])
            act_fn(bB[:, Wd], bA[:, Wd], AF.Exp, ["bA"], ["bB"], scale=prm[:, l, 56 + c:57 + c])
            act_fn(bA[:, Wd], bA[:, Wd], AF.Exp, ["bA", "bB"], ["bA"], scale=prm[:, l, 36 + c:37 + c])
            act_fn(bB[:, Wd], bB[:, Wd], AF.Sqrt, ["bB"], ["bB"], scale=-1.0, bias=onesf[:, 0:1])
            P.op("dve", lambda e: e.tensor_tensor(bC[:, Wd], bC[:, Wd], bB[:, Wd], ALU.mult), reads=["bB", "bC"], writes=["bC"])
            P.op("dve", lambda e: e.tensor_tensor(bC[:, Wd], bC[:, Wd], xc[:, Wd], ALU.mult), reads=["xc", "bC"], writes=["bC"])
            P.op("dve", lambda e: e.tensor_tensor_scan(bB[:, Wd], bA[:, Wd], bC[:, Wd], hcar[:, l, c:c + 1], ALU.mult, ALU.add),
                 reads=["bA", "bC", f"hcar{l}_{c}", "bB"], writes=["bB"])
            P.dma("sp", lambda e, s=s: e.dma_start(out=lso[l, s, c * 128:(c + 1) * 128].rearrange("(p o) -> p o", o=1), in_=bB[:, 31:32]), reads=["bB"])
            P.dma("sp", lambda e, s=s: e.dma_start(out=cso[l, s, :, c * 128:(c + 1) * 128].rearrange("t p -> p t"), in_=xrbuf[:, 32:35]), reads=["xrbuf"])
            P.op("dve", lambda e, cs_=cs_: e.tensor_tensor(yg[2][:, c, cs_], bB[:, Wd], sz[:, cs_], ALU.mult),
                 reads=["bB", "sz"], writes=["yg2_0"])

    def s_ln(l, write_y):
        pl = prm_all(l)
        R = slice(0, 64)
        bl = slice(0, 64)

        def fn(e):
            ins = None
            for rc in range(8):
                e.matmul(sm0[R, 0:64], xT[:, rc, bl], xT[:, rc, bl], start=(rc == 0), stop=(rc == 7))
            for rc in range(8):
                ins = e.matmul(sm0[R, 128:130], xT[:, rc, bl], onesf[:, 0:2], start=(rc == 0), stop=(rc == 7))
            return ins
        P.op("pe", fn, reads=xf_toks(0) + ["onesf"], writes=["sm0"])
        s_ = lnst
        P.op("dve", lambda e: e.tensor_tensor(lntmp[R, 0:64], sm0[R, 0:64], identf[R, 0:64], ALU.mult), reads=["sm0", "identf"], writes=["lntmp"])
        P.op("dve", lambda e: e.reduce_sum(s_[R, 1:2], lntmp[R, 0:64], AX.X), reads=["lntmp", "lnst"], writes=["lnst"])
        copy_op("dve", s_[R, 0:1], sm0[R, 128:129], ["sm0", "lnst"], ["lnst"])
        P.op("dve", lambda e: e.tensor_scalar(s_[R, 0:1], s_[R, 0:1], 1.0 / D, None, ALU.mult), reads=["lnst"], writes=["lnst"])
        P.op("dve", lambda e: e.tensor_tensor(s_[R, 2:3], s_[R, 0:1], s_[R, 0:1], ALU.mult), reads=["lnst"], writes=["lnst"])
        P.op("dve", lambda e: e.scalar_tensor_tensor(s_[R, 3:4], s_[R, 1:2], 1.0 / D, s_[R, 2:3], ALU.mult, ALU.subtract), reads=["lnst"], writes=["lnst"])
        P.op("dve", lambda e: e.tensor_scalar(s_[R, 3:4], s_[R, 3:4], LN_EPS, None, ALU.add), reads=["lnst"], writes=["lnst"])
        P.op("pool", lambda e: e.tensor_tensor(s_[R, 4:5], s_[R, 3:4], mhalf[R, 0:1], ALU.pow), reads=["lnst", "mhalf"], writes=["lnst"])
        P.op("dve", lambda e: e.scalar_tensor_tensor(s_[R, 5:6], s_[R, 0:1], -1.0, s_[R, 4:5], ALU.mult, ALU.mult), reads=["lnst"], writes=["lnst"])
        P.op("dve", lambda e: e.tensor_scalar(dA[R, 0:64], identf[R, 0:64], s_[R, 4:5], None, ALU.mult), reads=["lnst", "identf"], writes=["dA"])
        P.op("dve", lambda e: e.tensor_scalar(dB[R, 0:64], identf[R, 0:64], s_[R, 5:6], None, ALU.mult), reads=["lnst", "identf"], writes=["dB"])
        bcv = sm0[:, 0:128].rearrange("p (a t) -> p a t", a=2)

        def fnb(e):
            e.matmul(bcv[:, 0, :], onesf[R, :], dA[R, 0:64], start=True, stop=True)
            return e.matmul(bcv[:, 1, :], onesf[R, :], dB[R, 0:64], start=True, stop=True)
        P.op("pe", fnb, reads=["dA", "dB", "onesf"], writes=["sm0"])
        lv = lnt[:, :, 0:64]
        P.op("dve", lambda e: e.tensor_tensor(lv, xT[:, :, bl], bc(bcv[:, 0:1, :], [128, 8, 64]), ALU.mult), reads=["sm0"] + xf_toks(0), writes=["lnt"])
        P.op("dve", lambda e: e.tensor_tensor(lv, lv, bc(bcv[:, 1:2, :], [128, 8, 64]), ALU.add), reads=["sm0", "lnt"], writes=["lnt"])
        P.op("dve", lambda e: e.tensor_tensor(lv, lv, bc(prm[:, l, 40:48].rearrange("p (c o) -> p c o", o=1), [128, 8, 64]), ALU.mult),
             reads=["lnt"] + pl, writes=["lnt"])
        P.op("dve", lambda e: e.tensor_tensor(xT[:, :, bl], lv, bc(prm[:, l, 48:56].rearrange("p (c o) -> p c o", o=1), [128, 8, 64]), ALU.add),
             reads=["lnt"] + pl, writes=xf_toks(0))
        P.op("act", lambda e: e.copy(xB[:, :, bl], xT[:, :, bl]), reads=xf_toks(0), writes=xb_toks(0))
        if write_y:
            for hb in range(2):
                ai = nxt("acc", 2)
                av = acc[ai][R, :].rearrange("p (c t) -> p c t", c=4)

                def fnT(e, av=av, hb=hb):
                    ins = None
                    for c in range(4):
                        ins = e.transpose(av[:, c, :], xT[:, hb * 4 + c, bl], identf[:])
                    return ins
                P.op("pe", fnT, reads=xf_toks(0) + ["identf"], writes=[f"acc{ai}"])
                copy_op("act" if hb == 0 else "dve", yst[0][R, hb * 512:(hb + 1) * 512], acc[ai][R, :], [f"acc{ai}", "yst0"], ["yst0"])
            P.dma("sp", lambda e: e.dma_start(out=yso.rearrange("s t d -> (s t) d"), in_=yst[0][R, :]), reads=["yst0"])

    def wsrc_att(l, j):
        v = w_in[l].rearrange("(kc p) (g jj c) -> p kc g jj c", p=128, g=16, jj=4)
        return v[:, :, 0:4, j, :]

    def wsrc_ret(l, j):
        v = w_in[l].rearrange("(kc p) (g jj c) -> p kc g jj c", p=128, g=16, jj=4)
        return v[:, :, 4:8, j, :]

    def wsrc_lru(l, c):
        v = w_in[l].rearrange("(kc p) (g jj c) -> p kc g jj c", p=128, g=16, jj=4)
        return v[:, :, 8:10, c, :]

    def wsrc_gate(l, mc):
        v = w_in[l].rearrange("(kc p) (g mm c) -> p kc g mm c", p=128, g=8, mm=8)
        return v[:, :, 5:8, mc, :]

    def wsrc_br(l, mc):
        v = di["w_branch"][l].rearrange("br (kc p) (mm c) -> p kc br mm c", p=128, mm=8)
        return v[:, :, :, mc, :]

    def wsrc_sq(name, l, rc):
        v = di[name][l].rearrange("(kc p) (rr c) -> p kc rr c", p=128, rr=8)
        return v[:, :, rc, :]

    for l_ in range(L):
        build_EB(l_)
    P.barrier()
    import os as _os
    KSTOP = int(_os.environ.get("KSTOP", "99"))
    LPS = [(b_, hf_, l_) for b_ in range(NSEQ if (KSTOP > 0 and not _os.environ.get("KSKIPP")) else 0) for hf_ in range(NHF) for l_ in range(L)]
    for lpi, (b, hf, l) in enumerate(LPS):
        if True:
            last = (hf == NHF - 1)
            if True:
                CFG["pTpar"] = lpi % 2
                jobs = []
                for j in range(4):
                    jobs.append((lambda wi, j=j, l=l: load_w(wi, [
                        (wb[wi][:, 0:4096].rearrange("p (kc g c) -> p kc g c", kc=8, g=4), wsrc_att(l, j)),
                        (wb[wi][:, 4096:6144].rearrange("p (kc g c) -> p kc g c", kc=8, g=2), wsrc_lru(l, j))]),
                        lambda wi, j=j, l=l: interleave(att_job(l, b, hf, j, wi, last), lru_job(l, b, hf, j, wi, last, woff=4096))))
                for j in range(4):
                    jobs.append((lambda wi, j=j, l=l: load_w(wi, [(wb[wi][:, 0:4096].rearrange("p (kc g c) -> p kc g c", kc=8, g=4), wsrc_ret(l, j))]),
                                 lambda wi, j=j, l=l: ret_job(l, b, hf, j, wis[4:8], last)))
                for mc in range(8):
                    jobs.append((lambda wi, mc=mc, l=l: load_w(wi, [
                        (wb[wi][:, 0:3072].rearrange("p (kc g c) -> p kc g c", kc=8, g=3), wsrc_gate(l, mc)),
                        (wb[wi][:, 3072:4608].rearrange("p (kc g c) -> p kc g c", kc=4, g=3), wsrc_br(l, mc))]),
                        lambda wi, mc=mc, l=l: d1_job(l, mc, wi)))
                for rc in range(8):
                    jobs.append((lambda wi, rc=rc, l=l: load_w(wi, [(wb[wi][:, 0:1024].rearrange("p (kc c) -> p kc c", kc=8), wsrc_sq("w_out", l, rc))]),
                                 lambda wi, rc=rc, l=l: d2_job(l, rc, wi)))
                for rc in range(8):
                    jobs.append((lambda wi, rc=rc, l=l: load_w(wi, [
                        (wb[wi][:, 0:1024].rearrange("p (kc c) -> p kc c", kc=8), wsrc_sq("w_ple_gate", l, rc)),
                        (wb[wi][:, 1024:1280].rearrange("p (kc c) -> p kc c", kc=2),
                         di["w_ple"][l].rearrange("(kc p) (rr c) -> p kc rr c", p=128, rr=8)[:, :, rc, :])]),
                        lambda wi, rc=rc, l=l: d3_job(l, rc, wi)))
                wis = [nxt("wb", 3) for _ in jobs]
                P.barrier()
                KPRE = int(_os.environ.get("KPRE", "15"))
                if KPRE & 8:
                    jobs[0][0](wis[0])
                if l == 0 and (KPRE & 1):
                    load_x(b, hf)
                if lpi == 0:
                    load_p(l, b, hf, 0)
                if hf == 0:
                    for j in range(4):
                        P.op("dve", lambda e, j=j, l=l: e.memset(Scar[:, l, j, :], 0.0), writes=[f"Scar{l}_{j}"])
                        P.op("dve", lambda e, j=j, l=l: e.memset(convcar[:, l, j, :], 0.0), writes=[f"convcar{l}_{j}"])
                        P.op("dve", lambda e, j=j, l=l: e.memset(hcar[:, l, j:j + 1], 0.0), writes=[f"hcar{l}_{j}"])
                pass
                if KSTOP < 99 and (b, hf, l) != (0, 0, 0):
                    continue
                if KSTOP <= 1:
                    continue
                if len(jobs) > 1:
                    jobs[1][0](wis[1])
                for k, (ld, cp) in enumerate(jobs):
                    if KSTOP == 2 and k >= 4 or KSTOP == 3 and k >= 8 or KSTOP == 5 and k >= 16:
                        break
                    if k in (0, 4, 8):
                        P.barrier()
                    if k + 2 < len(jobs):
                        jobs[k + 2][0](wis[k + 2])
                    cp(wis[k])
                    if k == 15 and lpi + 1 < len(LPS) and KSTOP >= 99:
                        nb_, nhf_, nl_ = LPS[lpi + 1]
                        load_p(nl_, nb_, nhf_, (lpi + 1) % 2)
                if KSTOP >= 7:
                    ln_phase(l, b, hf, write_y=(l == L - 1))

    if NSMP > 0 and KSTOP >= 99:
        CFG["w"] = NSMP * 32
        CFG["ntt"] = 1
        CFG["pTpar"] = 0
        for l in range(L):
            jobs = []
            for j in range(4):
                jobs.append((lambda wi, j=j, l=l: load_w(wi, [(wb[wi][:, 0:4096].rearrange("p (kc g c) -> p kc g c", kc=8, g=4), wsrc_att(l, j))]),
                             lambda wi, j=j, l=l: s_att_job(l, j, wi)))
            for j in range(4):
                jobs.append((lambda wi, j=j, l=l: load_w(wi, [(wb[wi][:, 0:4096].rearrange("p (kc g c) -> p kc g c", kc=8, g=4), wsrc_ret(l, j))]),
                             lambda wi, j=j, l=l: s_ret_job(l, j, wi)))
            for c in range(4):
                jobs.append((lambda wi, c=c, l=l: load_w(wi, [(wb[wi][:, 0:2048].rearrange("p (kc g c) -> p kc g c", kc=8, g=2), wsrc_lru(l, c))]),
                             lambda wi, c=c, l=l: s_lru_job(l, c, wi)))
            for mc in range(8):
                jobs.append((lambda wi, mc=mc, l=l: load_w(wi, [
                    (wb[wi][:, 0:3072].rearrange("p (kc g c) -> p kc g c", kc=8, g=3), wsrc_gate(l, mc)),
                    (wb[wi][:, 3072:4608].rearrange("p (kc g c) -> p kc g c", kc=4, g=3), wsrc_br(l, mc))]),
                    lambda wi, mc=mc, l=l: d1_job(l, mc, wi)))
            for rc in range(8):
                jobs.append((lambda wi, rc=rc, l=l: load_w(wi, [(wb[wi][:, 0:1024].rearrange("p (kc c) -> p kc c", kc=8), wsrc_sq("w_out", l, rc))]),
                             lambda wi, rc=rc, l=l: d2_job(l, rc, wi)))
            for rc in range(8):
                jobs.append((lambda wi, rc=rc, l=l: load_w(wi, [
                    (wb[wi][:, 0:1024].rearrange("p (kc c) -> p kc c", kc=8), wsrc_sq("w_ple_gate", l, rc)),
                    (wb[wi][:, 1024:1280].rearrange("p (kc c) -> p kc c", kc=2),
                     di["w_ple"][l].rearrange("(kc p) (rr c) -> p kc rr c", p=128, rr=8)[:, :, rc, :])]),
                    lambda wi, rc=rc, l=l: d3_job(l, rc, wi)))
            wis = [nxt("wb", 3) for _ in jobs]
            P.barrier()
            jobs[0][0](wis[0])
            s_load(l)
            jobs[1][0](wis[1])
            for k, (ld, cp) in enumerate(jobs):
                if k in (0, 4, 8, 12):
                    P.barrier()
                if k + 2 < len(jobs):
                    jobs[k + 2][0](wis[k + 2])
                cp(wis[k])
            s_ln(l, write_y=(l == L - 1))

    P.wait_all("sp")
    print("PROG nrec", P.nrec, {e: len(v) for e, v in P.ops.items()})
    if _os.environ.get("KLOG"):
        with open(_os.environ["KLOG"], "w") as f:
            for r in P.log:
                f.write(repr(r) + "\n")
    with nc.allow_non_contiguous_dma(reason="small param / state vectors"):
        P.emit(sems)
    es.close()
    return nc


OUT_NAMES = ["y_prompt", "y_sample", "k_a_prompt", "v_a_prompt", "k_a_sample", "v_a_sample",
             "ret_prompt", "ret_sample", "conv_prompt", "conv_sample", "lru_prompt", "lru_sample"]


def kernel(**inputs):
    NSEQ = 4
    NSMP = 2
    nc = build(NSEQ=NSEQ, NSMP=NSMP)
    consts = host_consts()
    in_maps = []
    for c in range(N_CORES):
        m = {}
        m["x_prompt"] = np.ascontiguousarray(inputs["x_prompt"][c * NSEQ:(c + 1) * NSEQ])
        m["p_prompt"] = np.ascontiguousarray(inputs["p_prompt"][:, c * NSEQ:(c + 1) * NSEQ])
        m["x_sample"] = np.ascontiguousarray(inputs["x_sample"][c * NSMP:(c + 1) * NSMP])
        for k in ("p_sample", "cache_k_a", "cache_v_a", "state_ret", "state_conv", "state_lru"):
            m[k] = np.ascontiguousarray(inputs[k][:, c * NSMP:(c + 1) * NSMP])
        for k in W_NAMES:
            m[k] = np.ascontiguousarray(inputs[k])
        m.update(consts)
        in_maps.append(m)
    res = run_bass_kernel_spmd(nc, in_maps, core_ids=list(range(N_CORES)))
    R = res.results
    out = {}
    out["y_prompt"] = np.concatenate([r["y_prompt"] for r in R], 0)
    out["y_sample"] = np.concatenate([r["y_sample"] for r in R], 0)
    for nm in ("k_a_prompt", "v_a_prompt", "ret_prompt", "conv_prompt", "lru_prompt",
               "k_a_sample", "v_a_sample", "ret_sample", "conv_sample", "lru_sample"):
        out[nm] = np.concatenate([r[nm] for r in R], 1)
    return tuple(np.asarray(out[n], dtype=np.float32) for n in OUT_NAMES)
```
